# Optimizing a Trainium2 kernel written in Bass

```python
import math
import jax
import jax.numpy as jnp
from jax import lax
import numpy as np

D_MODEL = 2048
BATCH = 16
SEQ = 2048
DEPTH = 4

GRID_W = 64
CTX_LEN = 256
N_BRANCH = 4
BRANCH_W = D_MODEL // N_BRANCH
MIX_W = N_BRANCH * BRANCH_W
NORM_EPS = 1e-6
F32 = jnp.float32
S5_H = 16
S5_G = BRANCH_W // S5_H
S5_P = 64
HEAD_DIM = 64
ATT_HEADS = BRANCH_W // HEAD_DIM
ATT_KV_HEADS = 2
ATT_REP = ATT_HEADS // ATT_KV_HEADS
WINDOW = 128
BLOCK = 128
ROPE_BASE = 10000.0
NEG_INF = -1e30
RW_N = 64
RW_HEADS = BRANCH_W // RW_N
RW_LORA = 32
RW_SHIFT_W = 3 * BRANCH_W + 4 * RW_LORA
RW_LN_EPS = 64e-5
HY_ORDER = 2
HY_EMB = 33
HY_BANDS = (HY_EMB - 1) // 2
HY_FFN = 64
HY_N_FILT = 2 * HY_ORDER
HY_TARGET = 1e-2
HY_FAST_PCT = 0.3
HY_SLOW_PCT = 1.5
P_S5 = BRANCH_W
P_Q = ATT_HEADS * HEAD_DIM
P_KV = ATT_KV_HEADS * HEAD_DIM
P_RKV = 3 * BRANCH_W
P_LORA = 4 * RW_LORA
P_HY = (HY_ORDER + 1) * BRANCH_W
P_GATE = MIX_W
D_IN = P_S5 + P_Q + 2 * P_KV + P_RKV + P_LORA + P_HY + P_GATE

kernel_name = 'hybrid_parallel_heads_dit_block'


def rms_norm(x, g, eps=NORM_EPS):
    xf = x.astype(F32)
    y = xf * lax.rsqrt(jnp.mean(xf * xf, axis=-1, keepdims=True) + eps)
    return (y * g.astype(F32)).astype(x.dtype)


def centred_shift(z):
    prev = jnp.pad(z[:, :-1], ((0, 0), (1, 0), (0, 0)))
    nxt = jnp.pad(z[:, 1:], ((0, 0), (0, 1), (0, 0)))
    return prev, nxt


def centred_conv3(z, w, b):
    prev, nxt = centred_shift(z)
    return prev * w[0] + z * w[1] + nxt * w[2] + b


def split_proj(p):
    sizes = (P_S5, P_Q, P_KV, P_KV, P_RKV, P_LORA, P_HY, P_GATE)
    parts, start = [], 0
    for s in sizes:
        parts.append(p[..., start:start + s])
        start += s
    return parts


def s5_discretise(lam_re, lam_im, log_step, b_re, b_im):
    lam_re, lam_im = lam_re.astype(F32), lam_im.astype(F32)
    step = jnp.exp(log_step.astype(F32))[:, None]
    mag = jnp.exp(lam_re * step)
    lb_re, lb_im = mag * jnp.cos(lam_im * step), mag * jnp.sin(lam_im * step)
    den = lam_re * lam_re + lam_im * lam_im
    nr = lb_re - 1.0
    co_re = (nr * lam_re + lb_im * lam_im) / den
    co_im = (lb_im * lam_re - nr * lam_im) / den
    b_re, b_im = b_re.astype(F32), b_im.astype(F32)
    bb_re = co_re[..., None] * b_re - co_im[..., None] * b_im
    bb_im = co_re[..., None] * b_im + co_im[..., None] * b_re
    return lb_re, lb_im, bb_re, bb_im


def _ssm_combine(e1, e2):
    a1r, a1i, b1r, b1i = e1
    a2r, a2i, b2r, b2i = e2
    return (a2r * a1r - a2i * a1i, a2r * a1i + a2i * a1r,
            a2r * b1r - a2i * b1i + b2r, a2r * b1i + a2i * b1r + b2i)


def s5_states(lb_re, lb_im, bb_re, bb_im, u_tm, reverse, h0):
    if reverse:
        u_tm = u_tm[::-1]
    bu_re = jnp.einsum('lbgh,gph->lbgp', u_tm, bb_re)
    bu_im = jnp.einsum('lbgh,gph->lbgp', u_tm, bb_im)
    shape = (u_tm.shape[0], 1) + lb_re.shape
    a_re, a_im, h_re, h_im = lax.associative_scan(
        _ssm_combine,
        (jnp.broadcast_to(lb_re, shape), jnp.broadcast_to(lb_im, shape), bu_re, bu_im), axis=0)
    if h0 is not None:
        h_re, h_im = (h_re + a_re * h0[0] - a_im * h0[1],
                      h_im + a_re * h0[1] + a_im * h0[0])
    return h_re, h_im


def s5_readout(h, c_re, c_im, reverse):
    y = jnp.einsum('lbgp,ghp->lbgh', h[0], c_re) - jnp.einsum('lbgp,ghp->lbgh', h[1], c_im)
    return y[::-1] if reverse else y


def s5_mixer(u_lat, u_ctx, lam_re, lam_im, log_step, b_re, b_im, c_re, c_im, d_skip,
             glu_w, glu_b, ctx_out):
    dtype = u_lat.dtype

    def to_tm(u):
        return jnp.swapaxes(u.astype(F32).reshape(u.shape[0], u.shape[1], S5_G, S5_H), 0, 1)

    ul, uc = to_tm(u_lat), to_tm(u_ctx)
    d = d_skip.astype(F32)
    y_lat = d * ul
    y_ctx = d * uc if ctx_out else None
    for di, rev in enumerate((False, True)):
        lb_re, lb_im, bb_re, bb_im = s5_discretise(lam_re[di], lam_im[di], log_step[di],
                                                   b_re[di], b_im[di])
        cr, ci = c_re[di].astype(F32), c_im[di].astype(F32)
        hc = s5_states(lb_re, lb_im, bb_re, bb_im, uc, rev, None)
        hl = s5_states(lb_re, lb_im, bb_re, bb_im, ul, rev, (hc[0][-1], hc[1][-1]))
        y_lat = y_lat + s5_readout(hl, cr, ci, rev)
        if ctx_out:
            y_ctx = y_ctx + s5_readout(hc, cr, ci, rev)

    def glu(y_tm):
        y = jnp.swapaxes(y_tm, 0, 1)
        y = jax.nn.gelu(y.reshape(y.shape[0], y.shape[1], BRANCH_W), approximate=False)
        y = y * jax.nn.sigmoid(y @ glu_w.astype(F32) + glu_b.astype(F32))
        return y.astype(dtype)

    return glu(y_lat), (glu(y_ctx) if ctx_out else None)


def axial_rope(t, row, col):
    half = HEAD_DIM // 2
    quarter = half // 2
    inv = 1.0 / (ROPE_BASE ** (jnp.arange(quarter, dtype=F32) / quarter))

    def rot(u, pos):
        ang = pos.astype(F32)[:, None] * inv[None, :]
        cos, sin = jnp.cos(ang)[None, :, None, :], jnp.sin(ang)[None, :, None, :]
        u = u.astype(F32)
        u1, u2 = u[..., :quarter], u[..., quarter:]
        return jnp.concatenate([u1 * cos - u2 * sin, u2 * cos + u1 * sin], axis=-1)

    return jnp.concatenate([rot(t[..., :half], row), rot(t[..., half:], col)], axis=-1).astype(t.dtype)


def attn_mixer(q_l, k_l, v_l, q_c, k_c, v_c, q_g, k_g, sink, row, col, ctx_out):
    b_, n_, _ = q_l.shape
    n_ctx = k_c.shape[1]

    def heads(t, h):
        return t.reshape(t.shape[0], t.shape[1], h, HEAD_DIM)

    q = axial_rope(rms_norm(heads(q_l, ATT_HEADS), q_g), row, col)
    k = axial_rope(rms_norm(heads(k_l, ATT_KV_HEADS), k_g), row, col)
    v = heads(v_l, ATT_KV_HEADS)
    kc = rms_norm(heads(k_c, ATT_KV_HEADS), k_g)
    vc = heads(v_c, ATT_KV_HEADS)
    scale = HEAD_DIM ** -0.5
    sink_f = sink.astype(F32).reshape(ATT_KV_HEADS, ATT_REP)

    nb = n_ // BLOCK
    qb = q.reshape(b_, nb, BLOCK, ATT_KV_HEADS, ATT_REP, HEAD_DIM)

    def band(t):
        tp = jnp.pad(t, ((0, 0), (BLOCK, BLOCK), (0, 0), (0, 0)))
        tp = tp.reshape(b_, nb + 2, BLOCK, ATT_KV_HEADS, HEAD_DIM)
        return jnp.concatenate([tp[:, :-2], tp[:, 1:-1], tp[:, 2:]], axis=2)

    kw, vw = band(k), band(v)
    qpos = jnp.arange(nb)[:, None] * BLOCK + jnp.arange(BLOCK)[None, :]
    kpos = jnp.arange(nb)[:, None] * BLOCK - BLOCK + jnp.arange(3 * BLOCK)[None, :]
    valid = ((jnp.abs(qpos[:, :, None] - kpos[:, None, :]) <= WINDOW)
             & ((kpos >= 0) & (kpos < n_))[:, None, :])
    s_loc = jnp.einsum('bnqhrd,bnkhd->bnhrqk', qb, kw).astype(F32) * scale
    s_loc = jnp.where(valid[None, :, None, None], s_loc, NEG_INF)
    s_ctx = jnp.einsum('bnqhrd,bchd->bnhrqc', qb, kc).astype(F32) * scale
    s_sink = jnp.broadcast_to(sink_f[None, None, :, :, None, None], s_loc.shape[:-1] + (1,))
    p = jax.nn.softmax(jnp.concatenate([s_loc, s_ctx, s_sink], axis=-1), axis=-1).astype(v.dtype)
    o = (jnp.einsum('bnhrqk,bnkhd->bnqhrd', p[..., :3 * BLOCK], vw)
         + jnp.einsum('bnhrqc,bchd->bnqhrd', p[..., 3 * BLOCK:3 * BLOCK + n_ctx], vc))
    o_lat = o.reshape(b_, n_, ATT_HEADS * HEAD_DIM)

    o_ctx = None
    if ctx_out:
        qc = rms_norm(heads(q_c, ATT_HEADS), q_g).reshape(b_, n_ctx, ATT_KV_HEADS, ATT_REP, HEAD_DIM)
        s = jnp.einsum('bqhrd,bchd->bhrqc', qc, kc).astype(F32) * scale
        s_sink_c = jnp.broadcast_to(sink_f[None, :, :, None, None], s.shape[:-1] + (1,))
        pc = jax.nn.softmax(jnp.concatenate([s, s_sink_c], axis=-1), axis=-1).astype(vc.dtype)
        o_ctx = jnp.einsum('bhrqc,bchd->bqhrd', pc[..., :n_ctx], vc).reshape(b_, n_ctx, ATT_HEADS * HEAD_DIM)
    return o_lat, o_ctx


def rwkv_features(rkv, lora, mu_prev, mu_next):
    z = jnp.concatenate([rkv, lora], axis=-1).astype(F32)
    prev, nxt = centred_shift(z)
    z = z + mu_prev * (prev - z) + mu_next * (nxt - z)
    b_, l_, _ = z.shape

    def heads(t):
        return t.reshape(b_, l_, RW_HEADS, RW_N)

    r = heads(z[..., :BRANCH_W])
    k = heads(z[..., BRANCH_W:2 * BRANCH_W])
    v = heads(z[..., 2 * BRANCH_W:3 * BRANCH_W])
    return r, k, v, z[..., 3 * BRANCH_W:]


def rwkv_kk(k, k_k):
    kk = k * k_k
    return kk * lax.rsqrt(jnp.sum(kk * kk, axis=-1, keepdims=True) + 1e-12)


def rwkv_direction(k, kk, lora, di, w0, w2, a0, a2, k_a):
    b_, l_ = k.shape[:2]
    wl = lora[..., di * RW_LORA:(di + 1) * RW_LORA]
    al = lora[..., (2 + di) * RW_LORA:(3 + di) * RW_LORA]
    w_log = -jax.nn.softplus(-(w0[di].astype(F32) + jnp.tanh(wl) @ w2[di].astype(F32))) - 0.5
    decay = jnp.exp(-jnp.exp(w_log)).reshape(b_, l_, RW_HEADS, RW_N)
    a = jax.nn.sigmoid(a0[di].astype(F32) + al @ a2[di].astype(F32)).reshape(b_, l_, RW_HEADS, RW_N)
    return decay, k * (1.0 + (a - 1.0) * k_a), kk * a


def _rwkv_step(state, inp):
    r_t, w_t, k_t, v_t, kk_t, b_t = inp
    sa = jnp.einsum('bhij,bhj->bhi', state, -kk_t)
    state = (state * w_t[:, :, None, :] + sa[..., None] * b_t[:, :, None, :]
             + v_t[..., None] * k_t[:, :, None, :])
    return state, jnp.einsum('bhij,bhj->bhi', state, r_t)


def rwkv_run(r, decay, k, v, kk, b, s0, reverse):
    def tm(t):
        t = jnp.swapaxes(t, 0, 1)
        return t[::-1] if reverse else t

    s_fin, y = lax.scan(_rwkv_step, s0, (tm(r), tm(decay), tm(k), tm(v), tm(kk), tm(b)))
    y = y[::-1] if reverse else y
    return s_fin, jnp.swapaxes(y, 0, 1)


def rwkv_mixer(rkv_l, lora_l, rkv_c, lora_c, mu_prev, mu_next, w0, w2, a0, a2,
               k_k, k_a, r_k, ln_g, ln_b, ctx_out):
    dtype = rkv_l.dtype
    mu_prev, mu_next = mu_prev.astype(F32), mu_next.astype(F32)
    k_k = k_k.astype(F32).reshape(RW_HEADS, RW_N)
    k_a = k_a.astype(F32).reshape(RW_HEADS, RW_N)
    r_k = r_k.astype(F32)
    lat = rwkv_features(rkv_l, lora_l, mu_prev, mu_next)
    ctx = rwkv_features(rkv_c, lora_c, mu_prev, mu_next)
    kk_l, kk_c = rwkv_kk(lat[1], k_k), rwkv_kk(ctx[1], k_k)
    b_ = rkv_l.shape[0]
    y_l, bonus_l = jnp.zeros_like(lat[0]), jnp.zeros_like(lat[0][..., :1])
    y_c, bonus_c = jnp.zeros_like(ctx[0]), jnp.zeros_like(ctx[0][..., :1])
    for di, rev in enumerate((False, True)):
        dec_c, kd_c, b_c = rwkv_direction(ctx[1], kk_c, ctx[3], di, w0, w2, a0, a2, k_a)
        dec_l, kd_l, b_l = rwkv_direction(lat[1], kk_l, lat[3], di, w0, w2, a0, a2, k_a)
        s0 = jnp.zeros((b_, RW_HEADS, RW_N, RW_N), F32)
        s_c, yc = rwkv_run(ctx[0], dec_c, kd_c, ctx[2], kk_c, b_c, s0, rev)
        _, yl = rwkv_run(lat[0], dec_l, kd_l, lat[2], kk_l, b_l, s_c, rev)
        y_l = y_l + yl
        bonus_l = bonus_l + jnp.sum(lat[0] * kd_l * r_k, axis=-1, keepdims=True)
        if ctx_out:
            y_c = y_c + yc
            bonus_c = bonus_c + jnp.sum(ctx[0] * kd_c * r_k, axis=-1, keepdims=True)

    def finish(y, bonus, v):
        mu = jnp.mean(y, axis=-1, keepdims=True)
        var = jnp.mean(jnp.square(y - mu), axis=-1, keepdims=True)
        y = (y - mu) * lax.rsqrt(var + RW_LN_EPS)
        bb, ll = y.shape[:2]
        y = y.reshape(bb, ll, BRANCH_W) * ln_g.astype(F32) + ln_b.astype(F32)
        return (y + (bonus * v).reshape(bb, ll, BRANCH_W)).astype(dtype)

    return finish(y_l, bonus_l, lat[2]), (finish(y_c, bonus_c, ctx[2]) if ctx_out else None)


def hyena_filters(n, w1, b1, f1, w2, b2, f2, w3):
    t = jnp.linspace(0.0, 1.0, n, dtype=F32)[:, None]
    ang = 2.0 * math.pi * jnp.arange(n, dtype=F32)[:, None] / n
    bands = jnp.linspace(1e-4, HY_BANDS - 1, HY_BANDS, dtype=F32)[None, :]
    z = jnp.concatenate([t, jnp.cos(bands * ang), -jnp.sin(bands * ang)], axis=-1)
    h = jnp.sin(f1.astype(F32) * (z @ w1.astype(F32) + b1.astype(F32)))
    h = jnp.sin(f2.astype(F32) * (h @ w2.astype(F32) + b2.astype(F32)))
    h = (h @ w3.astype(F32)).reshape(n, HY_N_FILT, BRANCH_W)
    deltas = jnp.abs(jnp.linspace(math.log(HY_TARGET) / HY_SLOW_PCT, math.log(HY_TARGET) / HY_FAST_PCT,
                                  BRANCH_W, dtype=F32))
    return h * jnp.exp(-t[:, :, None] * deltas)


def bidir_fftconv(u, h_f, h_b, skip):
    n = u.shape[1]
    g = jnp.concatenate([h_f, jnp.zeros_like(h_f[:1]), h_b[1:][::-1]], axis=0)
    gf = jnp.fft.rfft(g, axis=0)
    uf = jnp.fft.rfft(u, n=2 * n, axis=1)
    y = jnp.fft.irfft(uf * gf[None], n=2 * n, axis=1)[:, :n]
    return y + skip * u


def hyena_seq(z, conv_w, conv_b, w1, b1, f1, w2, b2, f2, w3, skip):
    dtype = z.dtype
    z = centred_conv3(z.astype(F32), conv_w.astype(F32), conv_b.astype(F32))
    v, x1, x2 = z[..., :BRANCH_W], z[..., BRANCH_W:2 * BRANCH_W], z[..., 2 * BRANCH_W:]
    h = hyena_filters(z.shape[1], w1, b1, f1, w2, b2, f2, w3)
    skip = skip.astype(F32)
    y = x1 * bidir_fftconv(v, h[:, 0], h[:, 1], skip[0])
    y = x2 * bidir_fftconv(y, h[:, 2], h[:, 3], skip[1])
    return y.astype(dtype)


def merge_heads(y_s5, y_att, y_rw, y_hy, gate_pre, branch_g, w_out):
    y = jnp.concatenate([rms_norm(y_s5, branch_g[0]), rms_norm(y_att, branch_g[1]), y_rw,
                         rms_norm(y_hy, branch_g[2])], axis=-1)
    return (y * jax.nn.silu(gate_pre)) @ w_out


def setup_inputs(seed: int = 0) -> dict:
    key = jax.random.key(seed)
    keys = iter(jax.random.split(key, 64))
    W = BRANCH_W

    def nrm(shape, std):
        return std * jax.random.normal(next(keys), shape, F32)

    def uni(shape, lo, hi):
        return jax.random.uniform(next(keys), shape, F32, lo, hi)

    s5_n = jnp.arange(S5_P, dtype=F32)
    return {
        'x': nrm((BATCH, SEQ, D_MODEL), 1.0),
        'c': nrm((BATCH, D_MODEL), 1.0),
        'ctx': nrm((BATCH, CTX_LEN, D_MODEL), 1.0),
        'c_ctx': nrm((D_MODEL,), 1.0),
        'norm_g': 1.0 + nrm((DEPTH, D_MODEL), 0.02),
        'w_ada': nrm((DEPTH, D_MODEL, 3 * D_MODEL), 0.5 * D_MODEL ** -0.5),
        'b_ada': nrm((DEPTH, 3 * D_MODEL), 0.02),
        'w_in': nrm((DEPTH, D_MODEL, D_IN), D_MODEL ** -0.5),
        'w_out': nrm((DEPTH, MIX_W, D_MODEL), MIX_W ** -0.5),
        'branch_g': 1.0 + nrm((DEPTH, 3, W), 0.02),
        's5_lam_re': -0.5 + nrm((DEPTH, 2, S5_G, S5_P), 0.01),
        's5_lam_im': math.pi * s5_n + nrm((DEPTH, 2, S5_G, S5_P), 0.01),
        's5_log_step': uni((DEPTH, 2, S5_G), math.log(1e-3), math.log(1e-1)),
        's5_b_re': nrm((DEPTH, 2, S5_G, S5_P, S5_H), (2 * S5_H) ** -0.5),
        's5_b_im': nrm((DEPTH, 2, S5_G, S5_P, S5_H), (2 * S5_H) ** -0.5),
        's5_c_re': nrm((DEPTH, 2, S5_G, S5_H, S5_P), (2 * S5_P) ** -0.5),
        's5_c_im': nrm((DEPTH, 2, S5_G, S5_H, S5_P), (2 * S5_P) ** -0.5),
        's5_d': nrm((DEPTH, S5_G, S5_H), 1.0),
        's5_glu_w': nrm((DEPTH, W, W), W ** -0.5),
        's5_glu_b': nrm((DEPTH, W), 0.02),
        'att_q_g': 1.0 + nrm((DEPTH, HEAD_DIM), 0.02),
        'att_k_g': 1.0 + nrm((DEPTH, HEAD_DIM), 0.02),
        'att_sink': nrm((DEPTH, ATT_HEADS), 0.5),
        'rw_mu_prev': uni((DEPTH, RW_SHIFT_W), 0.0, 0.5),
        'rw_mu_next': uni((DEPTH, RW_SHIFT_W), 0.0, 0.5),
        'rw_w0': jnp.linspace(-6.0, -1.0, W, dtype=F32) + nrm((DEPTH, 2, W), 0.1),
        'rw_w2': nrm((DEPTH, 2, RW_LORA, W), 0.1),
        'rw_a0': nrm((DEPTH, 2, W), 0.1),
        'rw_a2': nrm((DEPTH, 2, RW_LORA, W), 0.1),
        'rw_k_k': 0.85 + nrm((DEPTH, W), 0.02),
        'rw_k_a': 1.0 + nrm((DEPTH, W), 0.02),
        'rw_r_k': nrm((DEPTH, RW_HEADS, RW_N), 0.1),
        'rw_ln_g': 1.0 + nrm((DEPTH, W), 0.02),
        'rw_ln_b': nrm((DEPTH, W), 0.02),
        'hy_conv_w': nrm((DEPTH, 3, 3 * W), 3 ** -0.5),
        'hy_conv_b': nrm((DEPTH, 3 * W), 0.02),
        'hy_w1': nrm((DEPTH, HY_EMB, HY_FFN), HY_EMB ** -0.5),
        'hy_b1': nrm((DEPTH, HY_FFN), 0.1),
        'hy_f1': 1.0 + nrm((DEPTH, HY_FFN), 0.05),
        'hy_w2': nrm((DEPTH, HY_FFN, HY_FFN), HY_FFN ** -0.5),
        'hy_b2': nrm((DEPTH, HY_FFN), 0.1),
        'hy_f2': 1.0 + nrm((DEPTH, HY_FFN), 0.05),
        'hy_w3': nrm((DEPTH, HY_FFN, HY_N_FILT * W), HY_FFN ** -0.5),
        'hy_skip': nrm((DEPTH, 2, W), 0.5),
    }


def reference(x, c, ctx, c_ctx, norm_g, w_ada, b_ada, w_in, w_out, branch_g,
              s5_lam_re, s5_lam_im, s5_log_step, s5_b_re, s5_b_im, s5_c_re, s5_c_im, s5_d,
              s5_glu_w, s5_glu_b, att_q_g, att_k_g, att_sink,
              rw_mu_prev, rw_mu_next, rw_w0, rw_w2, rw_a0, rw_a2, rw_k_k, rw_k_a, rw_r_k,
              rw_ln_g, rw_ln_b, hy_conv_w, hy_conv_b, hy_w1, hy_b1, hy_f1, hy_w2, hy_b2, hy_f2,
              hy_w3, hy_skip):
    n_tok = x.shape[1]
    rows = n_tok // GRID_W
    row = jnp.repeat(jnp.arange(rows, dtype=jnp.int32), GRID_W)
    col = jnp.tile(jnp.arange(GRID_W, dtype=jnp.int32), rows)
    xc = ctx
    silu_c, silu_cc = jax.nn.silu(c), jax.nn.silu(c_ctx)
    for l in range(DEPTH):
        ctx_out = l < DEPTH - 1
        mod_x = silu_c @ w_ada[l] + b_ada[l]
        shift_x, scale_x, gate_x = jnp.split(mod_x[:, None, :], 3, axis=-1)
        mod_c = silu_cc @ w_ada[l] + b_ada[l]
        shift_c, scale_c, gate_c = jnp.split(mod_c, 3)
        hx = rms_norm(x, norm_g[l]) * (1.0 + scale_x) + shift_x
        hc = rms_norm(xc, norm_g[l]) * (1.0 + scale_c) + shift_c
        s5u_l, q_l, k_l, v_l, rkv_l, lora_l, hy_l, g_l = split_proj(hx @ w_in[l])
        s5u_c, q_c, k_c, v_c, rkv_c, lora_c, hy_c, g_c = split_proj(hc @ w_in[l])

        y_s5_l, y_s5_c = s5_mixer(s5u_l, s5u_c, s5_lam_re[l], s5_lam_im[l], s5_log_step[l],
                                  s5_b_re[l], s5_b_im[l], s5_c_re[l], s5_c_im[l], s5_d[l],
                                  s5_glu_w[l], s5_glu_b[l], ctx_out)
        y_att_l, y_att_c = attn_mixer(q_l, k_l, v_l, q_c, k_c, v_c, att_q_g[l], att_k_g[l],
                                      att_sink[l], row, col, ctx_out)
        y_rw_l, y_rw_c = rwkv_mixer(rkv_l, lora_l, rkv_c, lora_c, rw_mu_prev[l], rw_mu_next[l],
                                    rw_w0[l], rw_w2[l], rw_a0[l], rw_a2[l], rw_k_k[l], rw_k_a[l],
                                    rw_r_k[l], rw_ln_g[l], rw_ln_b[l], ctx_out)
        y_hy_l = hyena_seq(hy_l, hy_conv_w[l], hy_conv_b[l], hy_w1[l], hy_b1[l], hy_f1[l],
                           hy_w2[l], hy_b2[l], hy_f2[l], hy_w3[l], hy_skip[l])
        x = x + gate_x * merge_heads(y_s5_l, y_att_l, y_rw_l, y_hy_l, g_l, branch_g[l], w_out[l])
        if ctx_out:
            y_hy_c = hyena_seq(hy_c, hy_conv_w[l], hy_conv_b[l], hy_w1[l], hy_b1[l], hy_f1[l],
                               hy_w2[l], hy_b2[l], hy_f2[l], hy_w3[l], hy_skip[l])
            xc = xc + gate_c * merge_heads(y_s5_c, y_att_c, y_rw_c, y_hy_c, g_c, branch_g[l], w_out[l])
    return x
```

```python
import math
from contextlib import ExitStack

import numpy as np
import ml_dtypes

import concourse.bass as bass
import concourse.mybir as mybir
from concourse.bass_utils import run_bass_kernel_spmd

F32 = mybir.dt.float32
BF16 = mybir.dt.bfloat16
I32 = mybir.dt.int32
AF = mybir.ActivationFunctionType
ALU = mybir.AluOpType
AX = mybir.AxisListType

D = 2048
DEPTH = 4
NB = 2
LCTX = 256
LLAT = 2048
LS = LCTX + LLAT
D_IN = 6528
W = 512
EPS = 1e-6
N_CORES = 8
R_S5, R_RKV, R_LORA, R_HY, R_GATE, R_FM = 0, 512, 2048, 2176, 3712, 5760
C_S5, C_Q, C_K, C_V, C_RKV, C_LORA, C_HY, C_GATE = 0, 512, 1024, 1152, 1280, 2816, 2944, 4480


class _Sem:
    __slots__ = ("h", "owner")

    def __init__(self, h, owner):
        self.h, self.owner = h, owner


class Buf:
    __slots__ = ("w", "r", "multi", "name")

    def __init__(self, name="", multi=False):
        self.w, self.r, self.multi, self.name = {}, {}, multi, name


class _Eng:
    def __init__(self, tk, name, h):
        self.tk, self.name, self.h = tk, name, h
        self.sem = tk._new_sem(name)
        self.cnt = 0
        self.seen = {}

    def tick(self):
        if self.cnt >= 30000:
            self.sem = self.tk._new_sem(self.name)
            self.cnt = 0
        self.cnt += 1
        return self.sem, self.cnt


class _Slot:
    def __init__(self, tk):
        self.sem = tk._new_sem(None)
        self.val = 0


class T:
    def __init__(self, t, name, multi=False):
        self.t = t
        self.b = Buf(name, multi)


class TK:
    def __init__(self, nc, es):
        self.nc, self.es = nc, es
        self.nsem = 0
        self.E = {}
        for n, h in (("pe", nc.tensor), ("act", nc.scalar), ("dve", nc.vector), ("pool", nc.gpsimd), ("sp", nc.sync)):
            self.E[n] = _Eng(self, n, h)
        self.slots = {"sp": [_Slot(self) for _ in range(14)], "pool": [_Slot(self) for _ in range(8)],
                      "act": [_Slot(self) for _ in range(6)]}
        self.slot_i = {q: 0 for q in self.slots}
        self.n_ins = 0

    def _new_sem(self, owner):
        self.nsem += 1
        h = self.es.enter_context(self.nc.semaphore(f"sm{self.nsem}"))
        return _Sem(h, owner)

    def sb(self, es, name, shape, dt, multi=False):
        self.uid = getattr(self, "uid", 0) + 1
        name = f"{name}_{self.uid}"
        return T(es.enter_context(self.nc.sbuf_tensor(name, shape, dt)), name, multi)

    def ps(self, es, name, shape, dt):
        return T(es.enter_context(self.nc.psum_tensor(name, shape, dt)), name)

    def dram(self, name, shape, dt, kind="Internal", multi=True):
        return T(self.nc.dram_tensor(name, shape, dt, kind=kind), name, multi)

    @staticmethod
    def _bufs(xs):
        return [x.b if isinstance(x, T) else x for x in xs]

    def _collect(self, r, w):
        deps = {}

        def add(d, raw):
            for k, (s, v) in d.items():
                if k not in deps or deps[k][1] < v:
                    deps[k] = (s, v, raw or (k in deps and deps[k][2]))
                elif raw:
                    deps[k] = (deps[k][0], deps[k][1], True)

        for b in r:
            add(b.w, True)
        for b in w:
            if b.multi and not b.r:
                continue
            add(b.w, False)
            add(b.r, False)
        return deps

    def _update(self, r, w, s, v):
        k = id(s)
        for b in w:
            if b.multi and not b.r:
                if k not in b.w or b.w[k][1] < v:
                    b.w[k] = (s, v)
            else:
                b.w = {k: (s, v)}
                b.r = {}
        for b in r:
            if k not in b.r or b.r[k][1] < v:
                b.r[k] = (s, v)

    def _wait(self, e, deps):
        for k, (s, v, raw) in deps.items():
            if s.owner == e.name and (not raw or e.name == "pe"):
                continue
            if e.seen.get(k, 0) >= v:
                continue
            e.h.wait_ge(s.h, v)
            e.seen[k] = v

    def op(self, eng, fn, r=(), w=()):
        e = self.E[eng]
        r, w = self._bufs(r), self._bufs(w)
        self._wait(e, self._collect(r, w))
        ins = fn(e.h)
        s, v = e.tick()
        ins.then_inc(s.h, 1)
        self._update(r, w, s, v)
        self.n_ins += 1

    def dma(self, q, out, in_, r=(), w=(), **kw):
        e = self.E[q]
        r, w = self._bufs(r), self._bufs(w)
        sl = self.slots[q][self.slot_i[q] % len(self.slots[q])]
        self.slot_i[q] += 1
        deps = self._collect(r, w)
        if sl.val:
            deps[id(sl.sem)] = (sl.sem, sl.val, True)
        self._wait(e, deps)
        ins = e.h.dma_start(out=out, in_=in_, **kw)
        sl.val += 16
        ins.then_inc(sl.sem.h, 16)
        self._update(r, w, sl.sem, sl.val)
        self.n_ins += 1

    def barrier(self, final=False):
        names = ["sp"] if final else list(self.E)
        for n in names:
            e = self.E[n]
            deps = {}
            for e2 in self.E.values():
                if e2.cnt:
                    deps[id(e2.sem)] = (e2.sem, e2.cnt, True)
            for q in self.slots.values():
                for sl in q:
                    if sl.val:
                        deps[id(sl.sem)] = (sl.sem, sl.val, True)
            self._wait(e, deps)


class Prog:
    def __init__(self, layers=DEPTH, dump=(), stop_after=None, mixers=("s5", "att", "rw", "hy")):
        self.layers = layers
        self.dump = set(dump)
        self.stop_after = stop_after
        self.mixers = mixers
        self.nc = bass.Bass("TRN2", target_bir_lowering=False)
        self.es = ExitStack()
        self.inputs = {}

    def inp(self, name, shape, dt=F32):
        t = self.nc.dram_tensor(name, list(shape), dt, kind="ExternalInput")
        self.inputs[name] = (tuple(shape), dt)
        return T(t, name)

    def build(self):
        nc = self.nc
        with self.es:
            tk = self.tk = TK(nc, self.es)
            self.declare_io()
            self.consts()
            for l in range(self.layers):
                self.layer(l)
            tk.barrier(final=True)
        return nc

    def declare_io(self):
        tk = self.tk
        L = DEPTH
        kd = lambda n: "ExternalOutput" if n in self.dump else "Internal"
        self.x_in = self.inp("x", [NB, LLAT, D])
        self.ctx_in = self.inp("ctx", [NB, LCTX, D])
        self.cT = self.inp("cT", [128, 16, 3])
        self.normgT = self.inp("normgT", [L, 128, 16])
        self.w_ada = self.inp("w_ada", [L, D, 3 * D])
        self.b_adaT = self.inp("b_adaT", [L, 128, 48])
        self.w_in = self.inp("w_in", [L, D, D_IN])
        self.w_out = self.inp("w_out", [L, D, D])
        self.s5_lreB = self.inp("s5_lreB", [L, 128, 4096])
        self.s5_limB = self.inp("s5_limB", [L, 128, 4096])
        self.s5_stpB = self.inp("s5_stpB", [L, 128, 4096])
        self.s5_bre = self.inp("s5_bre", [L, 128, 4096])
        self.s5_bim = self.inp("s5_bim", [L, 128, 4096])
        self.s5_c1 = self.inp("s5_c1", [L, 128, 8192])
        self.s5_c2 = self.inp("s5_c2", [L, 128, 8192])
        self.s5_lreP = self.inp("s5_lreP", [L, 128, 64])
        self.s5_limP = self.inp("s5_limP", [L, 128, 64])
        self.s5_stpP = self.inp("s5_stpP", [L, 128, 64])
        self.s5_dT = self.inp("s5_dT", [L, 128, 4])
        self.s5_glu_w = self.inp("s5_glu_w", [L, 512, 512])
        self.s5_glu_bT = self.inp("s5_glu_bT", [L, 128, 4])
        self.branch_gT = self.inp("branch_gT", [L, 3, 128, 4])
        self.ropeC = self.inp("ropeC", [128, 16, 64])
        self.ropeS = self.inp("ropeS", [128, 16, 64])
        self.att_gqk = self.inp("att_gqk", [L, 128, 640])
        self.att_sinkB = self.inp("att_sinkB", [L, 128, 8])
        self.maskA = self.inp("maskA", [128, 128], BF16)
        self.maskC = self.inp("maskC", [128, 128], BF16)
        self.hy_w1 = self.inp("hy_w1", [L, 33, 64])
        self.hy_w2 = self.inp("hy_w2", [L, 64, 64])
        self.hy_w3 = self.inp("hy_w3", [L, 64, 2048])
        self.hy_prm = self.inp("hy_prm", [L, 64, 4])
        self.hy_skipT = self.inp("hy_skipT", [L, 128, 8])
        self.hy_convT = self.inp("hy_convT", [L, 128, 12, 4])
        self.hy_zL = self.inp("hy_zL", [33, 4096])
        self.hy_zLr = self.inp("hy_zLr", [33, 4096])
        self.hy_zC = self.inp("hy_zC", [33, 512])
        self.hy_zCr = self.inp("hy_zCr", [33, 512])
        self.hy_dL = self.inp("hy_dL", [512, 4096])
        self.hy_dLr = self.inp("hy_dLr", [512, 4096])
        self.hy_dC = self.inp("hy_dC", [512, 512])
        self.hy_dCr = self.inp("hy_dCr", [512, 512])
        self.hy_GL = tk.dram("hy_GL", [2, 512, 4096], BF16, kind=kd("hy_GL"))
        self.hy_GC = tk.dram("hy_GC", [2, 512, 512], BF16, kind=kd("hy_GC"))
        self.rw_muH = self.inp("rw_muH", [L, 64, 24, 2])
        self.rw_muL = self.inp("rw_muL", [L, 128, 2])
        self.rw_hp = self.inp("rw_hp", [L, 64, 8, 7])
        self.rw_w2pad = self.inp("rw_w2pad", [L, 2, 128, 512])
        self.rw_a2pad = self.inp("rw_a2pad", [L, 2, 128, 512])
        self.rw_lnT = self.inp("rw_lnT", [L, 128, 4, 2])
        self.rw_mask = self.inp("rw_mask", [64, LS], BF16)
        self.rw_mk2 = self.inp("rw_mk2", [128, 256], BF16)
        self.rw_mkL = self.inp("rw_mkL", [128, 128], BF16)
        self.rw_J = self.inp("rw_J", [128, 128], BF16)
        self.rw_mkBD = self.inp("rw_mkBD", [128, 128], BF16)
        self.rw_mkOFF = self.inp("rw_mkOFF", [128, 128], BF16)
        self.rw_bd = self.inp("rw_bd", [128, 128])
        self.rw_s = tk.dram("rw_s", [NB, 2, 8, 5, 64, LS], BF16, kind=kd("rw_s"))
        self.rw_g = tk.dram("rw_g", [NB, 2, 8, 64, LS // 128], F32, kind=kd("rw_g"))
        self.rw_bv = tk.dram("rw_bv", [NB, 512, LS], F32, kind=kd("rw_bv"))
        self.tabA = self.inp("tabA", [128, LS], BF16)
        self.tabB = self.inp("tabB", [128, LS], BF16)
        self.out = tk.dram("out", [NB, LLAT, D], F32, kind="ExternalOutput")
        self.xc = tk.dram("xc_s", [NB, LCTX, D], F32, kind=kd("xc_s"))
        self.pfm = tk.dram("pfm", [R_FM, NB, LS], BF16, kind=kd("pfm"))
        self.qkv = tk.dram("qkv", [NB, LS, 768], BF16, kind=kd("qkv"))
        self.ym = tk.dram("ym", [D, NB, LS], BF16, kind=kd("ym"))
        self.dumps = {}

    def dump_t(self, name, shape, dt):
        t = self.tk.dram("dbg_" + name, shape, dt, kind="ExternalOutput")
        self.dumps[name] = t
        return t

    def consts(self):
        tk, es = self.tk, self.es
        self.ident_f = tk.sb(es, "ident_f", [128, 128], F32)
        self.ident_b = tk.sb(es, "ident_b", [128, 128], BF16)
        self.ones_f = tk.sb(es, "ones_f", [128, 128], F32)
        self.eps_t = tk.sb(es, "eps_t", [128, 1], F32)
        tk.op("pool", lambda e: e.memset(self.ident_f.t[:], 1.0), w=[self.ident_f])
        tk.op("pool", lambda e: e.affine_select(out=self.ident_f.t[:], in_=self.ident_f.t[:], pattern=[[-1, 128]],
                                                compare_op=ALU.is_equal, fill=0.0, base=0, channel_multiplier=1),
              r=[self.ident_f], w=[self.ident_f])
        tk.op("dve", lambda e: e.tensor_copy(out=self.ident_b.t[:], in_=self.ident_f.t[:]), r=[self.ident_f], w=[self.ident_b])
        tk.op("dve", lambda e: e.memset(self.ones_f.t[:], 1.0), w=[self.ones_f])
        tk.op("dve", lambda e: e.memset(self.eps_t.t[:], EPS), w=[self.eps_t])
        self.sc = tk.sb(es, "sc", [128, 16, 3], F32)
        self.modT = tk.sb(es, "modT", [128, 48, 3], F32)
        self.Am = tk.sb(es, "Am", [128, 16, 3], F32)
        self.psb = [tk.ps(es, f"psb{i}", [128, 512], F32) for i in range(8)]
        ctile = tk.sb(es, "ctile", [128, 16, 3], F32)
        tk.dma("sp", ctile.t[:], self.cT.t.ap(), w=[ctile])
        tk.op("act", lambda e: e.activation(out=self.sc.t[:], in_=ctile.t[:], func=AF.Silu), r=[ctile], w=[self.sc])

    def layer(self, l):
        self.stage_ada(l)
        self.tk.barrier()
        self.stage_inproj(l)
        self.tk.barrier()
        if self.stop_after == "inproj":
            return
        self.stage_mixers(l)
        self.tk.barrier()
        if self.stop_after == "mixers":
            return
        self.stage_outproj(l)
        self.tk.barrier()

    def stage_ada(self, l):
        tk = self.tk
        with ExitStack() as es:
            wa = [tk.sb(es, f"wa{i}", [128, 16, 512], F32) for i in range(2)]
            bada = tk.sb(es, "bada", [128, 48], F32)
            ng = tk.sb(es, "ng", [128, 16], F32)
            tmp = tk.sb(es, "adatmp", [128, 16], F32)
            pm = self.psb[0]
            tk.dma("sp", bada.t[:], self.b_adaT.t.ap()[l], w=[bada])
            tk.dma("sp", ng.t[:], self.normgT.t.ap()[l], w=[ng])
            for pc in range(12):
                wt = wa[pc % 2]
                tk.dma("sp" if pc % 2 == 0 else "act", wt.t[:],
                       self.w_ada.t.ap()[l, :, pc * 512:(pc + 1) * 512].rearrange("(k p) n -> p k n", p=128), w=[wt])
                for n4 in range(4):
                    n = pc * 4 + n4
                    for k in range(16):
                        tk.op("pe", lambda e, n=n, n4=n4, wt=wt, k=k: e.matmul(pm.t[:, n * 3:(n + 1) * 3], lhsT=wt.t[:, k, n4 * 128:(n4 + 1) * 128],
                                                                              rhs=self.sc.t[:, k, :], start=(k == 0), stop=(k == 15)),
                              r=[wt, self.sc], w=[pm])
            pv = pm.t[:, 0:144].rearrange("p (n j) -> p n j", j=3)
            for j in range(3):
                tk.op("dve", lambda e, j=j: e.tensor_tensor(out=self.modT.t[:, :, j], in0=pv[:, :, j], in1=bada.t[:], op=ALU.add),
                      r=[pm, bada], w=[self.modT])
            for j in range(3):
                tk.op("dve", lambda e, j=j: e.tensor_scalar(out=tmp.t[:], in0=self.modT.t[:, 16:32, j], scalar1=1.0, scalar2=None, op0=ALU.add),
                      r=[self.modT], w=[tmp])
                tk.op("dve", lambda e, j=j: e.tensor_tensor(out=self.Am.t[:, :, j], in0=tmp.t[:], in1=ng.t[:], op=ALU.mult),
                      r=[tmp, ng], w=[self.Am])
            if "modT" in self.dump and l == 0:
                d1 = self.dump_t("modT", [128, 144], F32)
                tk.dma("sp", d1.t.ap(), self.modT.t[:].rearrange("p n j -> p (n j)"), r=[self.modT], w=[d1])
                d3 = self.dump_t("Am", [128, 48], F32)
                tk.dma("sp", d3.t.ap(), self.Am.t[:].rearrange("p n j -> p (n j)"), r=[self.Am], w=[d3])

    def x_src(self, l, b, seg):
        if seg == 0:
            return (self.ctx_in if l == 0 else self.xc)
        return (self.x_in if l == 0 else self.out)

    def stage_inproj(self, l):
        tk = self.tk
        TG = 768
        with ExitStack() as es:
            hxT = [tk.sb(es, f"hxT{i}", [128, 16, TG], BF16) for i in range(2)]
            wb = [tk.sb(es, f"wb{i}", [128, 16, 512], BF16) for i in range(3)]
            xt = [tk.sb(es, f"xt{i}", [128, D], F32) for i in range(2)]
            sq = tk.sb(es, "sqj", [128, D], BF16)
            xs = [tk.sb(es, f"xs{i}", [128, D], BF16) for i in range(2)]
            st = [tk.sb(es, f"st{i}", [128, 3], F32) for i in range(2)]
            fo = [tk.sb(es, f"fo{i}", [128, TG], BF16) for i in range(4)]
            to = [tk.sb(es, f"to{i}", [128, 768], BF16) for i in range(2)]
            pieces = [(0, 512, "fm", R_S5)]
            c = C_RKV
            while c < D_IN:
                n = min(512, D_IN - c)
                pieces.append((c, n, "fm", R_RKV + (c - C_RKV)))
                c += n
            pieces += [(512, 512, "tm", 0), (1024, 256, "tm", 512)]
            wi = 0
            blk = 0
            foi = 0
            pi = 0
            for tg in range(6):
                b, p0 = tg // 3, (tg % 3) * TG
                h = hxT[tg % 2]
                for kb in range(6):
                    pos = p0 + kb * 128
                    seg = 0 if pos < LCTX else 1
                    j = 2 if seg == 0 else b
                    src = self.x_src(l, b, seg)
                    row0 = pos if seg == 0 else pos - LCTX
                    x_t, xs_t, st_t = xt[blk % 2], xs[blk % 2], st[blk % 2]
                    tk.dma("sp", x_t.t[:], src.t.ap()[b, row0:row0 + 128, :], r=[src], w=[x_t])
                    tk.op("act", lambda e, x_t=x_t: e.activation(out=sq.t[:], in_=x_t.t[:], func=AF.Square), r=[x_t], w=[sq])
                    tk.op("dve", lambda e, st_t=st_t: e.reduce_sum(out=st_t.t[:, 0:1], in_=sq.t[:], axis=AX.X), r=[sq], w=[st_t])
                    tk.op("act", lambda e, st_t=st_t: e.activation(out=st_t.t[:, 1:2], in_=st_t.t[:, 0:1], func=AF.Sqrt, bias=self.eps_t.t[:], scale=1.0 / D),
                          r=[st_t, self.eps_t], w=[st_t])
                    tk.op("dve", lambda e, st_t=st_t: e.reciprocal(out=st_t.t[:, 2:3], in_=st_t.t[:, 1:2]), r=[st_t], w=[st_t])
                    tk.op("act", lambda e, x_t=x_t, xs_t=xs_t, st_t=st_t: e.activation(out=xs_t.t[:], in_=x_t.t[:], func=AF.Copy, scale=st_t.t[:, 2:3]),
                          r=[x_t, st_t], w=[xs_t])
                    for half in range(2):
                        pb = self.psb[pi % 4]
                        pi += 1
                        pv = pb.t[:].bitcast(BF16)
                        for kk in range(8):
                            k = half * 8 + kk
                            tk.op("pe", lambda e, pv=pv, kk=kk, k=k, xs_t=xs_t: e.transpose(out=pv[:, kk * 128:(kk + 1) * 128], in_=xs_t.t[:, k * 128:(k + 1) * 128],
                                                                                             identity=self.ident_b.t[:]), r=[xs_t, self.ident_b], w=[pb])
                        for kk in range(8):
                            k = half * 8 + kk
                            eng = "dve" if kk % 2 == 0 else "pool"
                            if eng == "pool":
                                eng = "dve"
                            tk.op(eng, lambda e, pv=pv, kk=kk, k=k, j=j, kb=kb, h=h: e.tensor_scalar(
                                out=h.t[:, k, kb * 128:(kb + 1) * 128], in0=pv[:, kk * 128:(kk + 1) * 128],
                                scalar1=self.Am.t[:, k, j:j + 1], scalar2=self.modT.t[:, k, j:j + 1], op0=ALU.mult, op1=ALU.add),
                                  r=[pb, self.Am, self.modT], w=[h])
                    blk += 1
                for (c0, n, kind, r0) in pieces:
                    wt = wb[wi % 3]
                    wi += 1
                    tk.dma("pool", wt.t[:, :, 0:n], self.w_in.t.ap()[l, :, c0:c0 + n].rearrange("(k p) n -> p k n", p=128), w=[wt])
                    if kind == "fm":
                        for cc in range(n // 128):
                            f = fo[foi % 4]
                            foi += 1
                            for (t0, tn) in ((0, 512), (512, 256)):
                                pb = self.psb[4 + pi % 4]
                                pi += 1
                                for k in range(16):
                                    tk.op("pe", lambda e, pb=pb, wt=wt, cc=cc, k=k, t0=t0, tn=tn, h=h: e.matmul(
                                        pb.t[:, 0:tn], lhsT=wt.t[:, k, cc * 128:(cc + 1) * 128], rhs=h.t[:, k, t0:t0 + tn],
                                        start=(k == 0), stop=(k == 15)), r=[wt, h], w=[pb])
                                ev = "act" if (pi % 2 == 0) else "dve"
                                if ev == "act":
                                    tk.op("act", lambda e, pb=pb, f=f, t0=t0, tn=tn: e.activation(out=f.t[:, t0:t0 + tn], in_=pb.t[:, 0:tn], func=AF.Copy),
                                          r=[pb], w=[f])
                                else:
                                    tk.op("dve", lambda e, pb=pb, f=f, t0=t0, tn=tn: e.tensor_copy(out=f.t[:, t0:t0 + tn], in_=pb.t[:, 0:tn]),
                                          r=[pb], w=[f])
                            tk.dma("sp", self.pfm.t.ap()[r0 + cc * 128:r0 + (cc + 1) * 128, b, p0:p0 + TG], f.t[:], r=[f], w=[self.pfm])
                    else:
                        for kb in range(6):
                            pb = self.psb[4 + pi % 4]
                            pi += 1
                            o = to[(kb + (r0 // 512)) % 2]
                            for k in range(16):
                                tk.op("pe", lambda e, pb=pb, wt=wt, k=k, kb=kb, n=n, h=h: e.matmul(
                                    pb.t[:, 0:n], lhsT=h.t[:, k, kb * 128:(kb + 1) * 128], rhs=wt.t[:, k, 0:n],
                                    start=(k == 0), stop=(k == 15)), r=[wt, h], w=[pb])
                            tk.op("act", lambda e, pb=pb, o=o, n=n: e.activation(out=o.t[:, 0:n], in_=pb.t[:, 0:n], func=AF.Copy), r=[pb], w=[o])
                            tk.dma("sp", self.qkv.t.ap()[b, p0 + kb * 128:p0 + (kb + 1) * 128, r0:r0 + n], o.t[:, 0:n], r=[o], w=[self.qkv])

    def stage_mixers(self, l):
        last = (l == DEPTH - 1)
        if "s5" in self.mixers:
            self.mixer_s5(l)
            self.tk.barrier()
        if "att" in self.mixers:
            self.mixer_att(l)
            self.tk.barrier()
        if "hy" in self.mixers:
            self.mixer_hy(l)
            self.tk.barrier()
        if "rw" in self.mixers:
            self.mixer_rw(l)
            self.tk.barrier()

    def finish_branch(self, l, es, br, b, ytiles, norm, gcol=None):
        tk = self.tk
        nm = f"fb{br}"
        if not hasattr(self, "_fb") or self._fb[0] is not es:
            rstd = tk.sb(es, nm + "rstd", [128, LS], F32) if norm else None
            sqb = [tk.sb(es, nm + f"sq{i}", [128, LS], BF16) for i in range(2)] if norm else None
            gt_ = [tk.sb(es, nm + f"g{i}", [128, LS], BF16) for i in range(2)]
            sg = [tk.sb(es, nm + f"sg{i}", [128, LS], F32) for i in range(2)]
            ob = [tk.sb(es, nm + f"ob{i}", [128, LS], BF16) for i in range(2)]
            onesb = tk.sb(es, nm + "ones", [128, 128], BF16)
            tk.op("pool", lambda e: e.memset(onesb.t[:], 1.0), w=[onesb])
            self._fb = (es, rstd, sqb, gt_, sg, ob, onesb)
        _, rstd, sqb, gt_, sg, ob, onesb = self._fb
        pieces = [(0, 512), (512, 512), (1024, 512), (1536, 512), (2048, 256)]
        if norm:
            for i in range(4):
                tk.op("act", lambda e, i=i: e.activation(out=sqb[i % 2].t[:], in_=ytiles[i].t[:], func=AF.Square), r=[ytiles[i]], w=[sqb[i % 2]])
                for pi_, (t0, n) in enumerate(pieces):
                    pb = self.psb[pi_]
                    tk.op("pe", lambda e, i=i, pb=pb, t0=t0, n=n: e.matmul(pb.t[:, 0:n], lhsT=onesb.t[:], rhs=sqb[i % 2].t[:, t0:t0 + n],
                                                                             start=(i == 0), stop=(i == 3)), r=[onesb, sqb[i % 2]], w=[pb])
            for pi_, (t0, n) in enumerate(pieces):
                pb = self.psb[pi_]
                tk.op("act", lambda e, pb=pb, t0=t0, n=n: e.activation(out=rstd.t[:, t0:t0 + n], in_=pb.t[:, 0:n], func=AF.Sqrt, bias=self.eps_t.t[:], scale=1.0 / W),
                      r=[pb, self.eps_t], w=[rstd])
            tk.op("dve", lambda e: e.reciprocal(out=rstd.t[:], in_=rstd.t[:]), r=[rstd], w=[rstd])
        for i in range(4):
            g, s_, o = gt_[i % 2], sg[i % 2], ob[i % 2]
            row = R_GATE + br * 512 + i * 128
            tk.dma("sp", g.t[:], self.pfm.t.ap()[row:row + 128, b, :], r=[self.pfm], w=[g])
            tk.op("act", lambda e, g=g, s_=s_: e.activation(out=s_.t[:], in_=g.t[:], func=AF.Silu), r=[g], w=[s_])
            if norm:
                tk.op("pool", lambda e, s_=s_: e.tensor_tensor(out=s_.t[:], in0=s_.t[:], in1=rstd.t[:], op=ALU.mult), r=[s_, rstd], w=[s_])
                tk.op("dve", lambda e, i=i, s_=s_, o=o: e.scalar_tensor_tensor(out=o.t[:], in0=ytiles[i].t[:], scalar=gcol.t[:, i:i + 1], in1=s_.t[:],
                                                                                op0=ALU.mult, op1=ALU.mult), r=[ytiles[i], gcol, s_], w=[o])
            else:
                tk.op("dve", lambda e, i=i, s_=s_, o=o: e.tensor_tensor(out=o.t[:], in0=ytiles[i].t[:], in1=s_.t[:], op=ALU.mult), r=[ytiles[i], s_], w=[o])
            tk.dma("sp", self.ym.t.ap()[br * 512 + i * 128:br * 512 + (i + 1) * 128, b, :], o.t[:], r=[o], w=[self.ym])

    def mixer_s5(self, l):
        tk = self.tk
        MAGIC = 12582912.0
        TWO_PI = 2.0 * math.pi
        PIECES = [(0, 256), (256, 512), (768, 512), (1280, 512), (1792, 512)]
        with ExitStack() as es0, ExitStack() as es:
            ygel = [[tk.sb(es0, f"s5yg{gt}{b}", [128, LS], BF16) for b in range(NB)] for gt in range(4)]
            sb = lambda n, sh, dt=F32: tk.sb(es, "s5" + n, sh, dt)
            BBt = sb("BBt", [128, 64, 2, 128], BF16)
            Ct = sb("Ct", [128, 64, 2, 128], BF16)
            pl = sb("pl", [128, 6, 64])
            negpi = sb("negpi", [128, 1])
            tk.op("pool", lambda e: e.memset(negpi.t[:], 0.0), w=[negpi])

            def sincos(ph, n, sin_out, cos_out, tmp):
                V = lambda t: t.t[:, 0:n]
                tk.op("pool", lambda e: e.tensor_scalar(out=V(tmp), in0=V(ph), scalar1=MAGIC, scalar2=MAGIC, op0=ALU.add, op1=ALU.subtract), r=[ph], w=[tmp])
                tk.op("pool", lambda e: e.tensor_tensor(out=V(tmp), in0=V(ph), in1=V(tmp), op=ALU.subtract), r=[ph, tmp], w=[tmp])
                tk.op("act", lambda e: e.activation(out=V(sin_out), in_=V(tmp), func=AF.Sin, bias=negpi.t[:], scale=TWO_PI), r=[tmp, negpi], w=[sin_out])
                tk.op("pool", lambda e: e.tensor_scalar(out=V(cos_out), in0=V(ph), scalar1=0.25, scalar2=None, op0=ALU.add), r=[ph], w=[cos_out])
                tk.op("pool", lambda e: e.tensor_scalar(out=V(tmp), in0=V(cos_out), scalar1=MAGIC, scalar2=MAGIC, op0=ALU.add, op1=ALU.subtract), r=[cos_out], w=[tmp])
                tk.op("pool", lambda e: e.tensor_tensor(out=V(tmp), in0=V(cos_out), in1=V(tmp), op=ALU.subtract), r=[cos_out, tmp], w=[tmp])
                tk.op("act", lambda e: e.activation(out=V(cos_out), in_=V(tmp), func=AF.Sin, bias=negpi.t[:], scale=TWO_PI), r=[tmp, negpi], w=[cos_out])

            for hh in range(2):
                self.s5_prep(l, hh, BBt, Ct, sincos)
                tk.barrier()
            self.s5_main(l, es, sb, BBt, Ct, pl, sincos, ygel)
            tk.barrier()
            es.close()
            self.s5_glu(l, ygel)
            tk.barrier()

    def s5_prep(self, l, hh, BBt, Ct, sincos):
        tk = self.tk
        TWO_PI = 2.0 * math.pi
        with ExitStack() as es:
            sb = lambda n, sh, dt=F32: tk.sb(es, "s5p" + n, sh, dt)
            NP = 2048
            c0 = hh * NP
            lre, lim, stp = sb("lre", [128, NP]), sb("lim", [128, NP]), sb("stp", [128, NP])
            t1, t2, t3, t4 = sb("t1", [128, NP]), sb("t2", [128, NP]), sb("t3", [128, NP]), sb("t4", [128, NP])
            tk.dma("sp", lre.t[:], self.s5_lreB.t.ap()[l, :, c0:c0 + NP], w=[lre])
            tk.dma("sp", lim.t[:], self.s5_limB.t.ap()[l, :, c0:c0 + NP], w=[lim])
            tk.dma("sp", stp.t[:], self.s5_stpB.t.ap()[l, :, c0:c0 + NP], w=[stp])

            def disc(lre, lim, stp, t1, t2, t3, t4, n):
                V = lambda t: t.t[:, 0:n]
                tk.op("act", lambda e: e.activation(out=V(stp), in_=V(stp), func=AF.Exp), r=[stp], w=[stp])
                tk.op("dve", lambda e: e.tensor_tensor(out=V(t1), in0=V(lre), in1=V(stp), op=ALU.mult), r=[lre, stp], w=[t1])
                tk.op("act", lambda e: e.activation(out=V(t1), in_=V(t1), func=AF.Exp), r=[t1], w=[t1])
                tk.op("dve", lambda e: e.scalar_tensor_tensor(out=V(t2), in0=V(lim), scalar=1.0 / TWO_PI, in1=V(stp), op0=ALU.mult, op1=ALU.mult),
                      r=[lim, stp], w=[t2])

            disc(lre, lim, stp, t1, t2, t3, t4, NP)
            sn, cs = sb("sn", [128, NP]), sb("cs", [128, NP])
            sincos(t2, NP, sn, cs, t3)
            tk.op("dve", lambda e: e.tensor_tensor(out=cs.t[:], in0=cs.t[:], in1=t1.t[:], op=ALU.mult), r=[cs, t1], w=[cs])
            tk.op("dve", lambda e: e.tensor_tensor(out=sn.t[:], in0=sn.t[:], in1=t1.t[:], op=ALU.mult), r=[sn, t1], w=[sn])
            tk.op("dve", lambda e: e.tensor_scalar(out=cs.t[:], in0=cs.t[:], scalar1=-1.0, scalar2=None, op0=ALU.add), r=[cs], w=[cs])
            tk.op("dve", lambda e: e.tensor_tensor(out=t3.t[:], in0=lre.t[:], in1=lre.t[:], op=ALU.mult), r=[lre], w=[t3])
            tk.op("dve", lambda e: e.tensor_tensor(out=t4.t[:], in0=lim.t[:], in1=lim.t[:], op=ALU.mult), r=[lim], w=[t4])
            tk.op("dve", lambda e: e.tensor_tensor(out=t3.t[:], in0=t3.t[:], in1=t4.t[:], op=ALU.add), r=[t3, t4], w=[t3])
            tk.op("dve", lambda e: e.reciprocal(out=t3.t[:], in_=t3.t[:]), r=[t3], w=[t3])
            tk.op("dve", lambda e: e.tensor_tensor(out=t1.t[:], in0=cs.t[:], in1=lre.t[:], op=ALU.mult), r=[cs, lre], w=[t1])
            tk.op("dve", lambda e: e.tensor_tensor(out=t4.t[:], in0=sn.t[:], in1=lim.t[:], op=ALU.mult), r=[sn, lim], w=[t4])
            tk.op("dve", lambda e: e.tensor_tensor(out=t1.t[:], in0=t1.t[:], in1=t4.t[:], op=ALU.add), r=[t1, t4], w=[t1])
            tk.op("dve", lambda e: e.tensor_tensor(out=t1.t[:], in0=t1.t[:], in1=t3.t[:], op=ALU.mult), r=[t1, t3], w=[t1])
            tk.op("dve", lambda e: e.tensor_tensor(out=t2.t[:], in0=sn.t[:], in1=lre.t[:], op=ALU.mult), r=[sn, lre], w=[t2])
            tk.op("dve", lambda e: e.tensor_tensor(out=t4.t[:], in0=cs.t[:], in1=lim.t[:], op=ALU.mult), r=[cs, lim], w=[t4])
            tk.op("dve", lambda e: e.tensor_tensor(out=t2.t[:], in0=t2.t[:], in1=t4.t[:], op=ALU.subtract), r=[t2, t4], w=[t2])
            tk.op("dve", lambda e: e.tensor_tensor(out=t2.t[:], in0=t2.t[:], in1=t3.t[:], op=ALU.mult), r=[t2, t3], w=[t2])
            co_re, co_im = t1, t2
            bre, bim = lre, lim
            tk.dma("sp", bre.t[:], self.s5_bre.t.ap()[l, :, c0:c0 + NP], r=[], w=[bre])
            tk.dma("sp", bim.t[:], self.s5_bim.t.ap()[l, :, c0:c0 + NP], r=[], w=[bim])
            G0 = hh * 32
            v3 = lambda t: t.t[:].rearrange("q (a p) -> q a p", p=64)
            tk.op("dve", lambda e: e.tensor_tensor(out=t3.t[:], in0=co_re.t[:], in1=bre.t[:], op=ALU.mult), r=[co_re, bre], w=[t3])
            tk.op("pool", lambda e: e.tensor_tensor(out=sn.t[:], in0=co_im.t[:], in1=bim.t[:], op=ALU.mult), r=[co_im, bim], w=[sn])
            tk.op("dve", lambda e: e.tensor_tensor(out=t3.t[:], in0=t3.t[:], in1=sn.t[:], op=ALU.subtract), r=[t3, sn], w=[t3])
            tk.op("dve", lambda e: e.tensor_tensor(out=t4.t[:], in0=co_re.t[:], in1=bim.t[:], op=ALU.mult), r=[co_re, bim], w=[t4])
            tk.op("pool", lambda e: e.tensor_tensor(out=cs.t[:], in0=co_im.t[:], in1=bre.t[:], op=ALU.mult), r=[co_im, bre], w=[cs])
            tk.op("dve", lambda e: e.tensor_tensor(out=t4.t[:], in0=t4.t[:], in1=cs.t[:], op=ALU.add), r=[t4, cs], w=[t4])
            tk.op("act", lambda e: e.activation(out=BBt.t[:, G0:G0 + 32, 0, 0:64], in_=v3(t3), func=AF.Copy), r=[t3], w=[BBt])
            tk.op("act", lambda e: e.activation(out=BBt.t[:, G0:G0 + 32, 0, 64:128], in_=v3(t4), func=AF.Copy), r=[t4], w=[BBt])
            tk.op("act", lambda e: e.activation(out=BBt.t[:, G0:G0 + 32, 1, 0:64], in_=v3(t4), func=AF.Copy), r=[t4], w=[BBt])
            tk.op("act", lambda e: e.activation(out=BBt.t[:, G0:G0 + 32, 1, 64:128], in_=v3(t3), func=AF.Copy, scale=-1.0), r=[t3], w=[BBt])
            for q2 in range(2):
                src = (lre, lim)[q2]
                tk.dma("sp", src.t[:], self.s5_c1.t.ap()[l, :, hh * 4096 + q2 * NP:hh * 4096 + (q2 + 1) * NP], w=[src])
                sv = src.t[:].rearrange("q (a m) -> q a m", m=128)
                d0 = hh * 32 + q2 * 16
                tk.op("act", lambda e, sv=sv, d0=d0: e.activation(out=Ct.t[0:64, d0:d0 + 16, 0, :], in_=sv[0:64], func=AF.Copy), r=[src], w=[Ct])
                tk.op("act", lambda e, sv=sv, d0=d0: e.activation(out=Ct.t[64:128, d0:d0 + 16, 0, :], in_=sv[64:128], func=AF.Copy, scale=-1.0), r=[src], w=[Ct])
            for q2 in range(2):
                src = (t3, t4)[q2]
                tk.dma("sp", src.t[:], self.s5_c2.t.ap()[l, :, hh * 4096 + q2 * NP:hh * 4096 + (q2 + 1) * NP], w=[src])
                sv = src.t[:].rearrange("q (a m) -> q a m", m=128)
                d0 = hh * 32 + q2 * 16
                tk.op("act", lambda e, sv=sv, d0=d0: e.activation(out=Ct.t[:, d0:d0 + 16, 1, :], in_=sv, func=AF.Copy, scale=-1.0), r=[src], w=[Ct])

    def s5_main(self, l, es, sb, BBt, Ct, pl, sincos, ygel):
        tk = self.tk
        MAGIC = 12582912.0
        TWO_PI = 2.0 * math.pi
        PIECES = [(0, 256), (256, 512), (768, 512), (1280, 512), (1792, 512)]
        if True:
            tk.dma("sp", pl.t[:, 0, :], self.s5_lreP.t.ap()[l], w=[pl])
            tk.dma("sp", pl.t[:, 1, :], self.s5_limP.t.ap()[l], w=[pl])
            tk.dma("sp", pl.t[:, 2, :], self.s5_stpP.t.ap()[l], w=[pl])
            tk.op("act", lambda e: e.activation(out=pl.t[:, 2, :], in_=pl.t[:, 2, :], func=AF.Exp), r=[pl], w=[pl])
            tk.op("dve", lambda e: e.tensor_tensor(out=pl.t[:, 3, :], in0=pl.t[:, 0, :], in1=pl.t[:, 2, :], op=ALU.mult), r=[pl], w=[pl])
            tk.op("act", lambda e: e.activation(out=pl.t[:, 3, :], in_=pl.t[:, 3, :], func=AF.Exp), r=[pl], w=[pl])
            tk.op("dve", lambda e: e.scalar_tensor_tensor(out=pl.t[:, 4, :], in0=pl.t[:, 1, :], scalar=1.0 / TWO_PI, in1=pl.t[:, 2, :], op0=ALU.mult, op1=ALU.mult),
                  r=[pl], w=[pl])
            tk.op("dve", lambda e: e.tensor_scalar(out=pl.t[:, 5, :], in0=pl.t[:, 4, :], scalar1=64.0, scalar2=None, op0=ALU.mult), r=[pl], w=[pl])
            tk.op("dve", lambda e: e.tensor_scalar(out=pl.t[:, 0, :], in0=pl.t[:, 5, :], scalar1=MAGIC, scalar2=MAGIC, op0=ALU.add, op1=ALU.subtract), r=[pl], w=[pl])
            tk.op("dve", lambda e: e.tensor_tensor(out=pl.t[:, 5, :], in0=pl.t[:, 5, :], in1=pl.t[:, 0, :], op=ALU.subtract), r=[pl], w=[pl])
            tA, tB = sb("tA", [128, LS], BF16), sb("tB", [128, LS], BF16)
            tk.dma("sp", tA.t[:], self.tabA.t.ap(), w=[tA])
            tk.dma("sp", tB.t[:], self.tabB.t.ap(), w=[tB])
            dsk = sb("dsk", [128, 4])
            tk.dma("sp", dsk.t[:], self.s5_dT.t.ap()[l], w=[dsk])
            Rt = [sb(f"Rt{i}", [128, 512]) for i in range(2)]
            ones5 = sb("ones5", [128, 512])
            tk.op("pool", lambda e: e.memset(ones5.t[:], 1.0), w=[ones5])
            ph = [sb(f"ph{i}", [128, 512]) for i in range(2)]
            tmpt = [sb(f"tmpt{i}", [128, 512]) for i in range(2)]
            NC_, NS_ = [sb(f"NC{i}", [128, 512]) for i in range(2)], [sb(f"NS{i}", [128, 512]) for i in range(2)]
            Wt = [sb(f"Wt{i}", [128, 512]) for i in range(2)]
            W2 = [sb(f"W2{i}", [128, 512]) for i in range(2)]
            Gt = [sb(f"Gt{i}", [128, 512]) for i in range(2)]
            P1 = [sb(f"P1{i}", [128, 512], BF16) for i in range(2)]
            P2 = [sb(f"P2{i}", [128, 512], BF16) for i in range(2)]
            carry = sb("carry", [128, 64])
            ut = [sb(f"u{i}", [128, LS], BF16) for i in range(2)]
            ya2 = [sb(f"ya{b}", [128, LS]) for b in range(NB)]
            yacc = [ya2 for gt in range(4)]
            cnt = 0
            for gt in range(4):
                for b in range(NB):
                    tk.dma("sp", ut[b].t[:], self.pfm.t.ap()[R_S5 + gt * 128:R_S5 + (gt + 1) * 128, b, :], r=[self.pfm], w=[ut[b]])
                    tk.op("dve", lambda e, gt=gt, b=b: e.tensor_scalar(out=yacc[gt][b].t[:], in0=ut[b].t[:], scalar1=dsk.t[:, gt:gt + 1], scalar2=None, op0=ALU.mult),
                          r=[ut[b], dsk], w=[yacc[gt][b]])
                for di in range(2):
                    for pi_, (t0, n) in enumerate(PIECES):
                        ypb = [self.psb[6], self.psb[7]]
                        for g8 in range(8):
                            dg = di * 32 + gt * 8 + g8
                            i2 = cnt % 2
                            cnt += 1
                            tk.op("pool", lambda e, dg=dg, i2=i2, t0=t0, n=n: e.tensor_scalar(out=ph[i2].t[:, 0:n], in0=tA.t[:, t0:t0 + n], scalar1=pl.t[:, 5, dg:dg + 1], scalar2=None, op0=ALU.mult),
                                  r=[tA, pl], w=[ph[i2]])
                            tk.op("dve", lambda e, dg=dg, i2=i2, t0=t0, n=n: e.scalar_tensor_tensor(out=ph[i2].t[:, 0:n], in0=tB.t[:, t0:t0 + n], scalar=pl.t[:, 4, dg:dg + 1], in1=ph[i2].t[:, 0:n],
                                                                                                     op0=ALU.mult, op1=ALU.add), r=[tB, pl, ph[i2]], w=[ph[i2]])
                            sincos(ph[i2], n, NS_[i2], NC_[i2], tmpt[i2])
                            tk.op("pool", lambda e, dg=dg, i2=i2, n=n: e.tensor_scalar(out=Rt[i2].t[:, 0:n], in0=ones5.t[:, 0:n], scalar1=pl.t[:, 3, dg:dg + 1], scalar2=None, op0=ALU.mult),
                                  r=[ones5, pl], w=[Rt[i2]])
                            for b in range(NB):
                                j2 = (cnt * 2 + b) % 2
                                pbu, pbu2 = self.psb[2 * ((cnt * 2 + b) % 3)], self.psb[2 * ((cnt * 2 + b) % 3) + 1]
                                if di == 0:
                                    uv = ut[b].t[:, t0:t0 + n]
                                else:
                                    lo = (LCTX - t0 - n) if t0 < LCTX else (LCTX + (LS - t0 - n))
                                    uv = ut[b].t[:, lo:lo + n][:, ::-1]
                                    lo_ = lo
                                tk.op("pe", lambda e, pbu=pbu, dg=dg, uv=uv, n=n: e.matmul(pbu.t[:, 0:n], lhsT=BBt.t[:, dg, 0, :], rhs=uv, start=True, stop=True), r=[BBt, ut[b]], w=[pbu])
                                tk.op("pe", lambda e, pbu2=pbu2, dg=dg, uv=uv, n=n: e.matmul(pbu2.t[:, 0:n], lhsT=BBt.t[:, dg, 1, :], rhs=uv, start=True, stop=True), r=[BBt, ut[b]], w=[pbu2])
                                tk.op("dve", lambda e, pbu=pbu, i2=i2, j2=j2, n=n: e.tensor_tensor(out=Wt[j2].t[:, 0:n], in0=pbu.t[:, 0:n], in1=NC_[i2].t[:, 0:n], op=ALU.mult), r=[pbu, NC_[i2]], w=[Wt[j2]])
                                tk.op("dve", lambda e, pbu2=pbu2, i2=i2, j2=j2, n=n: e.tensor_tensor(out=W2[j2].t[:, 0:n], in0=pbu2.t[:, 0:n], in1=NS_[i2].t[:, 0:n], op=ALU.mult), r=[pbu2, NS_[i2]], w=[W2[j2]])
                                tk.op("pool", lambda e, j2=j2, n=n: e.tensor_tensor(out=Wt[j2].t[:, 0:n], in0=Wt[j2].t[:, 0:n], in1=W2[j2].t[:, 0:n], op=ALU.add), r=[Wt[j2], W2[j2]], w=[Wt[j2]])
                                ci = g8 * 2 + b + di * 16
                                init = 0.0 if pi_ == 0 else carry.t[:, ci:ci + 1]
                                tk.op("dve", lambda e, j2=j2, i2=i2, n=n, init=init: e.tensor_tensor_scan(out=Gt[j2].t[:, 0:n], data0=Rt[i2].t[:, 0:n], data1=Wt[j2].t[:, 0:n], initial=init,
                                                                                                         op0=ALU.mult, op1=ALU.add), r=[Rt[i2], Wt[j2], carry], w=[Gt[j2]])
                                tk.op("act", lambda e, j2=j2, n=n, ci=ci: e.activation(out=carry.t[:, ci:ci + 1], in_=Gt[j2].t[:, n - 1:n], func=AF.Copy), r=[Gt[j2]], w=[carry])
                                tk.op("pool", lambda e, j2=j2, i2=i2, n=n: e.tensor_tensor(out=P1[j2].t[:, 0:n], in0=Gt[j2].t[:, 0:n], in1=NC_[i2].t[:, 0:n], op=ALU.mult), r=[Gt[j2], NC_[i2]], w=[P1[j2]])
                                tk.op("dve", lambda e, j2=j2, i2=i2, n=n: e.tensor_tensor(out=P2[j2].t[:, 0:n], in0=Gt[j2].t[:, 0:n], in1=NS_[i2].t[:, 0:n], op=ALU.mult), r=[Gt[j2], NS_[i2]], w=[P2[j2]])
                                tk.op("pe", lambda e, b=b, dg=dg, j2=j2, n=n, g8=g8: e.matmul(ypb[b].t[:, 0:n], lhsT=Ct.t[:, dg, 0, :], rhs=P1[j2].t[:, 0:n], start=(g8 == 0), stop=False), r=[Ct, P1[j2]], w=[ypb[b]])
                                tk.op("pe", lambda e, b=b, dg=dg, j2=j2, n=n, g8=g8: e.matmul(ypb[b].t[:, 0:n], lhsT=Ct.t[:, dg, 1, :], rhs=P2[j2].t[:, 0:n], start=False, stop=(g8 == 7)), r=[Ct, P2[j2]], w=[ypb[b]])
                        for b in range(NB):
                            if di == 0:
                                yv = yacc[gt][b].t[:, t0:t0 + n]
                            else:
                                lo = (LCTX - t0 - n) if t0 < LCTX else (LCTX + (LS - t0 - n))
                                yv = yacc[gt][b].t[:, lo:lo + n][:, ::-1]
                            tk.op("dve", lambda e, b=b, yv=yv, n=n: e.tensor_tensor(out=yv, in0=yv, in1=ypb[b].t[:, 0:n], op=ALU.add), r=[yacc[gt][b], ypb[b]], w=[yacc[gt][b]])
                for b in range(NB):
                    if "s5_y" in self.dump and l == 0:
                        if gt == 0 and b == 0:
                            self._d_s5y = self.dump_t("s5_y", [4, NB, 128, LS], F32)
                        tk.dma("sp", self._d_s5y.t.ap()[gt, b], yacc[gt][b].t[:], r=[yacc[gt][b]], w=[self._d_s5y])
                    tk.op("act", lambda e, gt=gt, b=b: e.activation(out=ygel[gt][b].t[:], in_=yacc[gt][b].t[:], func=AF.Gelu), r=[yacc[gt][b]], w=[ygel[gt][b]])

    def s5_glu(self, l, ygel):
        tk = self.tk
        with ExitStack() as es:
            sb = lambda n, sh, dt=F32: tk.sb(es, "s5g" + n, sh, dt)
            gw = sb("gw", [128, 4, 512], BF16)
            gbT = sb("gbT", [128, 4])
            bgT = sb("bgT", [128, 4])
            tk.dma("pool", gw.t[:], self.s5_glu_w.t.ap()[l].rearrange("(k p) n -> p k n", p=128), w=[gw])
            tk.dma("sp", gbT.t[:], self.s5_glu_bT.t.ap()[l], w=[gbT])
            tk.dma("sp", bgT.t[:], self.branch_gT.t.ap()[l, 0], w=[bgT])
            res = [sb(f"res{i}", [128, LS]) for i in range(4)]
            sgm = [sb(f"sgm{i}", [128, 512]) for i in range(2)]
            for b in range(NB):
                yg = [ygel[i][b] for i in range(4)]
                k2 = 0
                for no in range(4):
                    for (t0, n) in [(0, 512), (512, 512), (1024, 512), (1536, 512), (2048, 256)]:
                        pb = self.psb[5 + k2 % 3]
                        sg_ = sgm[k2 % 2]
                        k2 += 1
                        for ki in range(4):
                            tk.op("pe", lambda e, pb=pb, ki=ki, no=no, t0=t0, n=n: e.matmul(pb.t[:, 0:n], lhsT=gw.t[:, ki, no * 128:(no + 1) * 128], rhs=yg[ki].t[:, t0:t0 + n],
                                                                                             start=(ki == 0), stop=(ki == 3)), r=[gw] + yg, w=[pb])
                        tk.op("act", lambda e, pb=pb, sg_=sg_, no=no, n=n: e.activation(out=sg_.t[:, 0:n], in_=pb.t[:, 0:n], func=AF.Sigmoid, bias=gbT.t[:, no:no + 1], scale=1.0),
                              r=[pb, gbT], w=[sg_])
                        tk.op("dve", lambda e, sg_=sg_, no=no, t0=t0, n=n: e.tensor_tensor(out=res[no].t[:, t0:t0 + n], in0=yg[no].t[:, t0:t0 + n], in1=sg_.t[:, 0:n], op=ALU.mult),
                              r=[yg[no], sg_], w=[res[no]])
                self.finish_branch(l, es, 0, b, res, True, bgT)

    def mixer_att(self, l):
        tk = self.tk
        last = (l == DEPTH - 1)
        NBLK = LS // 128
        with ExitStack() as es:
            sb = lambda n, sh, dt=F32: tk.sb(es, "at" + n, sh, dt)
            ropeC, ropeS = sb("ropeC", [128, 16, 64]), sb("ropeS", [128, 16, 64])
            gqk = sb("gqk", [128, 640])
            esink = sb("esink", [128, 8])
            mA, mC = sb("mA", [128, 128], BF16), sb("mC", [128, 128], BF16)
            bgT = sb("bgT", [128, 4])
            tk.dma("sp", ropeC.t[:], self.ropeC.t.ap(), w=[ropeC])
            tk.dma("sp", ropeS.t[:], self.ropeS.t.ap(), w=[ropeS])
            tk.dma("sp", gqk.t[:], self.att_gqk.t.ap()[l], w=[gqk])
            tk.dma("sp", esink.t[:], self.att_sinkB.t.ap()[l], w=[esink])
            tk.dma("sp", mA.t[:], self.maskA.t.ap(), w=[mA])
            tk.dma("sp", mC.t[:], self.maskC.t.ap(), w=[mC])
            tk.dma("sp", bgT.t[:], self.branch_gT.t.ap()[l, 1], w=[bgT])
            tk.op("act", lambda e: e.activation(out=esink.t[:], in_=esink.t[:], func=AF.Exp), r=[esink], w=[esink])
            QT = sb("QT", [64, 8, LS], BF16)
            KT = sb("KT", [64, 2, LS], BF16)
            Va = sb("Va", [128, NBLK, 2, 128], BF16)
            yatt = [sb(f"yatt{i}", [128, LS]) for i in range(4)]
            xin = [sb(f"xin{i}", [128, 768], BF16) for i in range(2)]
            sq = sb("sq", [128, 640])
            ss = [sb(f"ss{i}", [128, 10]) for i in range(2)]
            xn = [sb(f"xn{i}", [128, 640]) for i in range(2)]
            r1, r2 = sb("r1", [128, 640]), sb("r2", [128, 640])
            xb = [sb(f"xb{i}", [128, 768], BF16) for i in range(2)]
            E = [sb(f"E{i}", [128, 5, 512], BF16) for i in range(2)]
            den = [sb(f"den{i}", [128, 4]) for i in range(2)]
            oat = [sb(f"oat{i}", [128, 512], BF16) for i in range(2)]
            tk.op("pool", lambda e: e.memset(Va.t[:], 1.0), w=[Va])
            pi = 0
            for b in range(NB):
                import os
                P1 = int(os.environ.get("ATT_P1", "99"))
                for blk in range(NBLK):
                    xi, s_, x_n, x_b = xin[blk % 2], ss[blk % 2], xn[blk % 2], xb[blk % 2]
                    tk.dma("sp", xi.t[:], self.qkv.t.ap()[b, blk * 128:(blk + 1) * 128, :], r=[self.qkv], w=[xi])
                    tk.op("act", lambda e, xi=xi: e.activation(out=sq.t[:], in_=xi.t[:, 0:640], func=AF.Square), r=[xi], w=[sq])
                    tk.op("dve", lambda e, s_=s_: e.reduce_sum(out=s_.t[:], in_=sq.t[:].rearrange("p (h d) -> p h d", d=64), axis=AX.X), r=[sq], w=[s_])
                    tk.op("act", lambda e, s_=s_: e.activation(out=s_.t[:], in_=s_.t[:], func=AF.Sqrt, bias=self.eps_t.t[:], scale=1.0 / 64), r=[s_, self.eps_t], w=[s_])
                    tk.op("dve", lambda e, s_=s_: e.reciprocal(out=s_.t[:], in_=s_.t[:]), r=[s_], w=[s_])
                    for h in range(10):
                        tk.op("act" if h % 2 else "pool", (lambda e, h=h, xi=xi, s_=s_, x_n=x_n: e.activation(out=x_n.t[:, h * 64:(h + 1) * 64], in_=xi.t[:, h * 64:(h + 1) * 64], func=AF.Copy, scale=s_.t[:, h:h + 1]))
                              if h % 2 else (lambda e, h=h, xi=xi, s_=s_, x_n=x_n: e.tensor_scalar(out=x_n.t[:, h * 64:(h + 1) * 64], in0=xi.t[:, h * 64:(h + 1) * 64], scalar1=s_.t[:, h:h + 1], scalar2=None, op0=ALU.mult)),
                              r=[xi, s_], w=[x_n])
                    tk.op("dve", lambda e, x_n=x_n: e.tensor_tensor(out=x_n.t[:], in0=x_n.t[:], in1=gqk.t[:], op=ALU.mult), r=[x_n, gqk], w=[x_n])
                    if P1 <= 7:
                        continue
                    if blk >= 2:
                        lb = blk - 2
                        for h in range(10):
                            xv = x_n.t[:, h * 64:(h + 1) * 64]
                            xsw = xv.rearrange("p (a r i) -> p a r i", a=2, r=2)[:, :, ::-1, :]
                            tk.op("dve", lambda e, h=h, xv=xv, lb=lb: e.tensor_tensor(out=r1.t[:, h * 64:(h + 1) * 64], in0=xv, in1=ropeC.t[:, lb, :], op=ALU.mult), r=[x_n, ropeC], w=[r1])
                            tk.op("dve", lambda e, h=h, xsw=xsw, lb=lb: e.tensor_tensor(out=r2.t[:, h * 64:(h + 1) * 64].rearrange("p (a r i) -> p a r i", a=2, r=2), in0=xsw,
                                                                                          in1=ropeS.t[:, lb, :].rearrange("p (a r i) -> p a r i", a=2, r=2), op=ALU.mult), r=[x_n, ropeS], w=[r2])
                        tk.op("dve", lambda e, x_b=x_b: e.tensor_tensor(out=x_b.t[:, 0:640], in0=r1.t[:], in1=r2.t[:], op=ALU.add), r=[r1, r2], w=[x_b])
                    else:
                        tk.op("dve", lambda e, x_b=x_b, x_n=x_n: e.tensor_copy(out=x_b.t[:, 0:640], in_=x_n.t[:]), r=[x_n], w=[x_b])
                    pb = self.psb[pi % 4]
                    pi += 1
                    pb2 = self.psb[pi % 4]
                    pi += 1
                    pv = pb.t[:].bitcast(BF16)
                    pv2 = pb2.t[:].bitcast(BF16)
                    for h in range(8):
                        tk.op("pe", lambda e, pv=pv, h=h, x_b=x_b: e.transpose(out=pv[0:64, h * 128:(h + 1) * 128], in_=x_b.t[:, h * 64:(h + 1) * 64], identity=self.ident_b.t[:]),
                              r=[x_b, self.ident_b], w=[pb])
                    for kh in range(2):
                        tk.op("pe", lambda e, pv2=pv2, kh=kh, x_b=x_b: e.transpose(out=pv2[0:64, kh * 128:(kh + 1) * 128], in_=x_b.t[:, 512 + kh * 64:512 + (kh + 1) * 64], identity=self.ident_b.t[:]),
                              r=[x_b, self.ident_b], w=[pb2])
                    tk.op("dve", lambda e, pv=pv, blk=blk: e.tensor_copy(out=QT.t[0:64, :, blk * 128:(blk + 1) * 128], in_=pv[0:64, 0:1024].rearrange("p (h t) -> p h t", h=8)), r=[pb], w=[QT])
                    tk.op("dve", lambda e, pv2=pv2, blk=blk: e.tensor_copy(out=KT.t[0:64, :, blk * 128:(blk + 1) * 128], in_=pv2[0:64, 0:256].rearrange("p (h t) -> p h t", h=2)), r=[pb2], w=[KT])
                    tk.op("dve", lambda e, xi=xi, blk=blk: e.tensor_copy(out=Va.t[:, blk, :, 0:64], in_=xi.t[:, 640:768].rearrange("p (k d) -> p k d", k=2)), r=[xi], w=[Va])
                import os
                ASTOP = int(os.environ.get("ATT_STOP", "9"))
                if ASTOP <= 1:
                    continue
                qblocks = list(range(2, NBLK)) + ([] if last else [0, 1])
                for qi, qb in enumerate(qblocks):
                    if qb >= 2:
                        keys = [(kb, m) for kb, m in ((qb - 1, mA), (qb, None), (qb + 1, mC)) if 2 <= kb < NBLK] + [(0, None), (1, None)]
                    else:
                        keys = [(0, None), (1, None)]
                    o_t = oat[qi % 2]
                    for kh in range(2):
                        Et = E[(qi * 2 + kh) % 2]
                        for ki, (kb, msk) in enumerate(keys):
                            pb = self.psb[pi % 4]
                            pi += 1
                            tk.op("pe", lambda e, pb=pb, kh=kh, kb=kb, qb=qb: e.matmul(
                                pb.t[:].rearrange("p (j t) -> p j t", j=4), lhsT=KT.t[0:64, kh, kb * 128:(kb + 1) * 128],
                                rhs=QT.t[0:64, 4 * kh:4 * kh + 4, qb * 128:(qb + 1) * 128], start=True, stop=True), r=[KT, QT], w=[pb])
                            tk.op("act", lambda e, pb=pb, Et=Et, ki=ki: e.activation(out=Et.t[:, ki, :], in_=pb.t[:], func=AF.Exp, scale=0.125), r=[pb], w=[Et])
                            if msk is not None:
                                for j in range(4):
                                    tk.op("pool" if j % 2 else "dve", lambda e, Et=Et, ki=ki, j=j, msk=msk: e.tensor_tensor(out=Et.t[:, ki, j * 128:(j + 1) * 128], in0=Et.t[:, ki, j * 128:(j + 1) * 128],
                                                                                                                         in1=msk.t[:], op=ALU.mult), r=[Et, msk], w=[Et])
                        if ASTOP <= 2:
                            continue
                        po = self.psb[4 + (qi * 2 + kh) % 4]
                        for j in range(4):
                            for ki, (kb, msk) in enumerate(keys):
                                tk.op("pe", lambda e, po=po, j=j, ki=ki, kb=kb, kh=kh, Et=Et, nk=len(keys): e.matmul(
                                    po.t[:, j * 128:j * 128 + 65], lhsT=Et.t[:, ki, j * 128:(j + 1) * 128], rhs=Va.t[:, kb, kh, 0:65],
                                    start=(ki == 0), stop=(ki == nk - 1)), r=[Et, Va], w=[po])
                        if ASTOP <= 3:
                            continue
                        dn = den[(qi * 2 + kh) % 2]
                        pov = po.t[:, 0:512].rearrange("p (j c) -> p j c", c=128)
                        tk.op("dve", lambda e, dn=dn, pov=pov, kh=kh: e.tensor_tensor(out=dn.t[:], in0=pov[:, :, 64], in1=esink.t[:, 4 * kh:4 * kh + 4], op=ALU.add), r=[po, esink], w=[dn])
                        tk.op("dve", lambda e, dn=dn: e.reciprocal(out=dn.t[:], in_=dn.t[:]), r=[dn], w=[dn])
                        for j in range(4):
                            tk.op("act", lambda e, j=j, kh=kh, dn=dn, pov=pov, o_t=o_t: e.activation(out=o_t.t[:, (4 * kh + j) * 64:(4 * kh + j + 1) * 64], in_=pov[:, j, 0:64], func=AF.Copy, scale=dn.t[:, j:j + 1]),
                                  r=[po, dn], w=[o_t])
                    if ASTOP <= 4:
                        continue
                    pt = self.psb[pi % 4]
                    pi += 1
                    ptv = pt.t[:].bitcast(BF16)
                    for c4 in range(4):
                        tk.op("pe", lambda e, ptv=ptv, c4=c4, o_t=o_t: e.transpose(out=ptv[:, c4 * 128:(c4 + 1) * 128], in_=o_t.t[:, c4 * 128:(c4 + 1) * 128], identity=self.ident_b.t[:]),
                              r=[o_t, self.ident_b], w=[pt])
                    for c4 in range(4):
                        tk.op("dve", lambda e, ptv=ptv, c4=c4, qb=qb: e.tensor_copy(out=yatt[c4].t[:, qb * 128:(qb + 1) * 128], in_=ptv[:, c4 * 128:(c4 + 1) * 128]), r=[pt], w=[yatt[c4]])
                if last:
                    for c4 in range(4):
                        tk.op("pool", lambda e, c4=c4: e.memset(yatt[c4].t[:, 0:LCTX], 0.0), w=[yatt[c4]])
                if ASTOP <= 5:
                    continue
                self.finish_branch(l, es, 1, b, yatt, True, bgT)

    def mixer_hy(self, l):
        tk = self.tk
        last = (l == DEPTH - 1)
        MAGIC = 12582912.0
        TWO_PI = 2.0 * math.pi
        segs = [("L", LLAT, LCTX)] + ([] if last else [("C", LCTX, 0)])
        with ExitStack() as es:
            sb = lambda n, sh, dt=F32: tk.sb(es, "hf" + n, sh, dt)
            w1, w2, w3 = sb("w1", [33, 64]), sb("w2", [64, 64]), sb("w3", [64, 2048])
            prm = sb("prm", [64, 8])
            skp = sb("skp", [128, 8])
            zero1 = sb("zero1", [128, 1])
            tk.op("pool", lambda e: e.memset(zero1.t[:], 0.0), w=[zero1])
            tk.dma("sp", w1.t[:], self.hy_w1.t.ap()[l], w=[w1])
            tk.dma("sp", w2.t[:], self.hy_w2.t.ap()[l], w=[w2])
            tk.dma("sp", w3.t[:], self.hy_w3.t.ap()[l], w=[w3])
            tk.dma("sp", prm.t[:, 0:4], self.hy_prm.t.ap()[l], w=[prm])
            tk.dma("sp", skp.t[:], self.hy_skipT.t.ap()[l], w=[skp])
            tk.op("dve", lambda e: e.tensor_scalar(out=prm.t[:, 4:5], in0=prm.t[:, 1:2], scalar1=1.0 / TWO_PI, scalar2=None, op0=ALU.mult), r=[prm], w=[prm])
            tk.op("dve", lambda e: e.tensor_scalar(out=prm.t[:, 5:6], in0=prm.t[:, 3:4], scalar1=1.0 / TWO_PI, scalar2=None, op0=ALU.mult), r=[prm], w=[prm])
            zt = sb("zt", [33, 4096])
            H1, H2 = sb("H1", [64, 4096]), sb("H2", [64, 4096])
            ut_, rt_ = [sb(f"ut{i}", [64, 512]) for i in range(2)], [sb(f"rt{i}", [64, 512]) for i in range(2)]
            dt_ = [sb(f"dt{i}", [128, 512]) for i in range(2)]
            Gt = [sb(f"Gt{i}", [128, 4096], BF16) for i in range(2)]
            cnt = 0
            for (sn, Ls, _) in segs:
                Lx = 2 * Ls
                CH = 512 if Ls >= 512 else 256
                nch = Lx // CH
                for k in range(2):
                    ztab = {("L", 0): self.hy_zLr, ("L", 1): self.hy_zL, ("C", 0): self.hy_zCr, ("C", 1): self.hy_zC}[(sn, k)]
                    dtab = {("L", 0): self.hy_dLr, ("L", 1): self.hy_dL, ("C", 0): self.hy_dCr, ("C", 1): self.hy_dC}[(sn, k)]
                    gdst = {("L", 0): self.hy_GL, ("L", 1): self.hy_GL, ("C", 0): self.hy_GC, ("C", 1): self.hy_GC}[(sn, k)]
                    tk.dma("sp", zt.t[:, 0:Lx], ztab.t.ap(), w=[zt])
                    for (src, wt_, K_, bcol, fcol, dst) in ((zt, w1, 33, 0, 4, H1), (H1, w2, 64, 2, 5, H2)):
                        for c in range(nch):
                            pb = self.psb[cnt % 4]
                            u_, r_ = ut_[cnt % 2], rt_[cnt % 2]
                            cnt += 1
                            tk.op("pe", lambda e, pb=pb, src=src, wt_=wt_, K_=K_, c=c, CH=CH: e.matmul(pb.t[0:64, 0:CH], lhsT=wt_.t[0:K_, :], rhs=src.t[0:K_, c * CH:(c + 1) * CH], start=True, stop=True),
                                  r=[src, wt_], w=[pb])
                            tk.op("dve", lambda e, pb=pb, u_=u_, bcol=bcol, fcol=fcol, CH=CH: e.tensor_scalar(out=u_.t[:, 0:CH], in0=pb.t[0:64, 0:CH], scalar1=prm.t[:, bcol:bcol + 1], scalar2=prm.t[:, fcol:fcol + 1],
                                                                                                        op0=ALU.add, op1=ALU.mult), r=[pb, prm], w=[u_])
                            tk.op("pool", lambda e, u_=u_, r_=r_, CH=CH: e.tensor_scalar(out=r_.t[:, 0:CH], in0=u_.t[:, 0:CH], scalar1=MAGIC, scalar2=MAGIC, op0=ALU.add, op1=ALU.subtract), r=[u_], w=[r_])
                            tk.op("pool", lambda e, u_=u_, r_=r_, CH=CH: e.tensor_tensor(out=r_.t[:, 0:CH], in0=u_.t[:, 0:CH], in1=r_.t[:, 0:CH], op=ALU.subtract), r=[u_, r_], w=[r_])
                            tk.op("act", lambda e, r_=r_, dst=dst, c=c, CH=CH: e.activation(out=dst.t[:, c * CH:(c + 1) * CH], in_=r_.t[:, 0:CH], func=AF.Sin, bias=zero1.t[0:64, :], scale=TWO_PI),
                                  r=[r_, zero1], w=[dst])
                    for ct in range(4):
                        G = Gt[cnt % 2]
                        for c in range(nch):
                            fwd = (c * CH < Ls) if k == 0 else (c * CH >= Ls)
                            f = 2 * k + (0 if fwd else 1)
                            pb = self.psb[4 + cnt % 4]
                            d_ = dt_[cnt % 2]
                            cnt += 1
                            tk.dma("sp", d_.t[:, 0:CH], dtab.t.ap()[ct * 128:(ct + 1) * 128, c * CH:(c + 1) * CH], w=[d_])
                            tk.op("pe", lambda e, pb=pb, f=f, ct=ct, c=c, CH=CH: e.matmul(pb.t[:, 0:CH], lhsT=w3.t[:, f * 512 + ct * 128:f * 512 + (ct + 1) * 128], rhs=H2.t[:, c * CH:(c + 1) * CH],
                                                                                          start=True, stop=True), r=[w3, H2], w=[pb])
                            tk.op("dve", lambda e, pb=pb, d_=d_, G=G, c=c, CH=CH: e.tensor_tensor(out=G.t[:, c * CH:(c + 1) * CH], in0=pb.t[:, 0:CH], in1=d_.t[:, 0:CH], op=ALU.mult), r=[pb, d_], w=[G])
                            m0 = Ls - 1 if k == 0 else Ls
                            if c * CH <= m0 < (c + 1) * CH:
                                tk.op("dve", lambda e, pb=pb, d_=d_, G=G, m0=m0, k=k, ct=ct, c=c, CH=CH: e.scalar_tensor_tensor(
                                    out=G.t[:, m0:m0 + 1], in0=pb.t[:, m0 - c * CH:m0 - c * CH + 1], scalar=d_.t[:, m0 - c * CH:m0 - c * CH + 1], in1=skp.t[:, k * 4 + ct:k * 4 + ct + 1],
                                    op0=ALU.mult, op1=ALU.add), r=[pb, d_, skp], w=[G])
                        tk.dma("sp", gdst.t.ap()[k, ct * 128:(ct + 1) * 128, 0:Lx], G.t[:, 0:Lx], r=[G], w=[gdst])
        tk.barrier()
        with ExitStack() as es0:
            yhy = [[tk.sb(es0, f"yhy{ct}{b}", [128, LS], BF16) for b in range(NB)] for ct in range(4)]
            with ExitStack() as es:
                sb = lambda n, sh, dt=F32: tk.sb(es, "hc" + n, sh, dt)
                cw = sb("cw", [128, 12, 4])
                tk.dma("sp", cw.t[:], self.hy_convT.t.ap()[l], w=[cw])
                hin = [sb(f"hin{i}", [128, LS], BF16) for i in range(2)]
                zt = [[sb(f"z{j}{b}", [128, LS], BF16) for b in range(NB)] for j in range(3)]
                tmp = [sb(f"tmp{i}", [128, LS]) for i in range(2)]
                geo = {}
                for (sn, Ls, off) in segs:
                    nb_ = Ls // 128
                    nc_ = NB * nb_
                    geo[sn] = dict(U1=sb("U1" + sn, [128, 128, nc_], BF16), X1=sb("X1" + sn, [128, 128, nc_], BF16), X2=sb("X2" + sn, [128, 128, nc_], BF16),
                                   U2=sb("U2" + sn, [128, 128, nc_], BF16), Y=sb("Y" + sn, [128, 128, nc_], BF16),
                                   R=[sb(f"R{sn}{i}", [128, 2 * Ls - 128], BF16) for i in range(3 if sn == "L" else 2)])
                pi = 0
                ri = 0
                for ct in range(4):
                    for j in range(3):
                        row = R_HY + j * 512 + ct * 128
                        tci = j * 4 + ct
                        for b in range(NB):
                            hi, t_ = hin[(j * 2 + b) % 2], tmp[(j * 2 + b) % 2]
                            tk.dma("sp", hi.t[:], self.pfm.t.ap()[row:row + 128, b, :], r=[self.pfm], w=[hi])
                            tk.op("act", lambda e, hi=hi, t_=t_, tci=tci: e.activation(out=t_.t[:], in_=hi.t[:], func=AF.Identity, bias=cw.t[:, tci, 3:4], scale=cw.t[:, tci, 1:2]), r=[hi, cw], w=[t_])
                            for (sn, Ls, off) in [("L", LLAT, LCTX), ("C", LCTX, 0)]:
                                tk.op("dve", lambda e, hi=hi, t_=t_, tci=tci, Ls=Ls, off=off: e.scalar_tensor_tensor(out=t_.t[:, off + 1:off + Ls], in0=hi.t[:, off:off + Ls - 1], scalar=cw.t[:, tci, 0:1],
                                                                                                                   in1=t_.t[:, off + 1:off + Ls], op0=ALU.mult, op1=ALU.add), r=[hi, cw, t_], w=[t_])
                                tk.op("dve", lambda e, hi=hi, t_=t_, tci=tci, Ls=Ls, off=off: e.scalar_tensor_tensor(out=t_.t[:, off:off + Ls - 1], in0=hi.t[:, off + 1:off + Ls], scalar=cw.t[:, tci, 2:3],
                                                                                                                   in1=t_.t[:, off:off + Ls - 1], op0=ALU.mult, op1=ALU.add), r=[hi, cw, t_], w=[t_])
                            if j == 1:
                                tk.op("dve", lambda e, t_=t_, j=j, b=b: e.tensor_copy(out=zt[j][b].t[:].rearrange("p (n i) -> p n i", i=128),
                                                                                     in_=t_.t[:].rearrange("p (n i) -> p n i", i=128)[:, :, ::-1]), r=[t_], w=[zt[j][b]])
                            else:
                                tk.op("act", lambda e, t_=t_, j=j, b=b: e.activation(out=zt[j][b].t[:], in_=t_.t[:], func=AF.Copy), r=[t_], w=[zt[j][b]])
                    for (sn, Ls, off) in segs:
                        g = geo[sn]
                        nb_ = Ls // 128
                        for j, (dstT, rev) in enumerate(((g["U1"], False), (g["X1"], True), (g["X2"], False))):
                            for b in range(NB):
                                for k4 in range(0, nb_, 8):
                                    pb = self.psb[pi % 4]
                                    pi += 1
                                    pv = pb.t[:].bitcast(BF16)
                                    nn = min(8, nb_ - k4)
                                    for q in range(nn):
                                        blk = k4 + q
                                        src = zt[j][b].t[:, off + blk * 128:off + (blk + 1) * 128]
                                        tk.op("pe", lambda e, pv=pv, q=q, src=src: e.transpose(out=pv[:, q * 128:(q + 1) * 128], in_=src, identity=self.ident_b.t[:]), r=[zt[j][b], self.ident_b], w=[pb])
                                    tk.op("dve", lambda e, pv=pv, dstT=dstT, b=b, k4=k4, nn=nn, nb_=nb_: e.tensor_copy(
                                        out=dstT.t[:, :, b * nb_ + k4:b * nb_ + k4 + nn].rearrange("p c q -> p q c"), in_=pv[:, 0:nn * 128].rearrange("p (q c) -> p q c", q=nn)), r=[pb], w=[dstT])
                    for k in range(2):
                        for c16 in range(0, 128, 16):
                            pbs = {}
                            for (sn, Ls, off) in segs:
                                pbs[sn] = self.psb[4 + pi % 4]
                                pi += 1
                            for cc in range(16):
                                c = c16 + cc
                                for (sn, Ls, off) in segs:
                                    g = geo[sn]
                                    nb_ = Ls // 128
                                    nc_ = NB * nb_
                                    Rt_ = g["R"][ri % len(g["R"])]
                                    ri += 1
                                    gsrc = self.hy_GL if sn == "L" else self.hy_GC
                                    wdt = 2 * Ls - 128
                                    srcap = bass.AP(tensor=gsrc.t, offset=(k * 512 + ct * 128 + c) * gsrc.t.shape[2] + k, ap=[[1, 128], [1, wdt]])
                                    tk.dma("sp" if (ri % 2) else "act", Rt_.t[:], srcap, r=[gsrc], w=[Rt_])
                                    rhsT = g["U1"] if k == 0 else g["U2"]
                                    pb = pbs[sn]
                                    ov = pb.t[:, cc * nc_:(cc + 1) * nc_].rearrange("p (b a) -> p b a", b=NB)
                                    rv = rhsT.t[:, c, :].rearrange("p (b a) -> p b a", b=NB)
                                    ds = [0] + [d for d in range(-(nb_ - 1), nb_) if d != 0]
                                    for di_, d in enumerate(ds):
                                        X = (Ls - 128 - 128 * d) if k == 0 else (Ls - 128 + 128 * d)
                                        a_lo, a_hi = max(0, d), min(nb_, nb_ + d)
                                        tk.op("pe", lambda e, ov=ov, rv=rv, Rt_=Rt_, X=X, a_lo=a_lo, a_hi=a_hi, d=d, di_=di_, nd=len(ds): e.matmul(
                                            ov[:, :, a_lo:a_hi], lhsT=Rt_.t[:, X:X + 128], rhs=rv[:, :, a_lo - d:a_hi - d], start=(di_ == 0), stop=(di_ == nd - 1)),
                                            r=[Rt_, rhsT], w=[pb])
                            for (sn, Ls, off) in segs:
                                g = geo[sn]
                                nc_ = NB * (Ls // 128)
                                mul, dst = (g["X1"], g["U2"]) if k == 0 else (g["X2"], g["Y"])
                                tk.op("dve", lambda e, pb=pbs[sn], mul=mul, dst=dst, c16=c16, nc_=nc_: e.tensor_tensor(
                                    out=dst.t[:, c16:c16 + 16, :], in0=pb.t[:, 0:16 * nc_].rearrange("p (c a) -> p c a", c=16), in1=mul.t[:, c16:c16 + 16, :], op=ALU.mult), r=[pbs[sn], mul], w=[dst])
                    for (sn, Ls, off) in segs:
                        g = geo[sn]
                        nb_ = Ls // 128
                        for b in range(NB):
                            for k4 in range(0, nb_, 8):
                                pb = self.psb[pi % 4]
                                pi += 1
                                pv = pb.t[:].bitcast(BF16)
                                nn = min(8, nb_ - k4)
                                for q in range(nn):
                                    col = b * nb_ + k4 + q
                                    tk.op("pe", lambda e, pv=pv, q=q, g=g, col=col: e.transpose(out=pv[:, q * 128:(q + 1) * 128], in_=g["Y"].t[:, :, col], identity=self.ident_b.t[:]),
                                          r=[g["Y"], self.ident_b], w=[pb])
                                tk.op("dve", lambda e, pv=pv, ct=ct, b=b, off=off, k4=k4, nn=nn: e.tensor_copy(out=yhy[ct][b].t[:, off + k4 * 128:off + (k4 + nn) * 128], in_=pv[:, 0:nn * 128]),
                                      r=[pb], w=[yhy[ct][b]])
                    if last:
                        for b in range(NB):
                            tk.op("pool", lambda e, ct=ct, b=b: e.memset(yhy[ct][b].t[:, 0:LCTX], 0.0), w=[yhy[ct][b]])
            tk.barrier()
            with ExitStack() as es:
                bgT = tk.sb(es, "hybgT", [128, 4], F32)
                tk.dma("sp", bgT.t[:], self.branch_gT.t.ap()[l, 2], w=[bgT])
                for b in range(NB):
                    self.finish_branch(l, es, 3, b, [yhy[ct][b] for ct in range(4)], True, bgT)

    def mixer_rw(self, l):
        tk = self.tk
        last = (l == DEPTH - 1)
        NCH = LS // 128
        PIECES = [(0, 512), (512, 512), (1024, 512), (1536, 512), (2048, 256)]
        SEGS = [(0, LCTX), (LCTX, LLAT)]
        EH = math.exp(-0.5)

        def rev_copy(eng, dst, src, srcT, dstT):
            for (o, n) in SEGS:
                tk.op(eng, lambda e, o=o, n=n: e.tensor_copy(out=dst[:, o:o + n], in_=src[:, o:o + n][:, ::-1]), r=[srcT], w=[dstT])

        with ExitStack() as es:
            sb = lambda n, sh, dt=F32: tk.sb(es, "rp" + n, sh, dt)
            mu = sb("mu", [64, 24, 3])
            mul_ = sb("mul", [128, 3])
            hp = sb("hp", [64, 8, 8])
            w2p = sb("w2p", [128, 2, 512], BF16)
            a2p = sb("a2p", [128, 2, 512], BF16)
            ones64 = sb("ones64", [64, 64], BF16)
            e12 = sb("e12", [64, 1])
            msk = sb("msk", [64, LS], BF16)
            tk.dma("sp", mu.t[:, :, 0:2], self.rw_muH.t.ap()[l], w=[mu])
            tk.dma("sp", mul_.t[:, 0:2], self.rw_muL.t.ap()[l], w=[mul_])
            tk.dma("sp", hp.t[:, :, 0:7], self.rw_hp.t.ap()[l], w=[hp])
            tk.dma("pool", w2p.t[:], self.rw_w2pad.t.ap()[l].rearrange("d k n -> k d n"), w=[w2p])
            tk.dma("pool", a2p.t[:], self.rw_a2pad.t.ap()[l].rearrange("d k n -> k d n"), w=[a2p])
            tk.dma("sp", msk.t[:], self.rw_mask.t.ap(), w=[msk])
            tk.op("pool", lambda e: e.memset(ones64.t[:], 1.0), w=[ones64])
            tk.op("pool", lambda e: e.memset(e12.t[:], 1e-12), w=[e12])
            for (t_, nn) in ((mu, None), (mul_, None)):
                pass
            tk.op("dve", lambda e: e.tensor_tensor(out=mu.t[:, :, 2], in0=mu.t[:, :, 0], in1=mu.t[:, :, 1], op=ALU.add), r=[mu], w=[mu])
            tk.op("dve", lambda e: e.tensor_scalar(out=mu.t[:, :, 2], in0=mu.t[:, :, 2], scalar1=-1.0, scalar2=1.0, op0=ALU.mult, op1=ALU.add), r=[mu], w=[mu])
            tk.op("dve", lambda e: e.tensor_tensor(out=mul_.t[:, 2:3], in0=mul_.t[:, 0:1], in1=mul_.t[:, 1:2], op=ALU.add), r=[mul_], w=[mul_])
            tk.op("dve", lambda e: e.tensor_scalar(out=mul_.t[:, 2:3], in0=mul_.t[:, 2:3], scalar1=-1.0, scalar2=1.0, op0=ALU.mult, op1=ALU.add), r=[mul_], w=[mul_])

            def shift(src, dst, P, cp, cn, c0, rT):
                tk.op("act", lambda e: e.activation(out=dst.t[0:P, :], in_=src.t[0:P, :], func=AF.Copy, scale=c0), r=[src] + rT, w=[dst])
                for (o, n) in SEGS:
                    tk.op("dve", lambda e, o=o, n=n: e.scalar_tensor_tensor(out=dst.t[0:P, o + 1:o + n], in0=src.t[0:P, o:o + n - 1], scalar=cp, in1=dst.t[0:P, o + 1:o + n], op0=ALU.mult, op1=ALU.add),
                          r=[src, dst] + rT, w=[dst])
                    tk.op("dve", lambda e, o=o, n=n: e.scalar_tensor_tensor(out=dst.t[0:P, o:o + n - 1], in0=src.t[0:P, o + 1:o + n], scalar=cn, in1=dst.t[0:P, o:o + n - 1], op0=ALU.mult, op1=ALU.add),
                          r=[src, dst] + rT, w=[dst])

            lin = sb("lin", [128, LS], BF16)
            lsh = sb("lsh", [128, LS])
            lt = [sb(f"lt{i}", [128, LS], BF16) for i in range(2)]
            lr = [sb(f"lr{i}", [128, LS], BF16) for i in range(2)]
            xin = [sb(f"xin{i}", [64, LS], BF16) for i in range(3)]
            base = [sb(f"base{i}", [64, LS]) for i in range(4)]
            strm = [sb(f"strm{i}", [64, LS], BF16) for i in range(4)]
            f1, f2, f3, f4, f5 = sb("f1", [64, LS]), sb("f2", [64, LS]), sb("f3", [64, LS]), sb("f4", [64, LS]), sb("f5", [64, LS])
            ob = [sb(f"ob{i}", [64, LS], BF16) for i in range(5)]
            sqb = sb("sqb", [64, LS], BF16)
            bva = sb("bva", [64, LS])
            gam = sb("gam", [64, NCH])
            pi = 0
            for b in range(NB):
                tk.dma("sp", lin.t[:], self.pfm.t.ap()[R_LORA:R_LORA + 128, b, :], r=[self.pfm], w=[lin])
                shift(lin, lsh, 128, mul_.t[:, 0:1], mul_.t[:, 1:2], mul_.t[:, 2:3], [mul_])
                tk.op("act", lambda e: e.activation(out=lt[0].t[:], in_=lsh.t[:], func=AF.Tanh), r=[lsh], w=[lt[0]])
                tk.op("dve", lambda e: e.tensor_copy(out=lr[0].t[:], in_=lsh.t[:]), r=[lsh], w=[lr[0]])
                rev_copy("pool", lt[1].t, lt[0].t, lt[0], lt[1])
                rev_copy("pool", lr[1].t, lr[0].t, lr[0], lr[1])
                for h in range(8):
                    for q in range(3):
                        row = R_RKV + q * 512 + h * 64
                        tk.dma("sp", xin[q].t[:], self.pfm.t.ap()[row:row + 64, b, :], r=[self.pfm], w=[xin[q]])
                        mi = q * 8 + h
                        shift(xin[q], base[q], 64, mu.t[:, mi, 0:1], mu.t[:, mi, 1:2], mu.t[:, mi, 2:3], [mu])
                    tk.op("dve", lambda e, h=h: e.tensor_scalar(out=base[3].t[:], in0=base[1].t[:], scalar1=hp.t[:, h, 0:1], scalar2=None, op0=ALU.mult), r=[base[1], hp], w=[base[3]])
                    tk.op("act", lambda e: e.activation(out=sqb.t[:], in_=base[3].t[:], func=AF.Square), r=[base[3]], w=[sqb])
                    for (t0, n) in PIECES:
                        pb = self.psb[pi % 8]
                        pi += 1
                        tk.op("pe", lambda e, pb=pb, t0=t0, n=n: e.matmul(pb.t[0:64, 0:n], lhsT=ones64.t[:], rhs=sqb.t[:, t0:t0 + n], start=True, stop=True), r=[ones64, sqb], w=[pb])
                        tk.op("act", lambda e, pb=pb, t0=t0, n=n: e.activation(out=f1.t[:, t0:t0 + n], in_=pb.t[0:64, 0:n], func=AF.Sqrt, bias=e12.t[:], scale=1.0), r=[pb, e12], w=[f1])
                    tk.op("dve", lambda e: e.reciprocal(out=f1.t[:], in_=f1.t[:]), r=[f1], w=[f1])
                    tk.op("dve", lambda e: e.tensor_tensor(out=base[3].t[:], in0=base[3].t[:], in1=f1.t[:], op=ALU.mult), r=[base[3], f1], w=[base[3]])
                    for di in range(2):
                        if di == 0:
                            S = base
                        else:
                            for q in range(4):
                                rev_copy("pool" if q % 2 else "dve", strm[q].t, base[q].t, base[q], strm[q])
                            S = strm
                        rS, kS, vS, kkS = S
                        for (t0, n) in PIECES:
                            pb, pb2 = self.psb[pi % 8], self.psb[(pi + 1) % 8]
                            pi += 2
                            tk.op("pe", lambda e, pb=pb, t0=t0, n=n, di=di, h=h: e.matmul(pb.t[0:64, 0:n], lhsT=w2p.t[:, di, h * 64:(h + 1) * 64], rhs=lt[di].t[:, t0:t0 + n], start=True, stop=True),
                                  r=[w2p, lt[di]], w=[pb])
                            tk.op("pe", lambda e, pb2=pb2, t0=t0, n=n, di=di, h=h: e.matmul(pb2.t[0:64, 0:n], lhsT=a2p.t[:, di, h * 64:(h + 1) * 64], rhs=lr[di].t[:, t0:t0 + n], start=True, stop=True),
                                  r=[a2p, lr[di]], w=[pb2])
                            tk.op("act", lambda e, pb=pb, t0=t0, n=n, di=di, h=h: e.activation(out=f1.t[:, t0:t0 + n], in_=pb.t[0:64, 0:n], func=AF.Sigmoid, bias=hp.t[:, h, 3 + di:4 + di], scale=1.0),
                                  r=[pb, hp], w=[f1])
                            tk.op("act", lambda e, pb2=pb2, t0=t0, n=n, di=di, h=h: e.activation(out=f2.t[:, t0:t0 + n], in_=pb2.t[0:64, 0:n], func=AF.Sigmoid, bias=hp.t[:, h, 5 + di:6 + di], scale=1.0),
                                  r=[pb2, hp], w=[f2])
                        tk.op("pool", lambda e: e.tensor_scalar(out=f1.t[:], in0=f1.t[:], scalar1=-EH, scalar2=None, op0=ALU.mult), r=[f1], w=[f1])
                        tk.op("dve", lambda e: e.tensor_tensor_scan(out=f3.t[:], data0=msk.t[:], data1=f1.t[:], initial=0.0, op0=ALU.mult, op1=ALU.add), r=[msk, f1], w=[f3])
                        tk.op("pool", lambda e: e.tensor_tensor(out=f1.t[:], in0=f3.t[:], in1=f1.t[:], op=ALU.subtract), r=[f3, f1], w=[f1])
                        tk.op("act", lambda e: e.activation(out=f1.t[:], in_=f1.t[:], func=AF.Exp), r=[f1], w=[f1])
                        tk.op("act", lambda e: e.activation(out=f4.t[:], in_=f3.t[:], func=AF.Exp, scale=-1.0), r=[f3], w=[f4])
                        tk.op("act", lambda e: e.activation(out=f3.t[:], in_=f3.t[:], func=AF.Exp), r=[f3], w=[f3])
                        tk.op("dve", lambda e: e.scalar_tensor_tensor(out=ob[0].t[:], in0=kkS.t[:], scalar=-1.0, in1=f1.t[:], op0=ALU.mult, op1=ALU.mult), r=[kkS, f1], w=[ob[0]])
                        tk.op("pool", lambda e: e.tensor_tensor(out=ob[1].t[:], in0=rS.t[:], in1=f3.t[:], op=ALU.mult), r=[rS, f3], w=[ob[1]])
                        tk.op("dve", lambda e: e.tensor_tensor(out=f5.t[:], in0=kkS.t[:], in1=f2.t[:], op=ALU.mult), r=[kkS, f2], w=[f5])
                        tk.op("pool", lambda e: e.tensor_tensor(out=ob[2].t[:], in0=f5.t[:], in1=f4.t[:], op=ALU.mult), r=[f5, f4], w=[ob[2]])
                        tk.op("dve", lambda e, h=h: e.tensor_scalar(out=f2.t[:], in0=f2.t[:], scalar1=-1.0, scalar2=hp.t[:, h, 1:2], op0=ALU.add, op1=ALU.mult), r=[f2, hp], w=[f2])
                        tk.op("dve", lambda e: e.scalar_tensor_tensor(out=f5.t[:], in0=f2.t[:], scalar=1.0, in1=kS.t[:], op0=ALU.add, op1=ALU.mult), r=[f2, kS], w=[f5])
                        tk.op("pool", lambda e: e.tensor_tensor(out=ob[3].t[:], in0=f5.t[:], in1=f4.t[:], op=ALU.mult), r=[f5, f4], w=[ob[3]])
                        tk.op("act", lambda e: e.activation(out=ob[4].t[:], in_=vS.t[:], func=AF.Copy), r=[vS], w=[ob[4]])
                        for q in range(5):
                            tk.dma("sp", self.rw_s.t.ap()[b, di, h, q], ob[q].t[:], r=[ob[q]], w=[self.rw_s])
                        tk.op("act", lambda e: e.activation(out=gam.t[:], in_=f3.t[:, 127::128], func=AF.Copy), r=[f3], w=[gam])
                        tk.dma("sp", self.rw_g.t.ap()[b, di, h], gam.t[:], r=[gam], w=[self.rw_g])
                        tk.op("dve", lambda e, h=h: e.scalar_tensor_tensor(out=sqb.t[:], in0=rS.t[:], scalar=hp.t[:, h, 2:3], in1=f5.t[:], op0=ALU.mult, op1=ALU.mult), r=[rS, hp, f5], w=[sqb])
                        for (t0, n) in PIECES:
                            pb = self.psb[pi % 8]
                            pi += 1
                            tk.op("pe", lambda e, pb=pb, t0=t0, n=n: e.matmul(pb.t[0:64, 0:n], lhsT=ones64.t[:], rhs=sqb.t[:, t0:t0 + n], start=True, stop=True), r=[ones64, sqb], w=[pb])
                            tk.op("dve", lambda e, pb=pb, t0=t0, n=n: e.tensor_tensor(out=f4.t[:, t0:t0 + n], in0=pb.t[0:64, 0:n], in1=vS.t[:, t0:t0 + n], op=ALU.mult), r=[pb, vS], w=[f4])
                        if di == 0:
                            tk.op("pool", lambda e: e.tensor_copy(out=bva.t[:], in_=f4.t[:]), r=[f4], w=[bva])
                        else:
                            for (o, n) in SEGS:
                                tk.op("dve", lambda e, o=o, n=n: e.tensor_tensor(out=bva.t[:, o:o + n], in0=bva.t[:, o:o + n], in1=f4.t[:, o:o + n][:, ::-1], op=ALU.add), r=[bva, f4], w=[bva])
                    tk.dma("sp", self.rw_bv.t.ap()[b, h * 64:(h + 1) * 64, :], bva.t[:], r=[bva], w=[self.rw_bv])
        tk.barrier()
        with ExitStack() as es:
            sb = lambda n, sh, dt=F32: tk.sb(es, "rc" + n, sh, dt)
            mk2 = sb("mk2", [128, 256], BF16)
            mkL = sb("mkL", [128, 128], BF16)
            Jm = sb("Jm", [128, 128], BF16)
            mkBD, mkOFF = sb("mkBD", [128, 128], BF16), sb("mkOFF", [128, 128], BF16)
            tk.dma("sp", mkBD.t[:], self.rw_mkBD.t.ap(), w=[mkBD])
            tk.dma("sp", mkOFF.t[:], self.rw_mkOFF.t.ap(), w=[mkOFF])
            Aoff = [sb(f"Aoff{i}", [128, 8, 128], BF16) for i in range(2)]
            ZT = sb("ZT", [128, 512], BF16)
            WT = sb("WT", [128, 512], BF16)
            tk.dma("sp", mk2.t[:], self.rw_mk2.t.ap(), w=[mk2])
            tk.dma("sp", mkL.t[:], self.rw_mkL.t.ap(), w=[mkL])
            tk.dma("sp", Jm.t[:], self.rw_J.t.ap(), w=[Jm])
            FMc = [sb(f"FMc{i}", [64, 8, 5, 128], BF16) for i in range(2)]
            TMt = [sb(f"TMt{i}", [128, 24, 64], BF16) for i in range(2)]
            ABr = [sb(f"ABr{i}", [128, 8, 256], BF16) for i in range(2)]
            AkK = [sb(f"AkK{i}", [128, 8, 256], BF16) for i in range(2)]
            Mt_ = [sb(f"Mt{i}", [128, 8, 128]) for i in range(2)]
            Mm_ = [sb(f"Mm{i}", [128, 8, 128]) for i in range(2)]
            Tf = sb("Tf", [128, 8, 128])
            Tb = [sb(f"Tb{i}", [128, 8, 128], BF16) for i in range(2)]
            gamt = sb("gamt", [64, 8, NCH])
            Sf = sb("Sf", [64, 8, 64])
            St = sb("St", [64, 8, 64])
            Sb_ = sb("Sb", [64, 8, 64], BF16)
            XT = sb("XT", [128, 512], BF16)
            UT = sb("UT", [128, 512], BF16)
            Yt = [sb(f"Yt{i}", [128, NCH, 512], BF16) for i in range(2)]
            idb = self.ident_b

            def pre_steps(b, di, c):
                i2 = c % 2
                F, Tm, AB, AK = FMc[i2], TMt[i2], ABr[i2], AkK[i2]
                steps = []

                def s0():
                    tk.dma("sp", F.t[:], self.rw_s.t.ap()[b, di, :, :, :, c * 128:(c + 1) * 128].rearrange("h q j t -> j h q t"), r=[self.rw_s], w=[F])
                    for half in range(2):
                        pb = self.psb[4 + half]
                        pv = pb.t[:].bitcast(BF16)
                        for k in range(12):
                            idx = half * 12 + k
                            h, q = idx // 3, 2 + idx % 3
                            tk.op("pe", lambda e, pv=pv, k=k, h=h, q=q: e.transpose(out=pv[:, k * 64:(k + 1) * 64], in_=F.t[:, h, q, :], identity=idb.t[0:64, 0:64]), r=[F, idb], w=[pb])
                        tk.op("dve", lambda e, pv=pv, half=half: e.tensor_copy(out=Tm.t[:, half * 12:(half + 1) * 12, :], in_=pv[:, 0:768].rearrange("p (k d) -> p k d", d=64)), r=[pb], w=[Tm])
                steps.append(s0)

                def gram(lq, dst):
                    def g():
                        for half in range(2):
                            for hh in range(4):
                                h = half * 4 + hh
                                pb = self.psb[half * 2 + hh // 2]
                                tk.op("pe", lambda e, pb=pb, hh=hh, h=h: e.matmul(pb.t[:, (hh % 2) * 256:(hh % 2 + 1) * 256], lhsT=F.t[:, h, lq, :], rhs=F.t[:, h, 0:2, :], start=True, stop=True),
                                      r=[F], w=[pb])
                            for k2 in range(2):
                                pb = self.psb[half * 2 + k2]
                                tk.op("dve", lambda e, pb=pb, half=half, k2=k2: e.tensor_tensor(
                                    out=dst.t[:, half * 4 + k2 * 2:half * 4 + k2 * 2 + 2, :], in0=pb.t[:].rearrange("p (a m) -> p a m", a=2),
                                    in1=mk2.t[:].unsqueeze(1).to_broadcast([128, 2, 256]), op=ALU.mult), r=[pb, mk2], w=[dst])
                    return g
                steps.append(gram(2, AB))
                steps.append(gram(3, AK))

                def s3():
                    for h in range(8):
                        pb = self.psb[4 + h // 4]
                        tk.op("pe", lambda e, pb=pb, h=h: e.matmul(pb.t[:, (h % 4) * 128:(h % 4 + 1) * 128], lhsT=F.t[:, h, 0, :], rhs=F.t[:, h, 2, :], start=True, stop=True), r=[F], w=[pb])
                    for k2 in range(2):
                        pb = self.psb[4 + k2]
                        tk.op("dve", lambda e, pb=pb, k2=k2: e.tensor_tensor(out=Mt_[0].t[:, k2 * 4:(k2 + 1) * 4, :], in0=pb.t[:].rearrange("p (a m) -> p a m", a=4),
                                                                             in1=mkL.t[:].unsqueeze(1).to_broadcast([128, 4, 128]), op=ALU.mult), r=[pb, mkL], w=[Mt_[0]])
                    tk.op("dve", lambda e: e.tensor_tensor(out=Mm_[0].t[:], in0=AB.t[:, :, 0:128], in1=mkBD.t[:].unsqueeze(1).to_broadcast([128, 8, 128]), op=ALU.mult), r=[AB, mkBD], w=[Mm_[0]])
                    tk.op("dve", lambda e: e.tensor_tensor(out=Aoff[i2].t[:], in0=AB.t[:, :, 0:128], in1=mkOFF.t[:].unsqueeze(1).to_broadcast([128, 8, 128]), op=ALU.mult), r=[AB, mkOFF], w=[Aoff[i2]])
                    tk.op("dve", lambda e: e.tensor_tensor(out=Tf.t[:], in0=Mm_[0].t[:], in1=self.ident_f.t[:].unsqueeze(1).to_broadcast([128, 8, 128]), op=ALU.add), r=[Mm_[0], self.ident_f], w=[Tf])
                steps.append(s3)

                def rnd(k):
                    def g():
                        src, dst = (k - 1) % 2, k % 2
                        M, Mt, Mn, Mtn = Mm_[src], Mt_[src], Mm_[dst], Mt_[dst]
                        lastr = (k == 5)
                        for h in range(8):
                            if not lastr:
                                pb = self.psb[0 + h // 4]
                                tk.op("pe", lambda e, pb=pb, h=h: e.matmul(pb.t[:, (h % 4) * 128:(h % 4 + 1) * 128], lhsT=Mt.t[:, h, :], rhs=M.t[:, h, :], start=True, stop=True), r=[Mt, M], w=[pb])
                            pb = self.psb[2 + h // 4]
                            tk.op("pe", lambda e, pb=pb, h=h: e.matmul(pb.t[:, (h % 4) * 128:(h % 4 + 1) * 128], lhsT=M.t[:, h, :], rhs=Mt.t[:, h, :], start=True, stop=True), r=[Mt, M], w=[pb])
                        for k2 in range(2):
                            if not lastr:
                                tk.op("act", lambda e, k2=k2: e.activation(out=Mn.t[:, k2 * 4:(k2 + 1) * 4, :], in_=self.psb[k2].t[:].rearrange("p (a m) -> p a m", a=4), func=AF.Copy), r=[self.psb[k2]], w=[Mn])
                            tk.op("dve", lambda e, k2=k2: e.tensor_copy(out=Mtn.t[:, k2 * 4:(k2 + 1) * 4, :], in_=self.psb[2 + k2].t[:].rearrange("p (a m) -> p a m", a=4)), r=[self.psb[2 + k2]], w=[Mtn])
                        for h in range(8):
                            pb = self.psb[4 + h // 4]
                            tk.op("pe", lambda e, pb=pb, h=h: e.matmul(pb.t[:, (h % 4) * 128:(h % 4 + 1) * 128], lhsT=Mtn.t[:, h, :], rhs=Tf.t[:, h, :], start=True, stop=True), r=[Mtn, Tf], w=[pb])
                        for k2 in range(2):
                            tk.op("dve", lambda e, k2=k2: e.tensor_tensor(out=Tf.t[:, k2 * 4:(k2 + 1) * 4, :], in0=Tf.t[:, k2 * 4:(k2 + 1) * 4, :], in1=self.psb[4 + k2].t[:].rearrange("p (a m) -> p a m", a=4), op=ALU.add),
                                  r=[Tf, self.psb[4 + k2]], w=[Tf])
                        if lastr:
                            tk.op("act", lambda e: e.activation(out=Tb[i2].t[:], in_=Tf.t[:], func=AF.Copy), r=[Tf], w=[Tb[i2]])
                    return g
                for k in range(1, 6):
                    steps.append(rnd(k))
                return steps

            def chain_steps(b, di, c):
                i2 = c % 2
                F, Tm, AB, AK = FMc[i2], TMt[i2], ABr[i2], AkK[i2]
                p6, p7 = self.psb[6], self.psb[7]
                BT = lambda h: Tm.t[:, h * 3 + 0, :]
                KTt = lambda h: Tm.t[:, h * 3 + 1, :]
                VT = lambda h: Tm.t[:, h * 3 + 2, :]

                def c1():
                    for h in range(8):
                        tk.op("pe", lambda e, h=h: e.matmul(p6.t[:, h * 64:(h + 1) * 64], lhsT=F.t[:, h, 0, :], rhs=Sb_.t[:, h, :], start=True, stop=False), r=[F, Sb_], w=[p6])
                        tk.op("pe", lambda e, h=h: e.matmul(p6.t[:, h * 64:(h + 1) * 64], lhsT=AK.t[:, h, 0:128], rhs=VT(h), start=False, stop=True), r=[AK, Tm], w=[p6])
                    tk.op("dve", lambda e: e.tensor_copy(out=XT.t[:], in_=p6.t[:]), r=[p6], w=[XT])

                def c2a():
                    for h in range(8):
                        tk.op("pe", lambda e, h=h: e.matmul(p7.t[:, h * 64:(h + 1) * 64], lhsT=Tb[i2].t[:, h, :], rhs=XT.t[:, h * 64:(h + 1) * 64], start=True, stop=True), r=[Tb[i2], XT], w=[p7])
                    tk.op("dve", lambda e: e.tensor_copy(out=ZT.t[:], in_=p7.t[:]), r=[p7], w=[ZT])

                def c2b():
                    for h in range(8):
                        tk.op("pe", lambda e, h=h: e.matmul(p6.t[:, h * 64:(h + 1) * 64], lhsT=Aoff[i2].t[:, h, :], rhs=ZT.t[:, h * 64:(h + 1) * 64], start=True, stop=True), r=[Aoff[i2], ZT], w=[p6])
                    tk.op("dve", lambda e: e.tensor_tensor(out=WT.t[:], in0=p6.t[:], in1=XT.t[:], op=ALU.add), r=[p6, XT], w=[WT])

                def c2c():
                    for h in range(8):
                        tk.op("pe", lambda e, h=h: e.matmul(p7.t[:, h * 64:(h + 1) * 64], lhsT=Tb[i2].t[:, h, :], rhs=WT.t[:, h * 64:(h + 1) * 64], start=True, stop=True), r=[Tb[i2], WT], w=[p7])
                    tk.op("dve", lambda e: e.tensor_copy(out=UT.t[:], in_=p7.t[:]), r=[p7], w=[UT])

                def c3():
                    for h in range(8):
                        tk.op("pe", lambda e, h=h: e.matmul(p6.t[:, h * 64:(h + 1) * 64], lhsT=F.t[:, h, 1, :], rhs=Sb_.t[:, h, :], start=True, stop=False), r=[F, Sb_], w=[p6])
                        tk.op("pe", lambda e, h=h: e.matmul(p6.t[:, h * 64:(h + 1) * 64], lhsT=AB.t[:, h, 128:256], rhs=UT.t[:, h * 64:(h + 1) * 64], start=False, stop=False), r=[AB, UT], w=[p6])
                        tk.op("pe", lambda e, h=h: e.matmul(p6.t[:, h * 64:(h + 1) * 64], lhsT=AK.t[:, h, 128:256], rhs=VT(h), start=False, stop=True), r=[AK, Tm], w=[p6])
                    tk.op("act", lambda e: e.activation(out=Yt[di].t[:, c, :], in_=p6.t[:], func=AF.Copy), r=[p6], w=[Yt[di]])
                    for h in range(8):
                        tk.op("pe", lambda e, h=h: e.matmul(p7.t[0:64, h * 64:(h + 1) * 64], lhsT=BT(h), rhs=UT.t[:, h * 64:(h + 1) * 64], start=True, stop=False), r=[Tm, UT], w=[p7])
                        tk.op("pe", lambda e, h=h: e.matmul(p7.t[0:64, h * 64:(h + 1) * 64], lhsT=KTt(h), rhs=VT(h), start=False, stop=True), r=[Tm], w=[p7])

                def c4():
                    for h in range(8):
                        tk.op("pool", lambda e, h=h: e.tensor_scalar(out=St.t[:, h, :], in0=Sf.t[:, h, :], scalar1=gamt.t[:, h, c:c + 1], scalar2=None, op0=ALU.mult), r=[Sf, gamt], w=[St])
                    for h in range(8):
                        tk.op("dve", lambda e, h=h: e.scalar_tensor_tensor(out=Sf.t[:, h, :], in0=p7.t[0:64, h * 64:(h + 1) * 64], scalar=gamt.t[:, h, c:c + 1], in1=St.t[:, h, :],
                                                                           op0=ALU.mult, op1=ALU.add), r=[p7, gamt, St], w=[Sf])
                    tk.op("act", lambda e: e.activation(out=Sb_.t[:], in_=Sf.t[:], func=AF.Copy), r=[Sf], w=[Sb_])
                return [c1, c2a, c2b, c2c, c3, c4]

            yfm = [sb(f"yfm{i}", [128, LS]) for i in range(4)]
            bvt = sb("bvt", [128, LS])
            lnp = sb("lnp", [128, 4, 2])
            bd64 = sb("bd64", [128, 128])
            e64 = sb("e64", [128, 1])
            tk.dma("sp", lnp.t[:], self.rw_lnT.t.ap()[l], w=[lnp])
            tk.dma("sp", bd64.t[:], self.rw_bd.t.ap(), w=[bd64])
            tk.op("pool", lambda e: e.memset(e64.t[:], 64e-5), w=[e64])
            for b in range(NB):
                for di in range(2):
                    tk.dma("sp", gamt.t[:], self.rw_g.t.ap()[b, di].rearrange("h j c -> j h c"), r=[self.rw_g], w=[gamt])
                    tk.op("pool", lambda e: e.memset(Sf.t[:], 0.0), w=[Sf])
                    tk.op("pool", lambda e: e.memset(Sb_.t[:], 0.0), w=[Sb_])
                    for st in pre_steps(b, di, 0):
                        st()
                    for c in range(NCH):
                        cs = chain_steps(b, di, c)
                        ps_ = pre_steps(b, di, c + 1) if c + 1 < NCH else []
                        order = [ps_[0:1], cs[0:1], ps_[1:3], cs[1:2], ps_[3:5], cs[2:3], ps_[5:6], cs[3:4], ps_[6:7], cs[4:5], ps_[7:8], cs[5:6], ps_[8:]]
                        for grp in order:
                            for st in grp:
                                st()
                if "rw_yt" in self.dump and b == 0 and l == 0:
                    dy = self.dump_t("rw_yt", [2, 128, NCH * 512], BF16)
                    for di in range(2):
                        tk.dma("sp", dy.t.ap()[di], Yt[di].t[:].rearrange("p c f -> p (c f)"), r=[Yt[di]], w=[dy])
                pi = 0
                for q in range(4):
                    for n4 in range(0, NCH, 4):
                        pb = self.psb[pi % 4]
                        pi += 1
                        for n in range(n4, min(n4 + 4, NCH)):
                            cb = (1 - n) if n < 2 else (2 + (NCH - 1 - n))
                            tk.op("pe", lambda e, pb=pb, n=n, n4=n4, q=q: e.matmul(pb.t[:, (n - n4) * 128:(n - n4 + 1) * 128], lhsT=Yt[0].t[:, n, q * 128:(q + 1) * 128], rhs=idb.t[:], start=True, stop=False),
                                  r=[Yt[0], idb], w=[pb])
                            tk.op("pe", lambda e, pb=pb, n=n, n4=n4, q=q, cb=cb: e.matmul(pb.t[:, (n - n4) * 128:(n - n4 + 1) * 128], lhsT=Yt[1].t[:, cb, q * 128:(q + 1) * 128], rhs=Jm.t[:], start=False, stop=True),
                                  r=[Yt[1], Jm], w=[pb])
                        nn = min(4, NCH - n4)
                        tk.op("act", lambda e, pb=pb, n4=n4, nn=nn, q=q: e.activation(out=yfm[q].t[:, n4 * 128:(n4 + nn) * 128], in_=pb.t[:, 0:nn * 128], func=AF.Copy), r=[pb], w=[yfm[q]])
                    tk.dma("sp", bvt.t[:], self.rw_bv.t.ap()[b, q * 128:(q + 1) * 128, :], r=[self.rw_bv], w=[bvt])
                    for (t0, n) in PIECES:
                        pb, pb2 = self.psb[4 + pi % 4], self.psb[4 + (pi + 1) % 4]
                        pi += 2
                        yv = yfm[q].t[:, t0:t0 + n]
                        tk.op("pe", lambda e, pb=pb, yv=yv, n=n: e.matmul(pb.t[:, 0:n], lhsT=bd64.t[:], rhs=yv, start=True, stop=True), r=[bd64, yfm[q]], w=[pb])
                        tk.op("dve", lambda e, pb=pb, yv=yv, n=n: e.tensor_tensor(out=yv, in0=yv, in1=pb.t[:, 0:n], op=ALU.subtract), r=[yfm[q], pb], w=[yfm[q]])
                        tk.op("act", lambda e, yv=yv, n=n, t0=t0: e.activation(out=Tf.t[:].rearrange("p a m -> p (a m)")[:, 0:n], in_=yv, func=AF.Square), r=[yfm[q]], w=[Tf])
                        tk.op("pe", lambda e, pb2=pb2, n=n: e.matmul(pb2.t[:, 0:n], lhsT=bd64.t[:], rhs=Tf.t[:].rearrange("p a m -> p (a m)")[:, 0:n], start=True, stop=True), r=[bd64, Tf], w=[pb2])
                        tk.op("act", lambda e, pb2=pb2, n=n: e.activation(out=Tf.t[:].rearrange("p a m -> p (a m)")[:, 512:512 + n], in_=pb2.t[:, 0:n], func=AF.Sqrt, bias=e64.t[:], scale=1.0), r=[pb2, e64], w=[Tf])
                        tk.op("dve", lambda e, n=n: e.reciprocal(out=Tf.t[:].rearrange("p a m -> p (a m)")[:, 512:512 + n], in_=Tf.t[:].rearrange("p a m -> p (a m)")[:, 512:512 + n]), r=[Tf], w=[Tf])
                        tk.op("dve", lambda e, yv=yv, n=n: e.tensor_tensor(out=yv, in0=yv, in1=Tf.t[:].rearrange("p a m -> p (a m)")[:, 512:512 + n], op=ALU.mult), r=[yfm[q], Tf], w=[yfm[q]])
                    tk.op("act", lambda e, q=q: e.activation(out=yfm[q].t[:], in_=yfm[q].t[:], func=AF.Identity, bias=lnp.t[:, q, 1:2], scale=lnp.t[:, q, 0:1]), r=[yfm[q], lnp], w=[yfm[q]])
                    tk.op("dve", lambda e, q=q: e.tensor_tensor(out=yfm[q].t[:], in0=yfm[q].t[:], in1=bvt.t[:], op=ALU.add), r=[yfm[q], bvt], w=[yfm[q]])
                self.finish_branch(l, es, 2, b, yfm, False)

    def stage_outproj(self, l):
        tk = self.tk
        TG = 768
        last = (l == DEPTH - 1)
        with ExitStack() as es:
            wo = tk.sb(es, "wo", [128, 16, D], BF16)
            self.gate_b = tk.sb(es, "gate_b", [128, 3, D], F32)
            grep = [tk.sb(es, f"grep{i}", [128, 128], F32) for i in range(2)]
            i = 0
            for j in range(3):
                for n in range(16):
                    g = grep[i % 2]
                    pg = self.psb[1 + (i // 4) % 2]
                    tk.op("dve", lambda e, g=g, n=n, j=j: e.tensor_scalar(out=g.t[:], in0=self.ones_f.t[:], scalar1=self.modT.t[:, 32 + n, j:j + 1],
                                                                          scalar2=None, op0=ALU.mult), r=[self.ones_f, self.modT], w=[g])
                    tk.op("pe", lambda e, g=g, pg=pg, n=n: e.matmul(pg.t[:, (n % 4) * 128:(n % 4 + 1) * 128], lhsT=g.t[:], rhs=self.ident_f.t[:],
                                                                    start=True, stop=True), r=[g, self.ident_f], w=[pg])
                    if n % 4 == 3:
                        tk.op("act", lambda e, pg=pg, n=n, j=j: e.activation(out=self.gate_b.t[:, j, (n // 4) * 512:(n // 4 + 1) * 512], in_=pg.t[:],
                                                                             func=AF.Copy), r=[pg], w=[self.gate_b])
                    i += 1


            ymT = [tk.sb(es, f"ymT{i}", [128, 16, TG], BF16) for i in range(2)]
            xt = [tk.sb(es, f"oxt{i}", [128, D], F32) for i in range(2)]
            tmp = [tk.sb(es, f"otmp{i}", [128, 512], F32) for i in range(2)]
            for q4 in range(4):
                tk.dma("pool", wo.t[:, :, q4 * 512:(q4 + 1) * 512],
                       self.w_out.t.ap()[l, :, q4 * 512:(q4 + 1) * 512].rearrange("(k p) n -> p k n", p=128), w=[wo])
            pi = 0
            blk = 0
            for tg in range(6):
                b, p0 = tg // 3, (tg % 3) * TG
                y = ymT[tg % 2]
                tk.dma("sp", y.t[:], self.ym.t.ap()[:, b, p0:p0 + TG].rearrange("(k p) t -> p k t", p=128), r=[self.ym], w=[y])
                for kb in range(6):
                    pos = p0 + kb * 128
                    seg = 0 if pos < LCTX else 1
                    if seg == 0 and last:
                        continue
                    j = 2 if seg == 0 else b
                    src = self.x_src(l, b, seg)
                    dst = self.xc if seg == 0 else self.out
                    row0 = pos if seg == 0 else pos - LCTX
                    x_t = xt[blk % 2]
                    blk += 1
                    tk.dma("sp", x_t.t[:], src.t.ap()[b, row0:row0 + 128, :], r=[src], w=[x_t])
                    for q4 in range(4):
                        pb = self.psb[pi % 8]
                        tm = tmp[pi % 2]
                        pi += 1
                        for k in range(16):
                            tk.op("pe", lambda e, pb=pb, k=k, kb=kb, q4=q4, y=y: e.matmul(
                                pb.t[:], lhsT=y.t[:, k, kb * 128:(kb + 1) * 128], rhs=wo.t[:, k, q4 * 512:(q4 + 1) * 512],
                                start=(k == 0), stop=(k == 15)), r=[wo, y], w=[pb])
                        tk.op("dve", lambda e, pb=pb, tm=tm, j=j, q4=q4: e.tensor_tensor(out=tm.t[:], in0=pb.t[:], in1=self.gate_b.t[:, j, q4 * 512:(q4 + 1) * 512],
                                                                                         op=ALU.mult), r=[pb, self.gate_b], w=[tm])
                        tk.op("pool", lambda e, tm=tm, x_t=x_t, q4=q4: e.tensor_tensor(out=x_t.t[:, q4 * 512:(q4 + 1) * 512], in0=x_t.t[:, q4 * 512:(q4 + 1) * 512],
                                                                                      in1=tm.t[:], op=ALU.add), r=[tm, x_t], w=[x_t])
                    tk.dma("sp", dst.t.ap()[b, row0:row0 + 128, :], x_t.t[:], r=[x_t], w=[dst])


def host_inputs(inputs, core):
    f = lambda a: np.ascontiguousarray(np.asarray(a, dtype=np.float32))
    b0 = core * NB
    m = {}
    m["x"] = f(inputs["x"][b0:b0 + NB])
    m["ctx"] = f(inputs["ctx"][b0:b0 + NB])
    cv = np.stack([np.asarray(inputs["c"][b0]), np.asarray(inputs["c"][b0 + 1]), np.asarray(inputs["c_ctx"])], axis=-1)
    m["cT"] = f(cv.reshape(16, 128, 3).transpose(1, 0, 2))
    m["normgT"] = f(np.asarray(inputs["norm_g"]).reshape(DEPTH, 16, 128).transpose(0, 2, 1))
    m["w_ada"] = f(inputs["w_ada"])
    m["b_adaT"] = f(np.asarray(inputs["b_ada"]).reshape(DEPTH, 48, 128).transpose(0, 2, 1))
    m["w_in"] = f(inputs["w_in"])
    m["w_out"] = f(inputs["w_out"])
    L = DEPTH
    lre = np.asarray(inputs["s5_lam_re"]).reshape(L, 1, 4096)
    lim = np.asarray(inputs["s5_lam_im"]).reshape(L, 1, 4096)
    stp = np.repeat(np.asarray(inputs["s5_log_step"]).reshape(L, 64, 1), 64, axis=2).reshape(L, 1, 4096)
    m["s5_lreB"] = f(np.broadcast_to(lre, (L, 128, 4096)))
    m["s5_limB"] = f(np.broadcast_to(lim, (L, 128, 4096)))
    m["s5_stpB"] = f(np.broadcast_to(stp, (L, 128, 4096)))
    def padb(bx):
        o = np.zeros((L, 8, 16, 2, 32, 64), np.float32)
        bt = np.asarray(bx).transpose(0, 4, 1, 2, 3)
        for g in range(32):
            o[:, g % 8, :, :, g, :] = bt[:, :, :, g, :]
        return f(o.reshape(L, 128, 4096))
    m["s5_bre"] = padb(inputs["s5_b_re"])
    m["s5_bim"] = padb(inputs["s5_b_im"])
    def padc(cx):
        o = np.zeros((L, 64, 2, 32, 8, 16), np.float32)
        ct = np.asarray(cx).transpose(0, 4, 1, 2, 3)
        for g in range(32):
            o[:, :, :, g, g % 8, :] = ct[:, :, :, g, :]
        return o.reshape(L, 64, 8192)
    cre, cim = padc(inputs["s5_c_re"]), padc(inputs["s5_c_im"])
    m["s5_c1"] = f(np.concatenate([cre, cim], axis=1))
    m["s5_c2"] = f(np.concatenate([cim, cre], axis=1))
    lreP = np.asarray(inputs["s5_lam_re"]).reshape(L, 64, 64).transpose(0, 2, 1)
    limP = np.asarray(inputs["s5_lam_im"]).reshape(L, 64, 64).transpose(0, 2, 1)
    m["s5_lreP"] = f(np.concatenate([lreP, lreP], axis=1))
    m["s5_limP"] = f(np.concatenate([limP, limP], axis=1))
    m["s5_stpP"] = f(np.broadcast_to(np.asarray(inputs["s5_log_step"]).reshape(L, 1, 64), (L, 128, 64)))
    m["s5_dT"] = f(np.asarray(inputs["s5_d"]).reshape(L, 4, 128).transpose(0, 2, 1))
    m["s5_glu_w"] = f(inputs["s5_glu_w"])
    m["s5_glu_bT"] = f(np.asarray(inputs["s5_glu_b"]).reshape(L, 4, 128).transpose(0, 2, 1))
    m["branch_gT"] = f(np.asarray(inputs["branch_g"]).reshape(L, 3, 4, 128).transpose(0, 1, 3, 2))
    tpos = np.arange(LLAT)
    rowp, colp = (tpos // 64).astype(np.float32), (tpos % 64).astype(np.float32)
    inv = (1.0 / (10000.0 ** (np.arange(16, dtype=np.float32) / 16))).astype(np.float32)
    ang_r, ang_c = rowp[:, None] * inv[None, :], colp[:, None] * inv[None, :]
    cosT = np.concatenate([np.cos(ang_r), np.cos(ang_r), np.cos(ang_c), np.cos(ang_c)], axis=1)
    sinT = np.concatenate([-np.sin(ang_r), np.sin(ang_r), -np.sin(ang_c), np.sin(ang_c)], axis=1)
    m["ropeC"] = f(cosT.reshape(16, 128, 64).transpose(1, 0, 2))
    m["ropeS"] = f(sinT.reshape(16, 128, 64).transpose(1, 0, 2))
    gq, gk = np.asarray(inputs["att_q_g"]), np.asarray(inputs["att_k_g"])
    gqk = np.concatenate([np.tile(gq, (1, 8)), np.tile(gk, (1, 2))], axis=1)
    m["att_gqk"] = f(np.broadcast_to(gqk[:, None, :], (L, 128, 640)))
    m["att_sinkB"] = f(np.broadcast_to(np.asarray(inputs["att_sink"])[:, None, :], (L, 128, 8)))
    kk_, qq_ = np.arange(128)[:, None], np.arange(128)[None, :]
    m["maskA"] = (kk_ >= qq_).astype(np.float32).astype(ml_dtypes.bfloat16)
    m["maskC"] = (kk_ <= qq_).astype(np.float32).astype(ml_dtypes.bfloat16)
    m["hy_w1"] = f(inputs["hy_w1"]); m["hy_w2"] = f(inputs["hy_w2"]); m["hy_w3"] = f(inputs["hy_w3"])
    m["hy_prm"] = f(np.stack([inputs["hy_b1"], inputs["hy_f1"], inputs["hy_b2"], inputs["hy_f2"]], axis=-1))
    m["hy_skipT"] = f(np.asarray(inputs["hy_skip"]).reshape(L, 2, 4, 128).transpose(0, 3, 1, 2).reshape(L, 128, 8))
    cwt = np.concatenate([np.asarray(inputs["hy_conv_w"]), np.asarray(inputs["hy_conv_b"])[:, None, :]], axis=1)
    m["hy_convT"] = f(cwt.reshape(L, 4, 12, 128).transpose(0, 3, 2, 1))
    deltas = np.abs(np.linspace(math.log(1e-2) / 1.5, math.log(1e-2) / 0.3, 512, dtype=np.float32))
    for nm, n in (("L", LLAT), ("C", LCTX)):
        lag = np.arange(n)
        t = np.linspace(0.0, 1.0, n, dtype=np.float32)
        ang = (2.0 * math.pi * np.arange(n, dtype=np.float32) / n).astype(np.float32)
        bands = np.linspace(1e-4, 15, 16, dtype=np.float32)[None, :]
        z = np.concatenate([t[:, None], np.cos(bands * ang[:, None]), -np.sin(bands * ang[:, None])], axis=-1).astype(np.float32)
        dec = np.exp(-t[:, None] * deltas[None, :]).astype(np.float32)
        mm = np.arange(2 * n)
        lag_of = np.where(mm >= n, mm - n, n - mm)
        lag_of[0] = 0
        lag_r = lag_of[2 * n - 1 - mm]
        m["hy_z" + nm] = f(z[lag_of].T); m["hy_z" + nm + "r"] = f(z[lag_r].T)
        m["hy_d" + nm] = f(dec[lag_of].T); m["hy_d" + nm + "r"] = f(dec[lag_r].T)
    mp, mn = np.asarray(inputs["rw_mu_prev"]), np.asarray(inputs["rw_mu_next"])
    muH = np.stack([mp[:, :1536], mn[:, :1536]], axis=-1).reshape(L, 24, 64, 2).transpose(0, 2, 1, 3)
    m["rw_muH"] = f(muH)
    m["rw_muL"] = f(np.stack([mp[:, 1536:], mn[:, 1536:]], axis=-1))
    hh_ = lambda a: np.asarray(a).reshape(L, 8, 64).transpose(0, 2, 1)
    w0, a0 = np.asarray(inputs["rw_w0"]), np.asarray(inputs["rw_a0"])
    m["rw_hp"] = f(np.stack([hh_(inputs["rw_k_k"]), hh_(inputs["rw_k_a"]), hh_(inputs["rw_r_k"]), hh_(w0[:, 0]), hh_(w0[:, 1]), hh_(a0[:, 0]), hh_(a0[:, 1])], axis=-1))
    w2p = np.zeros((L, 2, 128, 512), np.float32); a2p = np.zeros((L, 2, 128, 512), np.float32)
    for di in range(2):
        w2p[:, di, di * 32:(di + 1) * 32] = np.asarray(inputs["rw_w2"])[:, di]
        a2p[:, di, (2 + di) * 32:(3 + di) * 32] = np.asarray(inputs["rw_a2"])[:, di]
    m["rw_w2pad"] = w2p; m["rw_a2pad"] = a2p
    m["rw_lnT"] = f(np.stack([np.asarray(inputs["rw_ln_g"]).reshape(L, 4, 128).transpose(0, 2, 1), np.asarray(inputs["rw_ln_b"]).reshape(L, 4, 128).transpose(0, 2, 1)], axis=-1))
    m["rw_mask"] = np.ascontiguousarray(np.broadcast_to((np.arange(LS) % 128 != 0).astype(np.float32), (64, LS))).astype(ml_dtypes.bfloat16)
    ss_, tt_ = np.arange(128)[:, None], np.arange(128)[None, :]
    m["rw_mk2"] = np.concatenate([(ss_ < tt_), (ss_ <= tt_)], axis=1).astype(np.float32).astype(ml_dtypes.bfloat16)
    bd_ = (ss_ // 64) == (tt_ // 64)
    m["rw_mkL"] = ((ss_ > tt_) & bd_).astype(np.float32).astype(ml_dtypes.bfloat16)
    m["rw_mkBD"] = bd_.astype(np.float32).astype(ml_dtypes.bfloat16)
    m["rw_mkOFF"] = ((ss_ < 64) & (tt_ >= 64)).astype(np.float32).astype(ml_dtypes.bfloat16)
    m["rw_J"] = np.ascontiguousarray(np.eye(128, dtype=np.float32)[::-1]).astype(ml_dtypes.bfloat16)
    m["rw_bd"] = f(np.kron(np.eye(2, dtype=np.float32), np.full((64, 64), 1.0 / 64, np.float32)))
    tt = np.arange(LS)
    m["tabA"] = np.ascontiguousarray(np.broadcast_to((tt // 64).astype(np.float32), (128, LS))).astype(ml_dtypes.bfloat16)
    m["tabB"] = np.ascontiguousarray(np.broadcast_to((tt % 64).astype(np.float32), (128, LS))).astype(ml_dtypes.bfloat16)
    return m


_CACHE = {}


def kernel(**inputs):
    if "nc" not in _CACHE:
        p = Prog()
        _CACHE["nc"] = p.build()
        _CACHE["p"] = p
    nc, p = _CACHE["nc"], _CACHE["p"]
    maps = []
    for c in range(N_CORES):
        m = host_inputs(inputs, c)
        maps.append({k: m[k] for k in p.inputs})
    res = run_bass_kernel_spmd(nc, maps, core_ids=list(range(N_CORES)))
    out = np.concatenate([np.asarray(r["out"]) for r in res.results], axis=0)
    return out.astype(np.float32)
```

```python
import math
from contextlib import ExitStack

import numpy as np
import ml_dtypes

import concourse.bass as bass
import concourse.mybir as mybir
from concourse.bass_utils import run_bass_kernel_spmd

F32 = mybir.dt.float32
BF16 = mybir.dt.bfloat16
I32 = mybir.dt.int32
AF = mybir.ActivationFunctionType
ALU = mybir.AluOpType
AX = mybir.AxisListType

D = 2048
DEPTH = 4
NB = 2
LCTX = 256
LLAT = 2048
LS = LCTX + LLAT
D_IN = 6528
W = 512
EPS = 1e-6
N_CORES = 8
R_S5, R_RKV, R_LORA, R_HY, R_GATE, R_FM = 0, 512, 2048, 2176, 3712, 5760
C_S5, C_Q, C_K, C_V, C_RKV, C_LORA, C_HY, C_GATE = 0, 512, 1024, 1152, 1280, 2816, 2944, 4480


class _Sem:
    __slots__ = ("h", "owner")

    def __init__(self, h, owner):
        self.h, self.owner = h, owner


class Buf:
    __slots__ = ("w", "r", "multi", "name")

    def __init__(self, name="", multi=False):
        self.w, self.r, self.multi, self.name = {}, {}, multi, name


class _Eng:
    def __init__(self, tk, name, h):
        self.tk, self.name, self.h = tk, name, h
        self.sem = tk._new_sem(name)
        self.cnt = 0
        self.seen = {}

    def tick(self):
        if self.cnt >= 30000:
            self.sem = self.tk._new_sem(self.name)
            self.cnt = 0
        self.cnt += 1
        return self.sem, self.cnt


class _Slot:
    def __init__(self, tk):
        self.sem = tk._new_sem(None)
        self.val = 0


class T:
    def __init__(self, t, name, multi=False):
        self.t = t
        self.b = Buf(name, multi)


class TK:
    def __init__(self, nc, es):
        self.nc, self.es = nc, es
        self.nsem = 0
        self.E = {}
        for n, h in (("pe", nc.tensor), ("act", nc.scalar), ("dve", nc.vector), ("pool", nc.gpsimd), ("sp", nc.sync)):
            self.E[n] = _Eng(self, n, h)
        self.slots = {"sp": [_Slot(self) for _ in range(14)], "pool": [_Slot(self) for _ in range(8)],
                      "act": [_Slot(self) for _ in range(6)]}
        self.slot_i = {q: 0 for q in self.slots}
        self.n_ins = 0

    def _new_sem(self, owner):
        self.nsem += 1
        h = self.es.enter_context(self.nc.semaphore(f"sm{self.nsem}"))
        return _Sem(h, owner)

    def sb(self, es, name, shape, dt, multi=False):
        self.uid = getattr(self, "uid", 0) + 1
        name = f"{name}_{self.uid}"
        return T(es.enter_context(self.nc.sbuf_tensor(name, shape, dt)), name, multi)

    def ps(self, es, name, shape, dt):
        return T(es.enter_context(self.nc.psum_tensor(name, shape, dt)), name)

    def dram(self, name, shape, dt, kind="Internal", multi=True):
        return T(self.nc.dram_tensor(name, shape, dt, kind=kind), name, multi)

    @staticmethod
    def _bufs(xs):
        return [x.b if isinstance(x, T) else x for x in xs]

    def _collect(self, r, w):
        deps = {}

        def add(d, raw):
            for k, (s, v) in d.items():
                if k not in deps or deps[k][1] < v:
                    deps[k] = (s, v, raw or (k in deps and deps[k][2]))
                elif raw:
                    deps[k] = (deps[k][0], deps[k][1], True)

        for b in r:
            add(b.w, True)
        for b in w:
            if b.multi and not b.r:
                continue
            add(b.w, False)
            add(b.r, False)
        return deps

    def _update(self, r, w, s, v):
        k = id(s)
        for b in w:
            if b.multi and not b.r:
                if k not in b.w or b.w[k][1] < v:
                    b.w[k] = (s, v)
            else:
                b.w = {k: (s, v)}
                b.r = {}
        for b in r:
            if k not in b.r or b.r[k][1] < v:
                b.r[k] = (s, v)

    def _wait(self, e, deps):
        for k, (s, v, raw) in deps.items():
            if s.owner == e.name and (not raw or e.name == "pe"):
                continue
            if e.seen.get(k, 0) >= v:
                continue
            e.h.wait_ge(s.h, v)
            e.seen[k] = v

    def op(self, eng, fn, r=(), w=()):
        e = self.E[eng]
        r, w = self._bufs(r), self._bufs(w)
        self._wait(e, self._collect(r, w))
        ins = fn(e.h)
        s, v = e.tick()
        ins.then_inc(s.h, 1)
        self._update(r, w, s, v)
        self.n_ins += 1

    def dma(self, q, out, in_, r=(), w=(), **kw):
        e = self.E[q]
        r, w = self._bufs(r), self._bufs(w)
        sl = self.slots[q][self.slot_i[q] % len(self.slots[q])]
        self.slot_i[q] += 1
        deps = self._collect(r, w)
        if sl.val:
            deps[id(sl.sem)] = (sl.sem, sl.val, True)
        self._wait(e, deps)
        ins = e.h.dma_start(out=out, in_=in_, **kw)
        sl.val += 16
        ins.then_inc(sl.sem.h, 16)
        self._update(r, w, sl.sem, sl.val)
        self.n_ins += 1

    def barrier(self, final=False):
        names = ["sp"] if final else list(self.E)
        for n in names:
            e = self.E[n]
            deps = {}
            for e2 in self.E.values():
                if e2.cnt:
                    deps[id(e2.sem)] = (e2.sem, e2.cnt, True)
            for q in self.slots.values():
                for sl in q:
                    if sl.val:
                        deps[id(sl.sem)] = (sl.sem, sl.val, True)
            self._wait(e, deps)


class Prog:
    def __init__(self, layers=DEPTH, dump=(), stop_after=None, mixers=("s5", "att", "rw", "hy")):
        self.layers = layers
        self.dump = set(dump)
        self.stop_after = stop_after
        self.mixers = mixers
        self.nc = bass.Bass("TRN2", target_bir_lowering=False)
        self.es = ExitStack()
        self.inputs = {}

    def inp(self, name, shape, dt=F32):
        t = self.nc.dram_tensor(name, list(shape), dt, kind="ExternalInput")
        self.inputs[name] = (tuple(shape), dt)
        return T(t, name)

    def build(self):
        nc = self.nc
        with self.es:
            tk = self.tk = TK(nc, self.es)
            self.declare_io()
            self.consts()
            for l in range(self.layers):
                self.layer(l)
            tk.barrier(final=True)
        return nc

    def declare_io(self):
        tk = self.tk
        L = DEPTH
        kd = lambda n: "ExternalOutput" if n in self.dump else "Internal"
        self.x_in = self.inp("x", [NB, LLAT, D])
        self.ctx_in = self.inp("ctx", [NB, LCTX, D])
        self.cT = self.inp("cT", [128, 16, 3])
        self.normgT = self.inp("normgT", [L, 128, 16])
        self.w_ada = self.inp("w_ada", [L, D, 3 * D])
        self.b_adaT = self.inp("b_adaT", [L, 128, 48])
        self.w_in = self.inp("w_in", [L, D, D_IN])
        self.w_out = self.inp("w_out", [L, D, D])
        self.s5_lreB = self.inp("s5_lreB", [L, 128, 4096])
        self.s5_limB = self.inp("s5_limB", [L, 128, 4096])
        self.s5_stpB = self.inp("s5_stpB", [L, 128, 4096])
        self.s5_bre = self.inp("s5_bre", [L, 128, 4096])
        self.s5_bim = self.inp("s5_bim", [L, 128, 4096])
        self.s5_c1 = self.inp("s5_c1", [L, 128, 8192])
        self.s5_c2 = self.inp("s5_c2", [L, 128, 8192])
        self.s5_lreP = self.inp("s5_lreP", [L, 128, 64])
        self.s5_limP = self.inp("s5_limP", [L, 128, 64])
        self.s5_stpP = self.inp("s5_stpP", [L, 128, 64])
        self.s5_dT = self.inp("s5_dT", [L, 128, 4])
        self.s5_glu_w = self.inp("s5_glu_w", [L, 512, 512])
        self.s5_glu_bT = self.inp("s5_glu_bT", [L, 128, 4])
        self.branch_gT = self.inp("branch_gT", [L, 3, 128, 4])
        self.ropeC = self.inp("ropeC", [128, 16, 64])
        self.ropeS = self.inp("ropeS", [128, 16, 64])
        self.att_gqk = self.inp("att_gqk", [L, 128, 640])
        self.att_sinkB = self.inp("att_sinkB", [L, 128, 8])
        self.maskA = self.inp("maskA", [128, 128], BF16)
        self.maskC = self.inp("maskC", [128, 128], BF16)
        self.hy_w1 = self.inp("hy_w1", [L, 33, 64])
        self.hy_w2 = self.inp("hy_w2", [L, 64, 64])
        self.hy_w3 = self.inp("hy_w3", [L, 64, 2048])
        self.hy_prm = self.inp("hy_prm", [L, 64, 4])
        self.hy_skipT = self.inp("hy_skipT", [L, 128, 8])
        self.hy_convT = self.inp("hy_convT", [L, 128, 12, 4])
        self.hy_zL = self.inp("hy_zL", [33, 4096])
        self.hy_zLr = self.inp("hy_zLr", [33, 4096])
        self.hy_zC = self.inp("hy_zC", [33, 512])
        self.hy_zCr = self.inp("hy_zCr", [33, 512])
        self.hy_dL = self.inp("hy_dL", [512, 4096])
        self.hy_dLr = self.inp("hy_dLr", [512, 4096])
        self.hy_dC = self.inp("hy_dC", [512, 512])
        self.hy_dCr = self.inp("hy_dCr", [512, 512])
        self.hy_GL = tk.dram("hy_GL", [2, 512, 4096], BF16, kind=kd("hy_GL"))
        self.hy_GC = tk.dram("hy_GC", [2, 512, 512], BF16, kind=kd("hy_GC"))
        self.rw_muH = self.inp("rw_muH", [L, 64, 24, 2])
        self.rw_muL = self.inp("rw_muL", [L, 128, 2])
        self.rw_hp = self.inp("rw_hp", [L, 64, 8, 7])
        self.rw_w2pad = self.inp("rw_w2pad", [L, 2, 128, 512])
        self.rw_a2pad = self.inp("rw_a2pad", [L, 2, 128, 512])
        self.rw_lnT = self.inp("rw_lnT", [L, 128, 4, 2])
        self.rw_mask = self.inp("rw_mask", [64, LS], BF16)
        self.rw_mk2 = self.inp("rw_mk2", [128, 256], BF16)
        self.rw_mkL = self.inp("rw_mkL", [128, 128], BF16)
        self.rw_J = self.inp("rw_J", [128, 128], BF16)
        self.rw_mkBD = self.inp("rw_mkBD", [128, 128], BF16)
        self.rw_mkOFF = self.inp("rw_mkOFF", [128, 128], BF16)
        self.rw_bd = self.inp("rw_bd", [128, 128])
        self.rw_s = tk.dram("rw_s", [NB, 2, 8, 5, 64, LS], BF16, kind=kd("rw_s"))
        self.rw_g = tk.dram("rw_g", [NB, 2, 8, 64, LS // 128], F32, kind=kd("rw_g"))
        self.rw_bv = tk.dram("rw_bv", [NB, 512, LS], F32, kind=kd("rw_bv"))
        self.tabA = self.inp("tabA", [128, LS], BF16)
        self.tabB = self.inp("tabB", [128, LS], BF16)
        self.out = tk.dram("out", [NB, LLAT, D], F32, kind="ExternalOutput")
        self.xc = tk.dram("xc_s", [NB, LCTX, D], F32, kind=kd("xc_s"))
        self.pfm = tk.dram("pfm", [R_FM, NB, LS], BF16, kind=kd("pfm"))
        self.qkv = tk.dram("qkv", [NB, LS, 768], BF16, kind=kd("qkv"))
        self.ym = tk.dram("ym", [D, NB, LS], BF16, kind=kd("ym"))
        self.dumps = {}

    def dump_t(self, name, shape, dt):
        t = self.tk.dram("dbg_" + name, shape, dt, kind="ExternalOutput")
        self.dumps[name] = t
        return t

    def consts(self):
        tk, es = self.tk, self.es
        self.ident_f = tk.sb(es, "ident_f", [128, 128], F32)
        self.ident_b = tk.sb(es, "ident_b", [128, 128], BF16)
        self.ones_f = tk.sb(es, "ones_f", [128, 128], F32)
        self.eps_t = tk.sb(es, "eps_t", [128, 1], F32)
        tk.op("pool", lambda e: e.memset(self.ident_f.t[:], 1.0), w=[self.ident_f])
        tk.op("pool", lambda e: e.affine_select(out=self.ident_f.t[:], in_=self.ident_f.t[:], pattern=[[-1, 128]],
                                                compare_op=ALU.is_equal, fill=0.0, base=0, channel_multiplier=1),
              r=[self.ident_f], w=[self.ident_f])
        tk.op("dve", lambda e: e.tensor_copy(out=self.ident_b.t[:], in_=self.ident_f.t[:]), r=[self.ident_f], w=[self.ident_b])
        tk.op("dve", lambda e: e.memset(self.ones_f.t[:], 1.0), w=[self.ones_f])
        tk.op("dve", lambda e: e.memset(self.eps_t.t[:], EPS), w=[self.eps_t])
        self.sc = tk.sb(es, "sc", [128, 16, 3], F32)
        self.modT = tk.sb(es, "modT", [128, 48, 3], F32)
        self.Am = tk.sb(es, "Am", [128, 16, 3], F32)
        self.psb = [tk.ps(es, f"psb{i}", [128, 512], F32) for i in range(8)]
        ctile = tk.sb(es, "ctile", [128, 16, 3], F32)
        tk.dma("sp", ctile.t[:], self.cT.t.ap(), w=[ctile])
        tk.op("act", lambda e: e.activation(out=self.sc.t[:], in_=ctile.t[:], func=AF.Silu), r=[ctile], w=[self.sc])

    def layer(self, l):
        self.stage_ada(l)
        self.tk.barrier()
        self.stage_inproj(l)
        self.tk.barrier()
        if self.stop_after == "inproj":
            return
        self.stage_mixers(l)
        self.tk.barrier()
        if self.stop_after == "mixers":
            return
        self.stage_outproj(l)
        self.tk.barrier()

    def stage_ada(self, l):
        tk = self.tk
        with ExitStack() as es:
            wa = [tk.sb(es, f"wa{i}", [128, 16, 512], F32) for i in range(2)]
            bada = tk.sb(es, "bada", [128, 48], F32)
            ng = tk.sb(es, "ng", [128, 16], F32)
            tmp = tk.sb(es, "adatmp", [128, 16], F32)
            pm = self.psb[0]
            tk.dma("sp", bada.t[:], self.b_adaT.t.ap()[l], w=[bada])
            tk.dma("sp", ng.t[:], self.normgT.t.ap()[l], w=[ng])
            for pc in range(12):
                wt = wa[pc % 2]
                tk.dma("sp" if pc % 2 == 0 else "act", wt.t[:],
                       self.w_ada.t.ap()[l, :, pc * 512:(pc + 1) * 512].rearrange("(k p) n -> p k n", p=128), w=[wt])
                for n4 in range(4):
                    n = pc * 4 + n4
                    for k in range(16):
                        tk.op("pe", lambda e, n=n, n4=n4, wt=wt, k=k: e.matmul(pm.t[:, n * 3:(n + 1) * 3], lhsT=wt.t[:, k, n4 * 128:(n4 + 1) * 128],
                                                                              rhs=self.sc.t[:, k, :], start=(k == 0), stop=(k == 15)),
                              r=[wt, self.sc], w=[pm])
            pv = pm.t[:, 0:144].rearrange("p (n j) -> p n j", j=3)
            for j in range(3):
                tk.op("dve", lambda e, j=j: e.tensor_tensor(out=self.modT.t[:, :, j], in0=pv[:, :, j], in1=bada.t[:], op=ALU.add),
                      r=[pm, bada], w=[self.modT])
            for j in range(3):
                tk.op("dve", lambda e, j=j: e.tensor_scalar(out=tmp.t[:], in0=self.modT.t[:, 16:32, j], scalar1=1.0, scalar2=None, op0=ALU.add),
                      r=[self.modT], w=[tmp])
                tk.op("dve", lambda e, j=j: e.tensor_tensor(out=self.Am.t[:, :, j], in0=tmp.t[:], in1=ng.t[:], op=ALU.mult),
                      r=[tmp, ng], w=[self.Am])
            if "modT" in self.dump and l == 0:
                d1 = self.dump_t("modT", [128, 144], F32)
                tk.dma("sp", d1.t.ap(), self.modT.t[:].rearrange("p n j -> p (n j)"), r=[self.modT], w=[d1])
                d3 = self.dump_t("Am", [128, 48], F32)
                tk.dma("sp", d3.t.ap(), self.Am.t[:].rearrange("p n j -> p (n j)"), r=[self.Am], w=[d3])

    def x_src(self, l, b, seg):
        if seg == 0:
            return (self.ctx_in if l == 0 else self.xc)
        return (self.x_in if l == 0 else self.out)

    def stage_inproj(self, l):
        tk = self.tk
        TG = 768
        with ExitStack() as es:
            hxT = [tk.sb(es, f"hxT{i}", [128, 16, TG], BF16) for i in range(2)]
            wb = [tk.sb(es, f"wb{i}", [128, 16, 512], BF16) for i in range(3)]
            xt = [tk.sb(es, f"xt{i}", [128, D], F32) for i in range(2)]
            sq = tk.sb(es, "sqj", [128, D], BF16)
            xs = [tk.sb(es, f"xs{i}", [128, D], BF16) for i in range(2)]
            st = [tk.sb(es, f"st{i}", [128, 3], F32) for i in range(2)]
            fo = [tk.sb(es, f"fo{i}", [128, TG], BF16) for i in range(4)]
            to = [tk.sb(es, f"to{i}", [128, 768], BF16) for i in range(2)]
            pieces = [(0, 512, "fm", R_S5)]
            c = C_RKV
            while c < D_IN:
                n = min(512, D_IN - c)
                pieces.append((c, n, "fm", R_RKV + (c - C_RKV)))
                c += n
            pieces += [(512, 512, "tm", 0), (1024, 256, "tm", 512)]
            wi = 0
            blk = 0
            foi = 0
            pi = 0
            for tg in range(6):
                b, p0 = tg // 3, (tg % 3) * TG
                h = hxT[tg % 2]
                for kb in range(6):
                    pos = p0 + kb * 128
                    seg = 0 if pos < LCTX else 1
                    j = 2 if seg == 0 else b
                    src = self.x_src(l, b, seg)
                    row0 = pos if seg == 0 else pos - LCTX
                    x_t, xs_t, st_t = xt[blk % 2], xs[blk % 2], st[blk % 2]
                    tk.dma("sp", x_t.t[:], src.t.ap()[b, row0:row0 + 128, :], r=[src], w=[x_t])
                    tk.op("act", lambda e, x_t=x_t: e.activation(out=sq.t[:], in_=x_t.t[:], func=AF.Square), r=[x_t], w=[sq])
                    tk.op("dve", lambda e, st_t=st_t: e.reduce_sum(out=st_t.t[:, 0:1], in_=sq.t[:], axis=AX.X), r=[sq], w=[st_t])
                    tk.op("act", lambda e, st_t=st_t: e.activation(out=st_t.t[:, 1:2], in_=st_t.t[:, 0:1], func=AF.Sqrt, bias=self.eps_t.t[:], scale=1.0 / D),
                          r=[st_t, self.eps_t], w=[st_t])
                    tk.op("dve", lambda e, st_t=st_t: e.reciprocal(out=st_t.t[:, 2:3], in_=st_t.t[:, 1:2]), r=[st_t], w=[st_t])
                    tk.op("act", lambda e, x_t=x_t, xs_t=xs_t, st_t=st_t: e.activation(out=xs_t.t[:], in_=x_t.t[:], func=AF.Copy, scale=st_t.t[:, 2:3]),
                          r=[x_t, st_t], w=[xs_t])
                    for half in range(2):
                        pb = self.psb[pi % 4]
                        pi += 1
                        pv = pb.t[:].bitcast(BF16)
                        for kk in range(8):
                            k = half * 8 + kk
                            tk.op("pe", lambda e, pv=pv, kk=kk, k=k, xs_t=xs_t: e.transpose(out=pv[:, kk * 128:(kk + 1) * 128], in_=xs_t.t[:, k * 128:(k + 1) * 128],
                                                                                             identity=self.ident_b.t[:]), r=[xs_t, self.ident_b], w=[pb])
                        for kk in range(8):
                            k = half * 8 + kk
                            eng = "dve" if kk % 2 == 0 else "pool"
                            if eng == "pool":
                                eng = "dve"
                            tk.op(eng, lambda e, pv=pv, kk=kk, k=k, j=j, kb=kb, h=h: e.tensor_scalar(
                                out=h.t[:, k, kb * 128:(kb + 1) * 128], in0=pv[:, kk * 128:(kk + 1) * 128],
                                scalar1=self.Am.t[:, k, j:j + 1], scalar2=self.modT.t[:, k, j:j + 1], op0=ALU.mult, op1=ALU.add),
                                  r=[pb, self.Am, self.modT], w=[h])
                    blk += 1
                for (c0, n, kind, r0) in pieces:
                    wt = wb[wi % 3]
                    wi += 1
                    tk.dma("pool", wt.t[:, :, 0:n], self.w_in.t.ap()[l, :, c0:c0 + n].rearrange("(k p) n -> p k n", p=128), w=[wt])
                    if kind == "fm":
                        for cc in range(n // 128):
                            f = fo[foi % 4]
                            foi += 1
                            for (t0, tn) in ((0, 512), (512, 256)):
                                pb = self.psb[4 + pi % 4]
                                pi += 1
                                for k in range(16):
                                    tk.op("pe", lambda e, pb=pb, wt=wt, cc=cc, k=k, t0=t0, tn=tn, h=h: e.matmul(
                                        pb.t[:, 0:tn], lhsT=wt.t[:, k, cc * 128:(cc + 1) * 128], rhs=h.t[:, k, t0:t0 + tn],
                                        start=(k == 0), stop=(k == 15)), r=[wt, h], w=[pb])
                                ev = "act" if (pi % 2 == 0) else "dve"
                                if ev == "act":
                                    tk.op("act", lambda e, pb=pb, f=f, t0=t0, tn=tn: e.activation(out=f.t[:, t0:t0 + tn], in_=pb.t[:, 0:tn], func=AF.Copy),
                                          r=[pb], w=[f])
                                else:
                                    tk.op("dve", lambda e, pb=pb, f=f, t0=t0, tn=tn: e.tensor_copy(out=f.t[:, t0:t0 + tn], in_=pb.t[:, 0:tn]),
                                          r=[pb], w=[f])
                            tk.dma("sp", self.pfm.t.ap()[r0 + cc * 128:r0 + (cc + 1) * 128, b, p0:p0 + TG], f.t[:], r=[f], w=[self.pfm])
                    else:
                        for kb in range(6):
                            pb = self.psb[4 + pi % 4]
                            pi += 1
                            o = to[(kb + (r0 // 512)) % 2]
                            for k in range(16):
                                tk.op("pe", lambda e, pb=pb, wt=wt, k=k, kb=kb, n=n, h=h: e.matmul(
                                    pb.t[:, 0:n], lhsT=h.t[:, k, kb * 128:(kb + 1) * 128], rhs=wt.t[:, k, 0:n],
                                    start=(k == 0), stop=(k == 15)), r=[wt, h], w=[pb])
                            tk.op("act", lambda e, pb=pb, o=o, n=n: e.activation(out=o.t[:, 0:n], in_=pb.t[:, 0:n], func=AF.Copy), r=[pb], w=[o])
                            tk.dma("sp", self.qkv.t.ap()[b, p0 + kb * 128:p0 + (kb + 1) * 128, r0:r0 + n], o.t[:, 0:n], r=[o], w=[self.qkv])

    def stage_mixers(self, l):
        last = (l == DEPTH - 1)
        if "s5" in self.mixers:
            self.mixer_s5(l)
            self.tk.barrier()
        if "att" in self.mixers:
            self.mixer_att(l)
            self.tk.barrier()
        if "hy" in self.mixers:
            self.mixer_hy(l)
            self.tk.barrier()
        if "rw" in self.mixers:
            self.mixer_rw(l)
            self.tk.barrier()

    def finish_branch(self, l, es, br, b, ytiles, norm, gcol=None):
        tk = self.tk
        nm = f"fb{br}"
        if not hasattr(self, "_fb") or self._fb[0] is not es:
            rstd = tk.sb(es, nm + "rstd", [128, LS], F32) if norm else None
            sqb = [tk.sb(es, nm + f"sq{i}", [128, LS], BF16) for i in range(2)] if norm else None
            gt_ = [tk.sb(es, nm + f"g{i}", [128, LS], BF16) for i in range(2)]
            sg = [tk.sb(es, nm + f"sg{i}", [128, LS], F32) for i in range(2)]
            ob = [tk.sb(es, nm + f"ob{i}", [128, LS], BF16) for i in range(2)]
            onesb = tk.sb(es, nm + "ones", [128, 128], BF16)
            tk.op("pool", lambda e: e.memset(onesb.t[:], 1.0), w=[onesb])
            self._fb = (es, rstd, sqb, gt_, sg, ob, onesb)
        _, rstd, sqb, gt_, sg, ob, onesb = self._fb
        pieces = [(0, 512), (512, 512), (1024, 512), (1536, 512), (2048, 256)]
        if norm:
            for i in range(4):
                tk.op("act", lambda e, i=i: e.activation(out=sqb[i % 2].t[:], in_=ytiles[i].t[:], func=AF.Square), r=[ytiles[i]], w=[sqb[i % 2]])
                for pi_, (t0, n) in enumerate(pieces):
                    pb = self.psb[pi_]
                    tk.op("pe", lambda e, i=i, pb=pb, t0=t0, n=n: e.matmul(pb.t[:, 0:n], lhsT=onesb.t[:], rhs=sqb[i % 2].t[:, t0:t0 + n],
                                                                             start=(i == 0), stop=(i == 3)), r=[onesb, sqb[i % 2]], w=[pb])
            for pi_, (t0, n) in enumerate(pieces):
                pb = self.psb[pi_]
                tk.op("act", lambda e, pb=pb, t0=t0, n=n: e.activation(out=rstd.t[:, t0:t0 + n], in_=pb.t[:, 0:n], func=AF.Sqrt, bias=self.eps_t.t[:], scale=1.0 / W),
                      r=[pb, self.eps_t], w=[rstd])
            tk.op("dve", lambda e: e.reciprocal(out=rstd.t[:], in_=rstd.t[:]), r=[rstd], w=[rstd])
        for i in range(4):
            g, s_, o = gt_[i % 2], sg[i % 2], ob[i % 2]
            row = R_GATE + br * 512 + i * 128
            tk.dma("sp", g.t[:], self.pfm.t.ap()[row:row + 128, b, :], r=[self.pfm], w=[g])
            tk.op("act", lambda e, g=g, s_=s_: e.activation(out=s_.t[:], in_=g.t[:], func=AF.Silu), r=[g], w=[s_])
            if norm:
                tk.op("pool", lambda e, s_=s_: e.tensor_tensor(out=s_.t[:], in0=s_.t[:], in1=rstd.t[:], op=ALU.mult), r=[s_, rstd], w=[s_])
                tk.op("dve", lambda e, i=i, s_=s_, o=o: e.scalar_tensor_tensor(out=o.t[:], in0=ytiles[i].t[:], scalar=gcol.t[:, i:i + 1], in1=s_.t[:],
                                                                                op0=ALU.mult, op1=ALU.mult), r=[ytiles[i], gcol, s_], w=[o])
            else:
                tk.op("dve", lambda e, i=i, s_=s_, o=o: e.tensor_tensor(out=o.t[:], in0=ytiles[i].t[:], in1=s_.t[:], op=ALU.mult), r=[ytiles[i], s_], w=[o])
            tk.dma("sp", self.ym.t.ap()[br * 512 + i * 128:br * 512 + (i + 1) * 128, b, :], o.t[:], r=[o], w=[self.ym])

    def mixer_s5(self, l):
        tk = self.tk
        MAGIC = 12582912.0
        TWO_PI = 2.0 * math.pi
        PIECES = [(0, 256), (256, 512), (768, 512), (1280, 512), (1792, 512)]
        with ExitStack() as es0, ExitStack() as es:
            ygel = [[tk.sb(es0, f"s5yg{gt}{b}", [128, LS], BF16) for b in range(NB)] for gt in range(4)]
            sb = lambda n, sh, dt=F32: tk.sb(es, "s5" + n, sh, dt)
            BBt = sb("BBt", [128, 64, 2, 128], BF16)
            Ct = sb("Ct", [128, 64, 2, 128], BF16)
            pl = sb("pl", [128, 6, 64])
            negpi = sb("negpi", [128, 1])
            tk.op("pool", lambda e: e.memset(negpi.t[:], 0.0), w=[negpi])

            def sincos(ph, n, sin_out, cos_out, tmp):
                V = lambda t: t.t[:, 0:n]
                tk.op("pool", lambda e: e.tensor_scalar(out=V(tmp), in0=V(ph), scalar1=MAGIC, scalar2=MAGIC, op0=ALU.add, op1=ALU.subtract), r=[ph], w=[tmp])
                tk.op("pool", lambda e: e.tensor_tensor(out=V(tmp), in0=V(ph), in1=V(tmp), op=ALU.subtract), r=[ph, tmp], w=[tmp])
                tk.op("act", lambda e: e.activation(out=V(sin_out), in_=V(tmp), func=AF.Sin, bias=negpi.t[:], scale=TWO_PI), r=[tmp, negpi], w=[sin_out])
                tk.op("pool", lambda e: e.tensor_scalar(out=V(cos_out), in0=V(ph), scalar1=0.25, scalar2=None, op0=ALU.add), r=[ph], w=[cos_out])
                tk.op("pool", lambda e: e.tensor_scalar(out=V(tmp), in0=V(cos_out), scalar1=MAGIC, scalar2=MAGIC, op0=ALU.add, op1=ALU.subtract), r=[cos_out], w=[tmp])
                tk.op("pool", lambda e: e.tensor_tensor(out=V(tmp), in0=V(cos_out), in1=V(tmp), op=ALU.subtract), r=[cos_out, tmp], w=[tmp])
                tk.op("act", lambda e: e.activation(out=V(cos_out), in_=V(tmp), func=AF.Sin, bias=negpi.t[:], scale=TWO_PI), r=[tmp, negpi], w=[cos_out])

            for hh in range(2):
                self.s5_prep(l, hh, BBt, Ct, sincos)
                tk.barrier()
            self.s5_main(l, es, sb, BBt, Ct, pl, sincos, ygel)
            tk.barrier()
            es.close()
            self.s5_glu(l, ygel)
            tk.barrier()

    def s5_prep(self, l, hh, BBt, Ct, sincos):
        tk = self.tk
        TWO_PI = 2.0 * math.pi
        with ExitStack() as es:
            sb = lambda n, sh, dt=F32: tk.sb(es, "s5p" + n, sh, dt)
            NP = 2048
            c0 = hh * NP
            lre, lim, stp = sb("lre", [128, NP]), sb("lim", [128, NP]), sb("stp", [128, NP])
            t1, t2, t3, t4 = sb("t1", [128, NP]), sb("t2", [128, NP]), sb("t3", [128, NP]), sb("t4", [128, NP])
            tk.dma("sp", lre.t[:], self.s5_lreB.t.ap()[l, :, c0:c0 + NP], w=[lre])
            tk.dma("sp", lim.t[:], self.s5_limB.t.ap()[l, :, c0:c0 + NP], w=[lim])
            tk.dma("sp", stp.t[:], self.s5_stpB.t.ap()[l, :, c0:c0 + NP], w=[stp])

            def disc(lre, lim, stp, t1, t2, t3, t4, n):
                V = lambda t: t.t[:, 0:n]
                tk.op("act", lambda e: e.activation(out=V(stp), in_=V(stp), func=AF.Exp), r=[stp], w=[stp])
                tk.op("dve", lambda e: e.tensor_tensor(out=V(t1), in0=V(lre), in1=V(stp), op=ALU.mult), r=[lre, stp], w=[t1])
                tk.op("act", lambda e: e.activation(out=V(t1), in_=V(t1), func=AF.Exp), r=[t1], w=[t1])
                tk.op("dve", lambda e: e.scalar_tensor_tensor(out=V(t2), in0=V(lim), scalar=1.0 / TWO_PI, in1=V(stp), op0=ALU.mult, op1=ALU.mult),
                      r=[lim, stp], w=[t2])

            disc(lre, lim, stp, t1, t2, t3, t4, NP)
            sn, cs = sb("sn", [128, NP]), sb("cs", [128, NP])
            sincos(t2, NP, sn, cs, t3)
            tk.op("dve", lambda e: e.tensor_tensor(out=cs.t[:], in0=cs.t[:], in1=t1.t[:], op=ALU.mult), r=[cs, t1], w=[cs])
            tk.op("dve", lambda e: e.tensor_tensor(out=sn.t[:], in0=sn.t[:], in1=t1.t[:], op=ALU.mult), r=[sn, t1], w=[sn])
            tk.op("dve", lambda e: e.tensor_scalar(out=cs.t[:], in0=cs.t[:], scalar1=-1.0, scalar2=None, op0=ALU.add), r=[cs], w=[cs])
            tk.op("dve", lambda e: e.tensor_tensor(out=t3.t[:], in0=lre.t[:], in1=lre.t[:], op=ALU.mult), r=[lre], w=[t3])
            tk.op("dve", lambda e: e.tensor_tensor(out=t4.t[:], in0=lim.t[:], in1=lim.t[:], op=ALU.mult), r=[lim], w=[t4])
            tk.op("dve", lambda e: e.tensor_tensor(out=t3.t[:], in0=t3.t[:], in1=t4.t[:], op=ALU.add), r=[t3, t4], w=[t3])
            tk.op("dve", lambda e: e.reciprocal(out=t3.t[:], in_=t3.t[:]), r=[t3], w=[t3])
            tk.op("dve", lambda e: e.tensor_tensor(out=t1.t[:], in0=cs.t[:], in1=lre.t[:], op=ALU.mult), r=[cs, lre], w=[t1])
            tk.op("dve", lambda e: e.tensor_tensor(out=t4.t[:], in0=sn.t[:], in1=lim.t[:], op=ALU.mult), r=[sn, lim], w=[t4])
            tk.op("dve", lambda e: e.tensor_tensor(out=t1.t[:], in0=t1.t[:], in1=t4.t[:], op=ALU.add), r=[t1, t4], w=[t1])
            tk.op("dve", lambda e: e.tensor_tensor(out=t1.t[:], in0=t1.t[:], in1=t3.t[:], op=ALU.mult), r=[t1, t3], w=[t1])
            tk.op("dve", lambda e: e.tensor_tensor(out=t2.t[:], in0=sn.t[:], in1=lre.t[:], op=ALU.mult), r=[sn, lre], w=[t2])
            tk.op("dve", lambda e: e.tensor_tensor(out=t4.t[:], in0=cs.t[:], in1=lim.t[:], op=ALU.mult), r=[cs, lim], w=[t4])
            tk.op("dve", lambda e: e.tensor_tensor(out=t2.t[:], in0=t2.t[:], in1=t4.t[:], op=ALU.subtract), r=[t2, t4], w=[t2])
            tk.op("dve", lambda e: e.tensor_tensor(out=t2.t[:], in0=t2.t[:], in1=t3.t[:], op=ALU.mult), r=[t2, t3], w=[t2])
            co_re, co_im = t1, t2
            bre, bim = lre, lim
            tk.dma("sp", bre.t[:], self.s5_bre.t.ap()[l, :, c0:c0 + NP], r=[], w=[bre])
            tk.dma("sp", bim.t[:], self.s5_bim.t.ap()[l, :, c0:c0 + NP], r=[], w=[bim])
            G0 = hh * 32
            v3 = lambda t: t.t[:].rearrange("q (a p) -> q a p", p=64)
            tk.op("dve", lambda e: e.tensor_tensor(out=t3.t[:], in0=co_re.t[:], in1=bre.t[:], op=ALU.mult), r=[co_re, bre], w=[t3])
            tk.op("pool", lambda e: e.tensor_tensor(out=sn.t[:], in0=co_im.t[:], in1=bim.t[:], op=ALU.mult), r=[co_im, bim], w=[sn])
            tk.op("dve", lambda e: e.tensor_tensor(out=t3.t[:], in0=t3.t[:], in1=sn.t[:], op=ALU.subtract), r=[t3, sn], w=[t3])
            tk.op("dve", lambda e: e.tensor_tensor(out=t4.t[:], in0=co_re.t[:], in1=bim.t[:], op=ALU.mult), r=[co_re, bim], w=[t4])
            tk.op("pool", lambda e: e.tensor_tensor(out=cs.t[:], in0=co_im.t[:], in1=bre.t[:], op=ALU.mult), r=[co_im, bre], w=[cs])
            tk.op("dve", lambda e: e.tensor_tensor(out=t4.t[:], in0=t4.t[:], in1=cs.t[:], op=ALU.add), r=[t4, cs], w=[t4])
            tk.op("act", lambda e: e.activation(out=BBt.t[:, G0:G0 + 32, 0, 0:64], in_=v3(t3), func=AF.Copy), r=[t3], w=[BBt])
            tk.op("act", lambda e: e.activation(out=BBt.t[:, G0:G0 + 32, 0, 64:128], in_=v3(t4), func=AF.Copy), r=[t4], w=[BBt])
            tk.op("act", lambda e: e.activation(out=BBt.t[:, G0:G0 + 32, 1, 0:64], in_=v3(t4), func=AF.Copy), r=[t4], w=[BBt])
            tk.op("act", lambda e: e.activation(out=BBt.t[:, G0:G0 + 32, 1, 64:128], in_=v3(t3), func=AF.Copy, scale=-1.0), r=[t3], w=[BBt])
            for q2 in range(2):
                src = (lre, lim)[q2]
                tk.dma("sp", src.t[:], self.s5_c1.t.ap()[l, :, hh * 4096 + q2 * NP:hh * 4096 + (q2 + 1) * NP], w=[src])
                sv = src.t[:].rearrange("q (a m) -> q a m", m=128)
                d0 = hh * 32 + q2 * 16
                tk.op("act", lambda e, sv=sv, d0=d0: e.activation(out=Ct.t[0:64, d0:d0 + 16, 0, :], in_=sv[0:64], func=AF.Copy), r=[src], w=[Ct])
                tk.op("act", lambda e, sv=sv, d0=d0: e.activation(out=Ct.t[64:128, d0:d0 + 16, 0, :], in_=sv[64:128], func=AF.Copy, scale=-1.0), r=[src], w=[Ct])
            for q2 in range(2):
                src = (t3, t4)[q2]
                tk.dma("sp", src.t[:], self.s5_c2.t.ap()[l, :, hh * 4096 + q2 * NP:hh * 4096 + (q2 + 1) * NP], w=[src])
                sv = src.t[:].rearrange("q (a m) -> q a m", m=128)
                d0 = hh * 32 + q2 * 16
                tk.op("act", lambda e, sv=sv, d0=d0: e.activation(out=Ct.t[:, d0:d0 + 16, 1, :], in_=sv, func=AF.Copy, scale=-1.0), r=[src], w=[Ct])

    def s5_main(self, l, es, sb, BBt, Ct, pl, sincos, ygel):
        tk = self.tk
        MAGIC = 12582912.0
        TWO_PI = 2.0 * math.pi
        PIECES = [(0, 256), (256, 512), (768, 512), (1280, 512), (1792, 512)]
        if True:
            tk.dma("sp", pl.t[:, 0, :], self.s5_lreP.t.ap()[l], w=[pl])
            tk.dma("sp", pl.t[:, 1, :], self.s5_limP.t.ap()[l], w=[pl])
            tk.dma("sp", pl.t[:, 2, :], self.s5_stpP.t.ap()[l], w=[pl])
            tk.op("act", lambda e: e.activation(out=pl.t[:, 2, :], in_=pl.t[:, 2, :], func=AF.Exp), r=[pl], w=[pl])
            tk.op("dve", lambda e: e.tensor_tensor(out=pl.t[:, 3, :], in0=pl.t[:, 0, :], in1=pl.t[:, 2, :], op=ALU.mult), r=[pl], w=[pl])
            tk.op("act", lambda e: e.activation(out=pl.t[:, 3, :], in_=pl.t[:, 3, :], func=AF.Exp), r=[pl], w=[pl])
            tk.op("dve", lambda e: e.scalar_tensor_tensor(out=pl.t[:, 4, :], in0=pl.t[:, 1, :], scalar=1.0 / TWO_PI, in1=pl.t[:, 2, :], op0=ALU.mult, op1=ALU.mult),
                  r=[pl], w=[pl])
            tk.op("dve", lambda e: e.tensor_scalar(out=pl.t[:, 5, :], in0=pl.t[:, 4, :], scalar1=64.0, scalar2=None, op0=ALU.mult), r=[pl], w=[pl])
            tk.op("dve", lambda e: e.tensor_scalar(out=pl.t[:, 0, :], in0=pl.t[:, 5, :], scalar1=MAGIC, scalar2=MAGIC, op0=ALU.add, op1=ALU.subtract), r=[pl], w=[pl])
            tk.op("dve", lambda e: e.tensor_tensor(out=pl.t[:, 5, :], in0=pl.t[:, 5, :], in1=pl.t[:, 0, :], op=ALU.subtract), r=[pl], w=[pl])
            tA, tB = sb("tA", [128, LS], BF16), sb("tB", [128, LS], BF16)
            tk.dma("sp", tA.t[:], self.tabA.t.ap(), w=[tA])
            tk.dma("sp", tB.t[:], self.tabB.t.ap(), w=[tB])
            dsk = sb("dsk", [128, 4])
            tk.dma("sp", dsk.t[:], self.s5_dT.t.ap()[l], w=[dsk])
            Rt = [sb(f"Rt{i}", [128, 512]) for i in range(2)]
            ones5 = sb("ones5", [128, 512])
            tk.op("pool", lambda e: e.memset(ones5.t[:], 1.0), w=[ones5])
            ph = [sb(f"ph{i}", [128, 512]) for i in range(2)]
            tmpt = [sb(f"tmpt{i}", [128, 512]) for i in range(2)]
            NC_, NS_ = [sb(f"NC{i}", [128, 512]) for i in range(2)], [sb(f"NS{i}", [128, 512]) for i in range(2)]
            Wt = [sb(f"Wt{i}", [128, 512]) for i in range(2)]
            W2 = [sb(f"W2{i}", [128, 512]) for i in range(2)]
            Gt = [sb(f"Gt{i}", [128, 512]) for i in range(2)]
            P1 = [sb(f"P1{i}", [128, 512], BF16) for i in range(2)]
            P2 = [sb(f"P2{i}", [128, 512], BF16) for i in range(2)]
            carry = sb("carry", [128, 64])
            ut = [sb(f"u{i}", [128, LS], BF16) for i in range(2)]
            ya2 = [sb(f"ya{b}", [128, LS]) for b in range(NB)]
            yacc = [ya2 for gt in range(4)]
            cnt = 0
            for gt in range(4):
                for b in range(NB):
                    tk.dma("sp", ut[b].t[:], self.pfm.t.ap()[R_S5 + gt * 128:R_S5 + (gt + 1) * 128, b, :], r=[self.pfm], w=[ut[b]])
                    tk.op("dve", lambda e, gt=gt, b=b: e.tensor_scalar(out=yacc[gt][b].t[:], in0=ut[b].t[:], scalar1=dsk.t[:, gt:gt + 1], scalar2=None, op0=ALU.mult),
                          r=[ut[b], dsk], w=[yacc[gt][b]])
                for di in range(2):
                    for pi_, (t0, n) in enumerate(PIECES):
                        ypb = [self.psb[6], self.psb[7]]
                        for g8 in range(8):
                            dg = di * 32 + gt * 8 + g8
                            i2 = cnt % 2
                            cnt += 1
                            tk.op("pool", lambda e, dg=dg, i2=i2, t0=t0, n=n: e.tensor_scalar(out=ph[i2].t[:, 0:n], in0=tA.t[:, t0:t0 + n], scalar1=pl.t[:, 5, dg:dg + 1], scalar2=None, op0=ALU.mult),
                                  r=[tA, pl], w=[ph[i2]])
                            tk.op("dve", lambda e, dg=dg, i2=i2, t0=t0, n=n: e.scalar_tensor_tensor(out=ph[i2].t[:, 0:n], in0=tB.t[:, t0:t0 + n], scalar=pl.t[:, 4, dg:dg + 1], in1=ph[i2].t[:, 0:n],
                                                                                                     op0=ALU.mult, op1=ALU.add), r=[tB, pl, ph[i2]], w=[ph[i2]])
                            sincos(ph[i2], n, NS_[i2], NC_[i2], tmpt[i2])
                            tk.op("pool", lambda e, dg=dg, i2=i2, n=n: e.tensor_scalar(out=Rt[i2].t[:, 0:n], in0=ones5.t[:, 0:n], scalar1=pl.t[:, 3, dg:dg + 1], scalar2=None, op0=ALU.mult),
                                  r=[ones5, pl], w=[Rt[i2]])
                            for b in range(NB):
                                j2 = (cnt * 2 + b) % 2
                                pbu, pbu2 = self.psb[2 * ((cnt * 2 + b) % 3)], self.psb[2 * ((cnt * 2 + b) % 3) + 1]
                                if di == 0:
                                    uv = ut[b].t[:, t0:t0 + n]
                                else:
                                    lo = (LCTX - t0 - n) if t0 < LCTX else (LCTX + (LS - t0 - n))
                                    uv = ut[b].t[:, lo:lo + n][:, ::-1]
                                    lo_ = lo
                                tk.op("pe", lambda e, pbu=pbu, dg=dg, uv=uv, n=n: e.matmul(pbu.t[:, 0:n], lhsT=BBt.t[:, dg, 0, :], rhs=uv, start=True, stop=True), r=[BBt, ut[b]], w=[pbu])
                                tk.op("pe", lambda e, pbu2=pbu2, dg=dg, uv=uv, n=n: e.matmul(pbu2.t[:, 0:n], lhsT=BBt.t[:, dg, 1, :], rhs=uv, start=True, stop=True), r=[BBt, ut[b]], w=[pbu2])
                                tk.op("dve", lambda e, pbu=pbu, i2=i2, j2=j2, n=n: e.tensor_tensor(out=Wt[j2].t[:, 0:n], in0=pbu.t[:, 0:n], in1=NC_[i2].t[:, 0:n], op=ALU.mult), r=[pbu, NC_[i2]], w=[Wt[j2]])
                                tk.op("dve", lambda e, pbu2=pbu2, i2=i2, j2=j2, n=n: e.tensor_tensor(out=W2[j2].t[:, 0:n], in0=pbu2.t[:, 0:n], in1=NS_[i2].t[:, 0:n], op=ALU.mult), r=[pbu2, NS_[i2]], w=[W2[j2]])
                                tk.op("pool", lambda e, j2=j2, n=n: e.tensor_tensor(out=Wt[j2].t[:, 0:n], in0=Wt[j2].t[:, 0:n], in1=W2[j2].t[:, 0:n], op=ALU.add), r=[Wt[j2], W2[j2]], w=[Wt[j2]])
                                ci = g8 * 2 + b + di * 16
                                init = 0.0 if pi_ == 0 else carry.t[:, ci:ci + 1]
                                tk.op("dve", lambda e, j2=j2, i2=i2, n=n, init=init: e.tensor_tensor_scan(out=Gt[j2].t[:, 0:n], data0=Rt[i2].t[:, 0:n], data1=Wt[j2].t[:, 0:n], initial=init,
                                                                                                         op0=ALU.mult, op1=ALU.add), r=[Rt[i2], Wt[j2], carry], w=[Gt[j2]])
                                tk.op("act", lambda e, j2=j2, n=n, ci=ci: e.activation(out=carry.t[:, ci:ci + 1], in_=Gt[j2].t[:, n - 1:n], func=AF.Copy), r=[Gt[j2]], w=[carry])
                                tk.op("pool", lambda e, j2=j2, i2=i2, n=n: e.tensor_tensor(out=P1[j2].t[:, 0:n], in0=Gt[j2].t[:, 0:n], in1=NC_[i2].t[:, 0:n], op=ALU.mult), r=[Gt[j2], NC_[i2]], w=[P1[j2]])
                                tk.op("dve", lambda e, j2=j2, i2=i2, n=n: e.tensor_tensor(out=P2[j2].t[:, 0:n], in0=Gt[j2].t[:, 0:n], in1=NS_[i2].t[:, 0:n], op=ALU.mult), r=[Gt[j2], NS_[i2]], w=[P2[j2]])
                                tk.op("pe", lambda e, b=b, dg=dg, j2=j2, n=n, g8=g8: e.matmul(ypb[b].t[:, 0:n], lhsT=Ct.t[:, dg, 0, :], rhs=P1[j2].t[:, 0:n], start=(g8 == 0), stop=False), r=[Ct, P1[j2]], w=[ypb[b]])
                                tk.op("pe", lambda e, b=b, dg=dg, j2=j2, n=n, g8=g8: e.matmul(ypb[b].t[:, 0:n], lhsT=Ct.t[:, dg, 1, :], rhs=P2[j2].t[:, 0:n], start=False, stop=(g8 == 7)), r=[Ct, P2[j2]], w=[ypb[b]])
                        for b in range(NB):
                            if di == 0:
                                yv = yacc[gt][b].t[:, t0:t0 + n]
                            else:
                                lo = (LCTX - t0 - n) if t0 < LCTX else (LCTX + (LS - t0 - n))
                                yv = yacc[gt][b].t[:, lo:lo + n][:, ::-1]
                            tk.op("dve", lambda e, b=b, yv=yv, n=n: e.tensor_tensor(out=yv, in0=yv, in1=ypb[b].t[:, 0:n], op=ALU.add), r=[yacc[gt][b], ypb[b]], w=[yacc[gt][b]])
                for b in range(NB):
                    if "s5_y" in self.dump and l == 0:
                        if gt == 0 and b == 0:
                            self._d_s5y = self.dump_t("s5_y", [4, NB, 128, LS], F32)
                        tk.dma("sp", self._d_s5y.t.ap()[gt, b], yacc[gt][b].t[:], r=[yacc[gt][b]], w=[self._d_s5y])
                    tk.op("act", lambda e, gt=gt, b=b: e.activation(out=ygel[gt][b].t[:], in_=yacc[gt][b].t[:], func=AF.Gelu), r=[yacc[gt][b]], w=[ygel[gt][b]])

    def s5_glu(self, l, ygel):
        tk = self.tk
        with ExitStack() as es:
            sb = lambda n, sh, dt=F32: tk.sb(es, "s5g" + n, sh, dt)
            gw = sb("gw", [128, 4, 512], BF16)
            gbT = sb("gbT", [128, 4])
            bgT = sb("bgT", [128, 4])
            tk.dma("pool", gw.t[:], self.s5_glu_w.t.ap()[l].rearrange("(k p) n -> p k n", p=128), w=[gw])
            tk.dma("sp", gbT.t[:], self.s5_glu_bT.t.ap()[l], w=[gbT])
            tk.dma("sp", bgT.t[:], self.branch_gT.t.ap()[l, 0], w=[bgT])
            res = [sb(f"res{i}", [128, LS]) for i in range(4)]
            sgm = [sb(f"sgm{i}", [128, 512]) for i in range(2)]
            for b in range(NB):
                yg = [ygel[i][b] for i in range(4)]
                k2 = 0
                for no in range(4):
                    for (t0, n) in [(0, 512), (512, 512), (1024, 512), (1536, 512), (2048, 256)]:
                        pb = self.psb[5 + k2 % 3]
                        sg_ = sgm[k2 % 2]
                        k2 += 1
                        for ki in range(4):
                            tk.op("pe", lambda e, pb=pb, ki=ki, no=no, t0=t0, n=n: e.matmul(pb.t[:, 0:n], lhsT=gw.t[:, ki, no * 128:(no + 1) * 128], rhs=yg[ki].t[:, t0:t0 + n],
                                                                                             start=(ki == 0), stop=(ki == 3)), r=[gw] + yg, w=[pb])
                        tk.op("act", lambda e, pb=pb, sg_=sg_, no=no, n=n: e.activation(out=sg_.t[:, 0:n], in_=pb.t[:, 0:n], func=AF.Sigmoid, bias=gbT.t[:, no:no + 1], scale=1.0),
                              r=[pb, gbT], w=[sg_])
                        tk.op("dve", lambda e, sg_=sg_, no=no, t0=t0, n=n: e.tensor_tensor(out=res[no].t[:, t0:t0 + n], in0=yg[no].t[:, t0:t0 + n], in1=sg_.t[:, 0:n], op=ALU.mult),
                              r=[yg[no], sg_], w=[res[no]])
                self.finish_branch(l, es, 0, b, res, True, bgT)

    def mixer_att(self, l):
        tk = self.tk
        last = (l == DEPTH - 1)
        NBLK = LS // 128
        with ExitStack() as es:
            sb = lambda n, sh, dt=F32: tk.sb(es, "at" + n, sh, dt)
            ropeC, ropeS = sb("ropeC", [128, 16, 64]), sb("ropeS", [128, 16, 64])
            gqk = sb("gqk", [128, 640])
            esink = sb("esink", [128, 8])
            mA, mC = sb("mA", [128, 128], BF16), sb("mC", [128, 128], BF16)
            bgT = sb("bgT", [128, 4])
            tk.dma("sp", ropeC.t[:], self.ropeC.t.ap(), w=[ropeC])
            tk.dma("sp", ropeS.t[:], self.ropeS.t.ap(), w=[ropeS])
            tk.dma("sp", gqk.t[:], self.att_gqk.t.ap()[l], w=[gqk])
            tk.dma("sp", esink.t[:], self.att_sinkB.t.ap()[l], w=[esink])
            tk.dma("sp", mA.t[:], self.maskA.t.ap(), w=[mA])
            tk.dma("sp", mC.t[:], self.maskC.t.ap(), w=[mC])
            tk.dma("sp", bgT.t[:], self.branch_gT.t.ap()[l, 1], w=[bgT])
            tk.op("act", lambda e: e.activation(out=esink.t[:], in_=esink.t[:], func=AF.Exp), r=[esink], w=[esink])
            QT = sb("QT", [64, 8, LS], BF16)
            KT = sb("KT", [64, 2, LS], BF16)
            Va = sb("Va", [128, NBLK, 2, 128], BF16)
            yatt = [sb(f"yatt{i}", [128, LS]) for i in range(4)]
            xin = [sb(f"xin{i}", [128, 768], BF16) for i in range(2)]
            sq = sb("sq", [128, 640])
            ss = [sb(f"ss{i}", [128, 10]) for i in range(2)]
            xn = [sb(f"xn{i}", [128, 640]) for i in range(2)]
            r1, r2 = sb("r1", [128, 640]), sb("r2", [128, 640])
            xb = [sb(f"xb{i}", [128, 768], BF16) for i in range(2)]
            E = [sb(f"E{i}", [128, 5, 512], BF16) for i in range(2)]
            den = [sb(f"den{i}", [128, 4]) for i in range(2)]
            oat = [sb(f"oat{i}", [128, 512], BF16) for i in range(2)]
            tk.op("pool", lambda e: e.memset(Va.t[:], 1.0), w=[Va])
            pi = 0
            for b in range(NB):
                import os
                P1 = int(os.environ.get("ATT_P1", "99"))
                for blk in range(NBLK):
                    xi, s_, x_n, x_b = xin[blk % 2], ss[blk % 2], xn[blk % 2], xb[blk % 2]
                    tk.dma("sp", xi.t[:], self.qkv.t.ap()[b, blk * 128:(blk + 1) * 128, :], r=[self.qkv], w=[xi])
                    tk.op("act", lambda e, xi=xi: e.activation(out=sq.t[:], in_=xi.t[:, 0:640], func=AF.Square), r=[xi], w=[sq])
                    tk.op("dve", lambda e, s_=s_: e.reduce_sum(out=s_.t[:], in_=sq.t[:].rearrange("p (h d) -> p h d", d=64), axis=AX.X), r=[sq], w=[s_])
                    tk.op("act", lambda e, s_=s_: e.activation(out=s_.t[:], in_=s_.t[:], func=AF.Sqrt, bias=self.eps_t.t[:], scale=1.0 / 64), r=[s_, self.eps_t], w=[s_])
                    tk.op("dve", lambda e, s_=s_: e.reciprocal(out=s_.t[:], in_=s_.t[:]), r=[s_], w=[s_])
                    for h in range(10):
                        tk.op("act" if h % 2 else "pool", (lambda e, h=h, xi=xi, s_=s_, x_n=x_n: e.activation(out=x_n.t[:, h * 64:(h + 1) * 64], in_=xi.t[:, h * 64:(h + 1) * 64], func=AF.Copy, scale=s_.t[:, h:h + 1]))
                              if h % 2 else (lambda e, h=h, xi=xi, s_=s_, x_n=x_n: e.tensor_scalar(out=x_n.t[:, h * 64:(h + 1) * 64], in0=xi.t[:, h * 64:(h + 1) * 64], scalar1=s_.t[:, h:h + 1], scalar2=None, op0=ALU.mult)),
                              r=[xi, s_], w=[x_n])
                    tk.op("dve", lambda e, x_n=x_n: e.tensor_tensor(out=x_n.t[:], in0=x_n.t[:], in1=gqk.t[:], op=ALU.mult), r=[x_n, gqk], w=[x_n])
                    if P1 <= 7:
                        continue
                    if blk >= 2:
                        lb = blk - 2
                        for h in range(10):
                            xv = x_n.t[:, h * 64:(h + 1) * 64]
                            xsw = xv.rearrange("p (a r i) -> p a r i", a=2, r=2)[:, :, ::-1, :]
                            tk.op("dve", lambda e, h=h, xv=xv, lb=lb: e.tensor_tensor(out=r1.t[:, h * 64:(h + 1) * 64], in0=xv, in1=ropeC.t[:, lb, :], op=ALU.mult), r=[x_n, ropeC], w=[r1])
                            tk.op("dve", lambda e, h=h, xsw=xsw, lb=lb: e.tensor_tensor(out=r2.t[:, h * 64:(h + 1) * 64].rearrange("p (a r i) -> p a r i", a=2, r=2), in0=xsw,
                                                                                          in1=ropeS.t[:, lb, :].rearrange("p (a r i) -> p a r i", a=2, r=2), op=ALU.mult), r=[x_n, ropeS], w=[r2])
                        tk.op("dve", lambda e, x_b=x_b: e.tensor_tensor(out=x_b.t[:, 0:640], in0=r1.t[:], in1=r2.t[:], op=ALU.add), r=[r1, r2], w=[x_b])
                    else:
                        tk.op("dve", lambda e, x_b=x_b, x_n=x_n: e.tensor_copy(out=x_b.t[:, 0:640], in_=x_n.t[:]), r=[x_n], w=[x_b])
                    pb = self.psb[pi % 4]
                    pi += 1
                    pb2 = self.psb[pi % 4]
                    pi += 1
                    pv = pb.t[:].bitcast(BF16)
                    pv2 = pb2.t[:].bitcast(BF16)
                    for h in range(8):
                        tk.op("pe", lambda e, pv=pv, h=h, x_b=x_b: e.transpose(out=pv[0:64, h * 128:(h + 1) * 128], in_=x_b.t[:, h * 64:(h + 1) * 64], identity=self.ident_b.t[:]),
                              r=[x_b, self.ident_b], w=[pb])
                    for kh in range(2):
                        tk.op("pe", lambda e, pv2=pv2, kh=kh, x_b=x_b: e.transpose(out=pv2[0:64, kh * 128:(kh + 1) * 128], in_=x_b.t[:, 512 + kh * 64:512 + (kh + 1) * 64], identity=self.ident_b.t[:]),
                              r=[x_b, self.ident_b], w=[pb2])
                    tk.op("dve", lambda e, pv=pv, blk=blk: e.tensor_copy(out=QT.t[0:64, :, blk * 128:(blk + 1) * 128], in_=pv[0:64, 0:1024].rearrange("p (h t) -> p h t", h=8)), r=[pb], w=[QT])
                    tk.op("dve", lambda e, pv2=pv2, blk=blk: e.tensor_copy(out=KT.t[0:64, :, blk * 128:(blk + 1) * 128], in_=pv2[0:64, 0:256].rearrange("p (h t) -> p h t", h=2)), r=[pb2], w=[KT])
                    tk.op("dve", lambda e, xi=xi, blk=blk: e.tensor_copy(out=Va.t[:, blk, :, 0:64], in_=xi.t[:, 640:768].rearrange("p (k d) -> p k d", k=2)), r=[xi], w=[Va])
                import os
                ASTOP = int(os.environ.get("ATT_STOP", "9"))
                if ASTOP <= 1:
                    continue
                qblocks = list(range(2, NBLK)) + ([] if last else [0, 1])
                for qi, qb in enumerate(qblocks):
                    if qb >= 2:
                        keys = [(kb, m) for kb, m in ((qb - 1, mA), (qb, None), (qb + 1, mC)) if 2 <= kb < NBLK] + [(0, None), (1, None)]
                    else:
                        keys = [(0, None), (1, None)]
                    o_t = oat[qi % 2]
                    for kh in range(2):
                        Et = E[(qi * 2 + kh) % 2]
                        for ki, (kb, msk) in enumerate(keys):
                            pb = self.psb[pi % 4]
                            pi += 1
                            tk.op("pe", lambda e, pb=pb, kh=kh, kb=kb, qb=qb: e.matmul(
                                pb.t[:].rearrange("p (j t) -> p j t", j=4), lhsT=KT.t[0:64, kh, kb * 128:(kb + 1) * 128],
                                rhs=QT.t[0:64, 4 * kh:4 * kh + 4, qb * 128:(qb + 1) * 128], start=True, stop=True), r=[KT, QT], w=[pb])
                            tk.op("act", lambda e, pb=pb, Et=Et, ki=ki: e.activation(out=Et.t[:, ki, :], in_=pb.t[:], func=AF.Exp, scale=0.125), r=[pb], w=[Et])
                            if msk is not None:
                                for j in range(4):
                                    tk.op("pool" if j % 2 else "dve", lambda e, Et=Et, ki=ki, j=j, msk=msk: e.tensor_tensor(out=Et.t[:, ki, j * 128:(j + 1) * 128], in0=Et.t[:, ki, j * 128:(j + 1) * 128],
                                                                                                                         in1=msk.t[:], op=ALU.mult), r=[Et, msk], w=[Et])
                        if ASTOP <= 2:
                            continue
                        po = self.psb[4 + (qi * 2 + kh) % 4]
                        for j in range(4):
                            for ki, (kb, msk) in enumerate(keys):
                                tk.op("pe", lambda e, po=po, j=j, ki=ki, kb=kb, kh=kh, Et=Et, nk=len(keys): e.matmul(
                                    po.t[:, j * 128:j * 128 + 65], lhsT=Et.t[:, ki, j * 128:(j + 1) * 128], rhs=Va.t[:, kb, kh, 0:65],
                                    start=(ki == 0), stop=(ki == nk - 1)), r=[Et, Va], w=[po])
                        if ASTOP <= 3:
                            continue
                        dn = den[(qi * 2 + kh) % 2]
                        pov = po.t[:, 0:512].rearrange("p (j c) -> p j c", c=128)
                        tk.op("dve", lambda e, dn=dn, pov=pov, kh=kh: e.tensor_tensor(out=dn.t[:], in0=pov[:, :, 64], in1=esink.t[:, 4 * kh:4 * kh + 4], op=ALU.add), r=[po, esink], w=[dn])
                        tk.op("dve", lambda e, dn=dn: e.reciprocal(out=dn.t[:], in_=dn.t[:]), r=[dn], w=[dn])
                        for j in range(4):
                            tk.op("act", lambda e, j=j, kh=kh, dn=dn, pov=pov, o_t=o_t: e.activation(out=o_t.t[:, (4 * kh + j) * 64:(4 * kh + j + 1) * 64], in_=pov[:, j, 0:64], func=AF.Copy, scale=dn.t[:, j:j + 1]),
                                  r=[po, dn], w=[o_t])
                    if ASTOP <= 4:
                        continue
                    pt = self.psb[pi % 4]
                    pi += 1
                    ptv = pt.t[:].bitcast(BF16)
                    for c4 in range(4):
                        tk.op("pe", lambda e, ptv=ptv, c4=c4, o_t=o_t: e.transpose(out=ptv[:, c4 * 128:(c4 + 1) * 128], in_=o_t.t[:, c4 * 128:(c4 + 1) * 128], identity=self.ident_b.t[:]),
                              r=[o_t, self.ident_b], w=[pt])
                    for c4 in range(4):
                        tk.op("dve", lambda e, ptv=ptv, c4=c4, qb=qb: e.tensor_copy(out=yatt[c4].t[:, qb * 128:(qb + 1) * 128], in_=ptv[:, c4 * 128:(c4 + 1) * 128]), r=[pt], w=[yatt[c4]])
                if last:
                    for c4 in range(4):
                        tk.op("pool", lambda e, c4=c4: e.memset(yatt[c4].t[:, 0:LCTX], 0.0), w=[yatt[c4]])
                if ASTOP <= 5:
                    continue
                self.finish_branch(l, es, 1, b, yatt, True, bgT)

    def mixer_hy(self, l):
        tk = self.tk
        last = (l == DEPTH - 1)
        MAGIC = 12582912.0
        TWO_PI = 2.0 * math.pi
        segs = [("L", LLAT, LCTX)] + ([] if last else [("C", LCTX, 0)])
        with ExitStack() as es:
            sb = lambda n, sh, dt=F32: tk.sb(es, "hf" + n, sh, dt)
            w1, w2, w3 = sb("w1", [33, 64]), sb("w2", [64, 64]), sb("w3", [64, 2048])
            prm = sb("prm", [64, 8])
            skp = sb("skp", [128, 8])
            zero1 = sb("zero1", [128, 1])
            tk.op("pool", lambda e: e.memset(zero1.t[:], 0.0), w=[zero1])
            tk.dma("sp", w1.t[:], self.hy_w1.t.ap()[l], w=[w1])
            tk.dma("sp", w2.t[:], self.hy_w2.t.ap()[l], w=[w2])
            tk.dma("sp", w3.t[:], self.hy_w3.t.ap()[l], w=[w3])
            tk.dma("sp", prm.t[:, 0:4], self.hy_prm.t.ap()[l], w=[prm])
            tk.dma("sp", skp.t[:], self.hy_skipT.t.ap()[l], w=[skp])
            tk.op("dve", lambda e: e.tensor_scalar(out=prm.t[:, 4:5], in0=prm.t[:, 1:2], scalar1=1.0 / TWO_PI, scalar2=None, op0=ALU.mult), r=[prm], w=[prm])
            tk.op("dve", lambda e: e.tensor_scalar(out=prm.t[:, 5:6], in0=prm.t[:, 3:4], scalar1=1.0 / TWO_PI, scalar2=None, op0=ALU.mult), r=[prm], w=[prm])
            zt = sb("zt", [33, 4096])
            H1, H2 = sb("H1", [64, 4096]), sb("H2", [64, 4096])
            ut_, rt_ = [sb(f"ut{i}", [64, 512]) for i in range(2)], [sb(f"rt{i}", [64, 512]) for i in range(2)]
            dt_ = [sb(f"dt{i}", [128, 512]) for i in range(2)]
            Gt = [sb(f"Gt{i}", [128, 4096], BF16) for i in range(2)]
            cnt = 0
            for (sn, Ls, _) in segs:
                Lx = 2 * Ls
                CH = 512 if Ls >= 512 else 256
                nch = Lx // CH
                for k in range(2):
                    ztab = {("L", 0): self.hy_zLr, ("L", 1): self.hy_zL, ("C", 0): self.hy_zCr, ("C", 1): self.hy_zC}[(sn, k)]
                    dtab = {("L", 0): self.hy_dLr, ("L", 1): self.hy_dL, ("C", 0): self.hy_dCr, ("C", 1): self.hy_dC}[(sn, k)]
                    gdst = {("L", 0): self.hy_GL, ("L", 1): self.hy_GL, ("C", 0): self.hy_GC, ("C", 1): self.hy_GC}[(sn, k)]
                    tk.dma("sp", zt.t[:, 0:Lx], ztab.t.ap(), w=[zt])
                    for (src, wt_, K_, bcol, fcol, dst) in ((zt, w1, 33, 0, 4, H1), (H1, w2, 64, 2, 5, H2)):
                        for c in range(nch):
                            pb = self.psb[cnt % 4]
                            u_, r_ = ut_[cnt % 2], rt_[cnt % 2]
                            cnt += 1
                            tk.op("pe", lambda e, pb=pb, src=src, wt_=wt_, K_=K_, c=c, CH=CH: e.matmul(pb.t[0:64, 0:CH], lhsT=wt_.t[0:K_, :], rhs=src.t[0:K_, c * CH:(c + 1) * CH], start=True, stop=True),
                                  r=[src, wt_], w=[pb])
                            tk.op("dve", lambda e, pb=pb, u_=u_, bcol=bcol, fcol=fcol, CH=CH: e.tensor_scalar(out=u_.t[:, 0:CH], in0=pb.t[0:64, 0:CH], scalar1=prm.t[:, bcol:bcol + 1], scalar2=prm.t[:, fcol:fcol + 1],
                                                                                                        op0=ALU.add, op1=ALU.mult), r=[pb, prm], w=[u_])
                            tk.op("pool", lambda e, u_=u_, r_=r_, CH=CH: e.tensor_scalar(out=r_.t[:, 0:CH], in0=u_.t[:, 0:CH], scalar1=MAGIC, scalar2=MAGIC, op0=ALU.add, op1=ALU.subtract), r=[u_], w=[r_])
                            tk.op("pool", lambda e, u_=u_, r_=r_, CH=CH: e.tensor_tensor(out=r_.t[:, 0:CH], in0=u_.t[:, 0:CH], in1=r_.t[:, 0:CH], op=ALU.subtract), r=[u_, r_], w=[r_])
                            tk.op("act", lambda e, r_=r_, dst=dst, c=c, CH=CH: e.activation(out=dst.t[:, c * CH:(c + 1) * CH], in_=r_.t[:, 0:CH], func=AF.Sin, bias=zero1.t[0:64, :], scale=TWO_PI),
                                  r=[r_, zero1], w=[dst])
                    for ct in range(4):
                        G = Gt[cnt % 2]
                        for c in range(nch):
                            fwd = (c * CH < Ls) if k == 0 else (c * CH >= Ls)
                            f = 2 * k + (0 if fwd else 1)
                            pb = self.psb[4 + cnt % 4]
                            d_ = dt_[cnt % 2]
                            cnt += 1
                            tk.dma("sp", d_.t[:, 0:CH], dtab.t.ap()[ct * 128:(ct + 1) * 128, c * CH:(c + 1) * CH], w=[d_])
                            tk.op("pe", lambda e, pb=pb, f=f, ct=ct, c=c, CH=CH: e.matmul(pb.t[:, 0:CH], lhsT=w3.t[:, f * 512 + ct * 128:f * 512 + (ct + 1) * 128], rhs=H2.t[:, c * CH:(c + 1) * CH],
                                                                                          start=True, stop=True), r=[w3, H2], w=[pb])
                            tk.op("dve", lambda e, pb=pb, d_=d_, G=G, c=c, CH=CH: e.tensor_tensor(out=G.t[:, c * CH:(c + 1) * CH], in0=pb.t[:, 0:CH], in1=d_.t[:, 0:CH], op=ALU.mult), r=[pb, d_], w=[G])
                            m0 = Ls - 1 if k == 0 else Ls
                            if c * CH <= m0 < (c + 1) * CH:
                                tk.op("dve", lambda e, pb=pb, d_=d_, G=G, m0=m0, k=k, ct=ct, c=c, CH=CH: e.scalar_tensor_tensor(
                                    out=G.t[:, m0:m0 + 1], in0=pb.t[:, m0 - c * CH:m0 - c * CH + 1], scalar=d_.t[:, m0 - c * CH:m0 - c * CH + 1], in1=skp.t[:, k * 4 + ct:k * 4 + ct + 1],
                                    op0=ALU.mult, op1=ALU.add), r=[pb, d_, skp], w=[G])
                        tk.dma("sp", gdst.t.ap()[k, ct * 128:(ct + 1) * 128, 0:Lx], G.t[:, 0:Lx], r=[G], w=[gdst])
        tk.barrier()
        with ExitStack() as es0:
            yhy = [[tk.sb(es0, f"yhy{ct}{b}", [128, LS], BF16) for b in range(NB)] for ct in range(4)]
            with ExitStack() as es:
                sb = lambda n, sh, dt=F32: tk.sb(es, "hc" + n, sh, dt)
                cw = sb("cw", [128, 12, 4])
                tk.dma("sp", cw.t[:], self.hy_convT.t.ap()[l], w=[cw])
                hin = [sb(f"hin{i}", [128, LS], BF16) for i in range(2)]
                zt = [[sb(f"z{j}{b}", [128, LS], BF16) for b in range(NB)] for j in range(3)]
                tmp = [sb(f"tmp{i}", [128, LS]) for i in range(2)]
                geo = {}
                for (sn, Ls, off) in segs:
                    nb_ = Ls // 128
                    nc_ = NB * nb_
                    geo[sn] = dict(U1=sb("U1" + sn, [128, 128, nc_], BF16), X1=sb("X1" + sn, [128, 128, nc_], BF16), X2=sb("X2" + sn, [128, 128, nc_], BF16),
                                   U2=sb("U2" + sn, [128, 128, nc_], BF16), Y=sb("Y" + sn, [128, 128, nc_], BF16),
                                   R=[sb(f"R{sn}{i}", [128, 2 * Ls - 128], BF16) for i in range(3 if sn == "L" else 2)])
                pi = 0
                ri = 0
                for ct in range(4):
                    for j in range(3):
                        row = R_HY + j * 512 + ct * 128
                        tci = j * 4 + ct
                        for b in range(NB):
                            hi, t_ = hin[(j * 2 + b) % 2], tmp[(j * 2 + b) % 2]
                            tk.dma("sp", hi.t[:], self.pfm.t.ap()[row:row + 128, b, :], r=[self.pfm], w=[hi])
                            tk.op("act", lambda e, hi=hi, t_=t_, tci=tci: e.activation(out=t_.t[:], in_=hi.t[:], func=AF.Identity, bias=cw.t[:, tci, 3:4], scale=cw.t[:, tci, 1:2]), r=[hi, cw], w=[t_])
                            for (sn, Ls, off) in [("L", LLAT, LCTX), ("C", LCTX, 0)]:
                                tk.op("dve", lambda e, hi=hi, t_=t_, tci=tci, Ls=Ls, off=off: e.scalar_tensor_tensor(out=t_.t[:, off + 1:off + Ls], in0=hi.t[:, off:off + Ls - 1], scalar=cw.t[:, tci, 0:1],
                                                                                                                   in1=t_.t[:, off + 1:off + Ls], op0=ALU.mult, op1=ALU.add), r=[hi, cw, t_], w=[t_])
                                tk.op("dve", lambda e, hi=hi, t_=t_, tci=tci, Ls=Ls, off=off: e.scalar_tensor_tensor(out=t_.t[:, off:off + Ls - 1], in0=hi.t[:, off + 1:off + Ls], scalar=cw.t[:, tci, 2:3],
                                                                                                                   in1=t_.t[:, off:off + Ls - 1], op0=ALU.mult, op1=ALU.add), r=[hi, cw, t_], w=[t_])
                            if j == 1:
                                tk.op("dve", lambda e, t_=t_, j=j, b=b: e.tensor_copy(out=zt[j][b].t[:].rearrange("p (n i) -> p n i", i=128),
                                                                                     in_=t_.t[:].rearrange("p (n i) -> p n i", i=128)[:, :, ::-1]), r=[t_], w=[zt[j][b]])
                            else:
                                tk.op("act", lambda e, t_=t_, j=j, b=b: e.activation(out=zt[j][b].t[:], in_=t_.t[:], func=AF.Copy), r=[t_], w=[zt[j][b]])
                    for (sn, Ls, off) in segs:
                        g = geo[sn]
                        nb_ = Ls // 128
                        for j, (dstT, rev) in enumerate(((g["U1"], False), (g["X1"], True), (g["X2"], False))):
                            for b in range(NB):
                                for k4 in range(0, nb_, 8):
                                    pb = self.psb[pi % 4]
                                    pi += 1
                                    pv = pb.t[:].bitcast(BF16)
                                    nn = min(8, nb_ - k4)
                                    for q in range(nn):
                                        blk = k4 + q
                                        src = zt[j][b].t[:, off + blk * 128:off + (blk + 1) * 128]
                                        tk.op("pe", lambda e, pv=pv, q=q, src=src: e.transpose(out=pv[:, q * 128:(q + 1) * 128], in_=src, identity=self.ident_b.t[:]), r=[zt[j][b], self.ident_b], w=[pb])
                                    tk.op("dve", lambda e, pv=pv, dstT=dstT, b=b, k4=k4, nn=nn, nb_=nb_: e.tensor_copy(
                                        out=dstT.t[:, :, b * nb_ + k4:b * nb_ + k4 + nn].rearrange("p c q -> p q c"), in_=pv[:, 0:nn * 128].rearrange("p (q c) -> p q c", q=nn)), r=[pb], w=[dstT])
                    for k in range(2):
                        for c16 in range(0, 128, 16):
                            pbs = {}
                            for (sn, Ls, off) in segs:
                                pbs[sn] = self.psb[4 + pi % 4]
                                pi += 1
                            for cc in range(16):
                                c = c16 + cc
                                for (sn, Ls, off) in segs:
                                    g = geo[sn]
                                    nb_ = Ls // 128
                                    nc_ = NB * nb_
                                    Rt_ = g["R"][ri % len(g["R"])]
                                    ri += 1
                                    gsrc = self.hy_GL if sn == "L" else self.hy_GC
                                    wdt = 2 * Ls - 128
                                    srcap = bass.AP(tensor=gsrc.t, offset=(k * 512 + ct * 128 + c) * gsrc.t.shape[2] + k, ap=[[1, 128], [1, wdt]])
                                    tk.dma("sp" if (ri % 2) else "act", Rt_.t[:], srcap, r=[gsrc], w=[Rt_])
                                    rhsT = g["U1"] if k == 0 else g["U2"]
                                    pb = pbs[sn]
                                    ov = pb.t[:, cc * nc_:(cc + 1) * nc_].rearrange("p (b a) -> p b a", b=NB)
                                    rv = rhsT.t[:, c, :].rearrange("p (b a) -> p b a", b=NB)
                                    ds = [0] + [d for d in range(-(nb_ - 1), nb_) if d != 0]
                                    for di_, d in enumerate(ds):
                                        X = (Ls - 128 - 128 * d) if k == 0 else (Ls - 128 + 128 * d)
                                        a_lo, a_hi = max(0, d), min(nb_, nb_ + d)
                                        tk.op("pe", lambda e, ov=ov, rv=rv, Rt_=Rt_, X=X, a_lo=a_lo, a_hi=a_hi, d=d, di_=di_, nd=len(ds): e.matmul(
                                            ov[:, :, a_lo:a_hi], lhsT=Rt_.t[:, X:X + 128], rhs=rv[:, :, a_lo - d:a_hi - d], start=(di_ == 0), stop=(di_ == nd - 1)),
                                            r=[Rt_, rhsT], w=[pb])
                            for (sn, Ls, off) in segs:
                                g = geo[sn]
                                nc_ = NB * (Ls // 128)
                                mul, dst = (g["X1"], g["U2"]) if k == 0 else (g["X2"], g["Y"])
                                tk.op("dve", lambda e, pb=pbs[sn], mul=mul, dst=dst, c16=c16, nc_=nc_: e.tensor_tensor(
                                    out=dst.t[:, c16:c16 + 16, :], in0=pb.t[:, 0:16 * nc_].rearrange("p (c a) -> p c a", c=16), in1=mul.t[:, c16:c16 + 16, :], op=ALU.mult), r=[pbs[sn], mul], w=[dst])
                    for (sn, Ls, off) in segs:
                        g = geo[sn]
                        nb_ = Ls // 128
                        for b in range(NB):
                            for k4 in range(0, nb_, 8):
                                pb = self.psb[pi % 4]
                                pi += 1
                                pv = pb.t[:].bitcast(BF16)
                                nn = min(8, nb_ - k4)
                                for q in range(nn):
                                    col = b * nb_ + k4 + q
                                    tk.op("pe", lambda e, pv=pv, q=q, g=g, col=col: e.transpose(out=pv[:, q * 128:(q + 1) * 128], in_=g["Y"].t[:, :, col], identity=self.ident_b.t[:]),
                                          r=[g["Y"], self.ident_b], w=[pb])
                                tk.op("dve", lambda e, pv=pv, ct=ct, b=b, off=off, k4=k4, nn=nn: e.tensor_copy(out=yhy[ct][b].t[:, off + k4 * 128:off + (k4 + nn) * 128], in_=pv[:, 0:nn * 128]),
                                      r=[pb], w=[yhy[ct][b]])
                    if last:
                        for b in range(NB):
                            tk.op("pool", lambda e, ct=ct, b=b: e.memset(yhy[ct][b].t[:, 0:LCTX], 0.0), w=[yhy[ct][b]])
            tk.barrier()
            with ExitStack() as es:
                bgT = tk.sb(es, "hybgT", [128, 4], F32)
                tk.dma("sp", bgT.t[:], self.branch_gT.t.ap()[l, 2], w=[bgT])
                for b in range(NB):
                    self.finish_branch(l, es, 3, b, [yhy[ct][b] for ct in range(4)], True, bgT)

    def mixer_rw(self, l):
        tk = self.tk
        last = (l == DEPTH - 1)
        NCH = LS // 128
        PIECES = [(0, 512), (512, 512), (1024, 512), (1536, 512), (2048, 256)]
        SEGS = [(0, LCTX), (LCTX, LLAT)]
        EH = math.exp(-0.5)

        def rev_copy(eng, dst, src, srcT, dstT):
            for (o, n) in SEGS:
                tk.op(eng, lambda e, o=o, n=n: e.tensor_copy(out=dst[:, o:o + n], in_=src[:, o:o + n][:, ::-1]), r=[srcT], w=[dstT])

        with ExitStack() as es:
            sb = lambda n, sh, dt=F32: tk.sb(es, "rp" + n, sh, dt)
            mu = sb("mu", [64, 24, 3])
            mul_ = sb("mul", [128, 3])
            hp = sb("hp", [64, 8, 8])
            w2p = sb("w2p", [128, 2, 512], BF16)
            a2p = sb("a2p", [128, 2, 512], BF16)
            ones64 = sb("ones64", [64, 64], BF16)
            e12 = sb("e12", [64, 1])
            msk = sb("msk", [64, LS], BF16)
            tk.dma("sp", mu.t[:, :, 0:2], self.rw_muH.t.ap()[l], w=[mu])
            tk.dma("sp", mul_.t[:, 0:2], self.rw_muL.t.ap()[l], w=[mul_])
            tk.dma("sp", hp.t[:, :, 0:7], self.rw_hp.t.ap()[l], w=[hp])
            tk.dma("pool", w2p.t[:], self.rw_w2pad.t.ap()[l].rearrange("d k n -> k d n"), w=[w2p])
            tk.dma("pool", a2p.t[:], self.rw_a2pad.t.ap()[l].rearrange("d k n -> k d n"), w=[a2p])
            tk.dma("sp", msk.t[:], self.rw_mask.t.ap(), w=[msk])
            tk.op("pool", lambda e: e.memset(ones64.t[:], 1.0), w=[ones64])
            tk.op("pool", lambda e: e.memset(e12.t[:], 1e-12), w=[e12])
            for (t_, nn) in ((mu, None), (mul_, None)):
                pass
            tk.op("dve", lambda e: e.tensor_tensor(out=mu.t[:, :, 2], in0=mu.t[:, :, 0], in1=mu.t[:, :, 1], op=ALU.add), r=[mu], w=[mu])
            tk.op("dve", lambda e: e.tensor_scalar(out=mu.t[:, :, 2], in0=mu.t[:, :, 2], scalar1=-1.0, scalar2=1.0, op0=ALU.mult, op1=ALU.add), r=[mu], w=[mu])
            tk.op("dve", lambda e: e.tensor_tensor(out=mul_.t[:, 2:3], in0=mul_.t[:, 0:1], in1=mul_.t[:, 1:2], op=ALU.add), r=[mul_], w=[mul_])
            tk.op("dve", lambda e: e.tensor_scalar(out=mul_.t[:, 2:3], in0=mul_.t[:, 2:3], scalar1=-1.0, scalar2=1.0, op0=ALU.mult, op1=ALU.add), r=[mul_], w=[mul_])

            def shift(src, dst, P, cp, cn, c0, rT):
                tk.op("act", lambda e: e.activation(out=dst.t[0:P, :], in_=src.t[0:P, :], func=AF.Copy, scale=c0), r=[src] + rT, w=[dst])
                for (o, n) in SEGS:
                    tk.op("dve", lambda e, o=o, n=n: e.scalar_tensor_tensor(out=dst.t[0:P, o + 1:o + n], in0=src.t[0:P, o:o + n - 1], scalar=cp, in1=dst.t[0:P, o + 1:o + n], op0=ALU.mult, op1=ALU.add),
                          r=[src, dst] + rT, w=[dst])
                    tk.op("dve", lambda e, o=o, n=n: e.scalar_tensor_tensor(out=dst.t[0:P, o:o + n - 1], in0=src.t[0:P, o + 1:o + n], scalar=cn, in1=dst.t[0:P, o:o + n - 1], op0=ALU.mult, op1=ALU.add),
                          r=[src, dst] + rT, w=[dst])

            lin = sb("lin", [128, LS], BF16)
            lsh = sb("lsh", [128, LS])
            lt = [sb(f"lt{i}", [128, LS], BF16) for i in range(2)]
            lr = [sb(f"lr{i}", [128, LS], BF16) for i in range(2)]
            xin = [sb(f"xin{i}", [64, LS], BF16) for i in range(3)]
            base = [sb(f"base{i}", [64, LS]) for i in range(4)]
            strm = [sb(f"strm{i}", [64, LS], BF16) for i in range(4)]
            f1, f2, f3, f4, f5 = sb("f1", [64, LS]), sb("f2", [64, LS]), sb("f3", [64, LS]), sb("f4", [64, LS]), sb("f5", [64, LS])
            ob = [sb(f"ob{i}", [64, LS], BF16) for i in range(5)]
            sqb = sb("sqb", [64, LS], BF16)
            bva = sb("bva", [64, LS])
            gam = sb("gam", [64, NCH])
            pi = 0
            for b in range(NB):
                tk.dma("sp", lin.t[:], self.pfm.t.ap()[R_LORA:R_LORA + 128, b, :], r=[self.pfm], w=[lin])
                shift(lin, lsh, 128, mul_.t[:, 0:1], mul_.t[:, 1:2], mul_.t[:, 2:3], [mul_])
                tk.op("act", lambda e: e.activation(out=lt[0].t[:], in_=lsh.t[:], func=AF.Tanh), r=[lsh], w=[lt[0]])
                tk.op("dve", lambda e: e.tensor_copy(out=lr[0].t[:], in_=lsh.t[:]), r=[lsh], w=[lr[0]])
                rev_copy("pool", lt[1].t, lt[0].t, lt[0], lt[1])
                rev_copy("pool", lr[1].t, lr[0].t, lr[0], lr[1])
                for h in range(8):
                    for q in range(3):
                        row = R_RKV + q * 512 + h * 64
                        tk.dma("sp", xin[q].t[:], self.pfm.t.ap()[row:row + 64, b, :], r=[self.pfm], w=[xin[q]])
                        mi = q * 8 + h
                        shift(xin[q], base[q], 64, mu.t[:, mi, 0:1], mu.t[:, mi, 1:2], mu.t[:, mi, 2:3], [mu])
                    tk.op("dve", lambda e, h=h: e.tensor_scalar(out=base[3].t[:], in0=base[1].t[:], scalar1=hp.t[:, h, 0:1], scalar2=None, op0=ALU.mult), r=[base[1], hp], w=[base[3]])
                    tk.op("act", lambda e: e.activation(out=sqb.t[:], in_=base[3].t[:], func=AF.Square), r=[base[3]], w=[sqb])
                    for (t0, n) in PIECES:
                        pb = self.psb[pi % 8]
                        pi += 1
                        tk.op("pe", lambda e, pb=pb, t0=t0, n=n: e.matmul(pb.t[0:64, 0:n], lhsT=ones64.t[:], rhs=sqb.t[:, t0:t0 + n], start=True, stop=True), r=[ones64, sqb], w=[pb])
                        tk.op("act", lambda e, pb=pb, t0=t0, n=n: e.activation(out=f1.t[:, t0:t0 + n], in_=pb.t[0:64, 0:n], func=AF.Sqrt, bias=e12.t[:], scale=1.0), r=[pb, e12], w=[f1])
                    tk.op("dve", lambda e: e.reciprocal(out=f1.t[:], in_=f1.t[:]), r=[f1], w=[f1])
                    tk.op("dve", lambda e: e.tensor_tensor(out=base[3].t[:], in0=base[3].t[:], in1=f1.t[:], op=ALU.mult), r=[base[3], f1], w=[base[3]])
                    for di in range(2):
                        if di == 0:
                            S = base
                        else:
                            for q in range(4):
                                rev_copy("pool" if q % 2 else "dve", strm[q].t, base[q].t, base[q], strm[q])
                            S = strm
                        rS, kS, vS, kkS = S
                        for (t0, n) in PIECES:
                            pb, pb2 = self.psb[pi % 8], self.psb[(pi + 1) % 8]
                            pi += 2
                            tk.op("pe", lambda e, pb=pb, t0=t0, n=n, di=di, h=h: e.matmul(pb.t[0:64, 0:n], lhsT=w2p.t[:, di, h * 64:(h + 1) * 64], rhs=lt[di].t[:, t0:t0 + n], start=True, stop=True),
                                  r=[w2p, lt[di]], w=[pb])
                            tk.op("pe", lambda e, pb2=pb2, t0=t0, n=n, di=di, h=h: e.matmul(pb2.t[0:64, 0:n], lhsT=a2p.t[:, di, h * 64:(h + 1) * 64], rhs=lr[di].t[:, t0:t0 + n], start=True, stop=True),
                                  r=[a2p, lr[di]], w=[pb2])
                            tk.op("act", lambda e, pb=pb, t0=t0, n=n, di=di, h=h: e.activation(out=f1.t[:, t0:t0 + n], in_=pb.t[0:64, 0:n], func=AF.Sigmoid, bias=hp.t[:, h, 3 + di:4 + di], scale=1.0),
                                  r=[pb, hp], w=[f1])
                            tk.op("act", lambda e, pb2=pb2, t0=t0, n=n, di=di, h=h: e.activation(out=f2.t[:, t0:t0 + n], in_=pb2.t[0:64, 0:n], func=AF.Sigmoid, bias=hp.t[:, h, 5 + di:6 + di], scale=1.0),
                                  r=[pb2, hp], w=[f2])
                        tk.op("pool", lambda e: e.tensor_scalar(out=f1.t[:], in0=f1.t[:], scalar1=-EH, scalar2=None, op0=ALU.mult), r=[f1], w=[f1])
                        tk.op("dve", lambda e: e.tensor_tensor_scan(out=f3.t[:], data0=msk.t[:], data1=f1.t[:], initial=0.0, op0=ALU.mult, op1=ALU.add), r=[msk, f1], w=[f3])
                        tk.op("pool", lambda e: e.tensor_tensor(out=f1.t[:], in0=f3.t[:], in1=f1.t[:], op=ALU.subtract), r=[f3, f1], w=[f1])
                        tk.op("act", lambda e: e.activation(out=f1.t[:], in_=f1.t[:], func=AF.Exp), r=[f1], w=[f1])
                        tk.op("act", lambda e: e.activation(out=f4.t[:], in_=f3.t[:], func=AF.Exp, scale=-1.0), r=[f3], w=[f4])
                        tk.op("act", lambda e: e.activation(out=f3.t[:], in_=f3.t[:], func=AF.Exp), r=[f3], w=[f3])
                        tk.op("dve", lambda e: e.scalar_tensor_tensor(out=ob[0].t[:], in0=kkS.t[:], scalar=-1.0, in1=f1.t[:], op0=ALU.mult, op1=ALU.mult), r=[kkS, f1], w=[ob[0]])
                        tk.op("pool", lambda e: e.tensor_tensor(out=ob[1].t[:], in0=rS.t[:], in1=f3.t[:], op=ALU.mult), r=[rS, f3], w=[ob[1]])
                        tk.op("dve", lambda e: e.tensor_tensor(out=f5.t[:], in0=kkS.t[:], in1=f2.t[:], op=ALU.mult), r=[kkS, f2], w=[f5])
                        tk.op("pool", lambda e: e.tensor_tensor(out=ob[2].t[:], in0=f5.t[:], in1=f4.t[:], op=ALU.mult), r=[f5, f4], w=[ob[2]])
                        tk.op("dve", lambda e, h=h: e.tensor_scalar(out=f2.t[:], in0=f2.t[:], scalar1=-1.0, scalar2=hp.t[:, h, 1:2], op0=ALU.add, op1=ALU.mult), r=[f2, hp], w=[f2])
                        tk.op("dve", lambda e: e.scalar_tensor_tensor(out=f5.t[:], in0=f2.t[:], scalar=1.0, in1=kS.t[:], op0=ALU.add, op1=ALU.mult), r=[f2, kS], w=[f5])
                        tk.op("pool", lambda e: e.tensor_tensor(out=ob[3].t[:], in0=f5.t[:], in1=f4.t[:], op=ALU.mult), r=[f5, f4], w=[ob[3]])
                        tk.op("act", lambda e: e.activation(out=ob[4].t[:], in_=vS.t[:], func=AF.Copy), r=[vS], w=[ob[4]])
                        for q in range(5):
                            tk.dma("sp", self.rw_s.t.ap()[b, di, h, q], ob[q].t[:], r=[ob[q]], w=[self.rw_s])
                        tk.op("act", lambda e: e.activation(out=gam.t[:], in_=f3.t[:, 127::128], func=AF.Copy), r=[f3], w=[gam])
                        tk.dma("sp", self.rw_g.t.ap()[b, di, h], gam.t[:], r=[gam], w=[self.rw_g])
                        tk.op("dve", lambda e, h=h: e.scalar_tensor_tensor(out=sqb.t[:], in0=rS.t[:], scalar=hp.t[:, h, 2:3], in1=f5.t[:], op0=ALU.mult, op1=ALU.mult), r=[rS, hp, f5], w=[sqb])
                        for (t0, n) in PIECES:
                            pb = self.psb[pi % 8]
                            pi += 1
                            tk.op("pe", lambda e, pb=pb, t0=t0, n=n: e.matmul(pb.t[0:64, 0:n], lhsT=ones64.t[:], rhs=sqb.t[:, t0:t0 + n], start=True, stop=True), r=[ones64, sqb], w=[pb])
                            tk.op("dve", lambda e, pb=pb, t0=t0, n=n: e.tensor_tensor(out=f4.t[:, t0:t0 + n], in0=pb.t[0:64, 0:n], in1=vS.t[:, t0:t0 + n], op=ALU.mult), r=[pb, vS], w=[f4])
                        if di == 0:
                            tk.op("pool", lambda e: e.tensor_copy(out=bva.t[:], in_=f4.t[:]), r=[f4], w=[bva])
                        else:
                            for (o, n) in SEGS:
                                tk.op("dve", lambda e, o=o, n=n: e.tensor_tensor(out=bva.t[:, o:o + n], in0=bva.t[:, o:o + n], in1=f4.t[:, o:o + n][:, ::-1], op=ALU.add), r=[bva, f4], w=[bva])
                    tk.dma("sp", self.rw_bv.t.ap()[b, h * 64:(h + 1) * 64, :], bva.t[:], r=[bva], w=[self.rw_bv])
        tk.barrier()
        with ExitStack() as es:
            sb = lambda n, sh, dt=F32: tk.sb(es, "rc" + n, sh, dt)
            mk2 = sb("mk2", [128, 256], BF16)
            mkL = sb("mkL", [128, 128], BF16)
            Jm = sb("Jm", [128, 128], BF16)
            mkBD, mkOFF = sb("mkBD", [128, 128], BF16), sb("mkOFF", [128, 128], BF16)
            tk.dma("sp", mkBD.t[:], self.rw_mkBD.t.ap(), w=[mkBD])
            tk.dma("sp", mkOFF.t[:], self.rw_mkOFF.t.ap(), w=[mkOFF])
            Aoff = [sb(f"Aoff{i}", [128, 8, 128], BF16) for i in range(2)]
            ZT = sb("ZT", [128, 512], BF16)
            WT = sb("WT", [128, 512], BF16)
            tk.dma("sp", mk2.t[:], self.rw_mk2.t.ap(), w=[mk2])
            tk.dma("sp", mkL.t[:], self.rw_mkL.t.ap(), w=[mkL])
            tk.dma("sp", Jm.t[:], self.rw_J.t.ap(), w=[Jm])
            FMc = [sb(f"FMc{i}", [64, 8, 5, 128], BF16) for i in range(2)]
            TMt = [sb(f"TMt{i}", [128, 24, 64], BF16) for i in range(2)]
            ABr = [sb(f"ABr{i}", [128, 8, 256], BF16) for i in range(2)]
            AkK = [sb(f"AkK{i}", [128, 8, 256], BF16) for i in range(2)]
            Mt_ = [sb(f"Mt{i}", [128, 8, 128]) for i in range(2)]
            Mm_ = [sb(f"Mm{i}", [128, 8, 128]) for i in range(2)]
            Tf = sb("Tf", [128, 8, 128])
            Tb = [sb(f"Tb{i}", [128, 8, 128], BF16) for i in range(2)]
            gamt = sb("gamt", [64, 8, NCH])
            Sf = sb("Sf", [64, 8, 64])
            St = sb("St", [64, 8, 64])
            Sb_ = sb("Sb", [64, 8, 64], BF16)
            XT = sb("XT", [128, 512], BF16)
            UT = sb("UT", [128, 512], BF16)
            Yt = [sb(f"Yt{i}", [128, NCH, 512], BF16) for i in range(2)]
            idb = self.ident_b

            def pre_steps(b, di, c):
                i2 = c % 2
                F, Tm, AB, AK = FMc[i2], TMt[i2], ABr[i2], AkK[i2]
                steps = []

                def s0():
                    tk.dma("sp", F.t[:], self.rw_s.t.ap()[b, di, :, :, :, c * 128:(c + 1) * 128].rearrange("h q j t -> j h q t"), r=[self.rw_s], w=[F])
                    for half in range(2):
                        pb = self.psb[4 + half]
                        pv = pb.t[:].bitcast(BF16)
                        for k in range(12):
                            idx = half * 12 + k
                            h, q = idx // 3, 2 + idx % 3
                            tk.op("pe", lambda e, pv=pv, k=k, h=h, q=q: e.transpose(out=pv[:, k * 64:(k + 1) * 64], in_=F.t[:, h, q, :], identity=idb.t[0:64, 0:64]), r=[F, idb], w=[pb])
                        tk.op("dve", lambda e, pv=pv, half=half: e.tensor_copy(out=Tm.t[:, half * 12:(half + 1) * 12, :], in_=pv[:, 0:768].rearrange("p (k d) -> p k d", d=64)), r=[pb], w=[Tm])
                steps.append(s0)

                def gram(lq, dst):
                    def g():
                        for half in range(2):
                            for hh in range(4):
                                h = half * 4 + hh
                                pb = self.psb[half * 2 + hh // 2]
                                tk.op("pe", lambda e, pb=pb, hh=hh, h=h: e.matmul(pb.t[:, (hh % 2) * 256:(hh % 2 + 1) * 256], lhsT=F.t[:, h, lq, :], rhs=F.t[:, h, 0:2, :], start=True, stop=True),
                                      r=[F], w=[pb])
                            for k2 in range(2):
                                pb = self.psb[half * 2 + k2]
                                tk.op("dve", lambda e, pb=pb, half=half, k2=k2: e.tensor_tensor(
                                    out=dst.t[:, half * 4 + k2 * 2:half * 4 + k2 * 2 + 2, :], in0=pb.t[:].rearrange("p (a m) -> p a m", a=2),
                                    in1=mk2.t[:].unsqueeze(1).to_broadcast([128, 2, 256]), op=ALU.mult), r=[pb, mk2], w=[dst])
                    return g
                steps.append(gram(2, AB))
                steps.append(gram(3, AK))

                def s3():
                    tk.op("dve", lambda e: e.tensor_tensor(out=Mm_[0].t[:], in0=AB.t[:, :, 0:128], in1=mkBD.t[:].unsqueeze(1).to_broadcast([128, 8, 128]), op=ALU.mult), r=[AB, mkBD], w=[Mm_[0]])
                    tk.op("dve", lambda e: e.tensor_tensor(out=Aoff[i2].t[:], in0=AB.t[:, :, 0:128], in1=mkOFF.t[:].unsqueeze(1).to_broadcast([128, 8, 128]), op=ALU.mult), r=[AB, mkOFF], w=[Aoff[i2]])
                    for h in range(8):
                        pb = self.psb[4 + h // 4]
                        tk.op("pe", lambda e, pb=pb, h=h: e.matmul(pb.t[:, (h % 4) * 128:(h % 4 + 1) * 128], lhsT=Mm_[0].t[:, h, :], rhs=self.ident_f.t[:], start=True, stop=True), r=[Mm_[0], self.ident_f], w=[pb])
                    for k2 in range(2):
                        pb = self.psb[4 + k2]
                        tk.op("act", lambda e, pb=pb, k2=k2: e.activation(out=Mt_[0].t[:, k2 * 4:(k2 + 1) * 4, :], in_=pb.t[:].rearrange("p (a m) -> p a m", a=4), func=AF.Copy), r=[pb], w=[Mt_[0]])
                    tk.op("dve", lambda e: e.tensor_tensor(out=Tf.t[:], in0=Mm_[0].t[:], in1=self.ident_f.t[:].unsqueeze(1).to_broadcast([128, 8, 128]), op=ALU.add), r=[Mm_[0], self.ident_f], w=[Tf])
                steps.append(s3)

                def rnd(k):
                    def g():
                        src, dst = (k - 1) % 2, k % 2
                        M, Mt, Mn, Mtn = Mm_[src], Mt_[src], Mm_[dst], Mt_[dst]
                        lastr = (k == 5)
                        for h in range(8):
                            if not lastr:
                                pb = self.psb[0 + h // 4]
                                tk.op("pe", lambda e, pb=pb, h=h: e.matmul(pb.t[:, (h % 4) * 128:(h % 4 + 1) * 128], lhsT=Mt.t[:, h, :], rhs=M.t[:, h, :], start=True, stop=True), r=[Mt, M], w=[pb])
                            pb = self.psb[2 + h // 4]
                            tk.op("pe", lambda e, pb=pb, h=h: e.matmul(pb.t[:, (h % 4) * 128:(h % 4 + 1) * 128], lhsT=M.t[:, h, :], rhs=Mt.t[:, h, :], start=True, stop=True), r=[Mt, M], w=[pb])
                        for k2 in range(2):
                            if not lastr:
                                tk.op("act", lambda e, k2=k2: e.activation(out=Mn.t[:, k2 * 4:(k2 + 1) * 4, :], in_=self.psb[k2].t[:].rearrange("p (a m) -> p a m", a=4), func=AF.Copy), r=[self.psb[k2]], w=[Mn])
                            tk.op("dve", lambda e, k2=k2: e.tensor_copy(out=Mtn.t[:, k2 * 4:(k2 + 1) * 4, :], in_=self.psb[2 + k2].t[:].rearrange("p (a m) -> p a m", a=4)), r=[self.psb[2 + k2]], w=[Mtn])
                        for h in range(8):
                            pb = self.psb[4 + h // 4]
                            tk.op("pe", lambda e, pb=pb, h=h: e.matmul(pb.t[:, (h % 4) * 128:(h % 4 + 1) * 128], lhsT=Mtn.t[:, h, :], rhs=Tf.t[:, h, :], start=True, stop=True), r=[Mtn, Tf], w=[pb])
                        for k2 in range(2):
                            tk.op("dve", lambda e, k2=k2: e.tensor_tensor(out=Tf.t[:, k2 * 4:(k2 + 1) * 4, :], in0=Tf.t[:, k2 * 4:(k2 + 1) * 4, :], in1=self.psb[4 + k2].t[:].rearrange("p (a m) -> p a m", a=4), op=ALU.add),
                                  r=[Tf, self.psb[4 + k2]], w=[Tf])
                        if lastr:
                            tk.op("act", lambda e: e.activation(out=Tb[i2].t[:], in_=Tf.t[:], func=AF.Copy), r=[Tf], w=[Tb[i2]])
                    return g
                for k in range(1, 6):
                    steps.append(rnd(k))
                return steps

            def chain_steps(b, di, c):
                i2 = c % 2
                F, Tm, AB, AK = FMc[i2], TMt[i2], ABr[i2], AkK[i2]
                p6, p7 = self.psb[6], self.psb[7]
                BT = lambda h: Tm.t[:, h * 3 + 0, :]
                KTt = lambda h: Tm.t[:, h * 3 + 1, :]
                VT = lambda h: Tm.t[:, h * 3 + 2, :]

                def c1():
                    for h in range(8):
                        tk.op("pe", lambda e, h=h: e.matmul(p6.t[:, h * 64:(h + 1) * 64], lhsT=F.t[:, h, 0, :], rhs=Sb_.t[:, h, :], start=True, stop=False), r=[F, Sb_], w=[p6])
                        tk.op("pe", lambda e, h=h: e.matmul(p6.t[:, h * 64:(h + 1) * 64], lhsT=AK.t[:, h, 0:128], rhs=VT(h), start=False, stop=True), r=[AK, Tm], w=[p6])
                    tk.op("dve", lambda e: e.tensor_copy(out=XT.t[:], in_=p6.t[:]), r=[p6], w=[XT])

                def c2a():
                    for h in range(8):
                        tk.op("pe", lambda e, h=h: e.matmul(p7.t[:, h * 64:(h + 1) * 64], lhsT=Tb[i2].t[:, h, :], rhs=XT.t[:, h * 64:(h + 1) * 64], start=True, stop=True), r=[Tb[i2], XT], w=[p7])
                    tk.op("dve", lambda e: e.tensor_copy(out=ZT.t[:], in_=p7.t[:]), r=[p7], w=[ZT])

                def c2b():
                    for h in range(8):
                        tk.op("pe", lambda e, h=h: e.matmul(p6.t[:, h * 64:(h + 1) * 64], lhsT=Aoff[i2].t[:, h, :], rhs=ZT.t[:, h * 64:(h + 1) * 64], start=True, stop=True), r=[Aoff[i2], ZT], w=[p6])
                    tk.op("dve", lambda e: e.tensor_tensor(out=WT.t[:], in0=p6.t[:], in1=XT.t[:], op=ALU.add), r=[p6, XT], w=[WT])

                def c2c():
                    for h in range(8):
                        tk.op("pe", lambda e, h=h: e.matmul(p7.t[:, h * 64:(h + 1) * 64], lhsT=Tb[i2].t[:, h, :], rhs=WT.t[:, h * 64:(h + 1) * 64], start=True, stop=True), r=[Tb[i2], WT], w=[p7])
                    tk.op("dve", lambda e: e.tensor_copy(out=UT.t[:], in_=p7.t[:]), r=[p7], w=[UT])

                def c3():
                    for h in range(8):
                        tk.op("pe", lambda e, h=h: e.matmul(p6.t[:, h * 64:(h + 1) * 64], lhsT=F.t[:, h, 1, :], rhs=Sb_.t[:, h, :], start=True, stop=False), r=[F, Sb_], w=[p6])
                        tk.op("pe", lambda e, h=h: e.matmul(p6.t[:, h * 64:(h + 1) * 64], lhsT=AB.t[:, h, 128:256], rhs=UT.t[:, h * 64:(h + 1) * 64], start=False, stop=False), r=[AB, UT], w=[p6])
                        tk.op("pe", lambda e, h=h: e.matmul(p6.t[:, h * 64:(h + 1) * 64], lhsT=AK.t[:, h, 128:256], rhs=VT(h), start=False, stop=True), r=[AK, Tm], w=[p6])
                    tk.op("act", lambda e: e.activation(out=Yt[di].t[:, c, :], in_=p6.t[:], func=AF.Copy), r=[p6], w=[Yt[di]])
                    for h in range(8):
                        tk.op("pe", lambda e, h=h: e.matmul(p7.t[0:64, h * 64:(h + 1) * 64], lhsT=BT(h), rhs=UT.t[:, h * 64:(h + 1) * 64], start=True, stop=False), r=[Tm, UT], w=[p7])
                        tk.op("pe", lambda e, h=h: e.matmul(p7.t[0:64, h * 64:(h + 1) * 64], lhsT=KTt(h), rhs=VT(h), start=False, stop=True), r=[Tm], w=[p7])

                def c4():
                    for h in range(8):
                        tk.op("pool", lambda e, h=h: e.tensor_scalar(out=St.t[:, h, :], in0=Sf.t[:, h, :], scalar1=gamt.t[:, h, c:c + 1], scalar2=None, op0=ALU.mult), r=[Sf, gamt], w=[St])
                    for h in range(8):
                        tk.op("dve", lambda e, h=h: e.scalar_tensor_tensor(out=Sf.t[:, h, :], in0=p7.t[0:64, h * 64:(h + 1) * 64], scalar=gamt.t[:, h, c:c + 1], in1=St.t[:, h, :],
                                                                           op0=ALU.mult, op1=ALU.add), r=[p7, gamt, St], w=[Sf])
                    tk.op("act", lambda e: e.activation(out=Sb_.t[:], in_=Sf.t[:], func=AF.Copy), r=[Sf], w=[Sb_])
                def dbg():
                    if "rw_dbg" in self.dump and b == 0 and di == 0 and c == 2 and l == 0:
                        d = self.dump_t("rw_dbg", [128, 4 * 512 + 1024 + 512], BF16)
                        for i_, tt_ in enumerate((XT, ZT, WT, UT)):
                            tk.dma("sp", d.t.ap()[:, i_ * 512:(i_ + 1) * 512], tt_.t[:], r=[tt_], w=[d])
                        tk.dma("sp", d.t.ap()[:, 2048:3072], Tb[i2].t[:].rearrange("p a m -> p (a m)"), r=[Tb[i2]], w=[d])
                        tk.dma("sp", d.t.ap()[0:64, 3072:3584], Sb_.t[:].rearrange("p a m -> p (a m)"), r=[Sb_], w=[d])
                return [c1, c2a, c2b, c2c, lambda: (dbg(), c3()), c4]

            yfm = [sb(f"yfm{i}", [128, LS]) for i in range(4)]
            bvt = sb("bvt", [128, LS])
            lnp = sb("lnp", [128, 4, 2])
            bd64 = sb("bd64", [128, 128])
            e64 = sb("e64", [128, 1])
            tk.dma("sp", lnp.t[:], self.rw_lnT.t.ap()[l], w=[lnp])
            tk.dma("sp", bd64.t[:], self.rw_bd.t.ap(), w=[bd64])
            tk.op("pool", lambda e: e.memset(e64.t[:], 64e-5), w=[e64])
            for b in range(NB):
                for di in range(2):
                    tk.dma("sp", gamt.t[:], self.rw_g.t.ap()[b, di].rearrange("h j c -> j h c"), r=[self.rw_g], w=[gamt])
                    tk.op("pool", lambda e: e.memset(Sf.t[:], 0.0), w=[Sf])
                    tk.op("pool", lambda e: e.memset(Sb_.t[:], 0.0), w=[Sb_])
                    for st in pre_steps(b, di, 0):
                        st()
                    for c in range(NCH):
                        cs = chain_steps(b, di, c)
                        ps_ = pre_steps(b, di, c + 1) if c + 1 < NCH else []
                        order = [ps_[0:1], cs[0:1], ps_[1:3], cs[1:2], ps_[3:5], cs[2:3], ps_[5:6], cs[3:4], ps_[6:7], cs[4:5], ps_[7:8], cs[5:6], ps_[8:]]
                        for grp in order:
                            for st in grp:
                                st()
                if "rw_yt" in self.dump and b == 0 and l == 0:
                    dy = self.dump_t("rw_yt", [2, 128, NCH * 512], BF16)
                    for di in range(2):
                        tk.dma("sp", dy.t.ap()[di], Yt[di].t[:].rearrange("p c f -> p (c f)"), r=[Yt[di]], w=[dy])
                pi = 0
                for q in range(4):
                    for n4 in range(0, NCH, 4):
                        pb = self.psb[pi % 4]
                        pi += 1
                        for n in range(n4, min(n4 + 4, NCH)):
                            cb = (1 - n) if n < 2 else (2 + (NCH - 1 - n))
                            tk.op("pe", lambda e, pb=pb, n=n, n4=n4, q=q: e.matmul(pb.t[:, (n - n4) * 128:(n - n4 + 1) * 128], lhsT=Yt[0].t[:, n, q * 128:(q + 1) * 128], rhs=idb.t[:], start=True, stop=False),
                                  r=[Yt[0], idb], w=[pb])
                            tk.op("pe", lambda e, pb=pb, n=n, n4=n4, q=q, cb=cb: e.matmul(pb.t[:, (n - n4) * 128:(n - n4 + 1) * 128], lhsT=Yt[1].t[:, cb, q * 128:(q + 1) * 128], rhs=Jm.t[:], start=False, stop=True),
                                  r=[Yt[1], Jm], w=[pb])
                        nn = min(4, NCH - n4)
                        tk.op("act", lambda e, pb=pb, n4=n4, nn=nn, q=q: e.activation(out=yfm[q].t[:, n4 * 128:(n4 + nn) * 128], in_=pb.t[:, 0:nn * 128], func=AF.Copy), r=[pb], w=[yfm[q]])
                    tk.dma("sp", bvt.t[:], self.rw_bv.t.ap()[b, q * 128:(q + 1) * 128, :], r=[self.rw_bv], w=[bvt])
                    for (t0, n) in PIECES:
                        pb, pb2 = self.psb[4 + pi % 4], self.psb[4 + (pi + 1) % 4]
                        pi += 2
                        yv = yfm[q].t[:, t0:t0 + n]
                        tk.op("pe", lambda e, pb=pb, yv=yv, n=n: e.matmul(pb.t[:, 0:n], lhsT=bd64.t[:], rhs=yv, start=True, stop=True), r=[bd64, yfm[q]], w=[pb])
                        tk.op("dve", lambda e, pb=pb, yv=yv, n=n: e.tensor_tensor(out=yv, in0=yv, in1=pb.t[:, 0:n], op=ALU.subtract), r=[yfm[q], pb], w=[yfm[q]])
                        tk.op("act", lambda e, yv=yv, n=n, t0=t0: e.activation(out=Tf.t[:].rearrange("p a m -> p (a m)")[:, 0:n], in_=yv, func=AF.Square), r=[yfm[q]], w=[Tf])
                        tk.op("pe", lambda e, pb2=pb2, n=n: e.matmul(pb2.t[:, 0:n], lhsT=bd64.t[:], rhs=Tf.t[:].rearrange("p a m -> p (a m)")[:, 0:n], start=True, stop=True), r=[bd64, Tf], w=[pb2])
                        tk.op("act", lambda e, pb2=pb2, n=n: e.activation(out=Tf.t[:].rearrange("p a m -> p (a m)")[:, 512:512 + n], in_=pb2.t[:, 0:n], func=AF.Sqrt, bias=e64.t[:], scale=1.0), r=[pb2, e64], w=[Tf])
                        tk.op("dve", lambda e, n=n: e.reciprocal(out=Tf.t[:].rearrange("p a m -> p (a m)")[:, 512:512 + n], in_=Tf.t[:].rearrange("p a m -> p (a m)")[:, 512:512 + n]), r=[Tf], w=[Tf])
                        tk.op("dve", lambda e, yv=yv, n=n: e.tensor_tensor(out=yv, in0=yv, in1=Tf.t[:].rearrange("p a m -> p (a m)")[:, 512:512 + n], op=ALU.mult), r=[yfm[q], Tf], w=[yfm[q]])
                    tk.op("act", lambda e, q=q: e.activation(out=yfm[q].t[:], in_=yfm[q].t[:], func=AF.Identity, bias=lnp.t[:, q, 1:2], scale=lnp.t[:, q, 0:1]), r=[yfm[q], lnp], w=[yfm[q]])
                    tk.op("dve", lambda e, q=q: e.tensor_tensor(out=yfm[q].t[:], in0=yfm[q].t[:], in1=bvt.t[:], op=ALU.add), r=[yfm[q], bvt], w=[yfm[q]])
                self.finish_branch(l, es, 2, b, yfm, False)

    def stage_outproj(self, l):
        tk = self.tk
        TG = 768
        last = (l == DEPTH - 1)
        with ExitStack() as es:
            wo = tk.sb(es, "wo", [128, 16, D], BF16)
            self.gate_b = tk.sb(es, "gate_b", [128, 3, D], F32)
            grep = [tk.sb(es, f"grep{i}", [128, 128], F32) for i in range(2)]
            i = 0
            for j in range(3):
                for n in range(16):
                    g = grep[i % 2]
                    pg = self.psb[1 + (i // 4) % 2]
                    tk.op("dve", lambda e, g=g, n=n, j=j: e.tensor_scalar(out=g.t[:], in0=self.ones_f.t[:], scalar1=self.modT.t[:, 32 + n, j:j + 1],
                                                                          scalar2=None, op0=ALU.mult), r=[self.ones_f, self.modT], w=[g])
                    tk.op("pe", lambda e, g=g, pg=pg, n=n: e.matmul(pg.t[:, (n % 4) * 128:(n % 4 + 1) * 128], lhsT=g.t[:], rhs=self.ident_f.t[:],
                                                                    start=True, stop=True), r=[g, self.ident_f], w=[pg])
                    if n % 4 == 3:
                        tk.op("act", lambda e, pg=pg, n=n, j=j: e.activation(out=self.gate_b.t[:, j, (n // 4) * 512:(n // 4 + 1) * 512], in_=pg.t[:],
                                                                             func=AF.Copy), r=[pg], w=[self.gate_b])
                    i += 1


            ymT = [tk.sb(es, f"ymT{i}", [128, 16, TG], BF16) for i in range(2)]
            xt = [tk.sb(es, f"oxt{i}", [128, D], F32) for i in range(2)]
            tmp = [tk.sb(es, f"otmp{i}", [128, 512], F32) for i in range(2)]
            for q4 in range(4):
                tk.dma("pool", wo.t[:, :, q4 * 512:(q4 + 1) * 512],
                       self.w_out.t.ap()[l, :, q4 * 512:(q4 + 1) * 512].rearrange("(k p) n -> p k n", p=128), w=[wo])
            pi = 0
            blk = 0
            for tg in range(6):
                b, p0 = tg // 3, (tg % 3) * TG
                y = ymT[tg % 2]
                tk.dma("sp", y.t[:], self.ym.t.ap()[:, b, p0:p0 + TG].rearrange("(k p) t -> p k t", p=128), r=[self.ym], w=[y])
                for kb in range(6):
                    pos = p0 + kb * 128
                    seg = 0 if pos < LCTX else 1
                    if seg == 0 and last:
                        continue
                    j = 2 if seg == 0 else b
                    src = self.x_src(l, b, seg)
                    dst = self.xc if seg == 0 else self.out
                    row0 = pos if seg == 0 else pos - LCTX
                    x_t = xt[blk % 2]
                    blk += 1
                    tk.dma("sp", x_t.t[:], src.t.ap()[b, row0:row0 + 128, :], r=[src], w=[x_t])
                    for q4 in range(4):
                        pb = self.psb[pi % 8]
                        tm = tmp[pi % 2]
                        pi += 1
                        for k in range(16):
                            tk.op("pe", lambda e, pb=pb, k=k, kb=kb, q4=q4, y=y: e.matmul(
                                pb.t[:], lhsT=y.t[:, k, kb * 128:(kb + 1) * 128], rhs=wo.t[:, k, q4 * 512:(q4 + 1) * 512],
                                start=(k == 0), stop=(k == 15)), r=[wo, y], w=[pb])
                        tk.op("dve", lambda e, pb=pb, tm=tm, j=j, q4=q4: e.tensor_tensor(out=tm.t[:], in0=pb.t[:], in1=self.gate_b.t[:, j, q4 * 512:(q4 + 1) * 512],
                                                                                         op=ALU.mult), r=[pb, self.gate_b], w=[tm])
                        tk.op("pool", lambda e, tm=tm, x_t=x_t, q4=q4: e.tensor_tensor(out=x_t.t[:, q4 * 512:(q4 + 1) * 512], in0=x_t.t[:, q4 * 512:(q4 + 1) * 512],
                                                                                      in1=tm.t[:], op=ALU.add), r=[tm, x_t], w=[x_t])
                    tk.dma("sp", dst.t.ap()[b, row0:row0 + 128, :], x_t.t[:], r=[x_t], w=[dst])


def host_inputs(inputs, core):
    f = lambda a: np.ascontiguousarray(np.asarray(a, dtype=np.float32))
    b0 = core * NB
    m = {}
    m["x"] = f(inputs["x"][b0:b0 + NB])
    m["ctx"] = f(inputs["ctx"][b0:b0 + NB])
    cv = np.stack([np.asarray(inputs["c"][b0]), np.asarray(inputs["c"][b0 + 1]), np.asarray(inputs["c_ctx"])], axis=-1)
    m["cT"] = f(cv.reshape(16, 128, 3).transpose(1, 0, 2))
    m["normgT"] = f(np.asarray(inputs["norm_g"]).reshape(DEPTH, 16, 128).transpose(0, 2, 1))
    m["w_ada"] = f(inputs["w_ada"])
    m["b_adaT"] = f(np.asarray(inputs["b_ada"]).reshape(DEPTH, 48, 128).transpose(0, 2, 1))
    m["w_in"] = f(inputs["w_in"])
    m["w_out"] = f(inputs["w_out"])
    L = DEPTH
    lre = np.asarray(inputs["s5_lam_re"]).reshape(L, 1, 4096)
    lim = np.asarray(inputs["s5_lam_im"]).reshape(L, 1, 4096)
    stp = np.repeat(np.asarray(inputs["s5_log_step"]).reshape(L, 64, 1), 64, axis=2).reshape(L, 1, 4096)
    m["s5_lreB"] = f(np.broadcast_to(lre, (L, 128, 4096)))
    m["s5_limB"] = f(np.broadcast_to(lim, (L, 128, 4096)))
    m["s5_stpB"] = f(np.broadcast_to(stp, (L, 128, 4096)))
    def padb(bx):
        o = np.zeros((L, 8, 16, 2, 32, 64), np.float32)
        bt = np.asarray(bx).transpose(0, 4, 1, 2, 3)
        for g in range(32):
            o[:, g % 8, :, :, g, :] = bt[:, :, :, g, :]
        return f(o.reshape(L, 128, 4096))
    m["s5_bre"] = padb(inputs["s5_b_re"])
    m["s5_bim"] = padb(inputs["s5_b_im"])
    def padc(cx):
        o = np.zeros((L, 64, 2, 32, 8, 16), np.float32)
        ct = np.asarray(cx).transpose(0, 4, 1, 2, 3)
        for g in range(32):
            o[:, :, :, g, g % 8, :] = ct[:, :, :, g, :]
        return o.reshape(L, 64, 8192)
    cre, cim = padc(inputs["s5_c_re"]), padc(inputs["s5_c_im"])
    m["s5_c1"] = f(np.concatenate([cre, cim], axis=1))
    m["s5_c2"] = f(np.concatenate([cim, cre], axis=1))
    lreP = np.asarray(inputs["s5_lam_re"]).reshape(L, 64, 64).transpose(0, 2, 1)
    limP = np.asarray(inputs["s5_lam_im"]).reshape(L, 64, 64).transpose(0, 2, 1)
    m["s5_lreP"] = f(np.concatenate([lreP, lreP], axis=1))
    m["s5_limP"] = f(np.concatenate([limP, limP], axis=1))
    m["s5_stpP"] = f(np.broadcast_to(np.asarray(inputs["s5_log_step"]).reshape(L, 1, 64), (L, 128, 64)))
    m["s5_dT"] = f(np.asarray(inputs["s5_d"]).reshape(L, 4, 128).transpose(0, 2, 1))
    m["s5_glu_w"] = f(inputs["s5_glu_w"])
    m["s5_glu_bT"] = f(np.asarray(inputs["s5_glu_b"]).reshape(L, 4, 128).transpose(0, 2, 1))
    m["branch_gT"] = f(np.asarray(inputs["branch_g"]).reshape(L, 3, 4, 128).transpose(0, 1, 3, 2))
    tpos = np.arange(LLAT)
    rowp, colp = (tpos // 64).astype(np.float32), (tpos % 64).astype(np.float32)
    inv = (1.0 / (10000.0 ** (np.arange(16, dtype=np.float32) / 16))).astype(np.float32)
    ang_r, ang_c = rowp[:, None] * inv[None, :], colp[:, None] * inv[None, :]
    cosT = np.concatenate([np.cos(ang_r), np.cos(ang_r), np.cos(ang_c), np.cos(ang_c)], axis=1)
    sinT = np.concatenate([-np.sin(ang_r), np.sin(ang_r), -np.sin(ang_c), np.sin(ang_c)], axis=1)
    m["ropeC"] = f(cosT.reshape(16, 128, 64).transpose(1, 0, 2))
    m["ropeS"] = f(sinT.reshape(16, 128, 64).transpose(1, 0, 2))
    gq, gk = np.asarray(inputs["att_q_g"]), np.asarray(inputs["att_k_g"])
    gqk = np.concatenate([np.tile(gq, (1, 8)), np.tile(gk, (1, 2))], axis=1)
    m["att_gqk"] = f(np.broadcast_to(gqk[:, None, :], (L, 128, 640)))
    m["att_sinkB"] = f(np.broadcast_to(np.asarray(inputs["att_sink"])[:, None, :], (L, 128, 8)))
    kk_, qq_ = np.arange(128)[:, None], np.arange(128)[None, :]
    m["maskA"] = (kk_ >= qq_).astype(np.float32).astype(ml_dtypes.bfloat16)
    m["maskC"] = (kk_ <= qq_).astype(np.float32).astype(ml_dtypes.bfloat16)
    m["hy_w1"] = f(inputs["hy_w1"]); m["hy_w2"] = f(inputs["hy_w2"]); m["hy_w3"] = f(inputs["hy_w3"])
    m["hy_prm"] = f(np.stack([inputs["hy_b1"], inputs["hy_f1"], inputs["hy_b2"], inputs["hy_f2"]], axis=-1))
    m["hy_skipT"] = f(np.asarray(inputs["hy_skip"]).reshape(L, 2, 4, 128).transpose(0, 3, 1, 2).reshape(L, 128, 8))
    cwt = np.concatenate([np.asarray(inputs["hy_conv_w"]), np.asarray(inputs["hy_conv_b"])[:, None, :]], axis=1)
    m["hy_convT"] = f(cwt.reshape(L, 4, 12, 128).transpose(0, 3, 2, 1))
    deltas = np.abs(np.linspace(math.log(1e-2) / 1.5, math.log(1e-2) / 0.3, 512, dtype=np.float32))
    for nm, n in (("L", LLAT), ("C", LCTX)):
        lag = np.arange(n)
        t = np.linspace(0.0, 1.0, n, dtype=np.float32)
        ang = (2.0 * math.pi * np.arange(n, dtype=np.float32) / n).astype(np.float32)
        bands = np.linspace(1e-4, 15, 16, dtype=np.float32)[None, :]
        z = np.concatenate([t[:, None], np.cos(bands * ang[:, None]), -np.sin(bands * ang[:, None])], axis=-1).astype(np.float32)
        dec = np.exp(-t[:, None] * deltas[None, :]).astype(np.float32)
        mm = np.arange(2 * n)
        lag_of = np.where(mm >= n, mm - n, n - mm)
        lag_of[0] = 0
        lag_r = lag_of[2 * n - 1 - mm]
        m["hy_z" + nm] = f(z[lag_of].T); m["hy_z" + nm + "r"] = f(z[lag_r].T)
        m["hy_d" + nm] = f(dec[lag_of].T); m["hy_d" + nm + "r"] = f(dec[lag_r].T)
    mp, mn = np.asarray(inputs["rw_mu_prev"]), np.asarray(inputs["rw_mu_next"])
    muH = np.stack([mp[:, :1536], mn[:, :1536]], axis=-1).reshape(L, 24, 64, 2).transpose(0, 2, 1, 3)
    m["rw_muH"] = f(muH)
    m["rw_muL"] = f(np.stack([mp[:, 1536:], mn[:, 1536:]], axis=-1))
    hh_ = lambda a: np.asarray(a).reshape(L, 8, 64).transpose(0, 2, 1)
    w0, a0 = np.asarray(inputs["rw_w0"]), np.asarray(inputs["rw_a0"])
    m["rw_hp"] = f(np.stack([hh_(inputs["rw_k_k"]), hh_(inputs["rw_k_a"]), hh_(inputs["rw_r_k"]), hh_(w0[:, 0]), hh_(w0[:, 1]), hh_(a0[:, 0]), hh_(a0[:, 1])], axis=-1))
    w2p = np.zeros((L, 2, 128, 512), np.float32); a2p = np.zeros((L, 2, 128, 512), np.float32)
    for di in range(2):
        w2p[:, di, di * 32:(di + 1) * 32] = np.asarray(inputs["rw_w2"])[:, di]
        a2p[:, di, (2 + di) * 32:(3 + di) * 32] = np.asarray(inputs["rw_a2"])[:, di]
    m["rw_w2pad"] = w2p; m["rw_a2pad"] = a2p
    m["rw_lnT"] = f(np.stack([np.asarray(inputs["rw_ln_g"]).reshape(L, 4, 128).transpose(0, 2, 1), np.asarray(inputs["rw_ln_b"]).reshape(L, 4, 128).transpose(0, 2, 1)], axis=-1))
    m["rw_mask"] = np.ascontiguousarray(np.broadcast_to((np.arange(LS) % 128 != 0).astype(np.float32), (64, LS))).astype(ml_dtypes.bfloat16)
    ss_, tt_ = np.arange(128)[:, None], np.arange(128)[None, :]
    m["rw_mk2"] = np.concatenate([(ss_ < tt_), (ss_ <= tt_)], axis=1).astype(np.float32).astype(ml_dtypes.bfloat16)
    bd_ = (ss_ // 64) == (tt_ // 64)
    m["rw_mkL"] = ((ss_ > tt_) & bd_).astype(np.float32).astype(ml_dtypes.bfloat16)
    m["rw_mkBD"] = bd_.astype(np.float32).astype(ml_dtypes.bfloat16)
    m["rw_mkOFF"] = ((ss_ < 64) & (tt_ >= 64)).astype(np.float32).astype(ml_dtypes.bfloat16)
    m["rw_J"] = np.ascontiguousarray(np.eye(128, dtype=np.float32)[::-1]).astype(ml_dtypes.bfloat16)
    m["rw_bd"] = f(np.kron(np.eye(2, dtype=np.float32), np.full((64, 64), 1.0 / 64, np.float32)))
    tt = np.arange(LS)
    m["tabA"] = np.ascontiguousarray(np.broadcast_to((tt // 64).astype(np.float32), (128, LS))).astype(ml_dtypes.bfloat16)
    m["tabB"] = np.ascontiguousarray(np.broadcast_to((tt % 64).astype(np.float32), (128, LS))).astype(ml_dtypes.bfloat16)
    return m


_CACHE = {}


def kernel(**inputs):
    if "nc" not in _CACHE:
        p = Prog()
        _CACHE["nc"] = p.build()
        _CACHE["p"] = p
    nc, p = _CACHE["nc"], _CACHE["p"]
    maps = []
    for c in range(N_CORES):
        m = host_inputs(inputs, c)
        maps.append({k: m[k] for k in p.inputs})
    res = run_bass_kernel_spmd(nc, maps, core_ids=list(range(N_CORES)))
    out = np.concatenate([np.asarray(r["out"]) for r in res.results], axis=0)
    return out.astype(np.float32)
```

```python
import math
from contextlib import ExitStack

import numpy as np
import ml_dtypes

import concourse.bass as bass
import concourse.mybir as mybir
from concourse.bass_utils import run_bass_kernel_spmd

F32 = mybir.dt.float32
BF16 = mybir.dt.bfloat16
I32 = mybir.dt.int32
AF = mybir.ActivationFunctionType
ALU = mybir.AluOpType
AX = mybir.AxisListType

D = 2048
DEPTH = 4
NB = 2
LCTX = 256
LLAT = 2048
LS = LCTX + LLAT
D_IN = 6528
W = 512
EPS = 1e-6
N_CORES = 8
R_S5, R_RKV, R_LORA, R_HY, R_GATE, R_FM = 0, 512, 2048, 2176, 3712, 5760
C_S5, C_Q, C_K, C_V, C_RKV, C_LORA, C_HY, C_GATE = 0, 512, 1024, 1152, 1280, 2816, 2944, 4480


class _Sem:
    __slots__ = ("h", "owner")

    def __init__(self, h, owner):
        self.h, self.owner = h, owner


class Buf:
    __slots__ = ("w", "r", "multi", "name")

    def __init__(self, name="", multi=False):
        self.w, self.r, self.multi, self.name = {}, {}, multi, name


class _Eng:
    def __init__(self, tk, name, h):
        self.tk, self.name, self.h = tk, name, h
        self.sem = tk._new_sem(name)
        self.cnt = 0
        self.seen = {}

    def tick(self):
        if self.cnt >= 30000:
            self.sem = self.tk._new_sem(self.name)
            self.cnt = 0
        self.cnt += 1
        return self.sem, self.cnt


class _Slot:
    def __init__(self, tk):
        self.sem = tk._new_sem(None)
        self.val = 0


class T:
    def __init__(self, t, name, multi=False):
        self.t = t
        self.b = Buf(name, multi)


class TK:
    def __init__(self, nc, es):
        self.nc, self.es = nc, es
        self.nsem = 0
        self.E = {}
        for n, h in (("pe", nc.tensor), ("act", nc.scalar), ("dve", nc.vector), ("pool", nc.gpsimd), ("sp", nc.sync)):
            self.E[n] = _Eng(self, n, h)
        self.slots = {"sp": [_Slot(self) for _ in range(14)], "pool": [_Slot(self) for _ in range(8)],
                      "act": [_Slot(self) for _ in range(6)]}
        self.slot_i = {q: 0 for q in self.slots}
        self.n_ins = 0

    def _new_sem(self, owner):
        self.nsem += 1
        h = self.es.enter_context(self.nc.semaphore(f"sm{self.nsem}"))
        return _Sem(h, owner)

    def sb(self, es, name, shape, dt, multi=False):
        self.uid = getattr(self, "uid", 0) + 1
        name = f"{name}_{self.uid}"
        return T(es.enter_context(self.nc.sbuf_tensor(name, shape, dt)), name, multi)

    def ps(self, es, name, shape, dt):
        return T(es.enter_context(self.nc.psum_tensor(name, shape, dt)), name)

    def dram(self, name, shape, dt, kind="Internal", multi=True):
        return T(self.nc.dram_tensor(name, shape, dt, kind=kind), name, multi)

    @staticmethod
    def _bufs(xs):
        return [x.b if isinstance(x, T) else x for x in xs]

    def _collect(self, r, w):
        deps = {}

        def add(d, raw):
            for k, (s, v) in d.items():
                if k not in deps or deps[k][1] < v:
                    deps[k] = (s, v, raw or (k in deps and deps[k][2]))
                elif raw:
                    deps[k] = (deps[k][0], deps[k][1], True)

        for b in r:
            add(b.w, True)
        for b in w:
            if b.multi and not b.r:
                continue
            add(b.w, False)
            add(b.r, False)
        return deps

    def _update(self, r, w, s, v):
        k = id(s)
        for b in w:
            if b.multi and not b.r:
                if k not in b.w or b.w[k][1] < v:
                    b.w[k] = (s, v)
            else:
                b.w = {k: (s, v)}
                b.r = {}
        for b in r:
            if k not in b.r or b.r[k][1] < v:
                b.r[k] = (s, v)

    def _wait(self, e, deps):
        for k, (s, v, raw) in deps.items():
            if s.owner == e.name and (not raw or e.name == "pe"):
                continue
            if e.seen.get(k, 0) >= v:
                continue
            e.h.wait_ge(s.h, v)
            e.seen[k] = v

    def op(self, eng, fn, r=(), w=()):
        e = self.E[eng]
        r, w = self._bufs(r), self._bufs(w)
        self._wait(e, self._collect(r, w))
        ins = fn(e.h)
        s, v = e.tick()
        ins.then_inc(s.h, 1)
        self._update(r, w, s, v)
        self.n_ins += 1

    def dma(self, q, out, in_, r=(), w=(), **kw):
        e = self.E[q]
        r, w = self._bufs(r), self._bufs(w)
        sl = self.slots[q][self.slot_i[q] % len(self.slots[q])]
        self.slot_i[q] += 1
        deps = self._collect(r, w)
        if sl.val:
            deps[id(sl.sem)] = (sl.sem, sl.val, True)
        self._wait(e, deps)
        ins = e.h.dma_start(out=out, in_=in_, **kw)
        sl.val += 16
        ins.then_inc(sl.sem.h, 16)
        self._update(r, w, sl.sem, sl.val)
        self.n_ins += 1

    def barrier(self, final=False):
        names = ["sp"] if final else list(self.E)
        for n in names:
            e = self.E[n]
            deps = {}
            for e2 in self.E.values():
                if e2.cnt:
                    deps[id(e2.sem)] = (e2.sem, e2.cnt, True)
            for q in self.slots.values():
                for sl in q:
                    if sl.val:
                        deps[id(sl.sem)] = (sl.sem, sl.val, True)
            self._wait(e, deps)


class Prog:
    def __init__(self, layers=DEPTH, dump=(), stop_after=None, mixers=("s5", "att", "rw", "hy")):
        self.layers = layers
        self.dump = set(dump)
        self.stop_after = stop_after
        self.mixers = mixers
        self.nc = bass.Bass("TRN2", target_bir_lowering=False)
        self.es = ExitStack()
        self.inputs = {}

    def inp(self, name, shape, dt=F32):
        t = self.nc.dram_tensor(name, list(shape), dt, kind="ExternalInput")
        self.inputs[name] = (tuple(shape), dt)
        return T(t, name)

    def build(self):
        nc = self.nc
        with self.es:
            tk = self.tk = TK(nc, self.es)
            self.declare_io()
            self.consts()
            for l in range(self.layers):
                self.layer(l)
            tk.barrier(final=True)
        return nc

    def declare_io(self):
        tk = self.tk
        L = DEPTH
        kd = lambda n: "ExternalOutput" if n in self.dump else "Internal"
        self.x_in = self.inp("x", [NB, LLAT, D])
        self.ctx_in = self.inp("ctx", [NB, LCTX, D])
        self.cT = self.inp("cT", [128, 16, 3])
        self.normgT = self.inp("normgT", [L, 128, 16])
        self.w_ada = self.inp("w_ada", [L, D, 3 * D])
        self.b_adaT = self.inp("b_adaT", [L, 128, 48])
        self.w_in = self.inp("w_in", [L, D, D_IN])
        self.w_out = self.inp("w_out", [L, D, D])
        self.s5_lreB = self.inp("s5_lreB", [L, 128, 4096])
        self.s5_limB = self.inp("s5_limB", [L, 128, 4096])
        self.s5_stpB = self.inp("s5_stpB", [L, 128, 4096])
        self.s5_bre = self.inp("s5_bre", [L, 128, 4096])
        self.s5_bim = self.inp("s5_bim", [L, 128, 4096])
        self.s5_c1 = self.inp("s5_c1", [L, 128, 8192])
        self.s5_c2 = self.inp("s5_c2", [L, 128, 8192])
        self.s5_lreP = self.inp("s5_lreP", [L, 128, 64])
        self.s5_limP = self.inp("s5_limP", [L, 128, 64])
        self.s5_stpP = self.inp("s5_stpP", [L, 128, 64])
        self.s5_dT = self.inp("s5_dT", [L, 128, 4])
        self.s5_glu_w = self.inp("s5_glu_w", [L, 512, 512])
        self.s5_glu_bT = self.inp("s5_glu_bT", [L, 128, 4])
        self.branch_gT = self.inp("branch_gT", [L, 3, 128, 4])
        self.ropeC = self.inp("ropeC", [128, 16, 64])
        self.ropeS = self.inp("ropeS", [128, 16, 64])
        self.att_gqk = self.inp("att_gqk", [L, 128, 640])
        self.att_sinkB = self.inp("att_sinkB", [L, 128, 8])
        self.maskA = self.inp("maskA", [128, 128], BF16)
        self.maskC = self.inp("maskC", [128, 128], BF16)
        self.hy_w1 = self.inp("hy_w1", [L, 33, 64])
        self.hy_w2 = self.inp("hy_w2", [L, 64, 64])
        self.hy_w3 = self.inp("hy_w3", [L, 64, 2048])
        self.hy_prm = self.inp("hy_prm", [L, 64, 4])
        self.hy_skipT = self.inp("hy_skipT", [L, 128, 8])
        self.hy_convT = self.inp("hy_convT", [L, 128, 12, 4])
        self.hy_zL = self.inp("hy_zL", [33, 4096])
        self.hy_zLr = self.inp("hy_zLr", [33, 4096])
        self.hy_zC = self.inp("hy_zC", [33, 512])
        self.hy_zCr = self.inp("hy_zCr", [33, 512])
        self.hy_dL = self.inp("hy_dL", [512, 4096])
        self.hy_dLr = self.inp("hy_dLr", [512, 4096])
        self.hy_dC = self.inp("hy_dC", [512, 512])
        self.hy_dCr = self.inp("hy_dCr", [512, 512])
        self.hy_GL = tk.dram("hy_GL", [2, 512, 4096], BF16, kind=kd("hy_GL"))
        self.hy_GC = tk.dram("hy_GC", [2, 512, 512], BF16, kind=kd("hy_GC"))
        self.rw_muH = self.inp("rw_muH", [L, 64, 24, 2])
        self.rw_muL = self.inp("rw_muL", [L, 128, 2])
        self.rw_hp = self.inp("rw_hp", [L, 64, 8, 7])
        self.rw_w2pad = self.inp("rw_w2pad", [L, 2, 128, 512])
        self.rw_a2pad = self.inp("rw_a2pad", [L, 2, 128, 512])
        self.rw_lnT = self.inp("rw_lnT", [L, 128, 4, 2])
        self.rw_mask = self.inp("rw_mask", [64, LS], BF16)
        self.rw_mk2 = self.inp("rw_mk2", [128, 256], BF16)
        self.rw_mkL = self.inp("rw_mkL", [128, 128], BF16)
        self.rw_J = self.inp("rw_J", [128, 128], BF16)
        self.rw_mkBD = self.inp("rw_mkBD", [128, 128], BF16)
        self.rw_mkOFF = self.inp("rw_mkOFF", [128, 128], BF16)
        self.rw_bd = self.inp("rw_bd", [128, 128])
        self.rw_s = tk.dram("rw_s", [NB, 2, 8, 5, 64, LS], BF16, kind=kd("rw_s"))
        self.rw_g = tk.dram("rw_g", [NB, 2, 8, 64, LS // 128], F32, kind=kd("rw_g"))
        self.rw_bv = tk.dram("rw_bv", [NB, 512, LS], F32, kind=kd("rw_bv"))
        self.tabA = self.inp("tabA", [128, LS], BF16)
        self.tabB = self.inp("tabB", [128, LS], BF16)
        self.out = tk.dram("out", [NB, LLAT, D], F32, kind="ExternalOutput")
        self.xc = tk.dram("xc_s", [NB, LCTX, D], F32, kind=kd("xc_s"))
        self.pfm = tk.dram("pfm", [R_FM, NB, LS], BF16, kind=kd("pfm"))
        self.qkv = tk.dram("qkv", [NB, LS, 768], BF16, kind=kd("qkv"))
        self.ym = tk.dram("ym", [D, NB, LS], BF16, kind=kd("ym"))
        self.dumps = {}

    def dump_t(self, name, shape, dt):
        t = self.tk.dram("dbg_" + name, shape, dt, kind="ExternalOutput")
        self.dumps[name] = t
        return t

    def consts(self):
        tk, es = self.tk, self.es
        self.ident_f = tk.sb(es, "ident_f", [128, 128], F32)
        self.ident_b = tk.sb(es, "ident_b", [128, 128], BF16)
        self.ones_f = tk.sb(es, "ones_f", [128, 128], F32)
        self.eps_t = tk.sb(es, "eps_t", [128, 1], F32)
        tk.op("pool", lambda e: e.memset(self.ident_f.t[:], 1.0), w=[self.ident_f])
        tk.op("pool", lambda e: e.affine_select(out=self.ident_f.t[:], in_=self.ident_f.t[:], pattern=[[-1, 128]],
                                                compare_op=ALU.is_equal, fill=0.0, base=0, channel_multiplier=1),
              r=[self.ident_f], w=[self.ident_f])
        tk.op("dve", lambda e: e.tensor_copy(out=self.ident_b.t[:], in_=self.ident_f.t[:]), r=[self.ident_f], w=[self.ident_b])
        tk.op("dve", lambda e: e.memset(self.ones_f.t[:], 1.0), w=[self.ones_f])
        tk.op("dve", lambda e: e.memset(self.eps_t.t[:], EPS), w=[self.eps_t])
        self.sc = tk.sb(es, "sc", [128, 16, 3], F32)
        self.modT = tk.sb(es, "modT", [128, 48, 3], F32)
        self.Am = tk.sb(es, "Am", [128, 16, 3], F32)
        self.psb = [tk.ps(es, f"psb{i}", [128, 512], F32) for i in range(8)]
        ctile = tk.sb(es, "ctile", [128, 16, 3], F32)
        tk.dma("sp", ctile.t[:], self.cT.t.ap(), w=[ctile])
        tk.op("act", lambda e: e.activation(out=self.sc.t[:], in_=ctile.t[:], func=AF.Silu), r=[ctile], w=[self.sc])

    def layer(self, l):
        self.stage_ada(l)
        self.tk.barrier()
        self.stage_inproj(l)
        self.tk.barrier()
        if self.stop_after == "inproj":
            return
        self.stage_mixers(l)
        self.tk.barrier()
        if self.stop_after == "mixers":
            return
        self.stage_outproj(l)
        self.tk.barrier()

    def stage_ada(self, l):
        tk = self.tk
        with ExitStack() as es:
            wa = [tk.sb(es, f"wa{i}", [128, 16, 512], F32) for i in range(2)]
            bada = tk.sb(es, "bada", [128, 48], F32)
            ng = tk.sb(es, "ng", [128, 16], F32)
            tmp = tk.sb(es, "adatmp", [128, 16], F32)
            pm = self.psb[0]
            tk.dma("sp", bada.t[:], self.b_adaT.t.ap()[l], w=[bada])
            tk.dma("sp", ng.t[:], self.normgT.t.ap()[l], w=[ng])
            for pc in range(12):
                wt = wa[pc % 2]
                tk.dma("sp" if pc % 2 == 0 else "act", wt.t[:],
                       self.w_ada.t.ap()[l, :, pc * 512:(pc + 1) * 512].rearrange("(k p) n -> p k n", p=128), w=[wt])
                for n4 in range(4):
                    n = pc * 4 + n4
                    for k in range(16):
                        tk.op("pe", lambda e, n=n, n4=n4, wt=wt, k=k: e.matmul(pm.t[:, n * 3:(n + 1) * 3], lhsT=wt.t[:, k, n4 * 128:(n4 + 1) * 128],
                                                                              rhs=self.sc.t[:, k, :], start=(k == 0), stop=(k == 15)),
                              r=[wt, self.sc], w=[pm])
            pv = pm.t[:, 0:144].rearrange("p (n j) -> p n j", j=3)
            for j in range(3):
                tk.op("dve", lambda e, j=j: e.tensor_tensor(out=self.modT.t[:, :, j], in0=pv[:, :, j], in1=bada.t[:], op=ALU.add),
                      r=[pm, bada], w=[self.modT])
            for j in range(3):
                tk.op("dve", lambda e, j=j: e.tensor_scalar(out=tmp.t[:], in0=self.modT.t[:, 16:32, j], scalar1=1.0, scalar2=None, op0=ALU.add),
                      r=[self.modT], w=[tmp])
                tk.op("dve", lambda e, j=j: e.tensor_tensor(out=self.Am.t[:, :, j], in0=tmp.t[:], in1=ng.t[:], op=ALU.mult),
                      r=[tmp, ng], w=[self.Am])
            if "modT" in self.dump and l == 0:
                d1 = self.dump_t("modT", [128, 144], F32)
                tk.dma("sp", d1.t.ap(), self.modT.t[:].rearrange("p n j -> p (n j)"), r=[self.modT], w=[d1])
                d3 = self.dump_t("Am", [128, 48], F32)
                tk.dma("sp", d3.t.ap(), self.Am.t[:].rearrange("p n j -> p (n j)"), r=[self.Am], w=[d3])

    def x_src(self, l, b, seg):
        if seg == 0:
            return (self.ctx_in if l == 0 else self.xc)
        return (self.x_in if l == 0 else self.out)

    def stage_inproj(self, l):
        tk = self.tk
        TG = 768
        with ExitStack() as es:
            hxT = [tk.sb(es, f"hxT{i}", [128, 16, TG], BF16) for i in range(2)]
            wb = [tk.sb(es, f"wb{i}", [128, 16, 512], BF16) for i in range(3)]
            xt = [tk.sb(es, f"xt{i}", [128, D], F32) for i in range(2)]
            sq = tk.sb(es, "sqj", [128, D], BF16)
            xs = [tk.sb(es, f"xs{i}", [128, D], BF16) for i in range(2)]
            st = [tk.sb(es, f"st{i}", [128, 3], F32) for i in range(2)]
            fo = [tk.sb(es, f"fo{i}", [128, TG], BF16) for i in range(4)]
            to = [tk.sb(es, f"to{i}", [128, 768], BF16) for i in range(2)]
            pieces = [(0, 512, "fm", R_S5)]
            c = C_RKV
            while c < D_IN:
                n = min(512, D_IN - c)
                pieces.append((c, n, "fm", R_RKV + (c - C_RKV)))
                c += n
            pieces += [(512, 512, "tm", 0), (1024, 256, "tm", 512)]
            wi = 0
            blk = 0
            foi = 0
            pi = 0
            for tg in range(6):
                b, p0 = tg // 3, (tg % 3) * TG
                h = hxT[tg % 2]
                for kb in range(6):
                    pos = p0 + kb * 128
                    seg = 0 if pos < LCTX else 1
                    j = 2 if seg == 0 else b
                    src = self.x_src(l, b, seg)
                    row0 = pos if seg == 0 else pos - LCTX
                    x_t, xs_t, st_t = xt[blk % 2], xs[blk % 2], st[blk % 2]
                    tk.dma("sp", x_t.t[:], src.t.ap()[b, row0:row0 + 128, :], r=[src], w=[x_t])
                    tk.op("act", lambda e, x_t=x_t: e.activation(out=sq.t[:], in_=x_t.t[:], func=AF.Square), r=[x_t], w=[sq])
                    tk.op("dve", lambda e, st_t=st_t: e.reduce_sum(out=st_t.t[:, 0:1], in_=sq.t[:], axis=AX.X), r=[sq], w=[st_t])
                    tk.op("act", lambda e, st_t=st_t: e.activation(out=st_t.t[:, 1:2], in_=st_t.t[:, 0:1], func=AF.Sqrt, bias=self.eps_t.t[:], scale=1.0 / D),
                          r=[st_t, self.eps_t], w=[st_t])
                    tk.op("dve", lambda e, st_t=st_t: e.reciprocal(out=st_t.t[:, 2:3], in_=st_t.t[:, 1:2]), r=[st_t], w=[st_t])
                    tk.op("act", lambda e, x_t=x_t, xs_t=xs_t, st_t=st_t: e.activation(out=xs_t.t[:], in_=x_t.t[:], func=AF.Copy, scale=st_t.t[:, 2:3]),
                          r=[x_t, st_t], w=[xs_t])
                    for half in range(2):
                        pb = self.psb[pi % 4]
                        pi += 1
                        pv = pb.t[:].bitcast(BF16)
                        for kk in range(8):
                            k = half * 8 + kk
                            tk.op("pe", lambda e, pv=pv, kk=kk, k=k, xs_t=xs_t: e.transpose(out=pv[:, kk * 128:(kk + 1) * 128], in_=xs_t.t[:, k * 128:(k + 1) * 128],
                                                                                             identity=self.ident_b.t[:]), r=[xs_t, self.ident_b], w=[pb])
                        for kk in range(8):
                            k = half * 8 + kk
                            eng = "dve" if kk % 2 == 0 else "pool"
                            if eng == "pool":
                                eng = "dve"
                            tk.op(eng, lambda e, pv=pv, kk=kk, k=k, j=j, kb=kb, h=h: e.tensor_scalar(
                                out=h.t[:, k, kb * 128:(kb + 1) * 128], in0=pv[:, kk * 128:(kk + 1) * 128],
                                scalar1=self.Am.t[:, k, j:j + 1], scalar2=self.modT.t[:, k, j:j + 1], op0=ALU.mult, op1=ALU.add),
                                  r=[pb, self.Am, self.modT], w=[h])
                    blk += 1
                for (c0, n, kind, r0) in pieces:
                    wt = wb[wi % 3]
                    wi += 1
                    tk.dma("pool", wt.t[:, :, 0:n], self.w_in.t.ap()[l, :, c0:c0 + n].rearrange("(k p) n -> p k n", p=128), w=[wt])
                    if kind == "fm":
                        for cc in range(n // 128):
                            f = fo[foi % 4]
                            foi += 1
                            for (t0, tn) in ((0, 512), (512, 256)):
                                pb = self.psb[4 + pi % 4]
                                pi += 1
                                for k in range(16):
                                    tk.op("pe", lambda e, pb=pb, wt=wt, cc=cc, k=k, t0=t0, tn=tn, h=h: e.matmul(
                                        pb.t[:, 0:tn], lhsT=wt.t[:, k, cc * 128:(cc + 1) * 128], rhs=h.t[:, k, t0:t0 + tn],
                                        start=(k == 0), stop=(k == 15)), r=[wt, h], w=[pb])
                                ev = "act" if (pi % 2 == 0) else "dve"
                                if ev == "act":
                                    tk.op("act", lambda e, pb=pb, f=f, t0=t0, tn=tn: e.activation(out=f.t[:, t0:t0 + tn], in_=pb.t[:, 0:tn], func=AF.Copy),
                                          r=[pb], w=[f])
                                else:
                                    tk.op("dve", lambda e, pb=pb, f=f, t0=t0, tn=tn: e.tensor_copy(out=f.t[:, t0:t0 + tn], in_=pb.t[:, 0:tn]),
                                          r=[pb], w=[f])
                            tk.dma("sp", self.pfm.t.ap()[r0 + cc * 128:r0 + (cc + 1) * 128, b, p0:p0 + TG], f.t[:], r=[f], w=[self.pfm])
                    else:
                        for kb in range(6):
                            pb = self.psb[4 + pi % 4]
                            pi += 1
                            o = to[(kb + (r0 // 512)) % 2]
                            for k in range(16):
                                tk.op("pe", lambda e, pb=pb, wt=wt, k=k, kb=kb, n=n, h=h: e.matmul(
                                    pb.t[:, 0:n], lhsT=h.t[:, k, kb * 128:(kb + 1) * 128], rhs=wt.t[:, k, 0:n],
                                    start=(k == 0), stop=(k == 15)), r=[wt, h], w=[pb])
                            tk.op("act", lambda e, pb=pb, o=o, n=n: e.activation(out=o.t[:, 0:n], in_=pb.t[:, 0:n], func=AF.Copy), r=[pb], w=[o])
                            tk.dma("sp", self.qkv.t.ap()[b, p0 + kb * 128:p0 + (kb + 1) * 128, r0:r0 + n], o.t[:, 0:n], r=[o], w=[self.qkv])

    def stage_mixers(self, l):
        last = (l == DEPTH - 1)
        if "s5" in self.mixers:
            self.mixer_s5(l)
            self.tk.barrier()
        if "att" in self.mixers:
            self.mixer_att(l)
            self.tk.barrier()
        if "hy" in self.mixers:
            self.mixer_hy(l)
            self.tk.barrier()
        if "rw" in self.mixers:
            self.mixer_rw(l)
            self.tk.barrier()

    def finish_branch(self, l, es, br, b, ytiles, norm, gcol=None):
        tk = self.tk
        nm = f"fb{br}"
        if not hasattr(self, "_fb") or self._fb[0] is not es:
            rstd = tk.sb(es, nm + "rstd", [128, LS], F32) if norm else None
            sqb = [tk.sb(es, nm + f"sq{i}", [128, LS], BF16) for i in range(2)] if norm else None
            gt_ = [tk.sb(es, nm + f"g{i}", [128, LS], BF16) for i in range(2)]
            sg = [tk.sb(es, nm + f"sg{i}", [128, LS], F32) for i in range(2)]
            ob = [tk.sb(es, nm + f"ob{i}", [128, LS], BF16) for i in range(2)]
            onesb = tk.sb(es, nm + "ones", [128, 128], BF16)
            tk.op("pool", lambda e: e.memset(onesb.t[:], 1.0), w=[onesb])
            self._fb = (es, rstd, sqb, gt_, sg, ob, onesb)
        _, rstd, sqb, gt_, sg, ob, onesb = self._fb
        pieces = [(0, 512), (512, 512), (1024, 512), (1536, 512), (2048, 256)]
        if norm:
            for i in range(4):
                tk.op("act", lambda e, i=i: e.activation(out=sqb[i % 2].t[:], in_=ytiles[i].t[:], func=AF.Square), r=[ytiles[i]], w=[sqb[i % 2]])
                for pi_, (t0, n) in enumerate(pieces):
                    pb = self.psb[pi_]
                    tk.op("pe", lambda e, i=i, pb=pb, t0=t0, n=n: e.matmul(pb.t[:, 0:n], lhsT=onesb.t[:], rhs=sqb[i % 2].t[:, t0:t0 + n],
                                                                             start=(i == 0), stop=(i == 3)), r=[onesb, sqb[i % 2]], w=[pb])
            for pi_, (t0, n) in enumerate(pieces):
                pb = self.psb[pi_]
                tk.op("act", lambda e, pb=pb, t0=t0, n=n: e.activation(out=rstd.t[:, t0:t0 + n], in_=pb.t[:, 0:n], func=AF.Sqrt, bias=self.eps_t.t[:], scale=1.0 / W),
                      r=[pb, self.eps_t], w=[rstd])
            tk.op("dve", lambda e: e.reciprocal(out=rstd.t[:], in_=rstd.t[:]), r=[rstd], w=[rstd])
        for i in range(4):
            g, s_, o = gt_[i % 2], sg[i % 2], ob[i % 2]
            row = R_GATE + br * 512 + i * 128
            tk.dma("sp", g.t[:], self.pfm.t.ap()[row:row + 128, b, :], r=[self.pfm], w=[g])
            tk.op("act", lambda e, g=g, s_=s_: e.activation(out=s_.t[:], in_=g.t[:], func=AF.Silu), r=[g], w=[s_])
            if norm:
                tk.op("pool", lambda e, s_=s_: e.tensor_tensor(out=s_.t[:], in0=s_.t[:], in1=rstd.t[:], op=ALU.mult), r=[s_, rstd], w=[s_])
                tk.op("dve", lambda e, i=i, s_=s_, o=o: e.scalar_tensor_tensor(out=o.t[:], in0=ytiles[i].t[:], scalar=gcol.t[:, i:i + 1], in1=s_.t[:],
                                                                                op0=ALU.mult, op1=ALU.mult), r=[ytiles[i], gcol, s_], w=[o])
            else:
                tk.op("dve", lambda e, i=i, s_=s_, o=o: e.tensor_tensor(out=o.t[:], in0=ytiles[i].t[:], in1=s_.t[:], op=ALU.mult), r=[ytiles[i], s_], w=[o])
            tk.dma("sp", self.ym.t.ap()[br * 512 + i * 128:br * 512 + (i + 1) * 128, b, :], o.t[:], r=[o], w=[self.ym])

    def mixer_s5(self, l):
        tk = self.tk
        MAGIC = 12582912.0
        TWO_PI = 2.0 * math.pi
        PIECES = [(0, 256), (256, 512), (768, 512), (1280, 512), (1792, 512)]
        with ExitStack() as es0, ExitStack() as es:
            ygel = [[tk.sb(es0, f"s5yg{gt}{b}", [128, LS], BF16) for b in range(NB)] for gt in range(4)]
            sb = lambda n, sh, dt=F32: tk.sb(es, "s5" + n, sh, dt)
            BBt = sb("BBt", [128, 64, 2, 128], BF16)
            Ct = sb("Ct", [128, 64, 2, 128], BF16)
            pl = sb("pl", [128, 6, 64])
            negpi = sb("negpi", [128, 1])
            tk.op("pool", lambda e: e.memset(negpi.t[:], 0.0), w=[negpi])

            def sincos(ph, n, sin_out, cos_out, tmp):
                V = lambda t: t.t[:, 0:n]
                tk.op("dve", lambda e: e.tensor_scalar(out=V(tmp), in0=V(ph), scalar1=MAGIC, scalar2=MAGIC, op0=ALU.add, op1=ALU.subtract), r=[ph], w=[tmp])
                tk.op("dve", lambda e: e.tensor_tensor(out=V(tmp), in0=V(ph), in1=V(tmp), op=ALU.subtract), r=[ph, tmp], w=[tmp])
                tk.op("act", lambda e: e.activation(out=V(sin_out), in_=V(tmp), func=AF.Sin, bias=negpi.t[:], scale=TWO_PI), r=[tmp, negpi], w=[sin_out])
                tk.op("dve", lambda e: e.tensor_scalar(out=V(cos_out), in0=V(ph), scalar1=0.25, scalar2=None, op0=ALU.add), r=[ph], w=[cos_out])
                tk.op("dve", lambda e: e.tensor_scalar(out=V(tmp), in0=V(cos_out), scalar1=MAGIC, scalar2=MAGIC, op0=ALU.add, op1=ALU.subtract), r=[cos_out], w=[tmp])
                tk.op("dve", lambda e: e.tensor_tensor(out=V(tmp), in0=V(cos_out), in1=V(tmp), op=ALU.subtract), r=[cos_out, tmp], w=[tmp])
                tk.op("act", lambda e: e.activation(out=V(cos_out), in_=V(tmp), func=AF.Sin, bias=negpi.t[:], scale=TWO_PI), r=[tmp, negpi], w=[cos_out])

            for hh in range(2):
                self.s5_prep(l, hh, BBt, Ct, sincos)
                tk.barrier()
            self.s5_main(l, es, sb, BBt, Ct, pl, sincos, ygel)
            tk.barrier()
            es.close()
            self.s5_glu(l, ygel)
            tk.barrier()

    def s5_prep(self, l, hh, BBt, Ct, sincos):
        tk = self.tk
        TWO_PI = 2.0 * math.pi
        with ExitStack() as es:
            sb = lambda n, sh, dt=F32: tk.sb(es, "s5p" + n, sh, dt)
            NP = 2048
            c0 = hh * NP
            lre, lim, stp = sb("lre", [128, NP]), sb("lim", [128, NP]), sb("stp", [128, NP])
            t1, t2, t3, t4 = sb("t1", [128, NP]), sb("t2", [128, NP]), sb("t3", [128, NP]), sb("t4", [128, NP])
            tk.dma("sp", lre.t[:], self.s5_lreB.t.ap()[l, :, c0:c0 + NP], w=[lre])
            tk.dma("sp", lim.t[:], self.s5_limB.t.ap()[l, :, c0:c0 + NP], w=[lim])
            tk.dma("sp", stp.t[:], self.s5_stpB.t.ap()[l, :, c0:c0 + NP], w=[stp])

            def disc(lre, lim, stp, t1, t2, t3, t4, n):
                V = lambda t: t.t[:, 0:n]
                tk.op("act", lambda e: e.activation(out=V(stp), in_=V(stp), func=AF.Exp), r=[stp], w=[stp])
                tk.op("dve", lambda e: e.tensor_tensor(out=V(t1), in0=V(lre), in1=V(stp), op=ALU.mult), r=[lre, stp], w=[t1])
                tk.op("act", lambda e: e.activation(out=V(t1), in_=V(t1), func=AF.Exp), r=[t1], w=[t1])
                tk.op("dve", lambda e: e.scalar_tensor_tensor(out=V(t2), in0=V(lim), scalar=1.0 / TWO_PI, in1=V(stp), op0=ALU.mult, op1=ALU.mult),
                      r=[lim, stp], w=[t2])

            disc(lre, lim, stp, t1, t2, t3, t4, NP)
            sn, cs = sb("sn", [128, NP]), sb("cs", [128, NP])
            sincos(t2, NP, sn, cs, t3)
            tk.op("dve", lambda e: e.tensor_tensor(out=cs.t[:], in0=cs.t[:], in1=t1.t[:], op=ALU.mult), r=[cs, t1], w=[cs])
            tk.op("dve", lambda e: e.tensor_tensor(out=sn.t[:], in0=sn.t[:], in1=t1.t[:], op=ALU.mult), r=[sn, t1], w=[sn])
            tk.op("dve", lambda e: e.tensor_scalar(out=cs.t[:], in0=cs.t[:], scalar1=-1.0, scalar2=None, op0=ALU.add), r=[cs], w=[cs])
            tk.op("dve", lambda e: e.tensor_tensor(out=t3.t[:], in0=lre.t[:], in1=lre.t[:], op=ALU.mult), r=[lre], w=[t3])
            tk.op("dve", lambda e: e.tensor_tensor(out=t4.t[:], in0=lim.t[:], in1=lim.t[:], op=ALU.mult), r=[lim], w=[t4])
            tk.op("dve", lambda e: e.tensor_tensor(out=t3.t[:], in0=t3.t[:], in1=t4.t[:], op=ALU.add), r=[t3, t4], w=[t3])
            tk.op("dve", lambda e: e.reciprocal(out=t3.t[:], in_=t3.t[:]), r=[t3], w=[t3])
            tk.op("dve", lambda e: e.tensor_tensor(out=t1.t[:], in0=cs.t[:], in1=lre.t[:], op=ALU.mult), r=[cs, lre], w=[t1])
            tk.op("dve", lambda e: e.tensor_tensor(out=t4.t[:], in0=sn.t[:], in1=lim.t[:], op=ALU.mult), r=[sn, lim], w=[t4])
            tk.op("dve", lambda e: e.tensor_tensor(out=t1.t[:], in0=t1.t[:], in1=t4.t[:], op=ALU.add), r=[t1, t4], w=[t1])
            tk.op("dve", lambda e: e.tensor_tensor(out=t1.t[:], in0=t1.t[:], in1=t3.t[:], op=ALU.mult), r=[t1, t3], w=[t1])
            tk.op("dve", lambda e: e.tensor_tensor(out=t2.t[:], in0=sn.t[:], in1=lre.t[:], op=ALU.mult), r=[sn, lre], w=[t2])
            tk.op("dve", lambda e: e.tensor_tensor(out=t4.t[:], in0=cs.t[:], in1=lim.t[:], op=ALU.mult), r=[cs, lim], w=[t4])
            tk.op("dve", lambda e: e.tensor_tensor(out=t2.t[:], in0=t2.t[:], in1=t4.t[:], op=ALU.subtract), r=[t2, t4], w=[t2])
            tk.op("dve", lambda e: e.tensor_tensor(out=t2.t[:], in0=t2.t[:], in1=t3.t[:], op=ALU.mult), r=[t2, t3], w=[t2])
            co_re, co_im = t1, t2
            bre, bim = lre, lim
            tk.dma("sp", bre.t[:], self.s5_bre.t.ap()[l, :, c0:c0 + NP], r=[], w=[bre])
            tk.dma("sp", bim.t[:], self.s5_bim.t.ap()[l, :, c0:c0 + NP], r=[], w=[bim])
            G0 = hh * 32
            v3 = lambda t: t.t[:].rearrange("q (a p) -> q a p", p=64)
            tk.op("dve", lambda e: e.tensor_tensor(out=t3.t[:], in0=co_re.t[:], in1=bre.t[:], op=ALU.mult), r=[co_re, bre], w=[t3])
            tk.op("pool", lambda e: e.tensor_tensor(out=sn.t[:], in0=co_im.t[:], in1=bim.t[:], op=ALU.mult), r=[co_im, bim], w=[sn])
            tk.op("dve", lambda e: e.tensor_tensor(out=t3.t[:], in0=t3.t[:], in1=sn.t[:], op=ALU.subtract), r=[t3, sn], w=[t3])
            tk.op("dve", lambda e: e.tensor_tensor(out=t4.t[:], in0=co_re.t[:], in1=bim.t[:], op=ALU.mult), r=[co_re, bim], w=[t4])
            tk.op("pool", lambda e: e.tensor_tensor(out=cs.t[:], in0=co_im.t[:], in1=bre.t[:], op=ALU.mult), r=[co_im, bre], w=[cs])
            tk.op("dve", lambda e: e.tensor_tensor(out=t4.t[:], in0=t4.t[:], in1=cs.t[:], op=ALU.add), r=[t4, cs], w=[t4])
            tk.op("act", lambda e: e.activation(out=BBt.t[:, G0:G0 + 32, 0, 0:64], in_=v3(t3), func=AF.Copy), r=[t3], w=[BBt])
            tk.op("act", lambda e: e.activation(out=BBt.t[:, G0:G0 + 32, 0, 64:128], in_=v3(t4), func=AF.Copy), r=[t4], w=[BBt])
            tk.op("act", lambda e: e.activation(out=BBt.t[:, G0:G0 + 32, 1, 0:64], in_=v3(t4), func=AF.Copy), r=[t4], w=[BBt])
            tk.op("act", lambda e: e.activation(out=BBt.t[:, G0:G0 + 32, 1, 64:128], in_=v3(t3), func=AF.Copy, scale=-1.0), r=[t3], w=[BBt])
            for q2 in range(2):
                src = (lre, lim)[q2]
                tk.dma("sp", src.t[:], self.s5_c1.t.ap()[l, :, hh * 4096 + q2 * NP:hh * 4096 + (q2 + 1) * NP], w=[src])
                sv = src.t[:].rearrange("q (a m) -> q a m", m=128)
                d0 = hh * 32 + q2 * 16
                tk.op("act", lambda e, sv=sv, d0=d0: e.activation(out=Ct.t[0:64, d0:d0 + 16, 0, :], in_=sv[0:64], func=AF.Copy), r=[src], w=[Ct])
                tk.op("act", lambda e, sv=sv, d0=d0: e.activation(out=Ct.t[64:128, d0:d0 + 16, 0, :], in_=sv[64:128], func=AF.Copy, scale=-1.0), r=[src], w=[Ct])
            for q2 in range(2):
                src = (t3, t4)[q2]
                tk.dma("sp", src.t[:], self.s5_c2.t.ap()[l, :, hh * 4096 + q2 * NP:hh * 4096 + (q2 + 1) * NP], w=[src])
                sv = src.t[:].rearrange("q (a m) -> q a m", m=128)
                d0 = hh * 32 + q2 * 16
                tk.op("act", lambda e, sv=sv, d0=d0: e.activation(out=Ct.t[:, d0:d0 + 16, 1, :], in_=sv, func=AF.Copy, scale=-1.0), r=[src], w=[Ct])

    def s5_main(self, l, es, sb, BBt, Ct, pl, sincos, ygel):
        tk = self.tk
        MAGIC = 12582912.0
        TWO_PI = 2.0 * math.pi
        PIECES = [(0, 256), (256, 512), (768, 512), (1280, 512), (1792, 512)]
        if True:
            tk.dma("sp", pl.t[:, 0, :], self.s5_lreP.t.ap()[l], w=[pl])
            tk.dma("sp", pl.t[:, 1, :], self.s5_limP.t.ap()[l], w=[pl])
            tk.dma("sp", pl.t[:, 2, :], self.s5_stpP.t.ap()[l], w=[pl])
            tk.op("act", lambda e: e.activation(out=pl.t[:, 2, :], in_=pl.t[:, 2, :], func=AF.Exp), r=[pl], w=[pl])
            tk.op("dve", lambda e: e.tensor_tensor(out=pl.t[:, 3, :], in0=pl.t[:, 0, :], in1=pl.t[:, 2, :], op=ALU.mult), r=[pl], w=[pl])
            tk.op("act", lambda e: e.activation(out=pl.t[:, 3, :], in_=pl.t[:, 3, :], func=AF.Exp), r=[pl], w=[pl])
            tk.op("dve", lambda e: e.scalar_tensor_tensor(out=pl.t[:, 4, :], in0=pl.t[:, 1, :], scalar=1.0 / TWO_PI, in1=pl.t[:, 2, :], op0=ALU.mult, op1=ALU.mult),
                  r=[pl], w=[pl])
            tk.op("dve", lambda e: e.tensor_scalar(out=pl.t[:, 5, :], in0=pl.t[:, 4, :], scalar1=64.0, scalar2=None, op0=ALU.mult), r=[pl], w=[pl])
            tk.op("dve", lambda e: e.tensor_scalar(out=pl.t[:, 0, :], in0=pl.t[:, 5, :], scalar1=MAGIC, scalar2=MAGIC, op0=ALU.add, op1=ALU.subtract), r=[pl], w=[pl])
            tk.op("dve", lambda e: e.tensor_tensor(out=pl.t[:, 5, :], in0=pl.t[:, 5, :], in1=pl.t[:, 0, :], op=ALU.subtract), r=[pl], w=[pl])
            tA, tB = sb("tA", [128, LS], BF16), sb("tB", [128, LS], BF16)
            tk.dma("sp", tA.t[:], self.tabA.t.ap(), w=[tA])
            tk.dma("sp", tB.t[:], self.tabB.t.ap(), w=[tB])
            dsk = sb("dsk", [128, 4])
            tk.dma("sp", dsk.t[:], self.s5_dT.t.ap()[l], w=[dsk])
            ph = sb("ph", [128, LS])
            NC_, NS_ = sb("NC", [128, LS]), sb("NS", [128, LS])
            Rt = sb("Rt", [128, LS])
            Wf = sb("Wf", [128, LS])
            Gf = sb("Gf", [128, LS])
            W2p = [sb(f"W2p{i}", [128, 512]) for i in range(2)]
            P1, P2 = sb("P1", [128, LS], BF16), sb("P2", [128, LS], BF16)
            ut = [sb(f"u{i}", [128, LS], BF16) for i in range(2)]
            ya2 = [sb(f"ya{b}", [128, LS]) for b in range(NB)]
            yacc = [ya2 for gt in range(4)]
            pc = 0
            for gt in range(4):
                for b in range(NB):
                    tk.dma("sp", ut[b].t[:], self.pfm.t.ap()[R_S5 + gt * 128:R_S5 + (gt + 1) * 128, b, :], r=[self.pfm], w=[ut[b]])
                    tk.op("dve", lambda e, gt=gt, b=b: e.tensor_scalar(out=yacc[gt][b].t[:], in0=ut[b].t[:], scalar1=dsk.t[:, gt:gt + 1], scalar2=None, op0=ALU.mult),
                          r=[ut[b], dsk], w=[yacc[gt][b]])
                for di in range(2):
                    for g8 in range(8):
                        dg = di * 32 + gt * 8 + g8
                        tk.op("act", lambda e, dg=dg: e.activation(out=ph.t[:], in_=tA.t[:], func=AF.Copy, scale=pl.t[:, 5, dg:dg + 1]), r=[tA, pl], w=[ph])
                        tk.op("dve", lambda e, dg=dg: e.scalar_tensor_tensor(out=ph.t[:], in0=tB.t[:], scalar=pl.t[:, 4, dg:dg + 1], in1=ph.t[:], op0=ALU.mult, op1=ALU.add), r=[tB, pl, ph], w=[ph])
                        sincos(ph, LS, NS_, NC_, Gf)
                        tk.op("act", lambda e, dg=dg: e.activation(out=Rt.t[:], in_=tA.t[:], func=AF.Identity, bias=pl.t[:, 3, dg:dg + 1], scale=0.0), r=[tA, pl], w=[Rt])
                        for b in range(NB):
                            for (t0, n) in PIECES:
                                pbu, pbu2 = self.psb[(pc * 2) % 6], self.psb[(pc * 2 + 1) % 6]
                                w2 = W2p[pc % 2]
                                pc += 1
                                if di == 0:
                                    uv = ut[b].t[:, t0:t0 + n]
                                else:
                                    lo = (LCTX - t0 - n) if t0 < LCTX else (LCTX + (LS - t0 - n))
                                    uv = ut[b].t[:, lo:lo + n][:, ::-1]
                                tk.op("pe", lambda e, pbu=pbu, dg=dg, uv=uv, n=n: e.matmul(pbu.t[:, 0:n], lhsT=BBt.t[:, dg, 0, :], rhs=uv, start=True, stop=True), r=[BBt, ut[b]], w=[pbu])
                                tk.op("pe", lambda e, pbu2=pbu2, dg=dg, uv=uv, n=n: e.matmul(pbu2.t[:, 0:n], lhsT=BBt.t[:, dg, 1, :], rhs=uv, start=True, stop=True), r=[BBt, ut[b]], w=[pbu2])
                                tk.op("dve", lambda e, pbu=pbu, t0=t0, n=n: e.tensor_tensor(out=Wf.t[:, t0:t0 + n], in0=pbu.t[:, 0:n], in1=NC_.t[:, t0:t0 + n], op=ALU.mult), r=[pbu, NC_], w=[Wf])
                                tk.op("dve", lambda e, pbu2=pbu2, w2=w2, t0=t0, n=n: e.tensor_tensor(out=w2.t[:, 0:n], in0=pbu2.t[:, 0:n], in1=NS_.t[:, t0:t0 + n], op=ALU.mult), r=[pbu2, NS_], w=[w2])
                                tk.op("dve", lambda e, w2=w2, t0=t0, n=n: e.tensor_tensor(out=Wf.t[:, t0:t0 + n], in0=Wf.t[:, t0:t0 + n], in1=w2.t[:, 0:n], op=ALU.add), r=[Wf, w2], w=[Wf])
                            tk.op("dve", lambda e: e.tensor_tensor_scan(out=Gf.t[:], data0=Rt.t[:], data1=Wf.t[:], initial=0.0, op0=ALU.mult, op1=ALU.add), r=[Rt, Wf], w=[Gf])
                            tk.op("pool", lambda e: e.tensor_tensor(out=P1.t[:], in0=Gf.t[:], in1=NC_.t[:], op=ALU.mult), r=[Gf, NC_], w=[P1])
                            tk.op("dve", lambda e: e.tensor_tensor(out=P2.t[:], in0=Gf.t[:], in1=NS_.t[:], op=ALU.mult), r=[Gf, NS_], w=[P2])
                            for (t0, n) in PIECES:
                                yp = self.psb[6 + pc % 2]
                                pc += 1
                                tk.op("pe", lambda e, yp=yp, dg=dg, t0=t0, n=n: e.matmul(yp.t[:, 0:n], lhsT=Ct.t[:, dg, 0, :], rhs=P1.t[:, t0:t0 + n], start=True, stop=False), r=[Ct, P1], w=[yp])
                                tk.op("pe", lambda e, yp=yp, dg=dg, t0=t0, n=n: e.matmul(yp.t[:, 0:n], lhsT=Ct.t[:, dg, 1, :], rhs=P2.t[:, t0:t0 + n], start=False, stop=True), r=[Ct, P2], w=[yp])
                                if di == 0:
                                    yv = yacc[gt][b].t[:, t0:t0 + n]
                                else:
                                    lo = (LCTX - t0 - n) if t0 < LCTX else (LCTX + (LS - t0 - n))
                                    yv = yacc[gt][b].t[:, lo:lo + n][:, ::-1]
                                tk.op("dve", lambda e, yp=yp, yv=yv, n=n, b=b, gt=gt: e.tensor_tensor(out=yv, in0=yv, in1=yp.t[:, 0:n], op=ALU.add), r=[yacc[gt][b], yp], w=[yacc[gt][b]])
                for b in range(NB):
                    if "s5_y" in self.dump and l == 0:
                        if gt == 0 and b == 0:
                            self._d_s5y = self.dump_t("s5_y", [4, NB, 128, LS], F32)
                        tk.dma("sp", self._d_s5y.t.ap()[gt, b], yacc[gt][b].t[:], r=[yacc[gt][b]], w=[self._d_s5y])
                    tk.op("act", lambda e, gt=gt, b=b: e.activation(out=ygel[gt][b].t[:], in_=yacc[gt][b].t[:], func=AF.Gelu), r=[yacc[gt][b]], w=[ygel[gt][b]])

    def s5_glu(self, l, ygel):
        tk = self.tk
        with ExitStack() as es:
            sb = lambda n, sh, dt=F32: tk.sb(es, "s5g" + n, sh, dt)
            gw = sb("gw", [128, 4, 512], BF16)
            gbT = sb("gbT", [128, 4])
            bgT = sb("bgT", [128, 4])
            tk.dma("pool", gw.t[:], self.s5_glu_w.t.ap()[l].rearrange("(k p) n -> p k n", p=128), w=[gw])
            tk.dma("sp", gbT.t[:], self.s5_glu_bT.t.ap()[l], w=[gbT])
            tk.dma("sp", bgT.t[:], self.branch_gT.t.ap()[l, 0], w=[bgT])
            res = [sb(f"res{i}", [128, LS]) for i in range(4)]
            sgm = [sb(f"sgm{i}", [128, 512]) for i in range(2)]
            for b in range(NB):
                yg = [ygel[i][b] for i in range(4)]
                k2 = 0
                for no in range(4):
                    for (t0, n) in [(0, 512), (512, 512), (1024, 512), (1536, 512), (2048, 256)]:
                        pb = self.psb[5 + k2 % 3]
                        sg_ = sgm[k2 % 2]
                        k2 += 1
                        for ki in range(4):
                            tk.op("pe", lambda e, pb=pb, ki=ki, no=no, t0=t0, n=n: e.matmul(pb.t[:, 0:n], lhsT=gw.t[:, ki, no * 128:(no + 1) * 128], rhs=yg[ki].t[:, t0:t0 + n],
                                                                                             start=(ki == 0), stop=(ki == 3)), r=[gw] + yg, w=[pb])
                        tk.op("act", lambda e, pb=pb, sg_=sg_, no=no, n=n: e.activation(out=sg_.t[:, 0:n], in_=pb.t[:, 0:n], func=AF.Sigmoid, bias=gbT.t[:, no:no + 1], scale=1.0),
                              r=[pb, gbT], w=[sg_])
                        tk.op("dve", lambda e, sg_=sg_, no=no, t0=t0, n=n: e.tensor_tensor(out=res[no].t[:, t0:t0 + n], in0=yg[no].t[:, t0:t0 + n], in1=sg_.t[:, 0:n], op=ALU.mult),
                              r=[yg[no], sg_], w=[res[no]])
                self.finish_branch(l, es, 0, b, res, True, bgT)

    def mixer_att(self, l):
        tk = self.tk
        last = (l == DEPTH - 1)
        NBLK = LS // 128
        with ExitStack() as es:
            sb = lambda n, sh, dt=F32: tk.sb(es, "at" + n, sh, dt)
            ropeC, ropeS = sb("ropeC", [128, 16, 64]), sb("ropeS", [128, 16, 64])
            gqk = sb("gqk", [128, 640])
            esink = sb("esink", [128, 8])
            mA, mC = sb("mA", [128, 128], BF16), sb("mC", [128, 128], BF16)
            bgT = sb("bgT", [128, 4])
            tk.dma("sp", ropeC.t[:], self.ropeC.t.ap(), w=[ropeC])
            tk.dma("sp", ropeS.t[:], self.ropeS.t.ap(), w=[ropeS])
            tk.dma("sp", gqk.t[:], self.att_gqk.t.ap()[l], w=[gqk])
            tk.dma("sp", esink.t[:], self.att_sinkB.t.ap()[l], w=[esink])
            tk.dma("sp", mA.t[:], self.maskA.t.ap(), w=[mA])
            tk.dma("sp", mC.t[:], self.maskC.t.ap(), w=[mC])
            tk.dma("sp", bgT.t[:], self.branch_gT.t.ap()[l, 1], w=[bgT])
            tk.op("act", lambda e: e.activation(out=esink.t[:], in_=esink.t[:], func=AF.Exp), r=[esink], w=[esink])
            QT = sb("QT", [64, 8, LS], BF16)
            KT = sb("KT", [64, 2, LS], BF16)
            Va = sb("Va", [128, NBLK, 2, 128], BF16)
            yatt = [sb(f"yatt{i}", [128, LS]) for i in range(4)]
            xin = [sb(f"xin{i}", [128, 768], BF16) for i in range(2)]
            sq = sb("sq", [128, 640])
            ss = [sb(f"ss{i}", [128, 10]) for i in range(2)]
            xn = [sb(f"xn{i}", [128, 640]) for i in range(2)]
            r1, r2 = sb("r1", [128, 640]), sb("r2", [128, 640])
            xb = [sb(f"xb{i}", [128, 768], BF16) for i in range(2)]
            E = [sb(f"E{i}", [128, 5, 512], BF16) for i in range(2)]
            den = [sb(f"den{i}", [128, 4]) for i in range(2)]
            oat = [sb(f"oat{i}", [128, 512], BF16) for i in range(2)]
            tk.op("pool", lambda e: e.memset(Va.t[:], 1.0), w=[Va])
            pi = 0
            for b in range(NB):
                import os
                P1 = int(os.environ.get("ATT_P1", "99"))
                for blk in range(NBLK):
                    xi, s_, x_n, x_b = xin[blk % 2], ss[blk % 2], xn[blk % 2], xb[blk % 2]
                    tk.dma("sp", xi.t[:], self.qkv.t.ap()[b, blk * 128:(blk + 1) * 128, :], r=[self.qkv], w=[xi])
                    tk.op("act", lambda e, xi=xi: e.activation(out=sq.t[:], in_=xi.t[:, 0:640], func=AF.Square), r=[xi], w=[sq])
                    tk.op("dve", lambda e, s_=s_: e.reduce_sum(out=s_.t[:], in_=sq.t[:].rearrange("p (h d) -> p h d", d=64), axis=AX.X), r=[sq], w=[s_])
                    tk.op("act", lambda e, s_=s_: e.activation(out=s_.t[:], in_=s_.t[:], func=AF.Sqrt, bias=self.eps_t.t[:], scale=1.0 / 64), r=[s_, self.eps_t], w=[s_])
                    tk.op("dve", lambda e, s_=s_: e.reciprocal(out=s_.t[:], in_=s_.t[:]), r=[s_], w=[s_])
                    for h in range(10):
                        tk.op("act" if h % 2 else "pool", (lambda e, h=h, xi=xi, s_=s_, x_n=x_n: e.activation(out=x_n.t[:, h * 64:(h + 1) * 64], in_=xi.t[:, h * 64:(h + 1) * 64], func=AF.Copy, scale=s_.t[:, h:h + 1]))
                              if h % 2 else (lambda e, h=h, xi=xi, s_=s_, x_n=x_n: e.tensor_scalar(out=x_n.t[:, h * 64:(h + 1) * 64], in0=xi.t[:, h * 64:(h + 1) * 64], scalar1=s_.t[:, h:h + 1], scalar2=None, op0=ALU.mult)),
                              r=[xi, s_], w=[x_n])
                    tk.op("dve", lambda e, x_n=x_n: e.tensor_tensor(out=x_n.t[:], in0=x_n.t[:], in1=gqk.t[:], op=ALU.mult), r=[x_n, gqk], w=[x_n])
                    if P1 <= 7:
                        continue
                    if blk >= 2:
                        lb = blk - 2
                        for h in range(10):
                            xv = x_n.t[:, h * 64:(h + 1) * 64]
                            xsw = xv.rearrange("p (a r i) -> p a r i", a=2, r=2)[:, :, ::-1, :]
                            tk.op("dve", lambda e, h=h, xv=xv, lb=lb: e.tensor_tensor(out=r1.t[:, h * 64:(h + 1) * 64], in0=xv, in1=ropeC.t[:, lb, :], op=ALU.mult), r=[x_n, ropeC], w=[r1])
                            tk.op("dve", lambda e, h=h, xsw=xsw, lb=lb: e.tensor_tensor(out=r2.t[:, h * 64:(h + 1) * 64].rearrange("p (a r i) -> p a r i", a=2, r=2), in0=xsw,
                                                                                          in1=ropeS.t[:, lb, :].rearrange("p (a r i) -> p a r i", a=2, r=2), op=ALU.mult), r=[x_n, ropeS], w=[r2])
                        tk.op("dve", lambda e, x_b=x_b: e.tensor_tensor(out=x_b.t[:, 0:640], in0=r1.t[:], in1=r2.t[:], op=ALU.add), r=[r1, r2], w=[x_b])
                    else:
                        tk.op("dve", lambda e, x_b=x_b, x_n=x_n: e.tensor_copy(out=x_b.t[:, 0:640], in_=x_n.t[:]), r=[x_n], w=[x_b])
                    pb = self.psb[pi % 4]
                    pi += 1
                    pb2 = self.psb[pi % 4]
                    pi += 1
                    pv = pb.t[:].bitcast(BF16)
                    pv2 = pb2.t[:].bitcast(BF16)
                    for h in range(8):
                        tk.op("pe", lambda e, pv=pv, h=h, x_b=x_b: e.transpose(out=pv[0:64, h * 128:(h + 1) * 128], in_=x_b.t[:, h * 64:(h + 1) * 64], identity=self.ident_b.t[:]),
                              r=[x_b, self.ident_b], w=[pb])
                    for kh in range(2):
                        tk.op("pe", lambda e, pv2=pv2, kh=kh, x_b=x_b: e.transpose(out=pv2[0:64, kh * 128:(kh + 1) * 128], in_=x_b.t[:, 512 + kh * 64:512 + (kh + 1) * 64], identity=self.ident_b.t[:]),
                              r=[x_b, self.ident_b], w=[pb2])
                    tk.op("dve", lambda e, pv=pv, blk=blk: e.tensor_copy(out=QT.t[0:64, :, blk * 128:(blk + 1) * 128], in_=pv[0:64, 0:1024].rearrange("p (h t) -> p h t", h=8)), r=[pb], w=[QT])
                    tk.op("dve", lambda e, pv2=pv2, blk=blk: e.tensor_copy(out=KT.t[0:64, :, blk * 128:(blk + 1) * 128], in_=pv2[0:64, 0:256].rearrange("p (h t) -> p h t", h=2)), r=[pb2], w=[KT])
                    tk.op("dve", lambda e, xi=xi, blk=blk: e.tensor_copy(out=Va.t[:, blk, :, 0:64], in_=xi.t[:, 640:768].rearrange("p (k d) -> p k d", k=2)), r=[xi], w=[Va])
                import os
                ASTOP = int(os.environ.get("ATT_STOP", "9"))
                if ASTOP <= 1:
                    continue
                qblocks = list(range(2, NBLK)) + ([] if last else [0, 1])
                for qi, qb in enumerate(qblocks):
                    if qb >= 2:
                        keys = [(kb, m) for kb, m in ((qb - 1, mA), (qb, None), (qb + 1, mC)) if 2 <= kb < NBLK] + [(0, None), (1, None)]
                    else:
                        keys = [(0, None), (1, None)]
                    o_t = oat[qi % 2]
                    for kh in range(2):
                        Et = E[(qi * 2 + kh) % 2]
                        for ki, (kb, msk) in enumerate(keys):
                            pb = self.psb[pi % 4]
                            pi += 1
                            tk.op("pe", lambda e, pb=pb, kh=kh, kb=kb, qb=qb: e.matmul(
                                pb.t[:].rearrange("p (j t) -> p j t", j=4), lhsT=KT.t[0:64, kh, kb * 128:(kb + 1) * 128],
                                rhs=QT.t[0:64, 4 * kh:4 * kh + 4, qb * 128:(qb + 1) * 128], start=True, stop=True), r=[KT, QT], w=[pb])
                            tk.op("act", lambda e, pb=pb, Et=Et, ki=ki: e.activation(out=Et.t[:, ki, :], in_=pb.t[:], func=AF.Exp, scale=0.125), r=[pb], w=[Et])
                            if msk is not None:
                                for j in range(4):
                                    tk.op("pool" if j % 2 else "dve", lambda e, Et=Et, ki=ki, j=j, msk=msk: e.tensor_tensor(out=Et.t[:, ki, j * 128:(j + 1) * 128], in0=Et.t[:, ki, j * 128:(j + 1) * 128],
                                                                                                                         in1=msk.t[:], op=ALU.mult), r=[Et, msk], w=[Et])
                        if ASTOP <= 2:
                            continue
                        po = self.psb[4 + (qi * 2 + kh) % 4]
                        for j in range(4):
                            for ki, (kb, msk) in enumerate(keys):
                                tk.op("pe", lambda e, po=po, j=j, ki=ki, kb=kb, kh=kh, Et=Et, nk=len(keys): e.matmul(
                                    po.t[:, j * 128:j * 128 + 65], lhsT=Et.t[:, ki, j * 128:(j + 1) * 128], rhs=Va.t[:, kb, kh, 0:65],
                                    start=(ki == 0), stop=(ki == nk - 1)), r=[Et, Va], w=[po])
                        if ASTOP <= 3:
                            continue
                        dn = den[(qi * 2 + kh) % 2]
                        pov = po.t[:, 0:512].rearrange("p (j c) -> p j c", c=128)
                        tk.op("dve", lambda e, dn=dn, pov=pov, kh=kh: e.tensor_tensor(out=dn.t[:], in0=pov[:, :, 64], in1=esink.t[:, 4 * kh:4 * kh + 4], op=ALU.add), r=[po, esink], w=[dn])
                        tk.op("dve", lambda e, dn=dn: e.reciprocal(out=dn.t[:], in_=dn.t[:]), r=[dn], w=[dn])
                        for j in range(4):
                            tk.op("act", lambda e, j=j, kh=kh, dn=dn, pov=pov, o_t=o_t: e.activation(out=o_t.t[:, (4 * kh + j) * 64:(4 * kh + j + 1) * 64], in_=pov[:, j, 0:64], func=AF.Copy, scale=dn.t[:, j:j + 1]),
                                  r=[po, dn], w=[o_t])
                    if ASTOP <= 4:
                        continue
                    pt = self.psb[pi % 4]
                    pi += 1
                    ptv = pt.t[:].bitcast(BF16)
                    for c4 in range(4):
                        tk.op("pe", lambda e, ptv=ptv, c4=c4, o_t=o_t: e.transpose(out=ptv[:, c4 * 128:(c4 + 1) * 128], in_=o_t.t[:, c4 * 128:(c4 + 1) * 128], identity=self.ident_b.t[:]),
                              r=[o_t, self.ident_b], w=[pt])
                    for c4 in range(4):
                        tk.op("dve", lambda e, ptv=ptv, c4=c4, qb=qb: e.tensor_copy(out=yatt[c4].t[:, qb * 128:(qb + 1) * 128], in_=ptv[:, c4 * 128:(c4 + 1) * 128]), r=[pt], w=[yatt[c4]])
                if last:
                    for c4 in range(4):
                        tk.op("pool", lambda e, c4=c4: e.memset(yatt[c4].t[:, 0:LCTX], 0.0), w=[yatt[c4]])
                if ASTOP <= 5:
                    continue
                self.finish_branch(l, es, 1, b, yatt, True, bgT)

    def mixer_hy(self, l):
        tk = self.tk
        last = (l == DEPTH - 1)
        MAGIC = 12582912.0
        TWO_PI = 2.0 * math.pi
        segs = [("L", LLAT, LCTX)] + ([] if last else [("C", LCTX, 0)])
        with ExitStack() as es:
            sb = lambda n, sh, dt=F32: tk.sb(es, "hf" + n, sh, dt)
            w1, w2, w3 = sb("w1", [33, 64]), sb("w2", [64, 64]), sb("w3", [64, 2048])
            prm = sb("prm", [64, 8])
            skp = sb("skp", [128, 8])
            zero1 = sb("zero1", [128, 1])
            tk.op("pool", lambda e: e.memset(zero1.t[:], 0.0), w=[zero1])
            tk.dma("sp", w1.t[:], self.hy_w1.t.ap()[l], w=[w1])
            tk.dma("sp", w2.t[:], self.hy_w2.t.ap()[l], w=[w2])
            tk.dma("sp", w3.t[:], self.hy_w3.t.ap()[l], w=[w3])
            tk.dma("sp", prm.t[:, 0:4], self.hy_prm.t.ap()[l], w=[prm])
            tk.dma("sp", skp.t[:], self.hy_skipT.t.ap()[l], w=[skp])
            tk.op("dve", lambda e: e.tensor_scalar(out=prm.t[:, 4:5], in0=prm.t[:, 1:2], scalar1=1.0 / TWO_PI, scalar2=None, op0=ALU.mult), r=[prm], w=[prm])
            tk.op("dve", lambda e: e.tensor_scalar(out=prm.t[:, 5:6], in0=prm.t[:, 3:4], scalar1=1.0 / TWO_PI, scalar2=None, op0=ALU.mult), r=[prm], w=[prm])
            zt = sb("zt", [33, 4096])
            H1, H2 = sb("H1", [64, 4096]), sb("H2", [64, 4096])
            ut_, rt_ = [sb(f"ut{i}", [64, 512]) for i in range(2)], [sb(f"rt{i}", [64, 512]) for i in range(2)]
            dt_ = [sb(f"dt{i}", [128, 512]) for i in range(2)]
            Gt = [sb(f"Gt{i}", [128, 4096], BF16) for i in range(2)]
            cnt = 0
            for (sn, Ls, _) in segs:
                Lx = 2 * Ls
                CH = 512 if Ls >= 512 else 256
                nch = Lx // CH
                for k in range(2):
                    ztab = {("L", 0): self.hy_zLr, ("L", 1): self.hy_zL, ("C", 0): self.hy_zCr, ("C", 1): self.hy_zC}[(sn, k)]
                    dtab = {("L", 0): self.hy_dLr, ("L", 1): self.hy_dL, ("C", 0): self.hy_dCr, ("C", 1): self.hy_dC}[(sn, k)]
                    gdst = {("L", 0): self.hy_GL, ("L", 1): self.hy_GL, ("C", 0): self.hy_GC, ("C", 1): self.hy_GC}[(sn, k)]
                    tk.dma("sp", zt.t[:, 0:Lx], ztab.t.ap(), w=[zt])
                    for (src, wt_, K_, bcol, fcol, dst) in ((zt, w1, 33, 0, 4, H1), (H1, w2, 64, 2, 5, H2)):
                        for c in range(nch):
                            pb = self.psb[cnt % 4]
                            u_, r_ = ut_[cnt % 2], rt_[cnt % 2]
                            cnt += 1
                            tk.op("pe", lambda e, pb=pb, src=src, wt_=wt_, K_=K_, c=c, CH=CH: e.matmul(pb.t[0:64, 0:CH], lhsT=wt_.t[0:K_, :], rhs=src.t[0:K_, c * CH:(c + 1) * CH], start=True, stop=True),
                                  r=[src, wt_], w=[pb])
                            tk.op("dve", lambda e, pb=pb, u_=u_, bcol=bcol, fcol=fcol, CH=CH: e.tensor_scalar(out=u_.t[:, 0:CH], in0=pb.t[0:64, 0:CH], scalar1=prm.t[:, bcol:bcol + 1], scalar2=prm.t[:, fcol:fcol + 1],
                                                                                                        op0=ALU.add, op1=ALU.mult), r=[pb, prm], w=[u_])
                            tk.op("pool", lambda e, u_=u_, r_=r_, CH=CH: e.tensor_scalar(out=r_.t[:, 0:CH], in0=u_.t[:, 0:CH], scalar1=MAGIC, scalar2=MAGIC, op0=ALU.add, op1=ALU.subtract), r=[u_], w=[r_])
                            tk.op("pool", lambda e, u_=u_, r_=r_, CH=CH: e.tensor_tensor(out=r_.t[:, 0:CH], in0=u_.t[:, 0:CH], in1=r_.t[:, 0:CH], op=ALU.subtract), r=[u_, r_], w=[r_])
                            tk.op("act", lambda e, r_=r_, dst=dst, c=c, CH=CH: e.activation(out=dst.t[:, c * CH:(c + 1) * CH], in_=r_.t[:, 0:CH], func=AF.Sin, bias=zero1.t[0:64, :], scale=TWO_PI),
                                  r=[r_, zero1], w=[dst])
                    for ct in range(4):
                        G = Gt[cnt % 2]
                        for c in range(nch):
                            fwd = (c * CH < Ls) if k == 0 else (c * CH >= Ls)
                            f = 2 * k + (0 if fwd else 1)
                            pb = self.psb[4 + cnt % 4]
                            d_ = dt_[cnt % 2]
                            cnt += 1
                            tk.dma("sp", d_.t[:, 0:CH], dtab.t.ap()[ct * 128:(ct + 1) * 128, c * CH:(c + 1) * CH], w=[d_])
                            tk.op("pe", lambda e, pb=pb, f=f, ct=ct, c=c, CH=CH: e.matmul(pb.t[:, 0:CH], lhsT=w3.t[:, f * 512 + ct * 128:f * 512 + (ct + 1) * 128], rhs=H2.t[:, c * CH:(c + 1) * CH],
                                                                                          start=True, stop=True), r=[w3, H2], w=[pb])
                            tk.op("dve", lambda e, pb=pb, d_=d_, G=G, c=c, CH=CH: e.tensor_tensor(out=G.t[:, c * CH:(c + 1) * CH], in0=pb.t[:, 0:CH], in1=d_.t[:, 0:CH], op=ALU.mult), r=[pb, d_], w=[G])
                            m0 = Ls - 1 if k == 0 else Ls
                            if c * CH <= m0 < (c + 1) * CH:
                                tk.op("dve", lambda e, pb=pb, d_=d_, G=G, m0=m0, k=k, ct=ct, c=c, CH=CH: e.scalar_tensor_tensor(
                                    out=G.t[:, m0:m0 + 1], in0=pb.t[:, m0 - c * CH:m0 - c * CH + 1], scalar=d_.t[:, m0 - c * CH:m0 - c * CH + 1], in1=skp.t[:, k * 4 + ct:k * 4 + ct + 1],
                                    op0=ALU.mult, op1=ALU.add), r=[pb, d_, skp], w=[G])
                        tk.dma("sp", gdst.t.ap()[k, ct * 128:(ct + 1) * 128, 0:Lx], G.t[:, 0:Lx], r=[G], w=[gdst])
        tk.barrier()
        with ExitStack() as es0:
            yhy = [[tk.sb(es0, f"yhy{ct}{b}", [128, LS], BF16) for b in range(NB)] for ct in range(4)]
            with ExitStack() as es:
                sb = lambda n, sh, dt=F32: tk.sb(es, "hc" + n, sh, dt)
                cw = sb("cw", [128, 12, 4])
                tk.dma("sp", cw.t[:], self.hy_convT.t.ap()[l], w=[cw])
                hin = [sb(f"hin{i}", [128, LS], BF16) for i in range(2)]
                zt = [[sb(f"z{j}{b}", [128, LS], BF16) for b in range(NB)] for j in range(3)]
                tmp = [sb(f"tmp{i}", [128, LS]) for i in range(2)]
                geo = {}
                for (sn, Ls, off) in segs:
                    nb_ = Ls // 128
                    nc_ = NB * nb_
                    geo[sn] = dict(U1=sb("U1" + sn, [128, 128, nc_], BF16), X1=sb("X1" + sn, [128, 128, nc_], BF16), X2=sb("X2" + sn, [128, 128, nc_], BF16),
                                   U2=sb("U2" + sn, [128, 128, nc_], BF16), Y=sb("Y" + sn, [128, 128, nc_], BF16),
                                   R=[sb(f"R{sn}{i}", [128, 2 * Ls - 128], BF16) for i in range(3 if sn == "L" else 2)])
                pi = 0
                ri = 0
                for ct in range(4):
                    for j in range(3):
                        row = R_HY + j * 512 + ct * 128
                        tci = j * 4 + ct
                        for b in range(NB):
                            hi, t_ = hin[(j * 2 + b) % 2], tmp[(j * 2 + b) % 2]
                            tk.dma("sp", hi.t[:], self.pfm.t.ap()[row:row + 128, b, :], r=[self.pfm], w=[hi])
                            tk.op("act", lambda e, hi=hi, t_=t_, tci=tci: e.activation(out=t_.t[:], in_=hi.t[:], func=AF.Identity, bias=cw.t[:, tci, 3:4], scale=cw.t[:, tci, 1:2]), r=[hi, cw], w=[t_])
                            for (sn, Ls, off) in [("L", LLAT, LCTX), ("C", LCTX, 0)]:
                                tk.op("dve", lambda e, hi=hi, t_=t_, tci=tci, Ls=Ls, off=off: e.scalar_tensor_tensor(out=t_.t[:, off + 1:off + Ls], in0=hi.t[:, off:off + Ls - 1], scalar=cw.t[:, tci, 0:1],
                                                                                                                   in1=t_.t[:, off + 1:off + Ls], op0=ALU.mult, op1=ALU.add), r=[hi, cw, t_], w=[t_])
                                tk.op("dve", lambda e, hi=hi, t_=t_, tci=tci, Ls=Ls, off=off: e.scalar_tensor_tensor(out=t_.t[:, off:off + Ls - 1], in0=hi.t[:, off + 1:off + Ls], scalar=cw.t[:, tci, 2:3],
                                                                                                                   in1=t_.t[:, off:off + Ls - 1], op0=ALU.mult, op1=ALU.add), r=[hi, cw, t_], w=[t_])
                            if j == 1:
                                tk.op("dve", lambda e, t_=t_, j=j, b=b: e.tensor_copy(out=zt[j][b].t[:].rearrange("p (n i) -> p n i", i=128),
                                                                                     in_=t_.t[:].rearrange("p (n i) -> p n i", i=128)[:, :, ::-1]), r=[t_], w=[zt[j][b]])
                            else:
                                tk.op("act", lambda e, t_=t_, j=j, b=b: e.activation(out=zt[j][b].t[:], in_=t_.t[:], func=AF.Copy), r=[t_], w=[zt[j][b]])
                    for (sn, Ls, off) in segs:
                        g = geo[sn]
                        nb_ = Ls // 128
                        for j, (dstT, rev) in enumerate(((g["U1"], False), (g["X1"], True), (g["X2"], False))):
                            for b in range(NB):
                                for k4 in range(0, nb_, 8):
                                    pb = self.psb[pi % 4]
                                    pi += 1
                                    pv = pb.t[:].bitcast(BF16)
                                    nn = min(8, nb_ - k4)
                                    for q in range(nn):
                                        blk = k4 + q
                                        src = zt[j][b].t[:, off + blk * 128:off + (blk + 1) * 128]
                                        tk.op("pe", lambda e, pv=pv, q=q, src=src: e.transpose(out=pv[:, q * 128:(q + 1) * 128], in_=src, identity=self.ident_b.t[:]), r=[zt[j][b], self.ident_b], w=[pb])
                                    tk.op("dve", lambda e, pv=pv, dstT=dstT, b=b, k4=k4, nn=nn, nb_=nb_: e.tensor_copy(
                                        out=dstT.t[:, :, b * nb_ + k4:b * nb_ + k4 + nn].rearrange("p c q -> p q c"), in_=pv[:, 0:nn * 128].rearrange("p (q c) -> p q c", q=nn)), r=[pb], w=[dstT])
                    for k in range(2):
                        for c16 in range(0, 128, 16):
                            pbs = {}
                            for (sn, Ls, off) in segs:
                                pbs[sn] = self.psb[4 + pi % 4]
                                pi += 1
                            for cc in range(16):
                                c = c16 + cc
                                for (sn, Ls, off) in segs:
                                    g = geo[sn]
                                    nb_ = Ls // 128
                                    nc_ = NB * nb_
                                    Rt_ = g["R"][ri % len(g["R"])]
                                    ri += 1
                                    gsrc = self.hy_GL if sn == "L" else self.hy_GC
                                    wdt = 2 * Ls - 128
                                    srcap = bass.AP(tensor=gsrc.t, offset=(k * 512 + ct * 128 + c) * gsrc.t.shape[2] + k, ap=[[1, 128], [1, wdt]])
                                    tk.dma("sp" if (ri % 2) else "act", Rt_.t[:], srcap, r=[gsrc], w=[Rt_])
                                    rhsT = g["U1"] if k == 0 else g["U2"]
                                    pb = pbs[sn]
                                    ov = pb.t[:, cc * nc_:(cc + 1) * nc_].rearrange("p (b a) -> p b a", b=NB)
                                    rv = rhsT.t[:, c, :].rearrange("p (b a) -> p b a", b=NB)
                                    ds = [0] + [d for d in range(-(nb_ - 1), nb_) if d != 0]
                                    for di_, d in enumerate(ds):
                                        X = (Ls - 128 - 128 * d) if k == 0 else (Ls - 128 + 128 * d)
                                        a_lo, a_hi = max(0, d), min(nb_, nb_ + d)
                                        tk.op("pe", lambda e, ov=ov, rv=rv, Rt_=Rt_, X=X, a_lo=a_lo, a_hi=a_hi, d=d, di_=di_, nd=len(ds): e.matmul(
                                            ov[:, :, a_lo:a_hi], lhsT=Rt_.t[:, X:X + 128], rhs=rv[:, :, a_lo - d:a_hi - d], start=(di_ == 0), stop=(di_ == nd - 1)),
                                            r=[Rt_, rhsT], w=[pb])
                            for (sn, Ls, off) in segs:
                                g = geo[sn]
                                nc_ = NB * (Ls // 128)
                                mul, dst = (g["X1"], g["U2"]) if k == 0 else (g["X2"], g["Y"])
                                tk.op("dve", lambda e, pb=pbs[sn], mul=mul, dst=dst, c16=c16, nc_=nc_: e.tensor_tensor(
                                    out=dst.t[:, c16:c16 + 16, :], in0=pb.t[:, 0:16 * nc_].rearrange("p (c a) -> p c a", c=16), in1=mul.t[:, c16:c16 + 16, :], op=ALU.mult), r=[pbs[sn], mul], w=[dst])
                    for (sn, Ls, off) in segs:
                        g = geo[sn]
                        nb_ = Ls // 128
                        for b in range(NB):
                            for k4 in range(0, nb_, 8):
                                pb = self.psb[pi % 4]
                                pi += 1
                                pv = pb.t[:].bitcast(BF16)
                                nn = min(8, nb_ - k4)
                                for q in range(nn):
                                    col = b * nb_ + k4 + q
                                    tk.op("pe", lambda e, pv=pv, q=q, g=g, col=col: e.transpose(out=pv[:, q * 128:(q + 1) * 128], in_=g["Y"].t[:, :, col], identity=self.ident_b.t[:]),
                                          r=[g["Y"], self.ident_b], w=[pb])
                                tk.op("dve", lambda e, pv=pv, ct=ct, b=b, off=off, k4=k4, nn=nn: e.tensor_copy(out=yhy[ct][b].t[:, off + k4 * 128:off + (k4 + nn) * 128], in_=pv[:, 0:nn * 128]),
                                      r=[pb], w=[yhy[ct][b]])
                    if last:
                        for b in range(NB):
                            tk.op("pool", lambda e, ct=ct, b=b: e.memset(yhy[ct][b].t[:, 0:LCTX], 0.0), w=[yhy[ct][b]])
            tk.barrier()
            with ExitStack() as es:
                bgT = tk.sb(es, "hybgT", [128, 4], F32)
                tk.dma("sp", bgT.t[:], self.branch_gT.t.ap()[l, 2], w=[bgT])
                for b in range(NB):
                    self.finish_branch(l, es, 3, b, [yhy[ct][b] for ct in range(4)], True, bgT)

    def mixer_rw(self, l):
        tk = self.tk
        last = (l == DEPTH - 1)
        NCH = LS // 128
        PIECES = [(0, 512), (512, 512), (1024, 512), (1536, 512), (2048, 256)]
        SEGS = [(0, LCTX), (LCTX, LLAT)]
        EH = math.exp(-0.5)

        def rev_copy(eng, dst, src, srcT, dstT):
            for (o, n) in SEGS:
                tk.op(eng, lambda e, o=o, n=n: e.tensor_copy(out=dst[:, o:o + n], in_=src[:, o:o + n][:, ::-1]), r=[srcT], w=[dstT])

        with ExitStack() as es:
            sb = lambda n, sh, dt=F32: tk.sb(es, "rp" + n, sh, dt)
            mu = sb("mu", [64, 24, 3])
            mul_ = sb("mul", [128, 3])
            hp = sb("hp", [64, 8, 8])
            w2p = sb("w2p", [128, 2, 512], BF16)
            a2p = sb("a2p", [128, 2, 512], BF16)
            ones64 = sb("ones64", [64, 64], BF16)
            e12 = sb("e12", [64, 1])
            msk = sb("msk", [64, LS], BF16)
            tk.dma("sp", mu.t[:, :, 0:2], self.rw_muH.t.ap()[l], w=[mu])
            tk.dma("sp", mul_.t[:, 0:2], self.rw_muL.t.ap()[l], w=[mul_])
            tk.dma("sp", hp.t[:, :, 0:7], self.rw_hp.t.ap()[l], w=[hp])
            tk.dma("pool", w2p.t[:], self.rw_w2pad.t.ap()[l].rearrange("d k n -> k d n"), w=[w2p])
            tk.dma("pool", a2p.t[:], self.rw_a2pad.t.ap()[l].rearrange("d k n -> k d n"), w=[a2p])
            tk.dma("sp", msk.t[:], self.rw_mask.t.ap(), w=[msk])
            tk.op("pool", lambda e: e.memset(ones64.t[:], 1.0), w=[ones64])
            tk.op("pool", lambda e: e.memset(e12.t[:], 1e-12), w=[e12])
            for (t_, nn) in ((mu, None), (mul_, None)):
                pass
            tk.op("dve", lambda e: e.tensor_tensor(out=mu.t[:, :, 2], in0=mu.t[:, :, 0], in1=mu.t[:, :, 1], op=ALU.add), r=[mu], w=[mu])
            tk.op("dve", lambda e: e.tensor_scalar(out=mu.t[:, :, 2], in0=mu.t[:, :, 2], scalar1=-1.0, scalar2=1.0, op0=ALU.mult, op1=ALU.add), r=[mu], w=[mu])
            tk.op("dve", lambda e: e.tensor_tensor(out=mul_.t[:, 2:3], in0=mul_.t[:, 0:1], in1=mul_.t[:, 1:2], op=ALU.add), r=[mul_], w=[mul_])
            tk.op("dve", lambda e: e.tensor_scalar(out=mul_.t[:, 2:3], in0=mul_.t[:, 2:3], scalar1=-1.0, scalar2=1.0, op0=ALU.mult, op1=ALU.add), r=[mul_], w=[mul_])

            def shift(src, dst, P, cp, cn, c0, rT):
                tk.op("act", lambda e: e.activation(out=dst.t[0:P, :], in_=src.t[0:P, :], func=AF.Copy, scale=c0), r=[src] + rT, w=[dst])
                for (o, n) in SEGS:
                    tk.op("dve", lambda e, o=o, n=n: e.scalar_tensor_tensor(out=dst.t[0:P, o + 1:o + n], in0=src.t[0:P, o:o + n - 1], scalar=cp, in1=dst.t[0:P, o + 1:o + n], op0=ALU.mult, op1=ALU.add),
                          r=[src, dst] + rT, w=[dst])
                    tk.op("dve", lambda e, o=o, n=n: e.scalar_tensor_tensor(out=dst.t[0:P, o:o + n - 1], in0=src.t[0:P, o + 1:o + n], scalar=cn, in1=dst.t[0:P, o:o + n - 1], op0=ALU.mult, op1=ALU.add),
                          r=[src, dst] + rT, w=[dst])

            lin = sb("lin", [128, LS], BF16)
            lsh = sb("lsh", [128, LS])
            lt = [sb(f"lt{i}", [128, LS], BF16) for i in range(2)]
            lr = [sb(f"lr{i}", [128, LS], BF16) for i in range(2)]
            xin = [sb(f"xin{i}", [64, LS], BF16) for i in range(3)]
            base = [sb(f"base{i}", [64, LS]) for i in range(4)]
            strm = [sb(f"strm{i}", [64, LS], BF16) for i in range(4)]
            f1, f2, f3, f4, f5 = sb("f1", [64, LS]), sb("f2", [64, LS]), sb("f3", [64, LS]), sb("f4", [64, LS]), sb("f5", [64, LS])
            ob = [sb(f"ob{i}", [64, LS], BF16) for i in range(5)]
            sqb = sb("sqb", [64, LS], BF16)
            bva = sb("bva", [64, LS])
            gam = sb("gam", [64, NCH])
            pi = 0
            for b in range(NB):
                tk.dma("sp", lin.t[:], self.pfm.t.ap()[R_LORA:R_LORA + 128, b, :], r=[self.pfm], w=[lin])
                shift(lin, lsh, 128, mul_.t[:, 0:1], mul_.t[:, 1:2], mul_.t[:, 2:3], [mul_])
                tk.op("act", lambda e: e.activation(out=lt[0].t[:], in_=lsh.t[:], func=AF.Tanh), r=[lsh], w=[lt[0]])
                tk.op("dve", lambda e: e.tensor_copy(out=lr[0].t[:], in_=lsh.t[:]), r=[lsh], w=[lr[0]])
                rev_copy("pool", lt[1].t, lt[0].t, lt[0], lt[1])
                rev_copy("pool", lr[1].t, lr[0].t, lr[0], lr[1])
                for h in range(8):
                    for q in range(3):
                        row = R_RKV + q * 512 + h * 64
                        tk.dma("sp", xin[q].t[:], self.pfm.t.ap()[row:row + 64, b, :], r=[self.pfm], w=[xin[q]])
                        mi = q * 8 + h
                        shift(xin[q], base[q], 64, mu.t[:, mi, 0:1], mu.t[:, mi, 1:2], mu.t[:, mi, 2:3], [mu])
                    tk.op("dve", lambda e, h=h: e.tensor_scalar(out=base[3].t[:], in0=base[1].t[:], scalar1=hp.t[:, h, 0:1], scalar2=None, op0=ALU.mult), r=[base[1], hp], w=[base[3]])
                    tk.op("act", lambda e: e.activation(out=sqb.t[:], in_=base[3].t[:], func=AF.Square), r=[base[3]], w=[sqb])
                    for (t0, n) in PIECES:
                        pb = self.psb[pi % 8]
                        pi += 1
                        tk.op("pe", lambda e, pb=pb, t0=t0, n=n: e.matmul(pb.t[0:64, 0:n], lhsT=ones64.t[:], rhs=sqb.t[:, t0:t0 + n], start=True, stop=True), r=[ones64, sqb], w=[pb])
                        tk.op("act", lambda e, pb=pb, t0=t0, n=n: e.activation(out=f1.t[:, t0:t0 + n], in_=pb.t[0:64, 0:n], func=AF.Sqrt, bias=e12.t[:], scale=1.0), r=[pb, e12], w=[f1])
                    tk.op("dve", lambda e: e.reciprocal(out=f1.t[:], in_=f1.t[:]), r=[f1], w=[f1])
                    tk.op("dve", lambda e: e.tensor_tensor(out=base[3].t[:], in0=base[3].t[:], in1=f1.t[:], op=ALU.mult), r=[base[3], f1], w=[base[3]])
                    for di in range(2):
                        if di == 0:
                            S = base
                        else:
                            for q in range(4):
                                rev_copy("pool" if q % 2 else "dve", strm[q].t, base[q].t, base[q], strm[q])
                            S = strm
                        rS, kS, vS, kkS = S
                        for (t0, n) in PIECES:
                            pb, pb2 = self.psb[pi % 8], self.psb[(pi + 1) % 8]
                            pi += 2
                            tk.op("pe", lambda e, pb=pb, t0=t0, n=n, di=di, h=h: e.matmul(pb.t[0:64, 0:n], lhsT=w2p.t[:, di, h * 64:(h + 1) * 64], rhs=lt[di].t[:, t0:t0 + n], start=True, stop=True),
                                  r=[w2p, lt[di]], w=[pb])
                            tk.op("pe", lambda e, pb2=pb2, t0=t0, n=n, di=di, h=h: e.matmul(pb2.t[0:64, 0:n], lhsT=a2p.t[:, di, h * 64:(h + 1) * 64], rhs=lr[di].t[:, t0:t0 + n], start=True, stop=True),
                                  r=[a2p, lr[di]], w=[pb2])
                            tk.op("act", lambda e, pb=pb, t0=t0, n=n, di=di, h=h: e.activation(out=f1.t[:, t0:t0 + n], in_=pb.t[0:64, 0:n], func=AF.Sigmoid, bias=hp.t[:, h, 3 + di:4 + di], scale=1.0),
                                  r=[pb, hp], w=[f1])
                            tk.op("act", lambda e, pb2=pb2, t0=t0, n=n, di=di, h=h: e.activation(out=f2.t[:, t0:t0 + n], in_=pb2.t[0:64, 0:n], func=AF.Sigmoid, bias=hp.t[:, h, 5 + di:6 + di], scale=1.0),
                                  r=[pb2, hp], w=[f2])
                        tk.op("act", lambda e: e.activation(out=f1.t[:], in_=f1.t[:], func=AF.Copy, scale=-EH), r=[f1], w=[f1])
                        tk.op("dve", lambda e: e.tensor_tensor_scan(out=f3.t[:], data0=msk.t[:], data1=f1.t[:], initial=0.0, op0=ALU.mult, op1=ALU.add), r=[msk, f1], w=[f3])
                        tk.op("dve", lambda e: e.tensor_tensor(out=f1.t[:], in0=f3.t[:], in1=f1.t[:], op=ALU.subtract), r=[f3, f1], w=[f1])
                        tk.op("act", lambda e: e.activation(out=f1.t[:], in_=f1.t[:], func=AF.Exp), r=[f1], w=[f1])
                        tk.op("act", lambda e: e.activation(out=f4.t[:], in_=f3.t[:], func=AF.Exp, scale=-1.0), r=[f3], w=[f4])
                        tk.op("act", lambda e: e.activation(out=f3.t[:], in_=f3.t[:], func=AF.Exp), r=[f3], w=[f3])
                        tk.op("dve", lambda e: e.scalar_tensor_tensor(out=ob[0].t[:], in0=kkS.t[:], scalar=-1.0, in1=f1.t[:], op0=ALU.mult, op1=ALU.mult), r=[kkS, f1], w=[ob[0]])
                        tk.op("dve", lambda e: e.tensor_tensor(out=ob[1].t[:], in0=rS.t[:], in1=f3.t[:], op=ALU.mult), r=[rS, f3], w=[ob[1]])
                        tk.op("dve", lambda e: e.tensor_tensor(out=f5.t[:], in0=kkS.t[:], in1=f2.t[:], op=ALU.mult), r=[kkS, f2], w=[f5])
                        tk.op("pool", lambda e: e.tensor_tensor(out=ob[2].t[:], in0=f5.t[:], in1=f4.t[:], op=ALU.mult), r=[f5, f4], w=[ob[2]])
                        tk.op("dve", lambda e, h=h: e.tensor_scalar(out=f2.t[:], in0=f2.t[:], scalar1=-1.0, scalar2=hp.t[:, h, 1:2], op0=ALU.add, op1=ALU.mult), r=[f2, hp], w=[f2])
                        tk.op("dve", lambda e: e.scalar_tensor_tensor(out=f5.t[:], in0=f2.t[:], scalar=1.0, in1=kS.t[:], op0=ALU.add, op1=ALU.mult), r=[f2, kS], w=[f5])
                        tk.op("pool", lambda e: e.tensor_tensor(out=ob[3].t[:], in0=f5.t[:], in1=f4.t[:], op=ALU.mult), r=[f5, f4], w=[ob[3]])
                        tk.op("act", lambda e: e.activation(out=ob[4].t[:], in_=vS.t[:], func=AF.Copy), r=[vS], w=[ob[4]])
                        for q in range(5):
                            tk.dma("sp", self.rw_s.t.ap()[b, di, h, q], ob[q].t[:], r=[ob[q]], w=[self.rw_s])
                        tk.op("act", lambda e: e.activation(out=gam.t[:], in_=f3.t[:, 127::128], func=AF.Copy), r=[f3], w=[gam])
                        tk.dma("sp", self.rw_g.t.ap()[b, di, h], gam.t[:], r=[gam], w=[self.rw_g])
                        tk.op("dve", lambda e, h=h: e.scalar_tensor_tensor(out=sqb.t[:], in0=rS.t[:], scalar=hp.t[:, h, 2:3], in1=f5.t[:], op0=ALU.mult, op1=ALU.mult), r=[rS, hp, f5], w=[sqb])
                        for (t0, n) in PIECES:
                            pb = self.psb[pi % 8]
                            pi += 1
                            tk.op("pe", lambda e, pb=pb, t0=t0, n=n: e.matmul(pb.t[0:64, 0:n], lhsT=ones64.t[:], rhs=sqb.t[:, t0:t0 + n], start=True, stop=True), r=[ones64, sqb], w=[pb])
                            tk.op("dve", lambda e, pb=pb, t0=t0, n=n: e.tensor_tensor(out=f4.t[:, t0:t0 + n], in0=pb.t[0:64, 0:n], in1=vS.t[:, t0:t0 + n], op=ALU.mult), r=[pb, vS], w=[f4])
                        if di == 0:
                            tk.op("act", lambda e: e.activation(out=bva.t[:], in_=f4.t[:], func=AF.Copy), r=[f4], w=[bva])
                        else:
                            for (o, n) in SEGS:
                                tk.op("dve", lambda e, o=o, n=n: e.tensor_tensor(out=bva.t[:, o:o + n], in0=bva.t[:, o:o + n], in1=f4.t[:, o:o + n][:, ::-1], op=ALU.add), r=[bva, f4], w=[bva])
                    tk.dma("sp", self.rw_bv.t.ap()[b, h * 64:(h + 1) * 64, :], bva.t[:], r=[bva], w=[self.rw_bv])
        tk.barrier()
        with ExitStack() as es:
            sb = lambda n, sh, dt=F32: tk.sb(es, "rc" + n, sh, dt)
            mk2 = sb("mk2", [128, 256], BF16)
            mkL = sb("mkL", [128, 128], BF16)
            Jm = sb("Jm", [128, 128], BF16)
            mkBD, mkOFF = sb("mkBD", [128, 128], BF16), sb("mkOFF", [128, 128], BF16)
            tk.dma("sp", mkBD.t[:], self.rw_mkBD.t.ap(), w=[mkBD])
            tk.dma("sp", mkOFF.t[:], self.rw_mkOFF.t.ap(), w=[mkOFF])
            Aoff = [sb(f"Aoff{i}", [128, 8, 128], BF16) for i in range(2)]
            ZT = sb("ZT", [128, 512], BF16)
            WT = sb("WT", [128, 512], BF16)
            tk.dma("sp", mk2.t[:], self.rw_mk2.t.ap(), w=[mk2])
            tk.dma("sp", mkL.t[:], self.rw_mkL.t.ap(), w=[mkL])
            tk.dma("sp", Jm.t[:], self.rw_J.t.ap(), w=[Jm])
            FMc = [sb(f"FMc{i}", [64, 8, 5, 128], BF16) for i in range(2)]
            TMt = [sb(f"TMt{i}", [128, 24, 64], BF16) for i in range(2)]
            ABr = [sb(f"ABr{i}", [128, 8, 256], BF16) for i in range(2)]
            AkK = [sb(f"AkK{i}", [128, 8, 256], BF16) for i in range(2)]
            Mt_ = [sb(f"Mt{i}", [128, 8, 128]) for i in range(2)]
            Mm_ = [sb(f"Mm{i}", [128, 8, 128]) for i in range(2)]
            Tf = sb("Tf", [128, 8, 128])
            Tb = [sb(f"Tb{i}", [128, 8, 128], BF16) for i in range(2)]
            gamt = sb("gamt", [64, 8, NCH])
            Sf = sb("Sf", [64, 8, 64])
            St = sb("St", [64, 8, 64])
            Sb_ = sb("Sb", [64, 8, 64], BF16)
            XT = sb("XT", [128, 512], BF16)
            UT = sb("UT", [128, 512], BF16)
            Yt = [sb(f"Yt{i}", [128, NCH, 512], BF16) for i in range(2)]
            idb = self.ident_b

            def pre_steps(b, di, c):
                i2 = c % 2
                F, Tm, AB, AK = FMc[i2], TMt[i2], ABr[i2], AkK[i2]
                steps = []

                def s0():
                    tk.dma("sp", F.t[:], self.rw_s.t.ap()[b, di, :, :, :, c * 128:(c + 1) * 128].rearrange("h q j t -> j h q t"), r=[self.rw_s], w=[F])
                    for half in range(2):
                        pb = self.psb[4 + half]
                        pv = pb.t[:].bitcast(BF16)
                        for k in range(12):
                            idx = half * 12 + k
                            h, q = idx // 3, 2 + idx % 3
                            tk.op("pe", lambda e, pv=pv, k=k, h=h, q=q: e.transpose(out=pv[:, k * 64:(k + 1) * 64], in_=F.t[:, h, q, :], identity=idb.t[0:64, 0:64]), r=[F, idb], w=[pb])
                        tk.op("dve", lambda e, pv=pv, half=half: e.tensor_copy(out=Tm.t[:, half * 12:(half + 1) * 12, :], in_=pv[:, 0:768].rearrange("p (k d) -> p k d", d=64)), r=[pb], w=[Tm])
                steps.append(s0)

                def gram(lq, dst):
                    def g():
                        for half in range(2):
                            for hh in range(4):
                                h = half * 4 + hh
                                pb = self.psb[half * 2 + hh // 2]
                                tk.op("pe", lambda e, pb=pb, hh=hh, h=h: e.matmul(pb.t[:, (hh % 2) * 256:(hh % 2 + 1) * 256], lhsT=F.t[:, h, lq, :], rhs=F.t[:, h, 0:2, :], start=True, stop=True),
                                      r=[F], w=[pb])
                            for k2 in range(2):
                                pb = self.psb[half * 2 + k2]
                                tk.op("dve", lambda e, pb=pb, half=half, k2=k2: e.tensor_tensor(
                                    out=dst.t[:, half * 4 + k2 * 2:half * 4 + k2 * 2 + 2, :], in0=pb.t[:].rearrange("p (a m) -> p a m", a=2),
                                    in1=mk2.t[:].unsqueeze(1).to_broadcast([128, 2, 256]), op=ALU.mult), r=[pb, mk2], w=[dst])
                    return g
                steps.append(gram(2, AB))
                steps.append(gram(3, AK))

                def s3():
                    tk.op("dve", lambda e: e.tensor_tensor(out=Mm_[0].t[:], in0=AB.t[:, :, 0:128], in1=mkBD.t[:].unsqueeze(1).to_broadcast([128, 8, 128]), op=ALU.mult), r=[AB, mkBD], w=[Mm_[0]])
                    tk.op("dve", lambda e: e.tensor_tensor(out=Aoff[i2].t[:], in0=AB.t[:, :, 0:128], in1=mkOFF.t[:].unsqueeze(1).to_broadcast([128, 8, 128]), op=ALU.mult), r=[AB, mkOFF], w=[Aoff[i2]])
                    for h in range(8):
                        pb = self.psb[4 + h // 4]
                        tk.op("pe", lambda e, pb=pb, h=h: e.matmul(pb.t[:, (h % 4) * 128:(h % 4 + 1) * 128], lhsT=Mm_[0].t[:, h, :], rhs=self.ident_f.t[:], start=True, stop=True), r=[Mm_[0], self.ident_f], w=[pb])
                    for k2 in range(2):
                        pb = self.psb[4 + k2]
                        tk.op("act", lambda e, pb=pb, k2=k2: e.activation(out=Mt_[0].t[:, k2 * 4:(k2 + 1) * 4, :], in_=pb.t[:].rearrange("p (a m) -> p a m", a=4), func=AF.Copy), r=[pb], w=[Mt_[0]])
                    tk.op("dve", lambda e: e.tensor_tensor(out=Tf.t[:], in0=Mm_[0].t[:], in1=self.ident_f.t[:].unsqueeze(1).to_broadcast([128, 8, 128]), op=ALU.add), r=[Mm_[0], self.ident_f], w=[Tf])
                steps.append(s3)

                def rnd(k):
                    def g():
                        src, dst = (k - 1) % 2, k % 2
                        M, Mt, Mn, Mtn = Mm_[src], Mt_[src], Mm_[dst], Mt_[dst]
                        lastr = (k == 5)
                        for h in range(8):
                            if not lastr:
                                pb = self.psb[0 + h // 4]
                                tk.op("pe", lambda e, pb=pb, h=h: e.matmul(pb.t[:, (h % 4) * 128:(h % 4 + 1) * 128], lhsT=Mt.t[:, h, :], rhs=M.t[:, h, :], start=True, stop=True), r=[Mt, M], w=[pb])
                            pb = self.psb[2 + h // 4]
                            tk.op("pe", lambda e, pb=pb, h=h: e.matmul(pb.t[:, (h % 4) * 128:(h % 4 + 1) * 128], lhsT=M.t[:, h, :], rhs=Mt.t[:, h, :], start=True, stop=True), r=[Mt, M], w=[pb])
                        for k2 in range(2):
                            if not lastr:
                                tk.op("act", lambda e, k2=k2: e.activation(out=Mn.t[:, k2 * 4:(k2 + 1) * 4, :], in_=self.psb[k2].t[:].rearrange("p (a m) -> p a m", a=4), func=AF.Copy), r=[self.psb[k2]], w=[Mn])
                            tk.op("dve", lambda e, k2=k2: e.tensor_copy(out=Mtn.t[:, k2 * 4:(k2 + 1) * 4, :], in_=self.psb[2 + k2].t[:].rearrange("p (a m) -> p a m", a=4)), r=[self.psb[2 + k2]], w=[Mtn])
                        for h in range(8):
                            pb = self.psb[4 + h // 4]
                            tk.op("pe", lambda e, pb=pb, h=h: e.matmul(pb.t[:, (h % 4) * 128:(h % 4 + 1) * 128], lhsT=Mtn.t[:, h, :], rhs=Tf.t[:, h, :], start=True, stop=True), r=[Mtn, Tf], w=[pb])
                        for k2 in range(2):
                            tk.op("dve", lambda e, k2=k2: e.tensor_tensor(out=Tf.t[:, k2 * 4:(k2 + 1) * 4, :], in0=Tf.t[:, k2 * 4:(k2 + 1) * 4, :], in1=self.psb[4 + k2].t[:].rearrange("p (a m) -> p a m", a=4), op=ALU.add),
                                  r=[Tf, self.psb[4 + k2]], w=[Tf])
                        if lastr:
                            tk.op("act", lambda e: e.activation(out=Tb[i2].t[:], in_=Tf.t[:], func=AF.Copy), r=[Tf], w=[Tb[i2]])
                    return g
                for k in range(1, 6):
                    steps.append(rnd(k))
                return steps

            def chain_steps(b, di, c):
                i2 = c % 2
                F, Tm, AB, AK = FMc[i2], TMt[i2], ABr[i2], AkK[i2]
                p6, p7 = self.psb[6], self.psb[7]
                BT = lambda h: Tm.t[:, h * 3 + 0, :]
                KTt = lambda h: Tm.t[:, h * 3 + 1, :]
                VT = lambda h: Tm.t[:, h * 3 + 2, :]

                def c1():
                    for h in range(8):
                        tk.op("pe", lambda e, h=h: e.matmul(p6.t[:, h * 64:(h + 1) * 64], lhsT=F.t[:, h, 0, :], rhs=Sb_.t[:, h, :], start=True, stop=False), r=[F, Sb_], w=[p6])
                        tk.op("pe", lambda e, h=h: e.matmul(p6.t[:, h * 64:(h + 1) * 64], lhsT=AK.t[:, h, 0:128], rhs=VT(h), start=False, stop=True), r=[AK, Tm], w=[p6])
                    tk.op("dve", lambda e: e.tensor_copy(out=XT.t[:], in_=p6.t[:]), r=[p6], w=[XT])

                def c2a():
                    for h in range(8):
                        tk.op("pe", lambda e, h=h: e.matmul(p7.t[:, h * 64:(h + 1) * 64], lhsT=Tb[i2].t[:, h, :], rhs=XT.t[:, h * 64:(h + 1) * 64], start=True, stop=True), r=[Tb[i2], XT], w=[p7])
                    tk.op("dve", lambda e: e.tensor_copy(out=ZT.t[:], in_=p7.t[:]), r=[p7], w=[ZT])

                def c2b():
                    for h in range(8):
                        tk.op("pe", lambda e, h=h: e.matmul(p6.t[:, h * 64:(h + 1) * 64], lhsT=Aoff[i2].t[:, h, :], rhs=ZT.t[:, h * 64:(h + 1) * 64], start=True, stop=True), r=[Aoff[i2], ZT], w=[p6])
                    tk.op("dve", lambda e: e.tensor_tensor(out=WT.t[:], in0=p6.t[:], in1=XT.t[:], op=ALU.add), r=[p6, XT], w=[WT])

                def c2c():
                    for h in range(8):
                        tk.op("pe", lambda e, h=h: e.matmul(p7.t[:, h * 64:(h + 1) * 64], lhsT=Tb[i2].t[:, h, :], rhs=WT.t[:, h * 64:(h + 1) * 64], start=True, stop=True), r=[Tb[i2], WT], w=[p7])
                    tk.op("dve", lambda e: e.tensor_copy(out=UT.t[:], in_=p7.t[:]), r=[p7], w=[UT])

                def c3():
                    for h in range(8):
                        tk.op("pe", lambda e, h=h: e.matmul(p6.t[:, h * 64:(h + 1) * 64], lhsT=F.t[:, h, 1, :], rhs=Sb_.t[:, h, :], start=True, stop=False), r=[F, Sb_], w=[p6])
                        tk.op("pe", lambda e, h=h: e.matmul(p6.t[:, h * 64:(h + 1) * 64], lhsT=AB.t[:, h, 128:256], rhs=UT.t[:, h * 64:(h + 1) * 64], start=False, stop=False), r=[AB, UT], w=[p6])
                        tk.op("pe", lambda e, h=h: e.matmul(p6.t[:, h * 64:(h + 1) * 64], lhsT=AK.t[:, h, 128:256], rhs=VT(h), start=False, stop=True), r=[AK, Tm], w=[p6])
                    tk.op("act", lambda e: e.activation(out=Yt[di].t[:, c, :], in_=p6.t[:], func=AF.Copy), r=[p6], w=[Yt[di]])
                    for h in range(8):
                        tk.op("pe", lambda e, h=h: e.matmul(p7.t[0:64, h * 64:(h + 1) * 64], lhsT=BT(h), rhs=UT.t[:, h * 64:(h + 1) * 64], start=True, stop=False), r=[Tm, UT], w=[p7])
                        tk.op("pe", lambda e, h=h: e.matmul(p7.t[0:64, h * 64:(h + 1) * 64], lhsT=KTt(h), rhs=VT(h), start=False, stop=True), r=[Tm], w=[p7])

                def c4():
                    for h in range(8):
                        tk.op("pool", lambda e, h=h: e.tensor_scalar(out=St.t[:, h, :], in0=Sf.t[:, h, :], scalar1=gamt.t[:, h, c:c + 1], scalar2=None, op0=ALU.mult), r=[Sf, gamt], w=[St])
                    for h in range(8):
                        tk.op("dve", lambda e, h=h: e.scalar_tensor_tensor(out=Sf.t[:, h, :], in0=p7.t[0:64, h * 64:(h + 1) * 64], scalar=gamt.t[:, h, c:c + 1], in1=St.t[:, h, :],
                                                                           op0=ALU.mult, op1=ALU.add), r=[p7, gamt, St], w=[Sf])
                    tk.op("act", lambda e: e.activation(out=Sb_.t[:], in_=Sf.t[:], func=AF.Copy), r=[Sf], w=[Sb_])
                def dbg():
                    if "rw_dbg" in self.dump and b == 0 and di == 0 and c == 2 and l == 0:
                        d = self.dump_t("rw_dbg", [128, 4 * 512 + 1024 + 512], BF16)
                        for i_, tt_ in enumerate((XT, ZT, WT, UT)):
                            tk.dma("sp", d.t.ap()[:, i_ * 512:(i_ + 1) * 512], tt_.t[:], r=[tt_], w=[d])
                        tk.dma("sp", d.t.ap()[:, 2048:3072], Tb[i2].t[:].rearrange("p a m -> p (a m)"), r=[Tb[i2]], w=[d])
                        tk.dma("sp", d.t.ap()[0:64, 3072:3584], Sb_.t[:].rearrange("p a m -> p (a m)"), r=[Sb_], w=[d])
                return [c1, c2a, c2b, c2c, lambda: (dbg(), c3()), c4]

            yfm = [sb(f"yfm{i}", [128, LS]) for i in range(4)]
            bvt = sb("bvt", [128, LS])
            lnp = sb("lnp", [128, 4, 2])
            bd64 = sb("bd64", [128, 128])
            e64 = sb("e64", [128, 1])
            tk.dma("sp", lnp.t[:], self.rw_lnT.t.ap()[l], w=[lnp])
            tk.dma("sp", bd64.t[:], self.rw_bd.t.ap(), w=[bd64])
            tk.op("pool", lambda e: e.memset(e64.t[:], 64e-5), w=[e64])
            for b in range(NB):
                for di in range(2):
                    tk.dma("sp", gamt.t[:], self.rw_g.t.ap()[b, di].rearrange("h j c -> j h c"), r=[self.rw_g], w=[gamt])
                    tk.op("pool", lambda e: e.memset(Sf.t[:], 0.0), w=[Sf])
                    tk.op("pool", lambda e: e.memset(Sb_.t[:], 0.0), w=[Sb_])
                    for st in pre_steps(b, di, 0):
                        st()
                    for c in range(NCH):
                        cs = chain_steps(b, di, c)
                        ps_ = pre_steps(b, di, c + 1) if c + 1 < NCH else []
                        order = [ps_[0:1], cs[0:1], ps_[1:3], cs[1:2], ps_[3:5], cs[2:3], ps_[5:6], cs[3:4], ps_[6:7], cs[4:5], ps_[7:8], cs[5:6], ps_[8:]]
                        for grp in order:
                            for st in grp:
                                st()
                if "rw_yt" in self.dump and b == 0 and l == 0:
                    dy = self.dump_t("rw_yt", [2, 128, NCH * 512], BF16)
                    for di in range(2):
                        tk.dma("sp", dy.t.ap()[di], Yt[di].t[:].rearrange("p c f -> p (c f)"), r=[Yt[di]], w=[dy])
                pi = 0
                for q in range(4):
                    for n4 in range(0, NCH, 4):
                        pb = self.psb[pi % 4]
                        pi += 1
                        for n in range(n4, min(n4 + 4, NCH)):
                            cb = (1 - n) if n < 2 else (2 + (NCH - 1 - n))
                            tk.op("pe", lambda e, pb=pb, n=n, n4=n4, q=q: e.matmul(pb.t[:, (n - n4) * 128:(n - n4 + 1) * 128], lhsT=Yt[0].t[:, n, q * 128:(q + 1) * 128], rhs=idb.t[:], start=True, stop=False),
                                  r=[Yt[0], idb], w=[pb])
                            tk.op("pe", lambda e, pb=pb, n=n, n4=n4, q=q, cb=cb: e.matmul(pb.t[:, (n - n4) * 128:(n - n4 + 1) * 128], lhsT=Yt[1].t[:, cb, q * 128:(q + 1) * 128], rhs=Jm.t[:], start=False, stop=True),
                                  r=[Yt[1], Jm], w=[pb])
                        nn = min(4, NCH - n4)
                        tk.op("act", lambda e, pb=pb, n4=n4, nn=nn, q=q: e.activation(out=yfm[q].t[:, n4 * 128:(n4 + nn) * 128], in_=pb.t[:, 0:nn * 128], func=AF.Copy), r=[pb], w=[yfm[q]])
                    tk.dma("sp", bvt.t[:], self.rw_bv.t.ap()[b, q * 128:(q + 1) * 128, :], r=[self.rw_bv], w=[bvt])
                    for (t0, n) in PIECES:
                        pb, pb2 = self.psb[4 + pi % 4], self.psb[4 + (pi + 1) % 4]
                        pi += 2
                        yv = yfm[q].t[:, t0:t0 + n]
                        tk.op("pe", lambda e, pb=pb, yv=yv, n=n: e.matmul(pb.t[:, 0:n], lhsT=bd64.t[:], rhs=yv, start=True, stop=True), r=[bd64, yfm[q]], w=[pb])
                        tk.op("dve", lambda e, pb=pb, yv=yv, n=n: e.tensor_tensor(out=yv, in0=yv, in1=pb.t[:, 0:n], op=ALU.subtract), r=[yfm[q], pb], w=[yfm[q]])
                        tk.op("act", lambda e, yv=yv, n=n, t0=t0: e.activation(out=Tf.t[:].rearrange("p a m -> p (a m)")[:, 0:n], in_=yv, func=AF.Square), r=[yfm[q]], w=[Tf])
                        tk.op("pe", lambda e, pb2=pb2, n=n: e.matmul(pb2.t[:, 0:n], lhsT=bd64.t[:], rhs=Tf.t[:].rearrange("p a m -> p (a m)")[:, 0:n], start=True, stop=True), r=[bd64, Tf], w=[pb2])
                        tk.op("act", lambda e, pb2=pb2, n=n: e.activation(out=Tf.t[:].rearrange("p a m -> p (a m)")[:, 512:512 + n], in_=pb2.t[:, 0:n], func=AF.Sqrt, bias=e64.t[:], scale=1.0), r=[pb2, e64], w=[Tf])
                        tk.op("dve", lambda e, n=n: e.reciprocal(out=Tf.t[:].rearrange("p a m -> p (a m)")[:, 512:512 + n], in_=Tf.t[:].rearrange("p a m -> p (a m)")[:, 512:512 + n]), r=[Tf], w=[Tf])
                        tk.op("dve", lambda e, yv=yv, n=n: e.tensor_tensor(out=yv, in0=yv, in1=Tf.t[:].rearrange("p a m -> p (a m)")[:, 512:512 + n], op=ALU.mult), r=[yfm[q], Tf], w=[yfm[q]])
                    tk.op("act", lambda e, q=q: e.activation(out=yfm[q].t[:], in_=yfm[q].t[:], func=AF.Identity, bias=lnp.t[:, q, 1:2], scale=lnp.t[:, q, 0:1]), r=[yfm[q], lnp], w=[yfm[q]])
                    tk.op("dve", lambda e, q=q: e.tensor_tensor(out=yfm[q].t[:], in0=yfm[q].t[:], in1=bvt.t[:], op=ALU.add), r=[yfm[q], bvt], w=[yfm[q]])
                self.finish_branch(l, es, 2, b, yfm, False)

    def stage_outproj(self, l):
        tk = self.tk
        TG = 768
        last = (l == DEPTH - 1)
        with ExitStack() as es:
            wo = tk.sb(es, "wo", [128, 16, D], BF16)
            self.gate_b = tk.sb(es, "gate_b", [128, 3, D], F32)
            grep = [tk.sb(es, f"grep{i}", [128, 128], F32) for i in range(2)]
            i = 0
            for j in range(3):
                for n in range(16):
                    g = grep[i % 2]
                    pg = self.psb[1 + (i // 4) % 2]
                    tk.op("dve", lambda e, g=g, n=n, j=j: e.tensor_scalar(out=g.t[:], in0=self.ones_f.t[:], scalar1=self.modT.t[:, 32 + n, j:j + 1],
                                                                          scalar2=None, op0=ALU.mult), r=[self.ones_f, self.modT], w=[g])
                    tk.op("pe", lambda e, g=g, pg=pg, n=n: e.matmul(pg.t[:, (n % 4) * 128:(n % 4 + 1) * 128], lhsT=g.t[:], rhs=self.ident_f.t[:],
                                                                    start=True, stop=True), r=[g, self.ident_f], w=[pg])
                    if n % 4 == 3:
                        tk.op("act", lambda e, pg=pg, n=n, j=j: e.activation(out=self.gate_b.t[:, j, (n // 4) * 512:(n // 4 + 1) * 512], in_=pg.t[:],
                                                                             func=AF.Copy), r=[pg], w=[self.gate_b])
                    i += 1


            ymT = [tk.sb(es, f"ymT{i}", [128, 16, TG], BF16) for i in range(2)]
            xt = [tk.sb(es, f"oxt{i}", [128, D], F32) for i in range(2)]
            tmp = [tk.sb(es, f"otmp{i}", [128, 512], F32) for i in range(2)]
            for q4 in range(4):
                tk.dma("pool", wo.t[:, :, q4 * 512:(q4 + 1) * 512],
                       self.w_out.t.ap()[l, :, q4 * 512:(q4 + 1) * 512].rearrange("(k p) n -> p k n", p=128), w=[wo])
            pi = 0
            blk = 0
            for tg in range(6):
                b, p0 = tg // 3, (tg % 3) * TG
                y = ymT[tg % 2]
                tk.dma("sp", y.t[:], self.ym.t.ap()[:, b, p0:p0 + TG].rearrange("(k p) t -> p k t", p=128), r=[self.ym], w=[y])
                for kb in range(6):
                    pos = p0 + kb * 128
                    seg = 0 if pos < LCTX else 1
                    if seg == 0 and last:
                        continue
                    j = 2 if seg == 0 else b
                    src = self.x_src(l, b, seg)
                    dst = self.xc if seg == 0 else self.out
                    row0 = pos if seg == 0 else pos - LCTX
                    x_t = xt[blk % 2]
                    blk += 1
                    tk.dma("sp", x_t.t[:], src.t.ap()[b, row0:row0 + 128, :], r=[src], w=[x_t])
                    for q4 in range(4):
                        pb = self.psb[pi % 8]
                        tm = tmp[pi % 2]
                        pi += 1
                        for k in range(16):
                            tk.op("pe", lambda e, pb=pb, k=k, kb=kb, q4=q4, y=y: e.matmul(
                                pb.t[:], lhsT=y.t[:, k, kb * 128:(kb + 1) * 128], rhs=wo.t[:, k, q4 * 512:(q4 + 1) * 512],
                                start=(k == 0), stop=(k == 15)), r=[wo, y], w=[pb])
                        tk.op("dve", lambda e, pb=pb, tm=tm, j=j, q4=q4: e.tensor_tensor(out=tm.t[:], in0=pb.t[:], in1=self.gate_b.t[:, j, q4 * 512:(q4 + 1) * 512],
                                                                                         op=ALU.mult), r=[pb, self.gate_b], w=[tm])
                        tk.op("pool", lambda e, tm=tm, x_t=x_t, q4=q4: e.tensor_tensor(out=x_t.t[:, q4 * 512:(q4 + 1) * 512], in0=x_t.t[:, q4 * 512:(q4 + 1) * 512],
                                                                                      in1=tm.t[:], op=ALU.add), r=[tm, x_t], w=[x_t])
                    tk.dma("sp", dst.t.ap()[b, row0:row0 + 128, :], x_t.t[:], r=[x_t], w=[dst])


def host_inputs(inputs, core):
    f = lambda a: np.ascontiguousarray(np.asarray(a, dtype=np.float32))
    b0 = core * NB
    m = {}
    m["x"] = f(inputs["x"][b0:b0 + NB])
    m["ctx"] = f(inputs["ctx"][b0:b0 + NB])
    cv = np.stack([np.asarray(inputs["c"][b0]), np.asarray(inputs["c"][b0 + 1]), np.asarray(inputs["c_ctx"])], axis=-1)
    m["cT"] = f(cv.reshape(16, 128, 3).transpose(1, 0, 2))
    m["normgT"] = f(np.asarray(inputs["norm_g"]).reshape(DEPTH, 16, 128).transpose(0, 2, 1))
    m["w_ada"] = f(inputs["w_ada"])
    m["b_adaT"] = f(np.asarray(inputs["b_ada"]).reshape(DEPTH, 48, 128).transpose(0, 2, 1))
    m["w_in"] = f(inputs["w_in"])
    m["w_out"] = f(inputs["w_out"])
    L = DEPTH
    lre = np.asarray(inputs["s5_lam_re"]).reshape(L, 1, 4096)
    lim = np.asarray(inputs["s5_lam_im"]).reshape(L, 1, 4096)
    stp = np.repeat(np.asarray(inputs["s5_log_step"]).reshape(L, 64, 1), 64, axis=2).reshape(L, 1, 4096)
    m["s5_lreB"] = f(np.broadcast_to(lre, (L, 128, 4096)))
    m["s5_limB"] = f(np.broadcast_to(lim, (L, 128, 4096)))
    m["s5_stpB"] = f(np.broadcast_to(stp, (L, 128, 4096)))
    def padb(bx):
        o = np.zeros((L, 8, 16, 2, 32, 64), np.float32)
        bt = np.asarray(bx).transpose(0, 4, 1, 2, 3)
        for g in range(32):
            o[:, g % 8, :, :, g, :] = bt[:, :, :, g, :]
        return f(o.reshape(L, 128, 4096))
    m["s5_bre"] = padb(inputs["s5_b_re"])
    m["s5_bim"] = padb(inputs["s5_b_im"])
    def padc(cx):
        o = np.zeros((L, 64, 2, 32, 8, 16), np.float32)
        ct = np.asarray(cx).transpose(0, 4, 1, 2, 3)
        for g in range(32):
            o[:, :, :, g, g % 8, :] = ct[:, :, :, g, :]
        return o.reshape(L, 64, 8192)
    cre, cim = padc(inputs["s5_c_re"]), padc(inputs["s5_c_im"])
    m["s5_c1"] = f(np.concatenate([cre, cim], axis=1))
    m["s5_c2"] = f(np.concatenate([cim, cre], axis=1))
    lreP = np.asarray(inputs["s5_lam_re"]).reshape(L, 64, 64).transpose(0, 2, 1)
    limP = np.asarray(inputs["s5_lam_im"]).reshape(L, 64, 64).transpose(0, 2, 1)
    m["s5_lreP"] = f(np.concatenate([lreP, lreP], axis=1))
    m["s5_limP"] = f(np.concatenate([limP, limP], axis=1))
    m["s5_stpP"] = f(np.broadcast_to(np.asarray(inputs["s5_log_step"]).reshape(L, 1, 64), (L, 128, 64)))
    m["s5_dT"] = f(np.asarray(inputs["s5_d"]).reshape(L, 4, 128).transpose(0, 2, 1))
    m["s5_glu_w"] = f(inputs["s5_glu_w"])
    m["s5_glu_bT"] = f(np.asarray(inputs["s5_glu_b"]).reshape(L, 4, 128).transpose(0, 2, 1))
    m["branch_gT"] = f(np.asarray(inputs["branch_g"]).reshape(L, 3, 4, 128).transpose(0, 1, 3, 2))
    tpos = np.arange(LLAT)
    rowp, colp = (tpos // 64).astype(np.float32), (tpos % 64).astype(np.float32)
    inv = (1.0 / (10000.0 ** (np.arange(16, dtype=np.float32) / 16))).astype(np.float32)
    ang_r, ang_c = rowp[:, None] * inv[None, :], colp[:, None] * inv[None, :]
    cosT = np.concatenate([np.cos(ang_r), np.cos(ang_r), np.cos(ang_c), np.cos(ang_c)], axis=1)
    sinT = np.concatenate([-np.sin(ang_r), np.sin(ang_r), -np.sin(ang_c), np.sin(ang_c)], axis=1)
    m["ropeC"] = f(cosT.reshape(16, 128, 64).transpose(1, 0, 2))
    m["ropeS"] = f(sinT.reshape(16, 128, 64).transpose(1, 0, 2))
    gq, gk = np.asarray(inputs["att_q_g"]), np.asarray(inputs["att_k_g"])
    gqk = np.concatenate([np.tile(gq, (1, 8)), np.tile(gk, (1, 2))], axis=1)
    m["att_gqk"] = f(np.broadcast_to(gqk[:, None, :], (L, 128, 640)))
    m["att_sinkB"] = f(np.broadcast_to(np.asarray(inputs["att_sink"])[:, None, :], (L, 128, 8)))
    kk_, qq_ = np.arange(128)[:, None], np.arange(128)[None, :]
    m["maskA"] = (kk_ >= qq_).astype(np.float32).astype(ml_dtypes.bfloat16)
    m["maskC"] = (kk_ <= qq_).astype(np.float32).astype(ml_dtypes.bfloat16)
    m["hy_w1"] = f(inputs["hy_w1"]); m["hy_w2"] = f(inputs["hy_w2"]); m["hy_w3"] = f(inputs["hy_w3"])
    m["hy_prm"] = f(np.stack([inputs["hy_b1"], inputs["hy_f1"], inputs["hy_b2"], inputs["hy_f2"]], axis=-1))
    m["hy_skipT"] = f(np.asarray(inputs["hy_skip"]).reshape(L, 2, 4, 128).transpose(0, 3, 1, 2).reshape(L, 128, 8))
    cwt = np.concatenate([np.asarray(inputs["hy_conv_w"]), np.asarray(inputs["hy_conv_b"])[:, None, :]], axis=1)
    m["hy_convT"] = f(cwt.reshape(L, 4, 12, 128).transpose(0, 3, 2, 1))
    deltas = np.abs(np.linspace(math.log(1e-2) / 1.5, math.log(1e-2) / 0.3, 512, dtype=np.float32))
    for nm, n in (("L", LLAT), ("C", LCTX)):
        lag = np.arange(n)
        t = np.linspace(0.0, 1.0, n, dtype=np.float32)
        ang = (2.0 * math.pi * np.arange(n, dtype=np.float32) / n).astype(np.float32)
        bands = np.linspace(1e-4, 15, 16, dtype=np.float32)[None, :]
        z = np.concatenate([t[:, None], np.cos(bands * ang[:, None]), -np.sin(bands * ang[:, None])], axis=-1).astype(np.float32)
        dec = np.exp(-t[:, None] * deltas[None, :]).astype(np.float32)
        mm = np.arange(2 * n)
        lag_of = np.where(mm >= n, mm - n, n - mm)
        lag_of[0] = 0
        lag_r = lag_of[2 * n - 1 - mm]
        m["hy_z" + nm] = f(z[lag_of].T); m["hy_z" + nm + "r"] = f(z[lag_r].T)
        m["hy_d" + nm] = f(dec[lag_of].T); m["hy_d" + nm + "r"] = f(dec[lag_r].T)
    mp, mn = np.asarray(inputs["rw_mu_prev"]), np.asarray(inputs["rw_mu_next"])
    muH = np.stack([mp[:, :1536], mn[:, :1536]], axis=-1).reshape(L, 24, 64, 2).transpose(0, 2, 1, 3)
    m["rw_muH"] = f(muH)
    m["rw_muL"] = f(np.stack([mp[:, 1536:], mn[:, 1536:]], axis=-1))
    hh_ = lambda a: np.asarray(a).reshape(L, 8, 64).transpose(0, 2, 1)
    w0, a0 = np.asarray(inputs["rw_w0"]), np.asarray(inputs["rw_a0"])
    m["rw_hp"] = f(np.stack([hh_(inputs["rw_k_k"]), hh_(inputs["rw_k_a"]), hh_(inputs["rw_r_k"]), hh_(w0[:, 0]), hh_(w0[:, 1]), hh_(a0[:, 0]), hh_(a0[:, 1])], axis=-1))
    w2p = np.zeros((L, 2, 128, 512), np.float32); a2p = np.zeros((L, 2, 128, 512), np.float32)
    for di in range(2):
        w2p[:, di, di * 32:(di + 1) * 32] = np.asarray(inputs["rw_w2"])[:, di]
        a2p[:, di, (2 + di) * 32:(3 + di) * 32] = np.asarray(inputs["rw_a2"])[:, di]
    m["rw_w2pad"] = w2p; m["rw_a2pad"] = a2p
    m["rw_lnT"] = f(np.stack([np.asarray(inputs["rw_ln_g"]).reshape(L, 4, 128).transpose(0, 2, 1), np.asarray(inputs["rw_ln_b"]).reshape(L, 4, 128).transpose(0, 2, 1)], axis=-1))
    m["rw_mask"] = np.ascontiguousarray(np.broadcast_to((np.arange(LS) % 128 != 0).astype(np.float32), (64, LS))).astype(ml_dtypes.bfloat16)
    ss_, tt_ = np.arange(128)[:, None], np.arange(128)[None, :]
    m["rw_mk2"] = np.concatenate([(ss_ < tt_), (ss_ <= tt_)], axis=1).astype(np.float32).astype(ml_dtypes.bfloat16)
    bd_ = (ss_ // 64) == (tt_ // 64)
    m["rw_mkL"] = ((ss_ > tt_) & bd_).astype(np.float32).astype(ml_dtypes.bfloat16)
    m["rw_mkBD"] = bd_.astype(np.float32).astype(ml_dtypes.bfloat16)
    m["rw_mkOFF"] = ((ss_ < 64) & (tt_ >= 64)).astype(np.float32).astype(ml_dtypes.bfloat16)
    m["rw_J"] = np.ascontiguousarray(np.eye(128, dtype=np.float32)[::-1]).astype(ml_dtypes.bfloat16)
    m["rw_bd"] = f(np.kron(np.eye(2, dtype=np.float32), np.full((64, 64), 1.0 / 64, np.float32)))
    tt = np.arange(LS)
    m["tabA"] = np.ascontiguousarray(np.broadcast_to((tt // 64).astype(np.float32), (128, LS))).astype(ml_dtypes.bfloat16)
    m["tabB"] = np.ascontiguousarray(np.broadcast_to((tt % 64).astype(np.float32), (128, LS))).astype(ml_dtypes.bfloat16)
    return m


_CACHE = {}


def kernel(**inputs):
    if "nc" not in _CACHE:
        p = Prog()
        _CACHE["nc"] = p.build()
        _CACHE["p"] = p
    nc, p = _CACHE["nc"], _CACHE["p"]
    maps = []
    for c in range(N_CORES):
        m = host_inputs(inputs, c)
        maps.append({k: m[k] for k in p.inputs})
    res = run_bass_kernel_spmd(nc, maps, core_ids=list(range(N_CORES)))
    out = np.concatenate([np.asarray(r["out"]) for r in res.results], axis=0)
    return out.astype(np.float32)
```
